# Optimizing a Trainium2 kernel written in Bass

```python
import math
import jax, jax.numpy as jnp
from jax import lax
import numpy as np

D_MODEL = 4096
BATCH = 4
SEQ = 2048
DEPTH = 1
DEC_BATCH = 128
DEC_SEQ = 8
PAST_LEN = 16384
PAGE_SIZE = 128

RET_HEADS = 16
RET_DK = 128
RET_DV = 256
RET_QK = RET_HEADS * RET_DK
RET_V = RET_HEADS * RET_DV
RET_CHUNK = 128
ROPE_BASE = 10000.0

S5_WIDTH = D_MODEL // 2
S5_GROUP = 16
S5_GROUPS = S5_WIDTH // S5_GROUP
S5_STATE = 64

X_HEADS = 4
X_WIDTH = D_MODEL // 2
X_HD = X_WIDTH // X_HEADS
MEM_LEN = 256

DN_ALPHA = (2.0 * DEPTH) ** 0.25
DN_BETA = (8.0 * DEPTH) ** -0.25
LN_EPS = 1e-5
GN_EPS = 1e-5

IN_SPLIT_SIZES = (RET_QK, RET_QK, RET_V, RET_V, S5_WIDTH, S5_WIDTH, X_WIDTH, X_WIDTH, D_MODEL, D_MODEL, D_MODEL)
IN_WIDTH = 2 * RET_QK + 2 * RET_V + 2 * S5_WIDTH + 2 * X_WIDTH + 3 * D_MODEL

kernel_name = "retnet_s5_memory_hybrid_step"

F32 = jnp.float32


def _split_points():
    pts, acc = [], 0
    for s in IN_SPLIT_SIZES[:-1]:
        acc += s
        pts.append(acc)
    return pts


def layer_norm(x, g, b):
    xf = x.astype(F32)
    mu = jnp.mean(xf, -1, keepdims=True)
    var = jnp.mean(jnp.square(xf - mu), -1, keepdims=True)
    return ((xf - mu) * lax.rsqrt(var + LN_EPS) * g.astype(F32) + b.astype(F32)).astype(x.dtype)


def head_norm(o):
    of = o.astype(F32)
    mu = jnp.mean(of, -1, keepdims=True)
    var = jnp.mean(jnp.square(of - mu), -1, keepdims=True)
    return ((of - mu) * lax.rsqrt(var + GN_EPS)).astype(o.dtype)


def rotary(x, pos):
    half = RET_DK // 2
    inv = 1.0 / (ROPE_BASE ** (jnp.arange(half, dtype=F32) / half))
    ang = pos.astype(F32)[:, None] * inv[None, :]
    cos = jnp.cos(ang)[None, :, None, :]
    sin = jnp.sin(ang)[None, :, None, :]
    xr = x.astype(F32).reshape(x.shape[:-1] + (half, 2))
    x0, x1 = xr[..., 0], xr[..., 1]
    out = jnp.stack([x0 * cos - x1 * sin, x0 * sin + x1 * cos], -1)
    return out.reshape(x.shape).astype(x.dtype)


def ret_log_decay():
    return jnp.log1p(-jnp.exp2(-5.0 - jnp.arange(RET_HEADS, dtype=F32)))


def retention(q, k, v, s0, chunk):
    B, L = q.shape[0], q.shape[1]
    nc = L // chunk
    lg = ret_log_decay()
    idx = jnp.arange(chunk, dtype=F32)
    rel = idx[:, None] - idx[None, :]
    inner = jnp.where(rel[None] >= 0, jnp.exp(lg[:, None, None] * jnp.maximum(rel, 0.0)[None]), 0.0)
    xi = jnp.exp(lg[:, None] * (idx + 1.0)).T[None, :, :, None]
    zeta = jnp.exp(lg[:, None] * (chunk - 1.0 - idx))
    g_chunk = jnp.exp(lg * chunk)[None, :, None, None]

    def to_chunks(t):
        t = t.astype(F32)
        return t.reshape((B, nc, chunk) + t.shape[2:]).swapaxes(0, 1)

    def step(s, inp):
        qc, kc, vc = inp
        sc = jnp.einsum('bihd,bjhd->bhij', qc, kc) * inner[None]
        o = jnp.einsum('bhij,bjhe->bihe', sc, vc) + jnp.einsum('bihd,bhde->bihe', qc, s) * xi
        s_new = g_chunk * s + jnp.einsum('bjhd,bjhe,hj->bhde', kc, vc, zeta)
        return s_new, o

    s_fin, o = lax.scan(step, s0.astype(F32), (to_chunks(q), to_chunks(k), to_chunks(v)))
    o = o.swapaxes(0, 1).reshape(B, L, RET_HEADS, RET_DV)
    return o.astype(v.dtype), s_fin.astype(s0.dtype)


def s5_discretize(a_re, a_im, log_step, b_re, b_im):
    dt = jnp.exp(log_step.astype(F32))[:, None]
    ar, ai = a_re.astype(F32), a_im.astype(F32)
    mag = jnp.exp(dt * ar)
    abar_re = mag * jnp.cos(dt * ai)
    abar_im = mag * jnp.sin(dt * ai)
    den = ar * ar + ai * ai
    x_re = abar_re - 1.0
    f_re = (x_re * ar + abar_im * ai) / den
    f_im = (abar_im * ar - x_re * ai) / den
    br, bi = b_re.astype(F32), b_im.astype(F32)
    bbar_re = f_re[..., None] * br - f_im[..., None] * bi
    bbar_im = f_re[..., None] * bi + f_im[..., None] * br
    return abar_re, abar_im, bbar_re, bbar_im


def s5_branch(u, h0_re, h0_im, a_re, a_im, log_step, b_re, b_im, c_re, c_im, d_skip):
    B, L = u.shape[0], u.shape[1]
    abar_re, abar_im, bbar_re, bbar_im = s5_discretize(a_re, a_im, log_step, b_re, b_im)
    uf = u.astype(F32)
    ug = uf.reshape(B, L, S5_GROUPS, S5_GROUP)
    bu_re = jnp.einsum('blgp,gnp->blgn', ug, bbar_re)
    bu_im = jnp.einsum('blgp,gnp->blgn', ug, bbar_im)
    h0r, h0i = h0_re.astype(F32), h0_im.astype(F32)
    bu_re = bu_re.at[:, 0].add(abar_re * h0r - abar_im * h0i)
    bu_im = bu_im.at[:, 0].add(abar_re * h0i + abar_im * h0r)
    a_r = jnp.broadcast_to(abar_re, bu_re.shape)
    a_i = jnp.broadcast_to(abar_im, bu_im.shape)

    def combine(e1, e2):
        a1r, a1i, b1r, b1i = e1
        a2r, a2i, b2r, b2i = e2
        return (a1r * a2r - a1i * a2i,
                a1r * a2i + a1i * a2r,
                a2r * b1r - a2i * b1i + b2r,
                a2r * b1i + a2i * b1r + b2i)

    _, _, h_re, h_im = lax.associative_scan(combine, (a_r, a_i, bu_re, bu_im), axis=1)
    y = (jnp.einsum('blgn,gpn->blgp', h_re, c_re.astype(F32))
         - jnp.einsum('blgn,gpn->blgp', h_im, c_im.astype(F32)))
    y = y.reshape(B, L, S5_WIDTH) + d_skip.astype(F32) * uf
    return y.astype(u.dtype), h_re[:, -1].astype(h0_re.dtype), h_im[:, -1].astype(h0_im.dtype)


def cross_attend(q, mk, mv):
    s = jnp.einsum('blhd,bmhd->bhlm', q, mk).astype(F32) * (X_HD ** -0.5)
    p = jax.nn.softmax(s, axis=-1).astype(mv.dtype)
    return jnp.einsum('bhlm,bmhd->blhd', p, mv)


def memory_kv(mem, w_mem_kv):
    B = mem.shape[0]
    kv = mem @ w_mem_kv
    mk, mv = jnp.split(kv, 2, axis=-1)
    return mk.reshape(B, MEM_LEN, X_HEADS, X_HD), mv.reshape(B, MEM_LEN, X_HEADS, X_HD)


def mixer_layer(x, pos, s_ret, h_re, h_im, mem_k, mem_v, chunk,
                w_in, a_re, a_im, log_step, b_re, b_im, c_re, c_im, d_skip, w_glu,
                w_proj_a, w_proj_b, w_proj_c, w_out, ln_g, ln_b):
    B, L = x.shape[0], x.shape[1]
    z = x @ w_in
    q, k, v, g_ret, u, g_s5, qx, g_x, m_a, m_b, m_c = jnp.split(z, _split_points(), axis=-1)

    q = rotary(q.reshape(B, L, RET_HEADS, RET_DK), pos)
    k = rotary(k.reshape(B, L, RET_HEADS, RET_DK), pos) * (RET_DK ** -0.5)
    v = v.reshape(B, L, RET_HEADS, RET_DV)
    o_ret, s_ret_new = retention(q, k, v, s_ret, chunk)
    o_ret = head_norm(o_ret).reshape(B, L, RET_V) * jax.nn.silu(g_ret)

    y_s5, h_re_new, h_im_new = s5_branch(u, h_re, h_im, a_re, a_im, log_step, b_re, b_im, c_re, c_im, d_skip)
    gl = jax.nn.gelu(y_s5)
    glu_a, glu_b = jnp.split(gl @ w_glu, 2, axis=-1)
    o_s5 = glu_a * jax.nn.sigmoid(glu_b) * jax.nn.silu(g_s5)

    o_x = cross_attend(qx.reshape(B, L, X_HEADS, X_HD), mem_k, mem_v).reshape(B, L, X_WIDTH) * jax.nn.silu(g_x)

    merged = (jax.nn.sigmoid(m_a) * (o_ret @ w_proj_a)
              + jax.nn.sigmoid(m_b) * (o_s5 @ w_proj_b)
              + jax.nn.sigmoid(m_c) * (o_x @ w_proj_c))
    out = merged @ w_out
    x_new = layer_norm(DN_ALPHA * x + out, ln_g, ln_b)
    return x_new, s_ret_new, h_re_new, h_im_new


def setup_inputs(seed: int = 0) -> dict:
    key = jax.random.key(seed)
    ks = jax.random.split(key, 32)
    nrm = jax.random.normal
    Dp = DEPTH
    inp = {}
    inp["x_prompt"] = nrm(ks[0], (BATCH, SEQ, D_MODEL), F32)
    inp["x_sample"] = nrm(ks[1], (DEC_BATCH, DEC_SEQ, D_MODEL), F32)
    inp["mem_prompt"] = nrm(ks[2], (BATCH, MEM_LEN, D_MODEL), F32)
    inp["state_ret"] = nrm(ks[3], (Dp, DEC_BATCH, RET_HEADS, RET_DK, RET_DV), F32)
    inp["state_s5_re"] = 0.1 * nrm(ks[4], (Dp, DEC_BATCH, S5_GROUPS, S5_STATE), F32)
    inp["state_s5_im"] = 0.1 * nrm(ks[5], (Dp, DEC_BATCH, S5_GROUPS, S5_STATE), F32)
    inp["cache_mem_k"] = nrm(ks[6], (Dp, DEC_BATCH, MEM_LEN, X_HEADS, X_HD), F32)
    inp["cache_mem_v"] = nrm(ks[7], (Dp, DEC_BATCH, MEM_LEN, X_HEADS, X_HD), F32)
    inp["w_in"] = nrm(ks[8], (Dp, D_MODEL, IN_WIDTH), F32) * (D_MODEL ** -0.5)
    inp["w_mem_kv"] = nrm(ks[9], (Dp, D_MODEL, 2 * X_WIDTH), F32) * (D_MODEL ** -0.5)
    inp["s5_a_re"] = -0.5 + 0.01 * nrm(ks[10], (Dp, S5_GROUPS, S5_STATE), F32)
    inp["s5_a_im"] = (jnp.pi * jnp.arange(S5_STATE, dtype=F32))[None, None, :] + 0.01 * nrm(ks[11], (Dp, S5_GROUPS, S5_STATE), F32)
    inp["s5_log_step"] = jax.random.uniform(ks[12], (Dp, S5_GROUPS), F32, minval=math.log(1e-3), maxval=math.log(1e-1))
    inp["s5_b_re"] = nrm(ks[13], (Dp, S5_GROUPS, S5_STATE, S5_GROUP), F32) * ((2.0 * S5_GROUP) ** -0.5)
    inp["s5_b_im"] = nrm(ks[14], (Dp, S5_GROUPS, S5_STATE, S5_GROUP), F32) * ((2.0 * S5_GROUP) ** -0.5)
    inp["s5_c_re"] = nrm(ks[15], (Dp, S5_GROUPS, S5_GROUP, S5_STATE), F32) * ((2.0 * S5_STATE) ** -0.5)
    inp["s5_c_im"] = nrm(ks[16], (Dp, S5_GROUPS, S5_GROUP, S5_STATE), F32) * ((2.0 * S5_STATE) ** -0.5)
    inp["s5_d"] = nrm(ks[17], (Dp, S5_WIDTH), F32)
    inp["w_glu"] = nrm(ks[18], (Dp, S5_WIDTH, 2 * S5_WIDTH), F32) * (S5_WIDTH ** -0.5)
    inp["w_proj_a"] = nrm(ks[19], (Dp, RET_V, D_MODEL), F32) * (RET_V ** -0.5) * DN_BETA
    inp["w_proj_b"] = nrm(ks[20], (Dp, S5_WIDTH, D_MODEL), F32) * (S5_WIDTH ** -0.5) * DN_BETA
    inp["w_proj_c"] = nrm(ks[21], (Dp, X_WIDTH, D_MODEL), F32) * (X_WIDTH ** -0.5) * DN_BETA
    inp["w_out"] = nrm(ks[22], (Dp, D_MODEL, D_MODEL), F32) * (D_MODEL ** -0.5) * DN_BETA
    inp["ln_g"] = 1.0 + 0.01 * nrm(ks[23], (Dp, D_MODEL), F32)
    inp["ln_b"] = 0.01 * nrm(ks[24], (Dp, D_MODEL), F32)
    return inp


def reference(x_prompt, x_sample, mem_prompt, state_ret, state_s5_re, state_s5_im, cache_mem_k, cache_mem_v,
              w_in, w_mem_kv, s5_a_re, s5_a_im, s5_log_step, s5_b_re, s5_b_im, s5_c_re, s5_c_im, s5_d, w_glu,
              w_proj_a, w_proj_b, w_proj_c, w_out, ln_g, ln_b):
    pos_p = jnp.arange(SEQ, dtype=jnp.int32)
    pos_s = PAST_LEN + jnp.arange(DEC_SEQ, dtype=jnp.int32)
    chunk_p = min(RET_CHUNK, SEQ)
    yp, ys = x_prompt, x_sample
    ret_p, hre_p, him_p, mk_p_all, mv_p_all = [], [], [], [], []
    ret_s, hre_s, him_s = [], [], []
    for l in range(DEPTH):
        lw = (w_in[l], s5_a_re[l], s5_a_im[l], s5_log_step[l], s5_b_re[l], s5_b_im[l], s5_c_re[l], s5_c_im[l],
              s5_d[l], w_glu[l], w_proj_a[l], w_proj_b[l], w_proj_c[l], w_out[l], ln_g[l], ln_b[l])
        mk_p, mv_p = memory_kv(mem_prompt, w_mem_kv[l])
        s0 = jnp.zeros((BATCH, RET_HEADS, RET_DK, RET_DV), x_prompt.dtype)
        h0 = jnp.zeros((BATCH, S5_GROUPS, S5_STATE), x_prompt.dtype)
        yp, sr, hr, hi = mixer_layer(yp, pos_p, s0, h0, h0, mk_p, mv_p, chunk_p, *lw)
        ret_p.append(sr); hre_p.append(hr); him_p.append(hi); mk_p_all.append(mk_p); mv_p_all.append(mv_p)
        ys, sr, hr, hi = mixer_layer(ys, pos_s, state_ret[l], state_s5_re[l], state_s5_im[l],
                                     cache_mem_k[l], cache_mem_v[l], DEC_SEQ, *lw)
        ret_s.append(sr); hre_s.append(hr); him_s.append(hi)
    return (yp, ys,
            jnp.stack(ret_p), jnp.stack(hre_p), jnp.stack(him_p), jnp.stack(mk_p_all), jnp.stack(mv_p_all),
            jnp.stack(ret_s), jnp.stack(hre_s), jnp.stack(him_s))
```

```python
import math
from contextlib import ExitStack

import numpy as np
import concourse.bass as bass
import concourse.mybir as mybir
from concourse.bass_utils import run_bass_kernel_spmd

F32 = mybir.dt.float32
BF16 = mybir.dt.bfloat16
AF = mybir.ActivationFunctionType
ALU = mybir.AluOpType
P = 128
NCORES = 8

D_MODEL = 4096
IN_WIDTH = 32768
NTOK_OWN = 1152
NTOK_PRE = 1024

ENGS = ("pe", "act", "dve", "pool", "sp")
SEM_LIMIT = 20000


class Ins:
    __slots__ = ("eng", "fn", "deps", "is_dma", "key", "need_inc", "semref")

    def __init__(self, eng, fn, is_dma=False, key=None):
        self.eng = eng
        self.fn = fn
        self.deps = []
        self.is_dma = is_dma
        self.key = key
        self.need_inc = False
        self.semref = None


class Sched:
    _stage = 0

    def __init__(self, nc):
        Sched._stage += 1
        self.sid = Sched._stage
        self.nc = nc
        self.ins = []
        self.last_w = {}
        self.readers = {}
        self.dma_count = {}
        self.out_keys = set()

    def _add(self, ins, reads, writes):
        deps = set()
        for k in reads:
            w = self.last_w.get(k)
            if w is not None:
                deps.add(w)
        for k in writes:
            w = self.last_w.get(k)
            if w is not None:
                deps.add(w)
            for r in self.readers.get(k, ()):
                deps.add(r)
        deps.discard(ins)
        for d in deps:
            if d.is_dma:
                ins.deps.append((d, 16 * self.dma_count[d.key]))
            elif d.eng == ins.eng and not ins.is_dma:
                if ins.eng != "pe":
                    ins.deps.append((d, 0))
                    d.need_inc = True
            else:
                ins.deps.append((d, 0))
                d.need_inc = True
        for k in reads:
            self.readers.setdefault(k, []).append(ins)
        for k in writes:
            self.last_w[k] = ins
            self.readers[k] = []
        self.ins.append(ins)
        return ins

    def op(self, eng, fn, reads=(), writes=()):
        return self._add(Ins(eng, fn), list(reads), list(writes))

    def dma(self, eng, out, in_, key, reads=(), writes=(), is_output=False):
        ins = Ins(eng, lambda e: e.dma_start(out=out, in_=in_), is_dma=True, key=key)
        if is_output:
            self.out_keys.add(key)
        self.dma_count.setdefault(key, 0)
        self._add(ins, list(reads), list(writes))
        self.dma_count[key] += 1
        return ins

    def emit(self):
        nc = self.nc
        sem_names = []
        cur = {}
        for ins in self.ins:
            if ins.is_dma or not ins.need_inc:
                continue
            c = cur.get(ins.eng)
            if c is None or c[1] >= SEM_LIMIT:
                c = [len(sem_names), 0]
                sem_names.append("c%d_%s_%d" % (self.sid, ins.eng, len(sem_names)))
                cur[ins.eng] = c
            c[1] += 1
            ins.semref = (c[0], c[1])
        dma_keys = sorted(self.dma_count.keys(), key=str)
        csem = [nc.alloc_semaphore(name=n) for n in sem_names]
        dsem = {k: nc.alloc_semaphore(name="d%d_%d" % (self.sid, i)) for i, k in enumerate(dma_keys)}
        streams = {e: [i for i in self.ins if i.eng == e] for e in ENGS}
        final_dma = dict((k, 16 * v) for k, v in self.dma_count.items())
        out_keys = self.out_keys

        def run(engname, e):
            waited = {}
            for ins in streams[engname]:
                need = {}
                for d, dv in ins.deps:
                    if d.is_dma:
                        sk = ("d", d.key)
                        v = dv
                    else:
                        sk = ("c", d.semref[0])
                        v = d.semref[1]
                    if v > need.get(sk, 0):
                        need[sk] = v
                for sk, v in need.items():
                    if waited.get(sk, 0) >= v:
                        continue
                    waited[sk] = v
                    sem = dsem[sk[1]] if sk[0] == "d" else csem[sk[1]]
                    e.wait_ge(sem, v)
                r = ins.fn(e)
                if ins.is_dma:
                    r.then_inc(dsem[ins.key], 16)
                elif ins.need_inc:
                    r.then_inc(csem[ins.semref[0]], 1)
            if engname == "sp":
                for k in sorted(out_keys, key=str):
                    e.wait_ge(dsem[k], final_dma[k])

        with nc.Block() as block:
            @block.tensor
            def _(e):
                run("pe", e)

            @block.scalar
            def _(e):
                run("act", e)

            @block.vector
            def _(e):
                run("dve", e)

            @block.gpsimd
            def _(e):
                run("pool", e)

            @block.sync
            def _(e):
                run("sp", e)

        if not getattr(Sched, "NOCLEAR", False):
            nc.clear_and_free_semaphores(csem + list(dsem.values()))
        if not getattr(Sched, "NOCLEAR", False):
            nc.all_engine_barrier()


class Alloc:
    def __init__(self, nc, st):
        self.nc = nc
        self.st = st

    def sb(self, name, shape, dt):
        return self.st.enter_context(self.nc.sbuf_tensor(name, list(shape), dt))

    def ps(self, name, shape, dt=F32):
        return self.st.enter_context(self.nc.psum_tensor(name, list(shape), dt))


def run_stage(nc, body):
    with ExitStack() as st:
        S = Sched(nc)
        A = Alloc(nc, st)
        body(S, A)
        S.emit()


def gemm_body(S, A, uid, x_ap, ntile, w_ap, blocks, ident_ap, nkt=32, a_T=None, x_sum=None, epi=None,
              store_eng="act", pre_block=None):
    xT = A.sb("xT" + uid, [P, ntile, nkt, P], BF16)
    ident = A.sb("ident" + uid, [P, P], F32)
    S.dma("sp", ident[:], ident_ap, key="ident", writes=["ident"])
    pT = [A.ps("pT%d%s" % (i, uid), [P, 4, P]) for i in range(2)]
    cnt = 0
    if a_T is not None:
        OTd, ft0 = a_T
        for t in range(ntile):
            S.dma("sp", xT[:, t, :, :], OTd[t, :, ft0:ft0 + nkt, :], key=("xTl", t % 4),
                  writes=[("xT", t, kq) for kq in range(nkt // 4)])
    else:
        xin = [A.sb("xin%d%s" % (i, uid), [P, nkt * P], F32) for i in range(2)]
        if x_sum:
            xad = A.sb("xad" + uid, [P, nkt * P], F32)
    for t in range(ntile if a_T is None else 0):
        xi = t % 2
        S.dma("sp", xin[xi][:], x_ap[t * P:(t + 1) * P, :], key=("xin", xi), writes=[("xin", xi)])
        for extra in (x_sum or ()):
            S.dma("sp", xad[:], extra[t * P:(t + 1) * P, :], key="xad", writes=["xad"])
            S.op("pool", lambda e, xi=xi: e.tensor_tensor(out=xin[xi][:], in0=xin[xi][:], in1=xad[:], op=ALU.add),
                 reads=[("xin", xi), "xad"], writes=[("xin", xi)])
        for kq in range(nkt // 4):
            b = cnt % 2
            cnt += 1
            for j in range(4):
                kt = kq * 4 + j
                S.op("pe", lambda e, b=b, j=j, xi=xi, kt=kt: e.transpose(
                    pT[b][:, j, :], xin[xi][:, kt * P:(kt + 1) * P], ident[:]),
                    reads=[("xin", xi), "ident"], writes=[("pT", b)])
            if kq % 2 == 0:
                S.op("act", lambda e, b=b, t=t, kq=kq: e.activation(
                    out=xT[:, t, kq * 4:(kq + 1) * 4, :], in_=pT[b][:], func=AF.Copy),
                    reads=[("pT", b)], writes=[("xT", t, kq)])
            else:
                S.op("dve", lambda e, b=b, t=t, kq=kq: e.tensor_copy(
                    out=xT[:, t, kq * 4:(kq + 1) * 4, :], in_=pT[b][:]),
                    reads=[("pT", b)], writes=[("xT", t, kq)])

    stg = [A.sb("stg%d%s" % (i, uid), [P, 8, 256], F32) for i in range(3)]
    wb = [A.sb("wb%d%s" % (i, uid), [P, nkt, 256], BF16) for i in range(2)]
    pz = [A.ps("pz%d%s" % (i, uid), [P, 512]) for i in range(4)]
    ob = [A.sb("ob%d%s" % (i, uid), [P, 256], F32) for i in range(4)]
    w_view = w_ap.rearrange("(kt p) c -> p kt c", p=P)
    ctr = {"stg": 0, "pz": 0}

    def load_block(bi):
        c0 = blocks[bi][0]
        if pre_block is not None:
            pre_block(S, bi)
        for c in range(nkt // 8):
            s = ctr["stg"] % 3
            ctr["stg"] += 1
            S.dma("sp", stg[s][:], w_view[:, c * 8:(c + 1) * 8, c0:c0 + 256],
                  key=("stg", s), writes=[("stg", s)])
            if c % 2 == 0:
                S.op("dve", lambda e, bi=bi, c=c, s=s: e.tensor_copy(
                    out=wb[bi % 2][:, c * 8:(c + 1) * 8, :], in_=stg[s][:]),
                    reads=[("stg", s)], writes=[("wb", bi % 2, c)])
            else:
                S.op("pool", lambda e, bi=bi, c=c, s=s: e.tensor_copy(
                    out=wb[bi % 2][:, c * 8:(c + 1) * 8, :], in_=stg[s][:]),
                    reads=[("stg", s)], writes=[("wb", bi % 2, c)])

    nb = len(blocks)
    if nb:
        load_block(0)
    for bi in range(nb):
        if bi + 1 < nb:
            load_block(bi + 1)
        _, func, out_fn = blocks[bi]
        for t in range(ntile if not getattr(Sched, 'NOMM', False) else 0):
            pb = ctr["pz"] % 4
            ctr["pz"] += 1
            for kt in range(nkt):
                S.op("pe", lambda e, pb=pb, t=t, kt=kt, bi=bi: e.matmul(
                    pz[pb][:, 0:256], lhsT=xT[:, t, kt, :], rhs=wb[bi % 2][:, kt, :],
                    start=(kt == 0), stop=(kt == nkt - 1)),
                    reads=[("xT", t, kt // 4), ("wb", bi % 2, kt // 8)], writes=[("pz", pb)])
            if epi is not None:
                epi(S, t, bi, pz[pb][:, 0:256], ("pz", pb), ob[pb], ("ob", pb))
            else:
                S.op("act", lambda e, pb=pb, func=func: e.activation(
                    out=ob[pb][:], in_=pz[pb][:, 0:256], func=func),
                    reads=[("pz", pb)], writes=[("ob", pb)])
            S.dma(store_eng, out_fn(t), ob[pb][:], key=("ob", pb), reads=[("ob", pb)], is_output=True)


def col_func(c0):
    if 8192 <= c0 < 12288 or 14336 <= c0 < 16384 or 18432 <= c0 < 20480:
        return AF.Silu
    if c0 >= 20480:
        return AF.Sigmoid
    return AF.Copy


RET_G = [1.0 - 2.0 ** (-5.0 - h) for h in range(16)]


def retention_body(S, A, zo, zp, tabs, sret_in, sret_out, sretp_out, OT, ident_ap):
    ident = A.sb("r_ident", [P, P], F32)
    identb = A.sb("r_identb", [P, P], BF16)
    S.dma("sp", ident[:], ident_ap, key="ident", writes=["ident"])
    S.op("dve", lambda e: e.tensor_copy(out=identb[:], in_=ident[:]), reads=["ident"], writes=["identb"])
    maskp = A.sb("r_maskp", [P, P], F32)
    masks = A.sb("r_masks", [P, P], F32)
    seqm = A.sb("r_seqm", [P, 16], F32)
    seqmT = A.sb("r_seqmT", [P, 16, P], F32)
    S.dma("sp", maskp[:], tabs["mask_p"], key="maskp", writes=["maskp"])
    S.dma("sp", masks[:], tabs["mask_s"], key="masks", writes=["masks"])
    S.dma("sp", seqm[:], tabs["seqm"], key="seqm", writes=["seqm"])
    S.dma("sp", seqmT[:], tabs["seqmT"], key="seqmT", writes=["seqmT"])

    St = A.sb("r_S", [P, 16, 256], F32)
    Sb = A.sb("r_Sb", [P, 16, 256], BF16)
    S.op("pool", lambda e: e.memset(St[:], 0.0), writes=["S"])
    S.op("pool", lambda e: e.memset(Sb[:], 0.0), writes=["Sb"])

    qin = A.sb("r_qin", [P, 2048], F32)
    kin = A.sb("r_kin", [P, 2048], F32)
    vin = A.sb("r_vin", [P, 4096], F32)
    gin = A.sb("r_gin", [P, 4096], F32)
    rt = A.sb("r_rt", [P, 2, 16, 64], F32)
    t1 = A.sb("r_t1", [P, 16, 64], F32)
    t2 = A.sb("r_t2", [P, 16, 64], F32)
    qt = A.sb("r_qt", [P, 16, 128], BF16)
    kt_ = A.sb("r_kt", [P, 16, 128], BF16)
    vb = A.sb("r_vb", [P, 16, 256], BF16)
    qT = A.sb("r_qT", [P, 16, 128], BF16)
    kT = A.sb("r_kT", [P, 16, 128], BF16)
    scs = A.sb("r_scs", [P, 16, 128], BF16)
    osb = A.sb("r_osb", [P, 16, 256], F32)
    sq = vin.rearrange("p (h e) -> p h e", h=16) if False else None
    og = A.sb("r_og", [P, 16, 256], BF16)
    oT = A.sb("r_oT", [P, 32, 128], BF16)
    st1 = A.sb("r_st1", [P, 16], F32)
    st2 = A.sb("r_st2", [P, 16], F32)
    st3 = A.sb("r_st3", [P, 16], F32)
    dtmp = A.sb("r_dtmp", [P, 2, 256], F32)
    ptr = [A.ps("r_ptr%d" % i, [P, 8, 128], BF16) for i in range(2)]
    psc = [A.ps("r_psc%d" % i, [P, 4, 128]) for i in range(2)]
    po = [A.ps("r_po%d" % i, [P, 2, 256]) for i in range(2)]
    pd = [A.ps("r_pd%d" % i, [P, 2, 256]) for i in range(2)]
    cn = {"tr": 0, "sc": 0, "o": 0, "d": 0}

    def rotary(src, dst, rt_ap, rkey, skey, dkey):
        S.dma("sp", rt[:], rt_ap, key="rt", writes=["rt"])
        sv = src[:].rearrange("p (h j two) -> p h j two", h=16, two=2)
        dv = dst[:].rearrange("p h (j two) -> p h j two", two=2)
        S.op("dve", lambda e: e.tensor_tensor(out=t1[:], in0=sv[:, :, :, 0], in1=rt[:, 0], op=ALU.mult),
             reads=[skey, "rt"], writes=["t1"])
        S.op("pool", lambda e: e.tensor_tensor(out=t2[:], in0=sv[:, :, :, 1], in1=rt[:, 1], op=ALU.mult),
             reads=[skey, "rt"], writes=["t2"])
        S.op("dve", lambda e: e.tensor_tensor(out=dv[:, :, :, 0], in0=t1[:], in1=t2[:], op=ALU.subtract),
             reads=["t1", "t2"], writes=[dkey + "0"])
        S.op("dve", lambda e: e.tensor_tensor(out=t1[:], in0=sv[:, :, :, 0], in1=rt[:, 1], op=ALU.mult),
             reads=[skey, "rt", dkey + "0"], writes=["t1"])
        S.op("pool", lambda e: e.tensor_tensor(out=t2[:], in0=sv[:, :, :, 1], in1=rt[:, 0], op=ALU.mult),
             reads=[skey, "rt", dkey + "0"], writes=["t2"])
        S.op("dve", lambda e: e.tensor_tensor(out=dv[:, :, :, 1], in0=t1[:], in1=t2[:], op=ALU.add),
             reads=["t1", "t2"], writes=[dkey + "1"])

    def transpose16(src, dst, skeys, dkey):
        for half in range(2):
            b = cn["tr"] % 2
            cn["tr"] += 1
            for j in range(8):
                h = half * 8 + j
                S.op("pe", lambda e, b=b, j=j, h=h: e.transpose(ptr[b][:, j, :], src[:, h, :], identb[:]),
                     reads=list(skeys) + ["identb"], writes=[("ptr", b)])
            S.op("act", lambda e, b=b, half=half: e.activation(
                out=dst[:, half * 8:(half + 1) * 8, :], in_=ptr[b][:], func=AF.Copy),
                reads=[("ptr", b)], writes=[(dkey, half)])

    def state_update(g, sample_head=None):
        pass

    def chunk(kind, t):
        own = kind != "pre"
        z = zo if own else zp
        r0 = t * P
        kcol = 2048 if own else 0
        vcol = 4096 if own else 2048
        S.dma("sp", kin[:], z[r0:r0 + P, kcol:kcol + 2048], key="kin", writes=["kin"])
        S.dma("sp", vin[:], z[r0:r0 + P, vcol:vcol + 4096], key="vin", writes=["vin"])
        S.op("act", lambda e: e.activation(out=vb[:].rearrange("p h e -> p (h e)"), in_=vin[:], func=AF.Copy),
             reads=["vin"], writes=["vb"])
        rotary(kin, kt_, (tabs["rk_own"] if own else tabs["rk_pre"])[t], "rk", "kin", "kt")
        if own:
            S.dma("sp", qin[:], z[r0:r0 + P, 0:2048], key="qin", writes=["qin"])
            S.dma("sp", gin[:], z[r0:r0 + P, 8192:12288], key="gin", writes=["gin"])
            rotary(qin, qt, tabs["rq"][t], "rq", "qin", "qt")
            transpose16(qt, qT, ["qt0", "qt1"], "qT")
            transpose16(kt_, kT, ["kt0", "kt1"], "kT")
        return own

    def scores_and_out(mask, sample):
        for hq in range(4):
            b = cn["sc"] % 2
            cn["sc"] += 1
            for j in range(4):
                h = hq * 4 + j
                S.op("pe", lambda e, b=b, j=j, h=h: e.matmul(psc[b][:, j, :], lhsT=kT[:, h, :], rhs=qT[:, h, :],
                                                             start=True, stop=True),
                     reads=[("kT", h // 8), ("qT", h // 8)], writes=[("psc", b)])
            S.op("dve", lambda e, b=b, hq=hq: e.tensor_tensor(
                out=scs[:, hq * 4:(hq + 1) * 4, :], in0=psc[b][:],
                in1=mask[:].unsqueeze(1).to_broadcast([P, 4, P]), op=ALU.mult),
                reads=[("psc", b), "maskp", "masks"], writes=[("scs", hq)])

    def finish_out(t):
        S.op("dve", lambda e: e.tensor_reduce(out=st1[:], in_=osb[:], op=ALU.add, axis=mybir.AxisListType.X),
             reads=["osb"], writes=["st1"])
        sqv = vin[:].rearrange("p (h e) -> p h e", h=16)
        S.op("pool", lambda e: e.tensor_tensor(out=sqv, in0=osb[:], in1=osb[:], op=ALU.mult),
             reads=["osb"], writes=["vin"])
        S.op("dve", lambda e: e.tensor_reduce(out=st2[:], in_=sqv, op=ALU.add, axis=mybir.AxisListType.X),
             reads=["vin"], writes=["st2"])
        S.op("dve", lambda e: e.tensor_scalar(out=st1[:], in0=st1[:], scalar1=1.0 / 256, scalar2=None, op0=ALU.mult),
             reads=["st1"], writes=["st1"])
        S.op("dve", lambda e: e.tensor_tensor(out=st3[:], in0=st1[:], in1=st1[:], op=ALU.mult),
             reads=["st1"], writes=["st3"])
        S.op("dve", lambda e: e.scalar_tensor_tensor(out=st2[:], in0=st2[:], scalar=1.0 / 256, in1=st3[:],
                                                     op0=ALU.mult, op1=ALU.subtract),
             reads=["st2", "st3"], writes=["st2"])
        S.op("dve", lambda e: e.tensor_scalar(out=st2[:], in0=st2[:], scalar1=1e-5, scalar2=None, op0=ALU.add),
             reads=["st2"], writes=["st2"])
        S.op("act", lambda e: e.activation(out=st2[:], in_=st2[:], func=AF.Sqrt), reads=["st2"], writes=["st2"])
        S.op("dve", lambda e: e.reciprocal(out=st2[:], in_=st2[:]), reads=["st2"], writes=["st2"])
        S.op("dve", lambda e: e.tensor_tensor(out=osb[:], in0=osb[:],
                                              in1=st1[:].unsqueeze(2).to_broadcast([P, 16, 256]), op=ALU.subtract),
             reads=["osb", "st1"], writes=["osb"])
        S.op("pool", lambda e: e.tensor_tensor(out=osb[:], in0=osb[:],
                                               in1=st2[:].unsqueeze(2).to_broadcast([P, 16, 256]), op=ALU.mult),
             reads=["osb", "st2"], writes=["osb"])
        S.op("dve", lambda e: e.tensor_tensor(out=og[:].rearrange("p h e -> p (h e)"),
                                              in0=osb[:].rearrange("p h e -> p (h e)"), in1=gin[:], op=ALU.mult),
             reads=["osb", "gin"], writes=["og"])
        ogv = og[:].rearrange("p h (two e) -> p (h two) e", two=2)
        for q4 in range(4):
            b = cn["tr"] % 2
            cn["tr"] += 1
            for j in range(8):
                ft = q4 * 8 + j
                S.op("pe", lambda e, b=b, j=j, ft=ft: e.transpose(ptr[b][:, j, :], ogv[:, ft, :], identb[:]),
                     reads=["og", "identb"], writes=[("ptr", b)])
            S.op("act", lambda e, b=b, q4=q4: e.activation(out=oT[:, q4 * 8:(q4 + 1) * 8, :], in_=ptr[b][:],
                                                           func=AF.Copy),
                 reads=[("ptr", b)], writes=[("oT", q4)])
        S.dma("act", OT[t, :, 0:32, :], oT[:], key="oT", reads=[("oT", q) for q in range(4)], is_output=True)

    def prompt_state_update():
        for hp in range(8):
            b = cn["d"] % 2
            cn["d"] += 1
            for j in range(2):
                h = hp * 2 + j
                S.op("pe", lambda e, b=b, j=j, h=h: e.matmul(pd[b][:, j, :], lhsT=kt_[:, h, :], rhs=vb[:, h, :],
                                                             start=True, stop=True),
                     reads=["kt0", "kt1", "vb"], writes=[("pd", b)])
            for j in range(2):
                h = hp * 2 + j
                g = float(RET_G[h] ** 128)
                S.op("act", lambda e, b=b, j=j, g=g: e.activation(out=dtmp[:, j, :], in_=pd[b][:, j, :],
                                                                  func=AF.Copy, scale=g),
                     reads=[("pd", b)], writes=[("dtmp", j)])
                S.op("dve", lambda e, h=h, j=j, g=g: e.scalar_tensor_tensor(
                    out=St[:, h, :], in0=St[:, h, :], scalar=g, in1=dtmp[:, j, :], op0=ALU.mult, op1=ALU.add),
                    reads=[("dtmp", j), "S"], writes=["S"])
        S.op("pool", lambda e: e.tensor_copy(out=Sb[:], in_=St[:]), reads=["S"], writes=["Sb"])

    for t in range(8):
        chunk("pre", t)
        prompt_state_update()

    for t in range(8):
        chunk("own", t)
        scores_and_out(maskp, False)
        for hp in range(8):
            b = cn["o"] % 2
            cn["o"] += 1
            for j in range(2):
                h = hp * 2 + j
                S.op("pe", lambda e, b=b, j=j, h=h: e.matmul(po[b][:, j, :], lhsT=scs[:, h, :], rhs=vb[:, h, :],
                                                             start=True, stop=False),
                     reads=[("scs", h // 4), "vb"], writes=[("po", b)])
                S.op("pe", lambda e, b=b, j=j, h=h: e.matmul(po[b][:, j, :], lhsT=qT[:, h, :], rhs=Sb[:, h, :],
                                                             start=False, stop=True),
                     reads=[("qT", h // 8), "Sb"], writes=[("po", b)])
            S.op("act", lambda e, b=b, hp=hp: e.activation(out=osb[:, hp * 2:hp * 2 + 2, :], in_=po[b][:],
                                                           func=AF.Copy),
                 reads=[("po", b)], writes=["osb"])
        prompt_state_update()
        finish_out(t)
    S.dma("sp", sretp_out.rearrange("h d e -> d h e"), St[:], key="St_out", reads=["S"], is_output=True)

    t = 8
    chunk("own", t)
    scores_and_out(masks, True)
    Ss = vin[:].rearrange("p (h e) -> p h e", h=16)
    Ssb = og
    qTm = A.sb("r_qTm", [P, 16, 128], BF16)
    ktm = A.sb("r_ktm", [P, 16, 128], BF16)
    for h in range(16):
        g8 = float(RET_G[h] ** 8)
        S.dma("sp", Ss, sret_in[:, h].rearrange("s d e -> d s e"), key="Ss", writes=["vin"])
        S.op("pool", lambda e: e.tensor_copy(out=Ssb[:], in_=Ss), reads=["vin"], writes=["og"])
        S.op("dve", lambda e, h=h: e.tensor_tensor(
            out=qTm[:], in0=qT[:, h, :].unsqueeze(1).to_broadcast([P, 16, P]), in1=seqmT[:], op=ALU.mult),
            reads=[("qT", h // 8), "seqmT"], writes=["qTm"])
        S.op("dve", lambda e, h=h: e.tensor_tensor(
            out=ktm[:], in0=kt_[:, h, :].unsqueeze(1).to_broadcast([P, 16, P]),
            in1=seqm[:].unsqueeze(2).to_broadcast([P, 16, P]), op=ALU.mult),
            reads=["kt0", "kt1", "seqm"], writes=["ktm"])
        b = cn["o"] % 2
        cn["o"] += 1
        S.op("pe", lambda e, b=b, h=h: e.matmul(po[b][:, 0, :], lhsT=scs[:, h, :], rhs=vb[:, h, :],
                                                start=True, stop=False),
             reads=[("scs", h // 4), "vb"], writes=[("po", b)])
        for s_ in range(16):
            S.op("pe", lambda e, b=b, s_=s_: e.matmul(po[b][:, 0, :], lhsT=qTm[:, s_, :], rhs=Ssb[:, s_, :],
                                                      start=False, stop=(s_ == 15)),
                 reads=["qTm", "og"], writes=[("po", b)])
        S.op("act", lambda e, b=b, h=h: e.activation(out=osb[:, h, :], in_=po[b][:, 0, :], func=AF.Copy),
             reads=[("po", b)], writes=["osb"])
        for sp_ in range(8):
            b2 = cn["d"] % 2
            cn["d"] += 1
            for j in range(2):
                s_ = sp_ * 2 + j
                S.op("pe", lambda e, b2=b2, j=j, s_=s_, h=h: e.matmul(
                    pd[b2][:, j, :], lhsT=ktm[:, s_, :], rhs=vb[:, h, :], start=True, stop=True),
                    reads=["ktm", "vb"], writes=[("pd", b2)])
            for j in range(2):
                s_ = sp_ * 2 + j
                S.op("act", lambda e, b2=b2, j=j, g8=g8: e.activation(out=dtmp[:, j, :], in_=pd[b2][:, j, :],
                                                                      func=AF.Copy, scale=g8),
                     reads=[("pd", b2)], writes=[("dtmp", j)])
                S.op("dve", lambda e, s_=s_, j=j, g8=g8: e.scalar_tensor_tensor(
                    out=Ss[:, s_, :], in0=Ss[:, s_, :], scalar=g8, in1=dtmp[:, j, :], op0=ALU.mult, op1=ALU.add),
                    reads=[("dtmp", j), "vin", "og"], writes=["vin"])
        S.dma("sp", sret_out[:, h].rearrange("s d e -> d s e"), Ss, key="Ss_out", reads=["vin"],
              writes=["Ss_dram"], is_output=True)
    finish_out(8)


def xattn_body(S, A, zo, memkv, cmk, cmv, seqmT_ap, OT, ident_ap):
    X = mybir.AxisListType.X
    scale = 512.0 ** -0.5
    ident = A.sb("x_ident", [P, P], F32)
    identb = A.sb("x_identb", [P, P], BF16)
    S.dma("sp", ident[:], ident_ap, key="ident", writes=["ident"])
    S.op("dve", lambda e: e.tensor_copy(out=identb[:], in_=ident[:]), reads=["ident"], writes=["identb"])
    seqmT = A.sb("x_seqmT", [P, 16, P], F32)
    S.dma("sp", seqmT[:], seqmT_ap, key="seqmT", writes=["seqmT"])

    qin = A.sb("x_qin", [P, 2048], F32)
    gin = A.sb("x_gin", [P, 2048], F32)
    qb = A.sb("x_qb", [P, 16, 128], BF16)
    qT = A.sb("x_qT", [P, 16, 128], BF16)
    kvin = [A.sb("x_kvin%d" % i, [P, 2, 2048], F32) for i in range(2)]
    kb = A.sb("x_kb", [P, 2, 16, 128], BF16)
    KT = A.sb("x_KT", [P, 16, 256], BF16)
    Vb = A.sb("x_Vb", [P, 2, 2048], BF16)
    pb = A.sb("x_pb", [P, 4, 256], BF16)
    pT = A.sb("x_pT", [P, 4, 2, 128], BF16)
    pTm = A.sb("x_pTm", [P, 16, 128], BF16)
    qTm = A.sb("x_qTm", [P, 16, 128], BF16)
    ob = A.sb("x_ob", [P, 16, 128], BF16)
    oT = A.sb("x_oT", [P, 16, 128], BF16)
    mx = A.sb("x_mx", [P, 4], F32)
    sm = A.sb("x_sm", [P, 4], F32)
    ptr = [A.ps("x_ptr%d" % i, [P, 8, 128], BF16) for i in range(2)]
    pso = [A.ps("x_pso%d" % i, [P, 512]) for i in range(4)]
    cn = {"tr": 0, "kv": 0}

    def tr_group(srcs, dst_ap, skeys, dkey):
        b = cn["tr"] % 2
        cn["tr"] += 1
        for j, src in enumerate(srcs):
            S.op("pe", lambda e, b=b, j=j, src=src: e.transpose(ptr[b][:, j, :], src, identb[:]),
                 reads=list(skeys) + ["identb"], writes=[("ptr", b)])
        n = len(srcs)
        S.op("act", lambda e, b=b, n=n: e.activation(out=dst_ap, in_=ptr[b][:, 0:n, :], func=AF.Copy),
             reads=[("ptr", b)], writes=[dkey])

    def load_q(t):
        r0 = t * P
        S.dma("sp", qin[:], zo[r0:r0 + P, 16384:18432], key="qin", writes=["qin"])
        S.dma("sp", gin[:], zo[r0:r0 + P, 18432:20480], key="gin", writes=["gin"])
        S.op("dve", lambda e: e.tensor_copy(out=qb[:].rearrange("p a b -> p (a b)"), in_=qin[:]),
             reads=["qin"], writes=["qb"])
        for half in range(2):
            tr_group([qb[:, half * 8 + j, :] for j in range(8)], qT[:, half * 8:(half + 1) * 8, :],
                     ["qb"], ("qT", half))

    def load_kv(src_ap, which):
        i = cn["kv"] % 2
        cn["kv"] += 1
        S.dma("sp", kvin[i][:], src_ap.rearrange("(mt p) c -> p mt c", p=P), key=("kvin", i),
              writes=[("kvin", i)])
        return i

    def make_KT(i):
        S.op("pool", lambda e: e.tensor_copy(out=kb[:].rearrange("p m a b -> p m (a b)"), in_=kvin[i][:]),
             reads=[("kvin", i)], writes=["kb"])
        for mt in range(2):
            for half in range(2):
                tr_group([kb[:, mt, half * 8 + j, :] for j in range(8)],
                         KT[:, half * 8:(half + 1) * 8, mt * P:(mt + 1) * P], ["kb"], ("KT", mt, half))

    def make_V(i):
        S.op("pool", lambda e: e.tensor_copy(out=Vb[:], in_=kvin[i][:]), reads=[("kvin", i)], writes=["Vb"])

    KT_keys = [("KT", mt, half) for mt in range(2) for half in range(2)]

    def sc_loc(h, sample):
        return pso[h][:, 0:256], ("pso", h)

    def score_mm(lhs, lkeys, first, last, sample=False):
        for h in range(4):
            for dt in range(4):
                k = h * 4 + dt
                loc, lk = sc_loc(h, sample)
                S.op("pe", lambda e, loc=loc, k=k, dt=dt: e.matmul(
                    loc, lhsT=lhs[:, k, :], rhs=KT[:, k, :],
                    start=(first and dt == 0), stop=(last and dt == 3)),
                    reads=list(lkeys) + KT_keys, writes=[lk])

    def softmax(sample=False):
        for h in range(4):
            sv, lk = sc_loc(h, sample)
            S.op("dve", lambda e, h=h, sv=sv: e.tensor_reduce(out=mx[:, h:h + 1], in_=sv, op=ALU.max, axis=X),
                 reads=[lk], writes=[("mx", h)])
            S.op("dve", lambda e, h=h: e.tensor_scalar(out=mx[:, h:h + 1], in0=mx[:, h:h + 1], scalar1=-scale,
                                                       scalar2=None, op0=ALU.mult),
                 reads=[("mx", h)], writes=[("mx", h)])
            S.op("act", lambda e, h=h, sv=sv: e.activation(out=pb[:, h, :], in_=sv, func=AF.Exp,
                                                           bias=mx[:, h:h + 1], scale=scale,
                                                           accum_out=sm[:, h:h + 1]),
                 reads=[lk, ("mx", h)], writes=[("pb", h), ("sm", h)])
            S.op("dve", lambda e, h=h: e.reciprocal(out=sm[:, h:h + 1], in_=sm[:, h:h + 1]),
                 reads=[("sm", h)], writes=[("sm", h)])
        tr_group([pb[:, h, mt * P:(mt + 1) * P] for h in range(4) for mt in range(2)],
                 pT[:].rearrange("p h m l -> p (h m) l"), [("pb", h) for h in range(4)], "pT")

    def finish(t):
        for h in range(4):
            S.op("dve", lambda e, h=h: e.scalar_tensor_tensor(
                out=ob[:, h * 4:(h + 1) * 4, :].rearrange("p a b -> p (a b)"), in0=pso[h][:],
                scalar=sm[:, h:h + 1], in1=gin[:, h * 512:(h + 1) * 512], op0=ALU.mult, op1=ALU.mult),
                reads=[("pso", h), ("sm", h), "gin"], writes=[("ob", h)])
        for half in range(2):
            tr_group([ob[:, half * 8 + j, :] for j in range(8)], oT[:, half * 8:(half + 1) * 8, :],
                     [("ob", h) for h in range(4)], ("oT", half))
        S.dma("act", OT[t, :, 48:64, :], oT[:], key="oT", reads=[("oT", 0), ("oT", 1)], is_output=True)

    i = load_kv(memkv[:, 0:2048], "k")
    make_KT(i)
    i = load_kv(memkv[:, 2048:4096], "v")
    make_V(i)
    for t in range(8):
        load_q(t)
        score_mm(qT, [("qT", 0), ("qT", 1)], True, True)
        softmax()
        for h in range(4):
            for mt in range(2):
                S.op("pe", lambda e, h=h, mt=mt: e.matmul(pso[h][:], lhsT=pT[:, h, mt, :],
                                                          rhs=Vb[:, mt, h * 512:(h + 1) * 512],
                                                          start=(mt == 0), stop=(mt == 1)),
                     reads=["pT", "Vb"], writes=[("pso", h)])
        finish(t)

    load_q(8)
    for s_ in range(16):
        i = load_kv(cmk[s_], "k")
        make_KT(i)
        S.op("dve", lambda e, s_=s_: e.tensor_tensor(
            out=qTm[:], in0=qT[:], in1=seqmT[:, s_, :].unsqueeze(1).to_broadcast([P, 16, P]), op=ALU.mult),
            reads=[("qT", 0), ("qT", 1), "seqmT"], writes=["qTm"])
        score_mm(qTm, ["qTm"], s_ == 0, s_ == 15, sample=True)
    softmax(sample=True)
    for s_ in range(16):
        i = load_kv(cmv[s_], "v")
        make_V(i)
        for h in range(4):
            S.op("dve", lambda e, h=h, s_=s_: e.tensor_tensor(
                out=pTm[:, h * 2:(h + 1) * 2, :], in0=pT[:, h, :, :],
                in1=seqmT[:, s_, :].unsqueeze(1).to_broadcast([P, 2, P]), op=ALU.mult),
                reads=["pT", "seqmT"], writes=[("pTm", h)])
            for mt in range(2):
                S.op("pe", lambda e, h=h, mt=mt, s_=s_: e.matmul(
                    pso[h][:], lhsT=pTm[:, h * 2 + mt, :], rhs=Vb[:, mt, h * 512:(h + 1) * 512],
                    start=(s_ == 0 and mt == 0), stop=(s_ == 15 and mt == 1)),
                    reads=[("pTm", h), "Vb"], writes=[("pso", h)])
    finish(8)


def s5prep_body(S, A, prm, ident_ap, maskM_ap, SM, SG, SE, A8S):
    PI = math.pi
    ident = A.sb("q_ident", [P, P], F32)
    identb = A.sb("q_identb", [P, P], BF16)
    S.dma("sp", ident[:], ident_ap, key="ident", writes=["ident"])
    S.op("dve", lambda e: e.tensor_copy(out=identb[:], in_=ident[:]), reads=["ident"], writes=["identb"])
    maskM = A.sb("q_maskM", [P, P], F32)
    S.dma("sp", maskM[:], maskM_ap, key="maskM", writes=["maskM"])
    H = 64
    uid = [0]

    def tl(shape, dt=F32):
        uid[0] += 1
        return A.sb("q_t%d" % uid[0], shape, dt)

    def dve(fn, reads, writes):
        S.op("dve", fn, reads=reads, writes=writes)

    def tt(out, a, b, op, okey, akey, bkey):
        dve(lambda e: e.tensor_tensor(out=out, in0=a, in1=b, op=op), [akey, bkey], [okey])

    araw = tl([P, 2, 64])
    S.dma("sp", araw[:, 0, :], prm["a_re"], key="araw0", writes=["araw"])
    S.dma("sp", araw[:, 1, :], prm["a_im"], key="araw1", writes=["araw"])
    pA = A.ps("q_pA", [P, 4, P])
    pA2 = A.ps("q_pA2", [P, 4, P])
    ar = tl([H, P]); ai = tl([H, P])
    for j in range(2):
        S.op("pe", lambda e, j=j: e.transpose(pA[0:H, j, :], araw[:, j, :], ident[:]), reads=["araw", "ident"],
             writes=["pA"])
    dve(lambda e: e.tensor_copy(out=ar[:], in_=pA[0:H, 0, :]), ["pA"], ["ar"])
    dve(lambda e: e.tensor_copy(out=ai[:], in_=pA[0:H, 1, :]), ["pA"], ["ai"])
    dtb = tl([H, P])
    S.dma("sp", dtb[:], prm["log_step"].to_broadcast([H, P]), key="dtb", writes=["dtb"])
    S.op("act", lambda e: e.activation(out=dtb[:], in_=dtb[:], func=AF.Exp), reads=["dtb"], writes=["dtb"])
    dtar = tl([H, P]); dtai = tl([H, P]); mag = tl([H, P])
    tt(dtar[:], dtb[:], ar[:], ALU.mult, "dtar", "dtb", "ar")
    tt(dtai[:], dtb[:], ai[:], ALU.mult, "dtai", "dtb", "ai")
    kq = tl([H, P]); ki = tl([H, P], mybir.dt.int32); rr = tl([H, P])
    dve(lambda e: e.tensor_scalar(out=kq[:], in0=dtai[:], scalar1=1.0 / (2 * PI), scalar2=None, op0=ALU.mult),
        ["dtai"], ["kq"])
    dve(lambda e: e.tensor_copy(out=ki[:], in_=kq[:]), ["kq"], ["ki"])
    dve(lambda e: e.tensor_copy(out=kq[:], in_=ki[:]), ["ki"], ["kq"])
    dve(lambda e: e.scalar_tensor_tensor(out=rr[:], in0=kq[:], scalar=-2 * PI, in1=dtai[:], op0=ALU.mult,
                                         op1=ALU.add), ["kq", "dtai"], ["rr"])
    rs = tl([H, P]); rc = tl([H, P]); sn = tl([H, P]); cs = tl([H, P])
    msk = tl([H, P])
    for t_, k_, sh in ((rs, "rs", 0.0), (rc, "rc", PI / 2)):
        dve(lambda e, t_=t_, sh=sh: e.tensor_scalar(out=t_[:], in0=rr[:], scalar1=sh, scalar2=None, op0=ALU.add),
            ["rr"], [k_])
        dve(lambda e, t_=t_: e.tensor_scalar(out=msk[:], in0=t_[:], scalar1=PI, scalar2=None, op0=ALU.is_gt),
            [k_], ["msk"])
        dve(lambda e, t_=t_: e.scalar_tensor_tensor(out=t_[:], in0=msk[:], scalar=-2 * PI, in1=t_[:],
                                                    op0=ALU.mult, op1=ALU.add), ["msk", k_], [k_])
        dve(lambda e, t_=t_: e.tensor_scalar(out=msk[:], in0=t_[:], scalar1=-PI, scalar2=None, op0=ALU.is_lt),
            [k_], ["msk"])
        dve(lambda e, t_=t_: e.scalar_tensor_tensor(out=t_[:], in0=msk[:], scalar=2 * PI, in1=t_[:],
                                                    op0=ALU.mult, op1=ALU.add), ["msk", k_], [k_])
    for t_, k_ in ((rs, "rs"), (rc, "rc")):
        dve(lambda e, t_=t_: e.tensor_scalar(out=t_[:], in0=t_[:], scalar1=3.1415925, scalar2=-3.1415925,
                                             op0=ALU.min, op1=ALU.max), [k_], [k_])
    hh = tl([H, P]); x2 = tl([H, P]); sh_ = tl([H, P]); ch_ = tl([H, P])
    dve(lambda e: e.tensor_scalar(out=hh[:], in0=rs[:], scalar1=0.5, scalar2=None, op0=ALU.mult), ["rs"], ["hh"])
    tt(x2[:], hh[:], hh[:], ALU.mult, "x2", "hh", "hh")
    sc_ = [(-1.0) ** k / math.factorial(2 * k + 1) for k in range(9)]
    cc_ = [(-1.0) ** k / math.factorial(2 * k) for k in range(9)]

    def horner(dst, dkey, co):
        dve(lambda e: e.tensor_scalar(out=dst[:], in0=x2[:], scalar1=co[-1], scalar2=None, op0=ALU.mult),
            ["x2"], [dkey])
        for c_ in co[-2:0:-1]:
            dve(lambda e, c_=c_: e.scalar_tensor_tensor(out=dst[:], in0=dst[:], scalar=c_, in1=x2[:],
                                                        op0=ALU.add, op1=ALU.mult), [dkey, "x2"], [dkey])
        dve(lambda e: e.tensor_scalar(out=dst[:], in0=dst[:], scalar1=co[0], scalar2=None, op0=ALU.add),
            [dkey], [dkey])
    horner(sh_, "sh", sc_)
    tt(sh_[:], sh_[:], hh[:], ALU.mult, "sh", "sh", "hh")
    horner(ch_, "ch", cc_)
    dve(lambda e: e.scalar_tensor_tensor(out=sn[:], in0=sh_[:], scalar=2.0, in1=ch_[:], op0=ALU.mult,
                                         op1=ALU.mult), ["sh", "ch"], ["sn"])
    tt(cs[:], sh_[:], sh_[:], ALU.mult, "cs", "sh", "sh")
    dve(lambda e: e.tensor_scalar(out=cs[:], in0=cs[:], scalar1=-2.0, scalar2=1.0, op0=ALU.mult, op1=ALU.add),
        ["cs"], ["cs"])
    ec_ = [1.0 / math.factorial(k) for k in range(9)]
    dve(lambda e: e.tensor_scalar(out=mag[:], in0=dtar[:], scalar1=ec_[-1], scalar2=None, op0=ALU.mult),
        ["dtar"], ["mag"])
    for c_ in ec_[-2:0:-1]:
        dve(lambda e, c_=c_: e.scalar_tensor_tensor(out=mag[:], in0=mag[:], scalar=c_, in1=dtar[:],
                                                    op0=ALU.add, op1=ALU.mult), ["mag", "dtar"], ["mag"])
    dve(lambda e: e.tensor_scalar(out=mag[:], in0=mag[:], scalar1=1.0, scalar2=None, op0=ALU.add),
        ["mag"], ["mag"])
    PW = tl([H, 16, 2, P])
    tmp = [tl([H, P]) for _ in range(4)]
    cm = [0]

    def cmul(ore, oim, xr, xi, yr, yi, okeys, ikeys):
        cm[0] += 1
        k = ["cm%d_%d" % (cm[0], i) for i in range(4)]
        dve(lambda e: e.tensor_tensor(out=tmp[0][:], in0=xr, in1=yr, op=ALU.mult), ikeys, ["tmp0"])
        dve(lambda e: e.tensor_tensor(out=tmp[1][:], in0=xi, in1=yi, op=ALU.mult), ikeys, ["tmp1"])
        dve(lambda e: e.tensor_tensor(out=tmp[2][:], in0=xr, in1=yi, op=ALU.mult), ikeys, ["tmp2"])
        dve(lambda e: e.tensor_tensor(out=tmp[3][:], in0=xi, in1=yr, op=ALU.mult), ikeys, ["tmp3"])
        dve(lambda e: e.tensor_tensor(out=ore, in0=tmp[0][:], in1=tmp[1][:], op=ALU.subtract),
            ["tmp0", "tmp1"], [okeys[0]])
        dve(lambda e: e.tensor_tensor(out=oim, in0=tmp[2][:], in1=tmp[3][:], op=ALU.add),
            ["tmp2", "tmp3"], [okeys[1]])

    def pw(e_, c):
        return PW[:, e_ + 7, c, :]

    def pk(e_):
        return ["pw%d_0" % e_, "pw%d_1" % e_]
    S.op("pool", lambda e: e.memset(pw(0, 0), 1.0), writes=["pw0_0"])
    S.op("pool", lambda e: e.memset(pw(0, 1), 0.0), writes=["pw0_1"])
    tt(pw(1, 0), mag[:], cs[:], ALU.mult, "pw1_0", "mag", "cs")
    tt(pw(1, 1), mag[:], sn[:], ALU.mult, "pw1_1", "mag", "sn")
    for e_ in range(2, 9):
        cmul(pw(e_, 0), pw(e_, 1), pw(e_ - 1, 0), pw(e_ - 1, 1), pw(1, 0), pw(1, 1), pk(e_), pk(e_ - 1) + pk(1))
    im2 = tl([H, P])
    tt(im2[:], mag[:], mag[:], ALU.mult, "im2", "mag", "mag")
    dve(lambda e: e.reciprocal(out=im2[:], in_=im2[:]), ["im2"], ["im2"])
    tt(pw(-1, 0), pw(1, 0), im2[:], ALU.mult, "pw-1_0", "pw1_0", "im2")
    dve(lambda e: e.scalar_tensor_tensor(out=pw(-1, 1), in0=pw(1, 1), scalar=-1.0, in1=im2[:], op0=ALU.mult,
                                         op1=ALU.mult), ["pw1_1", "im2"], ["pw-1_1"])
    for e_ in range(2, 8):
        cmul(pw(-e_, 0), pw(-e_, 1), pw(-e_ + 1, 0), pw(-e_ + 1, 1), pw(-1, 0), pw(-1, 1),
             pk(-e_), pk(-e_ + 1) + pk(-1))
    allpw = [k for e_ in range(-7, 9) for k in pk(e_)]
    S.dma("sp", A8S, PW[:, 15, :, :], key="a8s", reads=pk(8), is_output=True)
    den = tl([H, P]); xr_ = tl([H, P]); fre = tl([H, P]); fim = tl([H, P])
    tt(den[:], ar[:], ar[:], ALU.mult, "den", "ar", "ar")
    tt(tmp[0][:], ai[:], ai[:], ALU.mult, "tmp0", "ai", "ai")
    tt(den[:], den[:], tmp[0][:], ALU.add, "den", "den", "tmp0")
    dve(lambda e: e.reciprocal(out=den[:], in_=den[:]), ["den"], ["den"])
    dve(lambda e: e.tensor_scalar(out=xr_[:], in0=pw(1, 0), scalar1=-1.0, scalar2=None, op0=ALU.add),
        ["pw1_0"], ["xr"])
    tt(tmp[0][:], xr_[:], ar[:], ALU.mult, "tmp0", "xr", "ar")
    tt(tmp[1][:], pw(1, 1), ai[:], ALU.mult, "tmp1", "pw1_1", "ai")
    tt(fre[:], tmp[0][:], tmp[1][:], ALU.add, "fre", "tmp0", "tmp1")
    tt(fre[:], fre[:], den[:], ALU.mult, "fre", "fre", "den")
    tt(tmp[2][:], pw(1, 1), ar[:], ALU.mult, "tmp2", "pw1_1", "ar")
    tt(tmp[3][:], xr_[:], ai[:], ALU.mult, "tmp3", "xr", "ai")
    tt(fim[:], tmp[2][:], tmp[3][:], ALU.subtract, "fim", "tmp2", "tmp3")
    tt(fim[:], fim[:], den[:], ALU.mult, "fim", "fim", "den")
    Pst = tl([P, P, 8]); Qst = tl([P, P, 8])
    for s_ in range(8):
        cmul(Pst[0:H, :, s_], Qst[H:P, :, s_], pw(7 - s_, 0), pw(7 - s_, 1), fre[:], fim[:],
             ["Pst_lo", "Qst_hi"], pk(7 - s_) + ["fre", "fim"])
    S.op("act", lambda e: e.activation(out=Pst[H:P, :, :], in_=Pst[0:H, :, :], func=AF.Copy),
         reads=["Pst_lo"], writes=["Pst_hi"])
    S.op("act", lambda e: e.activation(out=Qst[0:H, :, :], in_=Qst[H:P, :, :], func=AF.Copy, scale=-1.0),
         reads=["Qst_hi"], writes=["Qst_lo"])
    Pv = tl([P, 16, P]); Qv = tl([P, 16, P])
    S.op("act", lambda e: e.activation(out=Pv[0:H], in_=PW[:, :, 0, :], func=AF.Copy), reads=allpw, writes=["Pv_lo"])
    S.op("act", lambda e: e.activation(out=Pv[H:P], in_=PW[:, :, 0, :], func=AF.Copy), reads=allpw, writes=["Pv_hi"])
    S.op("act", lambda e: e.activation(out=Qv[0:H], in_=PW[:, :, 1, :], func=AF.Copy, scale=-1.0), reads=allpw,
         writes=["Qv_lo"])
    S.op("act", lambda e: e.activation(out=Qv[H:P], in_=PW[:, :, 1, :], func=AF.Copy, scale=-1.0), reads=allpw,
         writes=["Qv_hi"])
    t1 = tl([P, 32, 128]); t2 = tl([P, 32, 128])
    raw = t1[:].rearrange("p a b -> p (a b)")
    R = tl([P, P, 16]); Sx = tl([P, P, 16]); Rp = tl([P, P, 16]); Sp = tl([P, P, 16])
    srcs = (("b_re", 0), ("b_im", 1), ("c_re", 2), ("c_im", 3))
    for nm, idx in srcs:
        S.dma("sp", raw[:, idx * 1024:(idx + 1) * 1024], prm[nm], key="raw%d" % idx, writes=["raw%d" % idx])
    cnt = [0]
    for nm, idx in srcs:
        rv = raw[:, idx * 1024:(idx + 1) * 1024]
        for q4 in range(4):
            pb_ = pA if cnt[0] % 2 == 0 else pA2
            pkey = "pA" if cnt[0] % 2 == 0 else "pA2"
            cnt[0] += 1
            for j in range(4):
                qq = q4 * 4 + j
                if idx < 2:
                    src = rv.rearrange("p (n q) -> p q n", q=16)[:, qq, :]
                else:
                    src = rv[:, qq * 64:(qq + 1) * 64]
                S.op("pe", lambda e, pb_=pb_, j=j, src=src: e.transpose(pb_[0:H, j, :], src, ident[:]),
                     reads=["raw%d" % idx, "ident"], writes=[pkey])
            qs = slice(q4 * 4, q4 * 4 + 4)
            pin = pb_[0:H, :, :]

            def outv(tile_, lo):
                v = tile_[0:H] if lo else tile_[H:P]
                return v.rearrange("p g q -> p q g")[:, qs, :]
            if idx == 0:
                dsts = ((R, True, 1.0), (Sx, False, 1.0))
            elif idx == 1:
                dsts = ((R, False, 1.0), (Sx, True, 1.0))
            elif idx == 2:
                dsts = ((Rp, True, 1.0), (Sp, False, 1.0))
            else:
                dsts = ((Rp, False, -1.0), (Sp, True, 1.0))
            for (tile_, lo, sc) in dsts:
                ov = outv(tile_, lo)
                S.op("act", lambda e, ov=ov, pin=pin, sc=sc: e.activation(out=ov, in_=pin, func=AF.Copy, scale=sc),
                     reads=[pkey], writes=["tab%d_%d_%d" % (id(tile_) % 997, lo, q4)])
    tabkeys = None
    X7c = tl([P, 32, 128], BF16); Ypc = tl([P, 32, 128], BF16); Ec = tl([P, 32, 128], BF16)
    Mc = tl([P, 32, 128], BF16); Gc = tl([P, 32, 128], BF16)
    pM = [A.ps("q_pM%d" % i, [P, 4, P]) for i in range(2)]
    pG = [A.ps("q_pG%d" % i, [P, 8, P], BF16) for i in range(2)]
    anytab = [k for k in S.last_w.keys() if isinstance(k, str) and k.startswith("tab")]
    for ch in range(4):
        gs = slice(ch * 32, ch * 32 + 32)
        t1v = t1[:].rearrange("p g (s q) -> p g s q", q=16)
        t2v = t2[:].rearrange("p g (s q) -> p g s q", q=16)

        def build(dst, dkey, Pt, Qt, Rt, St, pkeys):
            dve(lambda e: e.tensor_tensor(out=t1v, in0=Pt.unsqueeze(3).to_broadcast([P, 32, 8, 16]),
                                          in1=Rt.unsqueeze(2).to_broadcast([P, 32, 8, 16]), op=ALU.mult),
                pkeys + anytab + ["raw0", "raw1", "raw2", "raw3"], ["t1"])
            S.op("pool", lambda e: e.tensor_tensor(out=t2v, in0=Qt.unsqueeze(3).to_broadcast([P, 32, 8, 16]),
                                                   in1=St.unsqueeze(2).to_broadcast([P, 32, 8, 16]), op=ALU.mult),
                 reads=pkeys + anytab, writes=["t2"])
            dve(lambda e: e.tensor_tensor(out=dst[:], in0=t1[:], in1=t2[:], op=ALU.add), ["t1", "t2"], [dkey])
        build(X7c, "X7c", Pst[:, gs, :], Qst[:, gs, :], R[:, gs, :], Sx[:, gs, :],
              ["Pst_lo", "Pst_hi", "Qst_lo", "Qst_hi"])
        pvk = ["Pv_lo", "Pv_hi", "Qv_lo", "Qv_hi"]
        build(Ypc, "Ypc", Pv[:, 0:8, gs].rearrange("p e g -> p g e"), Qv[:, 0:8, gs].rearrange("p e g -> p g e"),
              Rp[:, gs, :], Sp[:, gs, :], pvk)
        build(Ec, "Ec", Pv[:, 8:16, gs].rearrange("p e g -> p g e"), Qv[:, 8:16, gs].rearrange("p e g -> p g e"),
              Rp[:, gs, :], Sp[:, gs, :], pvk)
        for g4 in range(8):
            b = g4 % 2
            for j in range(4):
                g = g4 * 4 + j
                S.op("pe", lambda e, b=b, j=j, g=g: e.matmul(pM[b][:, j, :], lhsT=X7c[:, g, :], rhs=Ypc[:, g, :],
                                                             start=True, stop=True),
                     reads=["X7c", "Ypc"], writes=[("pM", b)])
            dve(lambda e, b=b, g4=g4: e.tensor_tensor(
                out=Mc[:, g4 * 4:(g4 + 1) * 4, :], in0=pM[b][:],
                in1=maskM[:].unsqueeze(1).to_broadcast([P, 4, P]), op=ALU.mult),
                [("pM", b), "maskM"], [("Mc", g4)])
        for g8 in range(4):
            b = g8 % 2
            for j in range(8):
                g = g8 * 8 + j
                S.op("pe", lambda e, b=b, j=j, g=g: e.transpose(pG[b][:, j, :], X7c[:, g, :], identb[:]),
                     reads=["X7c", "identb"], writes=[("pG", b)])
            S.op("act", lambda e, b=b, g8=g8: e.activation(out=Gc[:, g8 * 8:(g8 + 1) * 8, :], in_=pG[b][:],
                                                           func=AF.Copy),
                 reads=[("pG", b)], writes=[("Gc", g8)])
        S.dma("sp", SM[:, gs, :], Mc[:], key="SMst", reads=[("Mc", i) for i in range(8)], is_output=True)
        S.dma("sp", SG[:, gs, :], Gc[:], key="SGst", reads=[("Gc", i) for i in range(4)], is_output=True)
        S.dma("sp", SE[:, gs, :], Ec[:], key="SEst", reads=["Ec"], is_output=True)


def s5main_body(S, A, zo, zp, SM, SG, SE, A8S, selm_ap, seqm_ap, ident_ap, s5in, YS, s5p_out, s5s_out):
    H = 64
    ident = A.sb("m_ident", [P, P], F32)
    S.dma("sp", ident[:], ident_ap, key="ident", writes=["ident"])
    selm = A.sb("m_selm", [P, 8], F32)
    S.dma("sp", selm[:], selm_ap, key="selm", writes=["selm"])
    bsf = A.sb("m_bsf", [P, 16], F32)
    bsel = A.sb("m_bsel", [P, 16], BF16)
    S.dma("sp", bsf[:], seqm_ap, key="bsf", writes=["bsf"])
    S.op("dve", lambda e: e.tensor_copy(out=bsel[:], in_=bsf[:]), reads=["bsf"], writes=["bsel"])
    a8 = A.sb("m_a8", [H, 2, P], F32)
    S.dma("sp", a8[:], A8S, key="a8", writes=["a8"])
    AA = A.sb("m_AA", [H, 2, P], F32)
    AB = A.sb("m_AB", [H, 2, P], F32)
    S.op("dve", lambda e: e.tensor_copy(out=AA[:, 0, :], in_=a8[:, 0, :]), reads=["a8"], writes=["AA0"])
    S.op("dve", lambda e: e.tensor_copy(out=AA[:, 1, :], in_=a8[:, 0, :]), reads=["a8"], writes=["AA1"])
    S.op("dve", lambda e: e.tensor_scalar(out=AB[:, 0, :], in0=a8[:, 1, :], scalar1=-1.0, scalar2=None,
                                          op0=ALU.mult), reads=["a8"], writes=["AB0"])
    S.op("dve", lambda e: e.tensor_copy(out=AB[:, 1, :], in_=a8[:, 1, :]), reads=["a8"], writes=["AB1"])
    AK = ["AA0", "AA1", "AB0", "AB1"]
    Gm = A.sb("m_G", [P, P, P], BF16)
    for ch in range(4):
        S.dma("sp", Gm[:, ch * 32:(ch + 1) * 32, :], SG[:, ch * 32:(ch + 1) * 32, :], key=("Gl", ch),
              writes=[("G", ch)])
    MEc = [A.sb("m_ME%d" % i, [P, 32, 2, P], BF16) for i in range(2)]
    uin = [A.sb("m_uin%d" % i, [P, 2048], F32) for i in range(2)]
    urep = A.sb("m_urep", [P, 32, 128], BF16)
    Ut = A.sb("m_Ut", [P, P, 16], BF16)
    VH = A.sb("m_VH", [H, 2, P, 17], F32)
    Hbf = A.sb("m_Hbf", [P, P, 16], BF16)
    ysb = A.sb("m_ysb", [16, 32, 128], F32)
    P1 = A.sb("m_P1", [H, 2, P], F32)
    P2 = A.sb("m_P2", [H, 2, P], F32)
    Vs = A.sb("m_Vs", [H, 2, P, 16], F32)
    H0s = A.sb("m_H0s", [H, 2, P, 16], F32)
    psU = [A.ps("m_psU%d" % i, [P, 32, 16]) for i in range(2)]
    psV = [A.ps("m_psV%d" % i, [P, 32, 16]) for i in range(2)]
    psY = [A.ps("m_psY%d" % i, [P, 4, P]) for i in range(2)]
    ptr = A.ps("m_ptr", [P, 4, P])
    S.op("pool", lambda e: e.memset(VH[:], 0.0), writes=["VH"])
    cn = {"u": 0, "U": 0, "V": 0, "Y": 0, "me": 0}

    def make_U(src_ap):
        i = cn["u"] % 2
        cn["u"] += 1
        S.dma("sp", uin[i][:], src_ap, key=("uin", i), writes=[("uin", i)])
        for ch in range(4):
            uv = uin[i][:, ch * 512:(ch + 1) * 512].rearrange("p (g q) -> p g q", q=16)
            S.op("dve", lambda e, uv=uv: e.tensor_tensor(
                out=urep[:].rearrange("p g (s q) -> p g s q", q=16),
                in0=uv.unsqueeze(2).to_broadcast([P, 32, 8, 16]),
                in1=selm[:].unsqueeze(1).unsqueeze(3).to_broadcast([P, 32, 8, 16]), op=ALU.mult),
                reads=[("uin", i), "selm"], writes=["urep"])
            b = cn["U"] % 2
            cn["U"] += 1
            for j in range(32):
                S.op("pe", lambda e, b=b, j=j: e.matmul(psU[b][:, j, :], lhsT=urep[:, j, :], rhs=bsel[:],
                                                        start=True, stop=True),
                     reads=["urep", "bsel"], writes=[("psU", b)])
            S.op("act", lambda e, b=b, ch=ch: e.activation(out=Ut[:, ch * 32:(ch + 1) * 32, :], in_=psU[b][:],
                                                           func=AF.Copy),
                 reads=[("psU", b)], writes=[("Ut", ch)])

    def make_V(dst_fn, dkey):
        for ch in range(4):
            b = cn["V"] % 2
            cn["V"] += 1
            for j in range(32):
                g = ch * 32 + j
                S.op("pe", lambda e, b=b, j=j, g=g: e.matmul(psV[b][:, j, :], lhsT=Gm[:, g, :], rhs=Ut[:, g, :],
                                                             start=True, stop=True),
                     reads=[("G", ch), ("Ut", ch)], writes=[("psV", b)])
            for c in range(2):
                S.op("act", lambda e, b=b, c=c, ch=ch: e.activation(
                    out=dst_fn(c, ch), in_=psV[b][c * H:(c + 1) * H, :, :], func=AF.Copy),
                    reads=[("psV", b)], writes=[dkey])

    def cstep(Hj0, Hj1, Hj, Hn, key, w, extra=()):
        p1, p2 = w
        S.op("dve", lambda e: e.tensor_tensor(out=p1, in0=AA_v(Hj), in1=Hj, op=ALU.mult),
             reads=[key] + AK + list(extra), writes=["P1"])
        S.op("dve", lambda e: e.tensor_tensor(out=sub(p2, 0), in0=AB_v(Hj, 0), in1=Hj1, op=ALU.mult),
             reads=[key] + AK + list(extra), writes=["P2a"])
        S.op("dve", lambda e: e.tensor_tensor(out=sub(p2, 1), in0=AB_v(Hj, 1), in1=Hj0, op=ALU.mult),
             reads=[key] + AK + list(extra), writes=["P2b"])
        S.op("dve", lambda e: e.tensor_tensor(out=Hn, in0=Hn, in1=p1, op=ALU.add), reads=[key, "P1"], writes=[key])
        S.op("dve", lambda e: e.tensor_tensor(out=Hn, in0=Hn, in1=p2, op=ALU.add),
             reads=[key, "P2a", "P2b"], writes=[key])

    def sub(ap, c):
        return ap[:, c]

    def AA_v(like):
        if len(like.shape) == 3:
            return AA[:]
        return AA[:].unsqueeze(3).to_broadcast([H, 2, P, 16])

    def AB_v(like, c):
        if len(like.shape) == 3:
            return AB[:, c, :]
        return AB[:, c, :].unsqueeze(2).to_broadcast([H, P, 16])

    def make_Hbf(src, skey):
        S.op("act", lambda e: e.activation(out=Hbf[0:H], in_=src[:, 0, :, 0:16], func=AF.Copy),
             reads=[skey], writes=["Hbf0"])
        S.op("pool", lambda e: e.tensor_copy(out=Hbf[H:P], in_=src[:, 1, :, 0:16]),
             reads=[skey], writes=["Hbf1"])

    def make_Y(t):
        for ch in range(4):
            gs = slice(ch * 32, ch * 32 + 32)
            mi = cn["me"] % 2
            cn["me"] += 1
            S.dma("sp", MEc[mi][:, :, 0, :], SM[:, gs, :], key=("ME", mi), writes=[("ME", mi)])
            S.dma("sp", MEc[mi][:, :, 1, :], SE[:, gs, :], key=("ME", mi), writes=[("ME", mi)])
            for g4 in range(8):
                b = cn["Y"] % 2
                cn["Y"] += 1
                for j in range(4):
                    gl = g4 * 4 + j
                    g = ch * 32 + gl
                    S.op("pe", lambda e, b=b, j=j, g=g, gl=gl, mi=mi: e.matmul(
                        psY[b][0:16, j, :], lhsT=Ut[:, g, :], rhs=MEc[mi][:, gl, 0, :], start=True, stop=False),
                        reads=[("Ut", ch), ("ME", mi)], writes=[("psY", b)])
                    S.op("pe", lambda e, b=b, j=j, g=g, gl=gl, mi=mi: e.matmul(
                        psY[b][0:16, j, :], lhsT=Hbf[:, g, :], rhs=MEc[mi][:, gl, 1, :], start=False, stop=True),
                        reads=["Hbf0", "Hbf1", ("ME", mi)], writes=[("psY", b)])
                S.op("act", lambda e, b=b, g4=g4: e.activation(out=ysb[:, g4 * 4:(g4 + 1) * 4, :],
                                                               in_=psY[b][0:16, :, :], func=AF.Copy),
                     reads=[("psY", b)], writes=["ysb"])
            dst = YS[t * P:(t + 1) * P, ch * 512:(ch + 1) * 512].rearrange("(b i) (g p) -> b i g p", i=8, p=16)
            for i_ in range(8):
                S.dma("act", dst[:, i_], ysb[:, :, i_ * 16:(i_ + 1) * 16], key="ysb_st", reads=["ysb"],
                      writes=["ys_dram"], is_output=True)

    def vh_dst(c, ch):
        return VH[:, c, ch * 32:(ch + 1) * 32, 1:17]

    def prompt_tile(src_ap, t, own):
        make_U(src_ap)
        make_V(vh_dst, "VH")
        for j in range(16):
            cstep(VH[:, 0, :, j], VH[:, 1, :, j], VH[:, :, :, j], VH[:, :, :, j + 1], "VH", (P1[:], P2[:]))
        if own:
            make_Hbf(VH, "VH")
            make_Y(t)
        S.op("dve", lambda e: e.tensor_copy(out=VH[:, :, :, 0], in_=VH[:, :, :, 16]), reads=["VH"], writes=["VH"])

    for t in range(8):
        prompt_tile(zp[t * P:(t + 1) * P, 6144:8192], t, False)
    for t in range(8):
        prompt_tile(zo[t * P:(t + 1) * P, 12288:14336], t, True)
    hout = A.sb("m_hout", [P, 16, H], F32)
    for c in range(2):
        S.op("pe", lambda e, c=c: e.transpose(ptr[:, c, 0:H], VH[:, c, :, 0], ident[0:H, 0:H]),
             reads=["VH", "ident"], writes=["ptr"])
    S.op("act", lambda e: e.activation(out=hout[:, 0:2, :], in_=ptr[:, 0:2, 0:H], func=AF.Copy),
         reads=["ptr"], writes=["hout"])
    S.dma("sp", s5p_out.rearrange("c g n -> g c n"), hout[:, 0:2, :], key="s5p_st", reads=["hout"],
          writes=["s5p_dram"], is_output=True)

    hraw = A.sb("m_hraw", [P, 16, H], F32)
    for c in range(2):
        S.dma("sp", hraw[:], s5in[c].rearrange("s g n -> g s n"), key="hraw", writes=["hraw"])
        for s4 in range(4):
            for j in range(4):
                s_ = s4 * 4 + j
                S.op("pe", lambda e, j=j, s_=s_: e.transpose(ptr[0:H, j, :], hraw[:, s_, :], ident[:]),
                     reads=["hraw", "ident"], writes=["ptr"])
            S.op("act", lambda e, c=c, s4=s4: e.activation(
                out=H0s[:, c, :, s4 * 4:(s4 + 1) * 4].rearrange("p g s -> p s g"), in_=ptr[0:H, :, :],
                func=AF.Copy), reads=["ptr"], writes=["H0s"])
    make_U(zo[1024:1152, 12288:14336])
    make_V(lambda c, ch: Vs[:, c, ch * 32:(ch + 1) * 32, :], "Vs")
    make_Hbf(H0s, "H0s")
    make_Y(8)
    P1s = A.sb("m_P1s", [H, 2, P, 16], F32)
    P2s = A.sb("m_P2s", [H, 2, P, 16], F32)
    cstep(H0s[:, 0], H0s[:, 1], H0s[:], Vs[:], "Vs", (P1s[:], P2s[:]), extra=["H0s"])
    for c in range(2):
        for s8 in range(2):
            for j in range(8):
                s_ = s8 * 8 + j
                S.op("pe", lambda e, c=c, j=j, s_=s_: e.transpose(
                    ptr[:, j // 2, (j % 2) * H:(j % 2 + 1) * H], Vs[:, c, :, s_], ident[0:H, 0:H]),
                    reads=["Vs", "ident"], writes=["ptr"])
            S.op("act", lambda e, s8=s8: e.activation(
                out=hout[:, s8 * 8:(s8 + 1) * 8, :], in_=ptr[:].rearrange("p a (b n) -> p (a b) n", n=H),
                func=AF.Copy), reads=["ptr"], writes=["hout"])
        S.dma("sp", s5s_out[c].rearrange("s g n -> g s n"), hout[:], key="s5s_st", reads=["hout"],
              writes=["s5s_dram"], is_output=True)


def build_program(debug=False, stages=None, scr_in=()):
    nc = bass.Bass("TRN2", target_bir_lowering=False)
    NT = NTOK_OWN // P

    def din(name, shape, dt=F32):
        return nc.dram_tensor(name, list(shape), dt, kind="ExternalInput").ap()

    def dout(name, shape, dt=F32):
        return nc.dram_tensor(name, list(shape), dt, kind="ExternalOutput").ap()

    def dscr(name, shape, dt=F32):
        kind = "ExternalOutput" if debug else "Internal"
        if name in scr_in:
            kind = "ExternalInput"
        return nc.dram_tensor(name, list(shape), dt, kind=kind).ap()

    xo = din("xo", [NTOK_OWN, D_MODEL])
    xp = din("xp", [NTOK_PRE, D_MODEL])
    mem = din("mem", [256, D_MODEL])
    w_in = din("w_in", [D_MODEL, IN_WIDTH])
    w_mem_kv = din("w_mem_kv", [D_MODEL, 4096])
    ident = din("ident", [P, P])

    memkv = dout("memkv", [256, 4096])
    zo = dscr("zo", [NTOK_OWN, IN_WIDTH])
    zp = dscr("zp", [NTOK_PRE, 8192])

    def st_mem(S, A):
        blocks = [(c0, AF.Copy, (lambda t, c0=c0: memkv[t * P:(t + 1) * P, c0:c0 + 256]))
                  for c0 in range(0, 4096, 256)]
        gemm_body(S, A, "m", mem, 2, w_mem_kv, blocks, ident)
    if stages is None or 'mem' in stages:
        run_stage(nc, st_mem)

    def st_pre(S, A):
        blocks = []
        for (src0, n, dst0) in ((2048, 2048, 0), (4096, 4096, 2048), (12288, 2048, 6144)):
            for c in range(0, n, 256):
                blocks.append((src0 + c, AF.Copy,
                               (lambda t, d=dst0 + c: zp[t * P:(t + 1) * P, d:d + 256])))
        gemm_body(S, A, "p", xp, NTOK_PRE // P, w_in, blocks, ident)
    if stages is None or 'pre' in stages:
        run_stage(nc, st_pre)

    def st_own(S, A):
        blocks = [(c0, col_func(c0), (lambda t, c0=c0: zo[t * P:(t + 1) * P, c0:c0 + 256]))
                  for c0 in range(0, IN_WIDTH, 256)]
        gemm_body(S, A, "o", xo, NTOK_OWN // P, w_in, blocks, ident)
    if stages is None or 'own' in stages:
        run_stage(nc, st_own)

    rq = din("rq", [9, P, 2, 16, 64])
    rk_own = din("rk_own", [9, P, 2, 16, 64])
    rk_pre = din("rk_pre", [8, P, 2, 16, 64])
    tabs = {"rq": rq, "rk_own": rk_own, "rk_pre": rk_pre,
            "mask_p": din("mask_p", [P, P]), "mask_s": din("mask_s", [P, P]),
            "seqm": din("seqm", [P, 16]), "seqmT": din("seqmT", [P, 16, P])}
    sret_in = din("sret_in", [16, 16, P, 256])
    sret_out = dout("sret_out", [16, 16, P, 256])
    sretp_out = dout("sretp_out", [16, P, 256])
    OT = dscr("OT", [9, P, 64, P], BF16)

    def st_ret(S, A):
        retention_body(S, A, zo, zp, tabs, sret_in, sret_out, sretp_out, OT, ident)
    if stages is None or 'ret' in stages:
        run_stage(nc, st_ret)

    cmk = din("cmk", [16, 256, 2048])
    cmv = din("cmv", [16, 256, 2048])

    def st_x(S, A):
        xattn_body(S, A, zo, memkv, cmk, cmv, tabs["seqmT"], OT, ident)
    if stages is None or 'x' in stages:
        run_stage(nc, st_x)

    prm = {"a_re": din("s5_a_re", [P, 64]), "a_im": din("s5_a_im", [P, 64]), "log_step": din("s5_log_step", [1, P]),
           "b_re": din("s5_b_re", [P, 1024]), "b_im": din("s5_b_im", [P, 1024]),
           "c_re": din("s5_c_re", [P, 1024]), "c_im": din("s5_c_im", [P, 1024])}
    maskM = din("maskM", [P, P])
    SM = dscr("SM", [P, P, P], BF16)
    SG = dscr("SG", [P, P, P], BF16)
    SE = dscr("SE", [P, P, P], BF16)
    A8S = dscr("A8S", [64, 2, P])

    def st_s5prep(S, A):
        s5prep_body(S, A, prm, ident, maskM, SM, SG, SE, A8S)
    if stages is None or 's5prep' in stages:
        run_stage(nc, st_s5prep)

    selm = din("selm", [P, 8])
    s5in = din("s5in", [2, 16, P, 64])
    YS = dscr("YS", [NTOK_OWN, 2048])
    s5p_out = dout("s5p_out", [2, P, 64])
    s5s_out = dout("s5s_out", [2, 16, P, 64])

    def st_s5main(S, A):
        s5main_body(S, A, zo, zp, SM, SG, SE, A8S, selm, tabs["seqm"], ident, s5in, YS, s5p_out, s5s_out)
    if stages is None or 's5main' in stages:
        run_stage(nc, st_s5main)

    s5d = din("s5_d", [1, 2048])
    w_glu = din("w_glu", [2048, 4096])
    GL = dscr("GL", [NTOK_OWN, 2048])
    GAB = dscr("GAB", [NTOK_OWN, 4096])

    def st_gelu(S, A):
        db = A.sb("g_db", [P, 2048], F32)
        S.dma("sp", db[:], s5d.to_broadcast([P, 2048]), key="db", writes=["db"])
        yb = [A.sb("g_y%d" % i, [P, 2048], F32) for i in range(2)]
        ub = [A.sb("g_u%d" % i, [P, 2048], F32) for i in range(2)]
        tb = [A.sb("g_t%d" % i, [P, 2048], F32) for i in range(2)]
        for t in range(NT):
            i = t % 2
            y, u, tt_ = yb[i], ub[i], tb[i]
            S.dma("sp", y[:], YS[t * P:(t + 1) * P, :], key=("y", i), writes=[("y", i)])
            S.dma("sp", u[:], zo[t * P:(t + 1) * P, 12288:14336], key=("u", i), writes=[("u", i)])
            S.op("pool", lambda e, u=u: e.tensor_tensor(out=u[:], in0=u[:], in1=db[:], op=ALU.mult),
                 reads=[("u", i), "db"], writes=[("u", i)])
            S.op("dve", lambda e, y=y, u=u: e.tensor_tensor(out=y[:], in0=y[:], in1=u[:], op=ALU.add),
                 reads=[("y", i), ("u", i)], writes=[("y", i)])
            S.op("pool", lambda e, y=y, tt_=tt_: e.tensor_tensor(out=tt_[:], in0=y[:], in1=y[:], op=ALU.mult),
                 reads=[("y", i)], writes=[("t", i)])
            S.op("dve", lambda e, tt_=tt_: e.tensor_scalar(out=tt_[:], in0=tt_[:], scalar1=0.044715, scalar2=1.0,
                                                           op0=ALU.mult, op1=ALU.add),
                 reads=[("t", i)], writes=[("t", i)])
            S.op("dve", lambda e, y=y, tt_=tt_: e.tensor_tensor(out=tt_[:], in0=tt_[:], in1=y[:], op=ALU.mult),
                 reads=[("t", i), ("y", i)], writes=[("t", i)])
            S.op("act", lambda e, tt_=tt_: e.activation(out=tt_[:], in_=tt_[:], func=AF.Sigmoid,
                                                        scale=1.5957691216057308),
                 reads=[("t", i)], writes=[("t", i)])
            S.op("dve", lambda e, y=y, tt_=tt_: e.tensor_tensor(out=y[:], in0=y[:], in1=tt_[:], op=ALU.mult),
                 reads=[("t", i), ("y", i)], writes=[("y", i)])
            S.dma("sp", GL[t * P:(t + 1) * P, :], y[:], key=("gl", i), reads=[("y", i)], is_output=True)

    def st_glu(S, A):
        blocks = [(c0, (AF.Copy if c0 < 2048 else AF.Sigmoid),
                   (lambda t, c0=c0: GAB[t * P:(t + 1) * P, c0:c0 + 256])) for c0 in range(0, 4096, 256)]
        gemm_body(S, A, "g", GL, NT, w_glu, blocks, ident, nkt=16)

    def st_s5fin(S, A):
        identf = A.sb("f_ident", [P, P], F32)
        identb = A.sb("f_identb", [P, P], BF16)
        S.dma("sp", identf[:], ident, key="ident", writes=["ident"])
        S.op("dve", lambda e: e.tensor_copy(out=identb[:], in_=identf[:]), reads=["ident"], writes=["identb"])
        ab = [A.sb("f_ab%d" % i, [P, 4096], F32) for i in range(2)]
        gg = [A.sb("f_g%d" % i, [P, 2048], F32) for i in range(2)]
        ob = [A.sb("f_ob%d" % i, [P, 16, P], BF16) for i in range(2)]
        oT = [A.sb("f_oT%d" % i, [P, 16, P], BF16) for i in range(2)]
        ptr = [A.ps("f_ptr%d" % i, [P, 8, P], BF16) for i in range(2)]
        cnt = 0
        for t in range(NT):
            i = t % 2
            S.dma("sp", ab[i][:], GAB[t * P:(t + 1) * P, :], key=("ab", i), writes=[("ab", i)])
            S.dma("sp", gg[i][:], zo[t * P:(t + 1) * P, 14336:16384], key=("gg", i), writes=[("gg", i)])
            S.op("pool", lambda e, i=i: e.tensor_tensor(out=gg[i][:], in0=gg[i][:], in1=ab[i][:, 2048:4096],
                                                        op=ALU.mult),
                 reads=[("gg", i), ("ab", i)], writes=[("gg", i)])
            S.op("dve", lambda e, i=i: e.tensor_tensor(out=ob[i][:].rearrange("p a b -> p (a b)"),
                                                       in0=ab[i][:, 0:2048], in1=gg[i][:], op=ALU.mult),
                 reads=[("gg", i), ("ab", i)], writes=[("ob", i)])
            for half in range(2):
                b = cnt % 2
                cnt += 1
                for j in range(8):
                    S.op("pe", lambda e, b=b, j=j, i=i, half=half: e.transpose(
                        ptr[b][:, j, :], ob[i][:, half * 8 + j, :], identb[:]),
                        reads=[("ob", i), "identb"], writes=[("ptr", b)])
                S.op("act", lambda e, b=b, i=i, half=half: e.activation(
                    out=oT[i][:, half * 8:(half + 1) * 8, :], in_=ptr[b][:], func=AF.Copy),
                    reads=[("ptr", b)], writes=[("oT", i, half)])
            S.dma("act", OT[t, :, 32:48, :], oT[i][:], key=("oTs", i), reads=[("oT", i, 0), ("oT", i, 1)],
                  is_output=True)
    if stages is None or 's5post' in stages:
        run_stage(nc, st_gelu)
        run_stage(nc, st_glu)
        run_stage(nc, st_s5fin)

    w_pa = din("w_proj_a", [4096, D_MODEL])
    w_pb = din("w_proj_b", [2048, D_MODEL])
    w_pc = din("w_proj_c", [2048, D_MODEL])
    w_o = din("w_out", [D_MODEL, D_MODEL])
    ln_g = din("ln_g", [1, D_MODEL])
    ln_b = din("ln_b", [1, D_MODEL])
    y_out = dout("y_out", [NTOK_OWN, D_MODEL])
    PR = [dscr("PR%d" % i, [NTOK_OWN, D_MODEL]) for i in range(3)]
    HP = dscr("HP", [NTOK_OWN, D_MODEL])
    NT = NTOK_OWN // P

    def proj_stage(i, w_ap, ft0, nkt, gate0):
        def body(S, A):
            gt = [A.sb("gt%d_%d" % (i, j), [P, NT, 256], F32) for j in range(2)]

            def pre_block(S, bi):
                c0 = bi * 256
                S.dma("sp", gt[bi % 2][:], zo[:, gate0 + c0:gate0 + c0 + 256].rearrange("(t p) c -> p t c", p=P),
                      key=("gt", bi % 2), writes=[("gt", bi % 2)])

            def epi(S, t, bi, ps_ap, pkey, ob_t, okey):
                j = bi % 2
                S.op("dve", lambda e, j=j, t=t: e.tensor_tensor(out=ob_t[:], in0=ps_ap, in1=gt[j][:, t, :],
                                                                op=ALU.mult),
                     reads=[pkey, ("gt", j)], writes=[okey])
            blocks = [(c0, None, (lambda t, c0=c0: PR[i][t * P:(t + 1) * P, c0:c0 + 256]))
                      for c0 in range(0, D_MODEL, 256)]
            gemm_body(S, A, "j%d" % i, None, NT, w_ap, blocks, ident, nkt=nkt, a_T=(OT, ft0), epi=epi,
                      pre_block=pre_block)
        return body
    if stages is None or 'tail' in stages:
        run_stage(nc, proj_stage(0, w_pa, 0, 32, 20480))
        run_stage(nc, proj_stage(1, w_pb, 32, 16, 24576))
        run_stage(nc, proj_stage(2, w_pc, 48, 16, 28672))

    def st_out(S, A):
        xr = [A.sb("xr%d" % j, [P, NT, 256], F32) for j in range(2)]
        alpha = float((2.0 * 1) ** 0.25)

        def pre_block(S, bi):
            c0 = bi * 256
            S.dma("sp", xr[bi % 2][:], xo[:, c0:c0 + 256].rearrange("(t p) c -> p t c", p=P),
                  key=("xr", bi % 2), writes=[("xr", bi % 2)])

        def epi(S, t, bi, ps_ap, pkey, ob_t, okey):
            j = bi % 2
            S.op("dve", lambda e, j=j, t=t: e.scalar_tensor_tensor(out=ob_t[:], in0=xr[j][:, t, :], scalar=alpha,
                                                                   in1=ps_ap, op0=ALU.mult, op1=ALU.add),
                 reads=[pkey, ("xr", j)], writes=[okey])
        blocks = [(c0, None, (lambda t, c0=c0: HP[t * P:(t + 1) * P, c0:c0 + 256]))
                  for c0 in range(0, D_MODEL, 256)]
        gemm_body(S, A, "w", PR[0], NT, w_o, blocks, ident, x_sum=[PR[1], PR[2]], epi=epi, pre_block=pre_block)
    if stages is None or 'tail' in stages or 'out' in stages:
        run_stage(nc, st_out)

    def st_ln(S, A):
        gb = A.sb("ln_gb", [P, D_MODEL], F32)
        bb = A.sb("ln_bb", [P, D_MODEL], F32)
        S.dma("sp", gb[:], ln_g.to_broadcast([P, D_MODEL]), key="gb", writes=["gb"])
        S.dma("sp", bb[:], ln_b.to_broadcast([P, D_MODEL]), key="bb", writes=["bb"])
        hb = [A.sb("ln_h%d" % j, [P, D_MODEL], F32) for j in range(2)]
        st = A.sb("ln_st", [P, 8, 6], F32)
        mv = A.sb("ln_mv", [P, 2], F32)
        nb = A.sb("ln_nb", [P, 1], F32)
        for t in range(NT):
            j = t % 2
            h = hb[j]
            S.dma("sp", h[:], HP[t * P:(t + 1) * P, :], key=("h", j), writes=[("h", j)])
            for c in range(8):
                S.op("dve", lambda e, c=c, h=h: e.bn_stats(out=st[:, c, :], in_=h[:, c * 512:(c + 1) * 512]),
                     reads=[("h", j)], writes=[("st", c)])
            S.op("dve", lambda e: e.bn_aggr(out=mv[:], in_=st[:].rearrange("p a b -> p (a b)")),
                 reads=[("st", c) for c in range(8)], writes=["mv"])
            S.op("dve", lambda e: e.tensor_scalar(out=mv[:, 1:2], in0=mv[:, 1:2], scalar1=1e-5, scalar2=None,
                                                  op0=ALU.add), reads=["mv"], writes=["mv"])
            S.op("act", lambda e: e.activation(out=mv[:, 1:2], in_=mv[:, 1:2], func=AF.Sqrt),
                 reads=["mv"], writes=["mv"])
            S.op("dve", lambda e: e.reciprocal(out=mv[:, 1:2], in_=mv[:, 1:2]), reads=["mv"], writes=["mv"])
            S.op("dve", lambda e: e.scalar_tensor_tensor(out=nb[:], in0=mv[:, 0:1], scalar=-1.0, in1=mv[:, 1:2],
                                                         op0=ALU.mult, op1=ALU.mult),
                 reads=["mv"], writes=["nb"])
            S.op("act", lambda e, h=h: e.activation(out=h[:], in_=h[:], func=AF.Identity, bias=nb[:],
                                                    scale=mv[:, 1:2]),
                 reads=[("h", j), "mv", "nb"], writes=[("h", j)])
            S.op("dve", lambda e, h=h: e.tensor_tensor(out=h[:], in0=h[:], in1=gb[:], op=ALU.mult),
                 reads=[("h", j), "gb"], writes=[("h", j)])
            S.op("pool", lambda e, h=h: e.tensor_tensor(out=h[:], in0=h[:], in1=bb[:], op=ALU.add),
                 reads=[("h", j), "bb"], writes=[("h", j)])
            S.dma("sp", y_out[t * P:(t + 1) * P, :], h[:], key=("hout", j), reads=[("h", j)], is_output=True)
    if stages is None or 'tail' in stages or 'ln' in stages:
        run_stage(nc, st_ln)

    return nc


def host_tables(hf):
    f = np.float32
    inv = (1.0 / (np.float32(10000.0) ** (np.arange(64, dtype=f) / np.float32(64)))).astype(f)
    g = np.array(RET_G, dtype=np.float64)
    i = np.arange(P)

    def tab(pos, il, kind):
        ang = pos.astype(f)[:, None] * inv[None, :]
        c, s_ = np.cos(ang).astype(f), np.sin(ang).astype(f)
        if kind == "q":
            sc = g[None, :] ** (il[:, None] + 1.0)
        else:
            sc = g[None, :] ** (-(il[:, None] + 1.0)) * (128.0 ** -0.5)
        out = np.empty((P, 2, 16, 64), f)
        out[:, 0] = (c[:, None, :] * sc[:, :, None]).astype(f)
        out[:, 1] = (s_[:, None, :] * sc[:, :, None]).astype(f)
        return out
    rq = np.stack([tab(hf * 1024 + t * P + i, i, "q") for t in range(8)] + [tab(16384 + (i % 8), i % 8, "q")])
    rk_own = np.stack([tab(hf * 1024 + t * P + i, i, "k") for t in range(8)] + [tab(16384 + (i % 8), i % 8, "k")])
    rk_pre = np.stack([tab(t * P + i, i, "k") for t in range(8)])
    mask_p = (i[None, :] >= i[:, None]).astype(f)
    same = (i[None, :] // 8) == (i[:, None] // 8)
    mask_s = (mask_p * same).astype(f)
    seqm = (i[:, None] // 8 == np.arange(16)[None, :]).astype(f)
    seqmT = np.ascontiguousarray(np.broadcast_to(seqm.T[None, :, :], (P, 16, P))).astype(f)
    maskM = ((i[None, :] // 16) >= (i[:, None] // 16)).astype(f)
    selm = (i[:, None] % 8 == np.arange(8)[None, :]).astype(f)
    return {"rq": rq, "rk_own": rk_own, "rk_pre": rk_pre, "mask_p": mask_p, "mask_s": mask_s,
            "seqm": seqm, "seqmT": seqmT, "maskM": maskM, "selm": selm}


_PROGRAM = None


def kernel(x_prompt, x_sample, mem_prompt, state_ret, state_s5_re, state_s5_im, cache_mem_k, cache_mem_v,
           w_in, w_mem_kv, s5_a_re, s5_a_im, s5_log_step, s5_b_re, s5_b_im, s5_c_re, s5_c_im, s5_d, w_glu,
           w_proj_a, w_proj_b, w_proj_c, w_out, ln_g, ln_b):
    global _PROGRAM
    if _PROGRAM is None:
        _PROGRAM = build_program()
    nc = _PROGRAM
    f = np.float32
    x_prompt = np.asarray(x_prompt, f)
    x_sample = np.asarray(x_sample, f)
    w_in0 = np.ascontiguousarray(np.asarray(w_in, f)[0])
    w_mem0 = np.ascontiguousarray(np.asarray(w_mem_kv, f)[0])
    ident = np.eye(P, dtype=f)
    wpa = np.ascontiguousarray(np.asarray(w_proj_a, f)[0])
    wpb = np.ascontiguousarray(np.asarray(w_proj_b, f)[0])
    wpc = np.ascontiguousarray(np.asarray(w_proj_c, f)[0])
    wout = np.ascontiguousarray(np.asarray(w_out, f)[0])
    lng = np.ascontiguousarray(np.asarray(ln_g, f).reshape(1, D_MODEL))
    lnb = np.ascontiguousarray(np.asarray(ln_b, f).reshape(1, D_MODEL))
    s5p = {"a_re": np.ascontiguousarray(np.asarray(s5_a_re, f)[0]), "a_im": np.ascontiguousarray(np.asarray(s5_a_im, f)[0]),
           "log_step": np.ascontiguousarray(np.asarray(s5_log_step, f).reshape(1, P)),
           "b_re": np.ascontiguousarray(np.asarray(s5_b_re, f)[0].reshape(P, 1024)),
           "b_im": np.ascontiguousarray(np.asarray(s5_b_im, f)[0].reshape(P, 1024)),
           "c_re": np.ascontiguousarray(np.asarray(s5_c_re, f)[0].reshape(P, 1024)),
           "c_im": np.ascontiguousarray(np.asarray(s5_c_im, f)[0].reshape(P, 1024)),
           "d": np.ascontiguousarray(np.asarray(s5_d, f).reshape(1, 2048))}
    wglu = np.ascontiguousarray(np.asarray(w_glu, f)[0])
    in_maps = []
    for c in range(NCORES):
        b, hf = c // 2, c % 2
        xo = np.concatenate([x_prompt[b, hf * 1024:(hf + 1) * 1024],
                             x_sample[16 * c:16 * c + 16].reshape(128, D_MODEL)], axis=0)
        xp = x_prompt[b, 0:1024] if hf == 1 else np.zeros((1024, D_MODEL), f)
        in_maps.append({
            "xo": np.ascontiguousarray(xo), "xp": np.ascontiguousarray(xp),
            "mem": np.ascontiguousarray(np.asarray(mem_prompt, f)[b]),
            "w_in": w_in0, "w_mem_kv": w_mem0, "ident": ident,
            "sret_in": np.ascontiguousarray(np.asarray(state_ret, f)[0, 16 * c:16 * c + 16]),
            "cmk": np.ascontiguousarray(np.asarray(cache_mem_k, f)[0, 16 * c:16 * c + 16]).reshape(16, 256, 2048),
            "cmv": np.ascontiguousarray(np.asarray(cache_mem_v, f)[0, 16 * c:16 * c + 16]).reshape(16, 256, 2048),
            "w_proj_a": wpa, "w_proj_b": wpb, "w_proj_c": wpc, "w_out": wout, "ln_g": lng, "ln_b": lnb,
            "s5_a_re": s5p["a_re"], "s5_a_im": s5p["a_im"], "s5_log_step": s5p["log_step"],
            "s5_b_re": s5p["b_re"], "s5_b_im": s5p["b_im"], "s5_c_re": s5p["c_re"], "s5_c_im": s5p["c_im"],
            "s5_d": s5p["d"], "w_glu": wglu,
            "s5in": np.ascontiguousarray(np.stack([np.asarray(state_s5_re, f)[0, 16 * c:16 * c + 16],
                                                   np.asarray(state_s5_im, f)[0, 16 * c:16 * c + 16]])),
        })
        in_maps[-1].update(host_tables(hf))
    res = run_bass_kernel_spmd(nc, in_maps, core_ids=list(range(NCORES)))
    R = res.results
    memk = np.stack([R[2 * b]["memkv"][:, 0:2048].reshape(256, 4, 512) for b in range(4)])[None]
    memv = np.stack([R[2 * b]["memkv"][:, 2048:4096].reshape(256, 4, 512) for b in range(4)])[None]
    y_p = np.stack([np.concatenate([R[2 * b]["y_out"][:1024], R[2 * b + 1]["y_out"][:1024]], axis=0)
                    for b in range(4)])
    y_s = np.concatenate([R[c]["y_out"][1024:] for c in range(NCORES)], axis=0).reshape(128, 8, D_MODEL)
    sretp = np.stack([R[2 * b + 1]["sretp_out"] for b in range(4)])[None]
    srets = np.concatenate([R[c]["sret_out"] for c in range(NCORES)], axis=0)[None]
    s5p_re = np.stack([R[2 * b + 1]["s5p_out"][0] for b in range(4)])[None]
    s5p_im = np.stack([R[2 * b + 1]["s5p_out"][1] for b in range(4)])[None]
    s5s_re = np.concatenate([R[c]["s5s_out"][0] for c in range(NCORES)], axis=0)[None]
    s5s_im = np.concatenate([R[c]["s5s_out"][1] for c in range(NCORES)], axis=0)[None]
    return (y_p, y_s, sretp, s5p_re, s5p_im, memk, memv, srets, s5s_re, s5s_im)
```

```python
import math
from contextlib import ExitStack

import numpy as np
import concourse.bass as bass
import concourse.mybir as mybir
from concourse.bass_utils import run_bass_kernel_spmd

F32 = mybir.dt.float32
BF16 = mybir.dt.bfloat16
AF = mybir.ActivationFunctionType
ALU = mybir.AluOpType
P = 128
NCORES = 8

D_MODEL = 4096
IN_WIDTH = 32768
NTOK_OWN = 1152
NTOK_PRE = 1024

ENGS = ("pe", "act", "dve", "pool", "sp")
SEM_LIMIT = 20000


class Ins:
    __slots__ = ("eng", "fn", "deps", "is_dma", "key", "need_inc", "semref")

    def __init__(self, eng, fn, is_dma=False, key=None):
        self.eng = eng
        self.fn = fn
        self.deps = []
        self.is_dma = is_dma
        self.key = key
        self.need_inc = False
        self.semref = None


class Sched:
    _stage = 0

    def __init__(self, nc):
        Sched._stage += 1
        self.sid = Sched._stage
        self.nc = nc
        self.ins = []
        self.last_w = {}
        self.readers = {}
        self.dma_count = {}
        self.out_keys = set()

    def _add(self, ins, reads, writes):
        deps = set()
        for k in reads:
            w = self.last_w.get(k)
            if w is not None:
                deps.add(w)
        for k in writes:
            w = self.last_w.get(k)
            if w is not None:
                deps.add(w)
            for r in self.readers.get(k, ()):
                deps.add(r)
        deps.discard(ins)
        for d in deps:
            if d.is_dma:
                ins.deps.append((d, 16 * self.dma_count[d.key]))
            elif d.eng == ins.eng and not ins.is_dma:
                if ins.eng != "pe":
                    ins.deps.append((d, 0))
                    d.need_inc = True
            else:
                ins.deps.append((d, 0))
                d.need_inc = True
        for k in reads:
            self.readers.setdefault(k, []).append(ins)
        for k in writes:
            self.last_w[k] = ins
            self.readers[k] = []
        self.ins.append(ins)
        return ins

    def op(self, eng, fn, reads=(), writes=()):
        return self._add(Ins(eng, fn), list(reads), list(writes))

    def dma(self, eng, out, in_, key, reads=(), writes=(), is_output=False):
        ins = Ins(eng, lambda e: e.dma_start(out=out, in_=in_), is_dma=True, key=key)
        if is_output:
            self.out_keys.add(key)
        self.dma_count.setdefault(key, 0)
        self._add(ins, list(reads), list(writes))
        self.dma_count[key] += 1
        return ins

    def emit(self):
        nc = self.nc
        sem_names = []
        cur = {}
        for ins in self.ins:
            if ins.is_dma or not ins.need_inc:
                continue
            c = cur.get(ins.eng)
            if c is None or c[1] >= SEM_LIMIT:
                c = [len(sem_names), 0]
                sem_names.append("c%d_%s_%d" % (self.sid, ins.eng, len(sem_names)))
                cur[ins.eng] = c
            c[1] += 1
            ins.semref = (c[0], c[1])
        dma_keys = sorted(self.dma_count.keys(), key=str)
        csem = [nc.alloc_semaphore(name=n) for n in sem_names]
        dsem = {k: nc.alloc_semaphore(name="d%d_%d" % (self.sid, i)) for i, k in enumerate(dma_keys)}
        streams = {e: [i for i in self.ins if i.eng == e] for e in ENGS}
        final_dma = dict((k, 16 * v) for k, v in self.dma_count.items())
        out_keys = self.out_keys

        def run(engname, e):
            waited = {}
            for ins in streams[engname]:
                need = {}
                for d, dv in ins.deps:
                    if d.is_dma:
                        sk = ("d", d.key)
                        v = dv
                    else:
                        sk = ("c", d.semref[0])
                        v = d.semref[1]
                    if v > need.get(sk, 0):
                        need[sk] = v
                for sk, v in need.items():
                    if waited.get(sk, 0) >= v:
                        continue
                    waited[sk] = v
                    sem = dsem[sk[1]] if sk[0] == "d" else csem[sk[1]]
                    e.wait_ge(sem, v)
                r = ins.fn(e)
                if ins.is_dma:
                    r.then_inc(dsem[ins.key], 16)
                elif ins.need_inc:
                    r.then_inc(csem[ins.semref[0]], 1)
            if engname == "sp":
                for k in sorted(out_keys, key=str):
                    e.wait_ge(dsem[k], final_dma[k])

        with nc.Block() as block:
            @block.tensor
            def _(e):
                run("pe", e)

            @block.scalar
            def _(e):
                run("act", e)

            @block.vector
            def _(e):
                run("dve", e)

            @block.gpsimd
            def _(e):
                run("pool", e)

            @block.sync
            def _(e):
                run("sp", e)

        if not getattr(Sched, "NOCLEAR", False):
            nc.clear_and_free_semaphores(csem + list(dsem.values()))
        if not getattr(Sched, "NOCLEAR", False):
            nc.all_engine_barrier()


class Alloc:
    def __init__(self, nc, st):
        self.nc = nc
        self.st = st

    def sb(self, name, shape, dt):
        return self.st.enter_context(self.nc.sbuf_tensor(name, list(shape), dt))

    def ps(self, name, shape, dt=F32):
        return self.st.enter_context(self.nc.psum_tensor(name, list(shape), dt))


def run_stage(nc, body):
    with ExitStack() as st:
        S = Sched(nc)
        A = Alloc(nc, st)
        body(S, A)
        S.emit()


def gemm_body(S, A, uid, x_ap, ntile, w_ap, blocks, ident_ap, nkt=32, a_T=None, x_sum=None, epi=None,
              store_eng="act", pre_block=None):
    xT = A.sb("xT" + uid, [P, ntile, nkt, P], BF16)
    ident = A.sb("ident" + uid, [P, P], F32)
    S.dma("sp", ident[:], ident_ap, key="ident", writes=["ident"])
    pT = [A.ps("pT%d%s" % (i, uid), [P, 4, P]) for i in range(2)]
    cnt = 0
    if a_T is not None:
        OTd, ft0 = a_T
        for t in range(ntile):
            S.dma("sp", xT[:, t, :, :], OTd[t, :, ft0:ft0 + nkt, :], key=("xTl", t % 4),
                  writes=[("xT", t, kq) for kq in range(nkt // 4)])
    else:
        xin = [A.sb("xin%d%s" % (i, uid), [P, nkt * P], F32) for i in range(2)]
        if x_sum:
            xad = A.sb("xad" + uid, [P, nkt * P], F32)
    for t in range(ntile if a_T is None else 0):
        xi = t % 2
        S.dma("sp", xin[xi][:], x_ap[t * P:(t + 1) * P, :], key=("xin", xi), writes=[("xin", xi)])
        for extra in (x_sum or ()):
            S.dma("sp", xad[:], extra[t * P:(t + 1) * P, :], key="xad", writes=["xad"])
            S.op("pool", lambda e, xi=xi: e.tensor_tensor(out=xin[xi][:], in0=xin[xi][:], in1=xad[:], op=ALU.add),
                 reads=[("xin", xi), "xad"], writes=[("xin", xi)])
        for kq in range(nkt // 4):
            b = cnt % 2
            cnt += 1
            for j in range(4):
                kt = kq * 4 + j
                S.op("pe", lambda e, b=b, j=j, xi=xi, kt=kt: e.transpose(
                    pT[b][:, j, :], xin[xi][:, kt * P:(kt + 1) * P], ident[:]),
                    reads=[("xin", xi), "ident"], writes=[("pT", b)])
            if kq % 2 == 0:
                S.op("act", lambda e, b=b, t=t, kq=kq: e.activation(
                    out=xT[:, t, kq * 4:(kq + 1) * 4, :], in_=pT[b][:], func=AF.Copy),
                    reads=[("pT", b)], writes=[("xT", t, kq)])
            else:
                S.op("dve", lambda e, b=b, t=t, kq=kq: e.tensor_copy(
                    out=xT[:, t, kq * 4:(kq + 1) * 4, :], in_=pT[b][:]),
                    reads=[("pT", b)], writes=[("xT", t, kq)])

    stg = [A.sb("stg%d%s" % (i, uid), [P, 8, 256], F32) for i in range(3)]
    wb = [A.sb("wb%d%s" % (i, uid), [P, nkt, 256], BF16) for i in range(2)]
    pz = [A.ps("pz%d%s" % (i, uid), [P, 512]) for i in range(4)]
    ob = [A.sb("ob%d%s" % (i, uid), [P, 256], F32) for i in range(4)]
    w_view = w_ap.rearrange("(kt p) c -> p kt c", p=P)
    ctr = {"stg": 0, "pz": 0}

    def load_block(bi):
        c0 = blocks[bi][0]
        if pre_block is not None:
            pre_block(S, bi)
        for c in range(nkt // 8):
            s = ctr["stg"] % 3
            ctr["stg"] += 1
            S.dma("sp", stg[s][:], w_view[:, c * 8:(c + 1) * 8, c0:c0 + 256],
                  key=("stg", s), writes=[("stg", s)])
            if c % 2 == 0:
                S.op("dve", lambda e, bi=bi, c=c, s=s: e.tensor_copy(
                    out=wb[bi % 2][:, c * 8:(c + 1) * 8, :], in_=stg[s][:]),
                    reads=[("stg", s)], writes=[("wb", bi % 2, c)])
            else:
                S.op("pool", lambda e, bi=bi, c=c, s=s: e.tensor_copy(
                    out=wb[bi % 2][:, c * 8:(c + 1) * 8, :], in_=stg[s][:]),
                    reads=[("stg", s)], writes=[("wb", bi % 2, c)])

    nb = len(blocks)
    if nb:
        load_block(0)
    for bi in range(nb):
        if bi + 1 < nb:
            load_block(bi + 1)
        _, func, out_fn = blocks[bi]
        for t in range(ntile if not getattr(Sched, 'NOMM', False) else 0):
            pb = ctr["pz"] % 4
            ctr["pz"] += 1
            for kt in range(nkt):
                S.op("pe", lambda e, pb=pb, t=t, kt=kt, bi=bi: e.matmul(
                    pz[pb][:, 0:256], lhsT=xT[:, t, kt, :], rhs=wb[bi % 2][:, kt, :],
                    start=(kt == 0), stop=(kt == nkt - 1)),
                    reads=[("xT", t, kt // 4), ("wb", bi % 2, kt // 8)], writes=[("pz", pb)])
            if epi is not None:
                epi(S, t, bi, pz[pb][:, 0:256], ("pz", pb), ob[pb], ("ob", pb))
            else:
                S.op("act", lambda e, pb=pb, func=func: e.activation(
                    out=ob[pb][:], in_=pz[pb][:, 0:256], func=func),
                    reads=[("pz", pb)], writes=[("ob", pb)])
            S.dma(store_eng, out_fn(t), ob[pb][:], key=("ob", pb), reads=[("ob", pb)], is_output=True)


def col_func(c0):
    if 8192 <= c0 < 12288 or 14336 <= c0 < 16384 or 18432 <= c0 < 20480:
        return AF.Silu
    if c0 >= 20480:
        return AF.Sigmoid
    return AF.Copy


RET_G = [1.0 - 2.0 ** (-5.0 - h) for h in range(16)]


def retention_body(S, A, zo, zp, tabs, sret_in, sret_out, sretp_out, OT, ident_ap):
    ident = A.sb("r_ident", [P, P], F32)
    identb = A.sb("r_identb", [P, P], BF16)
    S.dma("sp", ident[:], ident_ap, key="ident", writes=["ident"])
    S.op("dve", lambda e: e.tensor_copy(out=identb[:], in_=ident[:]), reads=["ident"], writes=["identb"])
    maskp = A.sb("r_maskp", [P, P], F32)
    masks = A.sb("r_masks", [P, P], F32)
    seqm = A.sb("r_seqm", [P, 16], F32)
    seqmT = A.sb("r_seqmT", [P, 16, P], F32)
    S.dma("sp", maskp[:], tabs["mask_p"], key="maskp", writes=["maskp"])
    S.dma("sp", masks[:], tabs["mask_s"], key="masks", writes=["masks"])
    S.dma("sp", seqm[:], tabs["seqm"], key="seqm", writes=["seqm"])
    S.dma("sp", seqmT[:], tabs["seqmT"], key="seqmT", writes=["seqmT"])

    St = A.sb("r_S", [P, 16, 256], F32)
    Sb = A.sb("r_Sb", [P, 16, 256], BF16)
    S.op("pool", lambda e: e.memset(St[:], 0.0), writes=["S"])
    S.op("pool", lambda e: e.memset(Sb[:], 0.0), writes=["Sb"])

    qins = [A.sb("r_qin", [P, 2048], F32)] * 2
    kins = [A.sb("r_kin", [P, 2048], F32)] * 2
    vins = [A.sb("r_vin%d" % i, [P, 4096], F32) for i in range(2)]
    gins = [A.sb("r_gin%d" % i, [P, 4096], F32) for i in range(2)]
    cur = {"i": 0}
    rt = A.sb("r_rt", [P, 2, 16, 64], F32)
    t1 = A.sb("r_t1", [P, 16, 64], F32)
    t2 = A.sb("r_t2", [P, 16, 64], F32)
    qt = A.sb("r_qt", [P, 16, 128], BF16)
    kt_ = A.sb("r_kt", [P, 16, 128], BF16)
    vb = A.sb("r_vb", [P, 16, 256], BF16)
    qT = A.sb("r_qT", [P, 16, 128], BF16)
    kT = A.sb("r_kT", [P, 16, 128], BF16)
    scs = A.sb("r_scs", [P, 16, 128], BF16)
    osb = A.sb("r_osb", [P, 16, 256], F32)
    sq = vin.rearrange("p (h e) -> p h e", h=16) if False else None
    og = A.sb("r_og", [P, 16, 256], BF16)
    oT = A.sb("r_oT", [P, 32, 128], BF16)
    st1 = A.sb("r_st1", [P, 16], F32)
    st2 = A.sb("r_st2", [P, 16], F32)
    st3 = A.sb("r_st3", [P, 16], F32)
    dtmp = A.sb("r_dtmp", [P, 2, 256], F32)
    ptr = [A.ps("r_ptr%d" % i, [P, 8, 128], BF16) for i in range(2)]
    psc = [A.ps("r_psc%d" % i, [P, 4, 128]) for i in range(2)]
    po = [A.ps("r_po%d" % i, [P, 2, 256]) for i in range(2)]
    pd = [A.ps("r_pd%d" % i, [P, 2, 256]) for i in range(2)]
    cn = {"tr": 0, "sc": 0, "o": 0, "d": 0}

    def rotary(src, dst, rt_ap, rkey, skey, dkey):
        S.dma("sp", rt[:], rt_ap, key="rt", writes=["rt"])
        sv = src[:].rearrange("p (h j two) -> p h j two", h=16, two=2)
        dv = dst[:].rearrange("p h (j two) -> p h j two", two=2)
        S.op("dve", lambda e: e.tensor_tensor(out=t1[:], in0=sv[:, :, :, 0], in1=rt[:, 0], op=ALU.mult),
             reads=[skey, "rt"], writes=["t1"])
        S.op("pool", lambda e: e.tensor_tensor(out=t2[:], in0=sv[:, :, :, 1], in1=rt[:, 1], op=ALU.mult),
             reads=[skey, "rt"], writes=["t2"])
        S.op("dve", lambda e: e.tensor_tensor(out=dv[:, :, :, 0], in0=t1[:], in1=t2[:], op=ALU.subtract),
             reads=["t1", "t2"], writes=[dkey + "0"])
        S.op("dve", lambda e: e.tensor_tensor(out=t1[:], in0=sv[:, :, :, 0], in1=rt[:, 1], op=ALU.mult),
             reads=[skey, "rt", dkey + "0"], writes=["t1"])
        S.op("pool", lambda e: e.tensor_tensor(out=t2[:], in0=sv[:, :, :, 1], in1=rt[:, 0], op=ALU.mult),
             reads=[skey, "rt", dkey + "0"], writes=["t2"])
        S.op("dve", lambda e: e.tensor_tensor(out=dv[:, :, :, 1], in0=t1[:], in1=t2[:], op=ALU.add),
             reads=["t1", "t2"], writes=[dkey + "1"])

    def transpose16(src, dst, skeys, dkey):
        for half in range(2):
            b = cn["tr"] % 2
            cn["tr"] += 1
            for j in range(8):
                h = half * 8 + j
                S.op("pe", lambda e, b=b, j=j, h=h: e.transpose(ptr[b][:, j, :], src[:, h, :], identb[:]),
                     reads=list(skeys) + ["identb"], writes=[("ptr", b)])
            S.op("act", lambda e, b=b, half=half: e.activation(
                out=dst[:, half * 8:(half + 1) * 8, :], in_=ptr[b][:], func=AF.Copy),
                reads=[("ptr", b)], writes=[(dkey, half)])

    def state_update(g, sample_head=None):
        pass

    def chunk(kind, t):
        own = kind != "pre"
        cur["n"] = cur.get("n", -1) + 1
        ci = cur["n"] % 2
        cur["i"] = ci
        qin, kin, vin, gin = qins[ci], kins[ci], vins[ci], gins[ci]
        KI, VI, QI, GI = "kin", ("vin", ci), "qin", ("gin", ci)
        z = zo if own else zp
        r0 = t * P
        kcol = 2048 if own else 0
        vcol = 4096 if own else 2048
        S.dma("sp", kin[:], z[r0:r0 + P, kcol:kcol + 2048], key=KI, writes=[KI])
        S.dma("sp", vin[:], z[r0:r0 + P, vcol:vcol + 4096], key=VI, writes=[VI])
        S.op("act", lambda e: e.activation(out=vb[:].rearrange("p h e -> p (h e)"), in_=vin[:], func=AF.Copy),
             reads=[VI], writes=["vb"])
        rotary(kin, kt_, (tabs["rk_own"] if own else tabs["rk_pre"])[t], "rk", KI, "kt")
        if own:
            S.dma("sp", qin[:], z[r0:r0 + P, 0:2048], key=QI, writes=[QI])
            S.dma("sp", gin[:], z[r0:r0 + P, 8192:12288], key=GI, writes=[GI])
            rotary(qin, qt, tabs["rq"][t], "rq", QI, "qt")
            transpose16(qt, qT, ["qt0", "qt1"], "qT")
            transpose16(kt_, kT, ["kt0", "kt1"], "kT")
        return own

    def scores_and_out(mask, sample):
        for hq in range(4):
            b = cn["sc"] % 2
            cn["sc"] += 1
            for j in range(4):
                h = hq * 4 + j
                S.op("pe", lambda e, b=b, j=j, h=h: e.matmul(psc[b][:, j, :], lhsT=kT[:, h, :], rhs=qT[:, h, :],
                                                             start=True, stop=True),
                     reads=[("kT", h // 8), ("qT", h // 8)], writes=[("psc", b)])
            S.op("dve", lambda e, b=b, hq=hq: e.tensor_tensor(
                out=scs[:, hq * 4:(hq + 1) * 4, :], in0=psc[b][:],
                in1=mask[:].unsqueeze(1).to_broadcast([P, 4, P]), op=ALU.mult),
                reads=[("psc", b), "maskp", "masks"], writes=[("scs", hq)])

    def finish_out(t):
        ci = cur["i"]
        vin, gin = vins[ci], gins[ci]
        VI, GI = ("vin", ci), ("gin", ci)
        S.op("dve", lambda e: e.tensor_reduce(out=st1[:], in_=osb[:], op=ALU.add, axis=mybir.AxisListType.X),
             reads=["osb"], writes=["st1"])
        sqv = vin[:].rearrange("p (h e) -> p h e", h=16)
        S.op("act", lambda e: e.activation(out=sqv, in_=osb[:], func=AF.Square),
             reads=["osb"], writes=[VI])
        S.op("dve", lambda e: e.tensor_reduce(out=st2[:], in_=sqv, op=ALU.add, axis=mybir.AxisListType.X),
             reads=[VI], writes=["st2"])
        S.op("dve", lambda e: e.tensor_scalar(out=st1[:], in0=st1[:], scalar1=1.0 / 256, scalar2=None, op0=ALU.mult),
             reads=["st1"], writes=["st1"])
        S.op("dve", lambda e: e.tensor_tensor(out=st3[:], in0=st1[:], in1=st1[:], op=ALU.mult),
             reads=["st1"], writes=["st3"])
        S.op("dve", lambda e: e.scalar_tensor_tensor(out=st2[:], in0=st2[:], scalar=1.0 / 256, in1=st3[:],
                                                     op0=ALU.mult, op1=ALU.subtract),
             reads=["st2", "st3"], writes=["st2"])
        S.op("dve", lambda e: e.tensor_scalar(out=st2[:], in0=st2[:], scalar1=1e-5, scalar2=None, op0=ALU.add),
             reads=["st2"], writes=["st2"])
        S.op("act", lambda e: e.activation(out=st2[:], in_=st2[:], func=AF.Sqrt), reads=["st2"], writes=["st2"])
        S.op("dve", lambda e: e.reciprocal(out=st2[:], in_=st2[:]), reads=["st2"], writes=["st2"])
        S.op("dve", lambda e: e.tensor_tensor(out=osb[:], in0=osb[:],
                                              in1=st1[:].unsqueeze(2).to_broadcast([P, 16, 256]), op=ALU.subtract),
             reads=["osb", "st1"], writes=["osb"])
        S.op("dve", lambda e: e.tensor_tensor(out=osb[:], in0=osb[:],
                                              in1=st2[:].unsqueeze(2).to_broadcast([P, 16, 256]), op=ALU.mult),
             reads=["osb", "st2"], writes=["osb"])
        S.op("dve", lambda e: e.tensor_tensor(out=og[:].rearrange("p h e -> p (h e)"),
                                              in0=osb[:].rearrange("p h e -> p (h e)"), in1=gin[:], op=ALU.mult),
             reads=["osb", GI], writes=["og"])
        ogv = og[:].rearrange("p h (two e) -> p (h two) e", two=2)
        for q4 in range(4):
            b = cn["tr"] % 2
            cn["tr"] += 1
            for j in range(8):
                ft = q4 * 8 + j
                S.op("pe", lambda e, b=b, j=j, ft=ft: e.transpose(ptr[b][:, j, :], ogv[:, ft, :], identb[:]),
                     reads=["og", "identb"], writes=[("ptr", b)])
            S.op("act", lambda e, b=b, q4=q4: e.activation(out=oT[:, q4 * 8:(q4 + 1) * 8, :], in_=ptr[b][:],
                                                           func=AF.Copy),
                 reads=[("ptr", b)], writes=[("oT", q4)])
        S.dma("act", OT[t, :, 0:32, :], oT[:], key="oT", reads=[("oT", q) for q in range(4)], is_output=True)

    def prompt_state_update():
        for hp in range(8):
            b = cn["d"] % 2
            cn["d"] += 1
            for j in range(2):
                h = hp * 2 + j
                S.op("pe", lambda e, b=b, j=j, h=h: e.matmul(pd[b][:, j, :], lhsT=kt_[:, h, :], rhs=vb[:, h, :],
                                                             start=True, stop=True),
                     reads=["kt0", "kt1", "vb"], writes=[("pd", b)])
            for j in range(2):
                h = hp * 2 + j
                g = float(RET_G[h] ** 128)
                S.op("act", lambda e, b=b, j=j, g=g: e.activation(out=dtmp[:, j, :], in_=pd[b][:, j, :],
                                                                  func=AF.Copy, scale=g),
                     reads=[("pd", b)], writes=[("dtmp", j)])
                S.op("dve", lambda e, h=h, j=j, g=g: e.scalar_tensor_tensor(
                    out=St[:, h, :], in0=St[:, h, :], scalar=g, in1=dtmp[:, j, :], op0=ALU.mult, op1=ALU.add),
                    reads=[("dtmp", j), "S"], writes=["S"])
        S.op("act", lambda e: e.activation(out=Sb[:], in_=St[:], func=AF.Copy), reads=["S"], writes=["Sb"])

    for t in range(8):
        chunk("pre", t)
        prompt_state_update()

    for t in range(8):
        chunk("own", t)
        scores_and_out(maskp, False)
        for hp in range(8):
            b = cn["o"] % 2
            cn["o"] += 1
            for j in range(2):
                h = hp * 2 + j
                S.op("pe", lambda e, b=b, j=j, h=h: e.matmul(po[b][:, j, :], lhsT=scs[:, h, :], rhs=vb[:, h, :],
                                                             start=True, stop=False),
                     reads=[("scs", h // 4), "vb"], writes=[("po", b)])
                S.op("pe", lambda e, b=b, j=j, h=h: e.matmul(po[b][:, j, :], lhsT=qT[:, h, :], rhs=Sb[:, h, :],
                                                             start=False, stop=True),
                     reads=[("qT", h // 8), "Sb"], writes=[("po", b)])
            S.op("act", lambda e, b=b, hp=hp: e.activation(out=osb[:, hp * 2:hp * 2 + 2, :], in_=po[b][:],
                                                           func=AF.Copy),
                 reads=[("po", b)], writes=["osb"])
        prompt_state_update()
        finish_out(t)
    S.dma("sp", sretp_out.rearrange("h d e -> d h e"), St[:], key="St_out", reads=["S"], is_output=True)

    t = 8
    chunk("own", t)
    scores_and_out(masks, True)
    vin = vins[cur["i"]]
    VI = ("vin", cur["i"])
    Ss = vin[:].rearrange("p (h e) -> p h e", h=16)
    Ssb = og
    qTm = oT[:, 0:16, :]
    ktm = oT[:, 16:32, :]
    for h in range(16):
        g8 = float(RET_G[h] ** 8)
        S.dma("sp", Ss, sret_in[:, h].rearrange("s d e -> d s e"), key="Ss", writes=[VI])
        S.op("act", lambda e: e.activation(out=Ssb[:], in_=Ss, func=AF.Copy), reads=[VI], writes=["og"])
        S.op("dve", lambda e, h=h: e.tensor_tensor(
            out=qTm, in0=qT[:, h, :].unsqueeze(1).to_broadcast([P, 16, P]), in1=seqmT[:], op=ALU.mult),
            reads=[("qT", h // 8), "seqmT"], writes=[("oT", 0), ("oT", 1)])
        S.op("dve", lambda e, h=h: e.tensor_tensor(
            out=ktm, in0=kt_[:, h, :].unsqueeze(1).to_broadcast([P, 16, P]),
            in1=seqm[:].unsqueeze(2).to_broadcast([P, 16, P]), op=ALU.mult),
            reads=["kt0", "kt1", "seqm"], writes=[("oT", 2), ("oT", 3)])
        b = cn["o"] % 2
        cn["o"] += 1
        S.op("pe", lambda e, b=b, h=h: e.matmul(po[b][:, 0, :], lhsT=scs[:, h, :], rhs=vb[:, h, :],
                                                start=True, stop=False),
             reads=[("scs", h // 4), "vb"], writes=[("po", b)])
        for s_ in range(16):
            S.op("pe", lambda e, b=b, s_=s_: e.matmul(po[b][:, 0, :], lhsT=qTm[:, s_, :], rhs=Ssb[:, s_, :],
                                                      start=False, stop=(s_ == 15)),
                 reads=[("oT", 0), ("oT", 1), "og"], writes=[("po", b)])
        S.op("act", lambda e, b=b, h=h: e.activation(out=osb[:, h, :], in_=po[b][:, 0, :], func=AF.Copy),
             reads=[("po", b)], writes=["osb"])
        for sp_ in range(8):
            b2 = cn["d"] % 2
            cn["d"] += 1
            for j in range(2):
                s_ = sp_ * 2 + j
                S.op("pe", lambda e, b2=b2, j=j, s_=s_, h=h: e.matmul(
                    pd[b2][:, j, :], lhsT=ktm[:, s_, :], rhs=vb[:, h, :], start=True, stop=True),
                    reads=[("oT", 2), ("oT", 3), "vb"], writes=[("pd", b2)])
            for j in range(2):
                s_ = sp_ * 2 + j
                S.op("act", lambda e, b2=b2, j=j, g8=g8: e.activation(out=dtmp[:, j, :], in_=pd[b2][:, j, :],
                                                                      func=AF.Copy, scale=g8),
                     reads=[("pd", b2)], writes=[("dtmp", j)])
                S.op("dve", lambda e, s_=s_, j=j, g8=g8: e.scalar_tensor_tensor(
                    out=Ss[:, s_, :], in0=Ss[:, s_, :], scalar=g8, in1=dtmp[:, j, :], op0=ALU.mult, op1=ALU.add),
                    reads=[("dtmp", j), VI, "og"], writes=[VI])
        S.dma("sp", sret_out[:, h].rearrange("s d e -> d s e"), Ss, key="Ss_out", reads=[VI],
              writes=["Ss_dram"], is_output=True)
    finish_out(8)


def xattn_body(S, A, zo, memkv, cmk, cmv, seqmT_ap, OT, ident_ap):
    X = mybir.AxisListType.X
    scale = 512.0 ** -0.5
    ident = A.sb("x_ident", [P, P], F32)
    identb = A.sb("x_identb", [P, P], BF16)
    S.dma("sp", ident[:], ident_ap, key="ident", writes=["ident"])
    S.op("dve", lambda e: e.tensor_copy(out=identb[:], in_=ident[:]), reads=["ident"], writes=["identb"])
    seqmT = A.sb("x_seqmT", [P, 16, P], F32)
    S.dma("sp", seqmT[:], seqmT_ap, key="seqmT", writes=["seqmT"])

    qin = A.sb("x_qin", [P, 2048], F32)
    gin = A.sb("x_gin", [P, 2048], F32)
    qb = A.sb("x_qb", [P, 16, 128], BF16)
    qT = A.sb("x_qT", [P, 16, 128], BF16)
    kvin = [A.sb("x_kvin%d" % i, [P, 2, 2048], F32) for i in range(2)]
    kb = A.sb("x_kb", [P, 2, 16, 128], BF16)
    KT = A.sb("x_KT", [P, 16, 256], BF16)
    Vb = A.sb("x_Vb", [P, 2, 2048], BF16)
    pb = A.sb("x_pb", [P, 4, 256], BF16)
    pT = A.sb("x_pT", [P, 4, 2, 128], BF16)
    pTm = A.sb("x_pTm", [P, 16, 128], BF16)
    qTm = A.sb("x_qTm", [P, 16, 128], BF16)
    ob = A.sb("x_ob", [P, 16, 128], BF16)
    oT = A.sb("x_oT", [P, 16, 128], BF16)
    mx = A.sb("x_mx", [P, 4], F32)
    sm = A.sb("x_sm", [P, 4], F32)
    ptr = [A.ps("x_ptr%d" % i, [P, 8, 128], BF16) for i in range(2)]
    pso = [A.ps("x_pso%d" % i, [P, 512]) for i in range(4)]
    cn = {"tr": 0, "kv": 0}

    def tr_group(srcs, dst_ap, skeys, dkey):
        b = cn["tr"] % 2
        cn["tr"] += 1
        for j, src in enumerate(srcs):
            S.op("pe", lambda e, b=b, j=j, src=src: e.transpose(ptr[b][:, j, :], src, identb[:]),
                 reads=list(skeys) + ["identb"], writes=[("ptr", b)])
        n = len(srcs)
        S.op("act", lambda e, b=b, n=n: e.activation(out=dst_ap, in_=ptr[b][:, 0:n, :], func=AF.Copy),
             reads=[("ptr", b)], writes=[dkey])

    def load_q(t):
        r0 = t * P
        S.dma("sp", qin[:], zo[r0:r0 + P, 16384:18432], key="qin", writes=["qin"])
        S.dma("sp", gin[:], zo[r0:r0 + P, 18432:20480], key="gin", writes=["gin"])
        S.op("dve", lambda e: e.tensor_copy(out=qb[:].rearrange("p a b -> p (a b)"), in_=qin[:]),
             reads=["qin"], writes=["qb"])
        for half in range(2):
            tr_group([qb[:, half * 8 + j, :] for j in range(8)], qT[:, half * 8:(half + 1) * 8, :],
                     ["qb"], ("qT", half))

    def load_kv(src_ap, which):
        i = cn["kv"] % 2
        cn["kv"] += 1
        S.dma("sp", kvin[i][:], src_ap.rearrange("(mt p) c -> p mt c", p=P), key=("kvin", i),
              writes=[("kvin", i)])
        return i

    def make_KT(i):
        S.op("pool", lambda e: e.tensor_copy(out=kb[:].rearrange("p m a b -> p m (a b)"), in_=kvin[i][:]),
             reads=[("kvin", i)], writes=["kb"])
        for mt in range(2):
            for half in range(2):
                tr_group([kb[:, mt, half * 8 + j, :] for j in range(8)],
                         KT[:, half * 8:(half + 1) * 8, mt * P:(mt + 1) * P], ["kb"], ("KT", mt, half))

    def make_V(i):
        S.op("pool", lambda e: e.tensor_copy(out=Vb[:], in_=kvin[i][:]), reads=[("kvin", i)], writes=["Vb"])

    KT_keys = [("KT", mt, half) for mt in range(2) for half in range(2)]

    def sc_loc(h, sample):
        return pso[h][:, 0:256], ("pso", h)

    def score_mm(lhs, lkeys, first, last, sample=False):
        for h in range(4):
            for dt in range(4):
                k = h * 4 + dt
                loc, lk = sc_loc(h, sample)
                S.op("pe", lambda e, loc=loc, k=k, dt=dt: e.matmul(
                    loc, lhsT=lhs[:, k, :], rhs=KT[:, k, :],
                    start=(first and dt == 0), stop=(last and dt == 3)),
                    reads=list(lkeys) + KT_keys, writes=[lk])

    def softmax(sample=False):
        for h in range(4):
            sv, lk = sc_loc(h, sample)
            S.op("dve", lambda e, h=h, sv=sv: e.tensor_reduce(out=mx[:, h:h + 1], in_=sv, op=ALU.max, axis=X),
                 reads=[lk], writes=[("mx", h)])
            S.op("dve", lambda e, h=h: e.tensor_scalar(out=mx[:, h:h + 1], in0=mx[:, h:h + 1], scalar1=-scale,
                                                       scalar2=None, op0=ALU.mult),
                 reads=[("mx", h)], writes=[("mx", h)])
            S.op("act", lambda e, h=h, sv=sv: e.activation(out=pb[:, h, :], in_=sv, func=AF.Exp,
                                                           bias=mx[:, h:h + 1], scale=scale,
                                                           accum_out=sm[:, h:h + 1]),
                 reads=[lk, ("mx", h)], writes=[("pb", h), ("sm", h)])
            S.op("dve", lambda e, h=h: e.reciprocal(out=sm[:, h:h + 1], in_=sm[:, h:h + 1]),
                 reads=[("sm", h)], writes=[("sm", h)])
        tr_group([pb[:, h, mt * P:(mt + 1) * P] for h in range(4) for mt in range(2)],
                 pT[:].rearrange("p h m l -> p (h m) l"), [("pb", h) for h in range(4)], "pT")

    def finish(t):
        for h in range(4):
            S.op("dve", lambda e, h=h: e.scalar_tensor_tensor(
                out=ob[:, h * 4:(h + 1) * 4, :].rearrange("p a b -> p (a b)"), in0=pso[h][:],
                scalar=sm[:, h:h + 1], in1=gin[:, h * 512:(h + 1) * 512], op0=ALU.mult, op1=ALU.mult),
                reads=[("pso", h), ("sm", h), "gin"], writes=[("ob", h)])
        for half in range(2):
            tr_group([ob[:, half * 8 + j, :] for j in range(8)], oT[:, half * 8:(half + 1) * 8, :],
                     [("ob", h) for h in range(4)], ("oT", half))
        S.dma("act", OT[t, :, 48:64, :], oT[:], key="oT", reads=[("oT", 0), ("oT", 1)], is_output=True)

    i = load_kv(memkv[:, 0:2048], "k")
    make_KT(i)
    i = load_kv(memkv[:, 2048:4096], "v")
    make_V(i)
    for t in range(8):
        load_q(t)
        score_mm(qT, [("qT", 0), ("qT", 1)], True, True)
        softmax()
        for h in range(4):
            for mt in range(2):
                S.op("pe", lambda e, h=h, mt=mt: e.matmul(pso[h][:], lhsT=pT[:, h, mt, :],
                                                          rhs=Vb[:, mt, h * 512:(h + 1) * 512],
                                                          start=(mt == 0), stop=(mt == 1)),
                     reads=["pT", "Vb"], writes=[("pso", h)])
        finish(t)

    load_q(8)
    for s_ in range(16):
        i = load_kv(cmk[s_], "k")
        make_KT(i)
        S.op("dve", lambda e, s_=s_: e.tensor_tensor(
            out=qTm[:], in0=qT[:], in1=seqmT[:, s_, :].unsqueeze(1).to_broadcast([P, 16, P]), op=ALU.mult),
            reads=[("qT", 0), ("qT", 1), "seqmT"], writes=["qTm"])
        score_mm(qTm, ["qTm"], s_ == 0, s_ == 15, sample=True)
    softmax(sample=True)
    for s_ in range(16):
        i = load_kv(cmv[s_], "v")
        make_V(i)
        for h in range(4):
            S.op("dve", lambda e, h=h, s_=s_: e.tensor_tensor(
                out=pTm[:, h * 2:(h + 1) * 2, :], in0=pT[:, h, :, :],
                in1=seqmT[:, s_, :].unsqueeze(1).to_broadcast([P, 2, P]), op=ALU.mult),
                reads=["pT", "seqmT"], writes=[("pTm", h)])
            for mt in range(2):
                S.op("pe", lambda e, h=h, mt=mt, s_=s_: e.matmul(
                    pso[h][:], lhsT=pTm[:, h * 2 + mt, :], rhs=Vb[:, mt, h * 512:(h + 1) * 512],
                    start=(s_ == 0 and mt == 0), stop=(s_ == 15 and mt == 1)),
                    reads=[("pTm", h), "Vb"], writes=[("pso", h)])
    finish(8)


def s5prep_body(S, A, prm, ident_ap, maskM_ap, SM, SG, SE, A8S):
    PI = math.pi
    ident = A.sb("q_ident", [P, P], F32)
    identb = A.sb("q_identb", [P, P], BF16)
    S.dma("sp", ident[:], ident_ap, key="ident", writes=["ident"])
    S.op("dve", lambda e: e.tensor_copy(out=identb[:], in_=ident[:]), reads=["ident"], writes=["identb"])
    maskM = A.sb("q_maskM", [P, P], F32)
    S.dma("sp", maskM[:], maskM_ap, key="maskM", writes=["maskM"])
    H = 64
    uid = [0]

    def tl(shape, dt=F32):
        uid[0] += 1
        return A.sb("q_t%d" % uid[0], shape, dt)

    def dve(fn, reads, writes):
        S.op("dve", fn, reads=reads, writes=writes)

    def tt(out, a, b, op, okey, akey, bkey):
        dve(lambda e: e.tensor_tensor(out=out, in0=a, in1=b, op=op), [akey, bkey], [okey])

    araw = tl([P, 2, 64])
    S.dma("sp", araw[:, 0, :], prm["a_re"], key="araw0", writes=["araw"])
    S.dma("sp", araw[:, 1, :], prm["a_im"], key="araw1", writes=["araw"])
    pA = A.ps("q_pA", [P, 4, P])
    pA2 = A.ps("q_pA2", [P, 4, P])
    ar = tl([H, P]); ai = tl([H, P])
    for j in range(2):
        S.op("pe", lambda e, j=j: e.transpose(pA[0:H, j, :], araw[:, j, :], ident[:]), reads=["araw", "ident"],
             writes=["pA"])
    dve(lambda e: e.tensor_copy(out=ar[:], in_=pA[0:H, 0, :]), ["pA"], ["ar"])
    dve(lambda e: e.tensor_copy(out=ai[:], in_=pA[0:H, 1, :]), ["pA"], ["ai"])
    dtb = tl([H, P])
    S.dma("sp", dtb[:], prm["log_step"].to_broadcast([H, P]), key="dtb", writes=["dtb"])
    S.op("act", lambda e: e.activation(out=dtb[:], in_=dtb[:], func=AF.Exp), reads=["dtb"], writes=["dtb"])
    dtar = tl([H, P]); dtai = tl([H, P]); mag = tl([H, P])
    tt(dtar[:], dtb[:], ar[:], ALU.mult, "dtar", "dtb", "ar")
    tt(dtai[:], dtb[:], ai[:], ALU.mult, "dtai", "dtb", "ai")
    kq = tl([H, P]); ki = tl([H, P], mybir.dt.int32); rr = tl([H, P])
    dve(lambda e: e.tensor_scalar(out=kq[:], in0=dtai[:], scalar1=1.0 / (2 * PI), scalar2=None, op0=ALU.mult),
        ["dtai"], ["kq"])
    dve(lambda e: e.tensor_copy(out=ki[:], in_=kq[:]), ["kq"], ["ki"])
    dve(lambda e: e.tensor_copy(out=kq[:], in_=ki[:]), ["ki"], ["kq"])
    dve(lambda e: e.scalar_tensor_tensor(out=rr[:], in0=kq[:], scalar=-2 * PI, in1=dtai[:], op0=ALU.mult,
                                         op1=ALU.add), ["kq", "dtai"], ["rr"])
    rs = tl([H, P]); rc = tl([H, P]); sn = tl([H, P]); cs = tl([H, P])
    msk = tl([H, P])
    for t_, k_, sh in ((rs, "rs", 0.0), (rc, "rc", PI / 2)):
        dve(lambda e, t_=t_, sh=sh: e.tensor_scalar(out=t_[:], in0=rr[:], scalar1=sh, scalar2=None, op0=ALU.add),
            ["rr"], [k_])
        dve(lambda e, t_=t_: e.tensor_scalar(out=msk[:], in0=t_[:], scalar1=PI, scalar2=None, op0=ALU.is_gt),
            [k_], ["msk"])
        dve(lambda e, t_=t_: e.scalar_tensor_tensor(out=t_[:], in0=msk[:], scalar=-2 * PI, in1=t_[:],
                                                    op0=ALU.mult, op1=ALU.add), ["msk", k_], [k_])
        dve(lambda e, t_=t_: e.tensor_scalar(out=msk[:], in0=t_[:], scalar1=-PI, scalar2=None, op0=ALU.is_lt),
            [k_], ["msk"])
        dve(lambda e, t_=t_: e.scalar_tensor_tensor(out=t_[:], in0=msk[:], scalar=2 * PI, in1=t_[:],
                                                    op0=ALU.mult, op1=ALU.add), ["msk", k_], [k_])
    for t_, k_ in ((rs, "rs"), (rc, "rc")):
        dve(lambda e, t_=t_: e.tensor_scalar(out=t_[:], in0=t_[:], scalar1=3.1415925, scalar2=-3.1415925,
                                             op0=ALU.min, op1=ALU.max), [k_], [k_])
    hh = tl([H, P]); x2 = tl([H, P]); sh_ = tl([H, P]); ch_ = tl([H, P])
    dve(lambda e: e.tensor_scalar(out=hh[:], in0=rs[:], scalar1=0.5, scalar2=None, op0=ALU.mult), ["rs"], ["hh"])
    tt(x2[:], hh[:], hh[:], ALU.mult, "x2", "hh", "hh")
    sc_ = [(-1.0) ** k / math.factorial(2 * k + 1) for k in range(9)]
    cc_ = [(-1.0) ** k / math.factorial(2 * k) for k in range(9)]

    def horner(dst, dkey, co):
        dve(lambda e: e.tensor_scalar(out=dst[:], in0=x2[:], scalar1=co[-1], scalar2=None, op0=ALU.mult),
            ["x2"], [dkey])
        for c_ in co[-2:0:-1]:
            dve(lambda e, c_=c_: e.scalar_tensor_tensor(out=dst[:], in0=dst[:], scalar=c_, in1=x2[:],
                                                        op0=ALU.add, op1=ALU.mult), [dkey, "x2"], [dkey])
        dve(lambda e: e.tensor_scalar(out=dst[:], in0=dst[:], scalar1=co[0], scalar2=None, op0=ALU.add),
            [dkey], [dkey])
    horner(sh_, "sh", sc_)
    tt(sh_[:], sh_[:], hh[:], ALU.mult, "sh", "sh", "hh")
    horner(ch_, "ch", cc_)
    dve(lambda e: e.scalar_tensor_tensor(out=sn[:], in0=sh_[:], scalar=2.0, in1=ch_[:], op0=ALU.mult,
                                         op1=ALU.mult), ["sh", "ch"], ["sn"])
    tt(cs[:], sh_[:], sh_[:], ALU.mult, "cs", "sh", "sh")
    dve(lambda e: e.tensor_scalar(out=cs[:], in0=cs[:], scalar1=-2.0, scalar2=1.0, op0=ALU.mult, op1=ALU.add),
        ["cs"], ["cs"])
    ec_ = [1.0 / math.factorial(k) for k in range(9)]
    dve(lambda e: e.tensor_scalar(out=mag[:], in0=dtar[:], scalar1=ec_[-1], scalar2=None, op0=ALU.mult),
        ["dtar"], ["mag"])
    for c_ in ec_[-2:0:-1]:
        dve(lambda e, c_=c_: e.scalar_tensor_tensor(out=mag[:], in0=mag[:], scalar=c_, in1=dtar[:],
                                                    op0=ALU.add, op1=ALU.mult), ["mag", "dtar"], ["mag"])
    dve(lambda e: e.tensor_scalar(out=mag[:], in0=mag[:], scalar1=1.0, scalar2=None, op0=ALU.add),
        ["mag"], ["mag"])
    PW = tl([H, 16, 2, P])
    tmp = [tl([H, P]) for _ in range(4)]
    cm = [0]

    def cmul(ore, oim, xr, xi, yr, yi, okeys, ikeys):
        cm[0] += 1
        k = ["cm%d_%d" % (cm[0], i) for i in range(4)]
        dve(lambda e: e.tensor_tensor(out=tmp[0][:], in0=xr, in1=yr, op=ALU.mult), ikeys, ["tmp0"])
        dve(lambda e: e.tensor_tensor(out=tmp[1][:], in0=xi, in1=yi, op=ALU.mult), ikeys, ["tmp1"])
        dve(lambda e: e.tensor_tensor(out=tmp[2][:], in0=xr, in1=yi, op=ALU.mult), ikeys, ["tmp2"])
        dve(lambda e: e.tensor_tensor(out=tmp[3][:], in0=xi, in1=yr, op=ALU.mult), ikeys, ["tmp3"])
        dve(lambda e: e.tensor_tensor(out=ore, in0=tmp[0][:], in1=tmp[1][:], op=ALU.subtract),
            ["tmp0", "tmp1"], [okeys[0]])
        dve(lambda e: e.tensor_tensor(out=oim, in0=tmp[2][:], in1=tmp[3][:], op=ALU.add),
            ["tmp2", "tmp3"], [okeys[1]])

    def pw(e_, c):
        return PW[:, e_ + 7, c, :]

    def pk(e_):
        return ["pw%d_0" % e_, "pw%d_1" % e_]
    S.op("pool", lambda e: e.memset(pw(0, 0), 1.0), writes=["pw0_0"])
    S.op("pool", lambda e: e.memset(pw(0, 1), 0.0), writes=["pw0_1"])
    tt(pw(1, 0), mag[:], cs[:], ALU.mult, "pw1_0", "mag", "cs")
    tt(pw(1, 1), mag[:], sn[:], ALU.mult, "pw1_1", "mag", "sn")
    for e_ in range(2, 9):
        cmul(pw(e_, 0), pw(e_, 1), pw(e_ - 1, 0), pw(e_ - 1, 1), pw(1, 0), pw(1, 1), pk(e_), pk(e_ - 1) + pk(1))
    im2 = tl([H, P])
    tt(im2[:], mag[:], mag[:], ALU.mult, "im2", "mag", "mag")
    dve(lambda e: e.reciprocal(out=im2[:], in_=im2[:]), ["im2"], ["im2"])
    tt(pw(-1, 0), pw(1, 0), im2[:], ALU.mult, "pw-1_0", "pw1_0", "im2")
    dve(lambda e: e.scalar_tensor_tensor(out=pw(-1, 1), in0=pw(1, 1), scalar=-1.0, in1=im2[:], op0=ALU.mult,
                                         op1=ALU.mult), ["pw1_1", "im2"], ["pw-1_1"])
    for e_ in range(2, 8):
        cmul(pw(-e_, 0), pw(-e_, 1), pw(-e_ + 1, 0), pw(-e_ + 1, 1), pw(-1, 0), pw(-1, 1),
             pk(-e_), pk(-e_ + 1) + pk(-1))
    allpw = [k for e_ in range(-7, 9) for k in pk(e_)]
    S.dma("sp", A8S, PW[:, 15, :, :], key="a8s", reads=pk(8), is_output=True)
    den = tl([H, P]); xr_ = tl([H, P]); fre = tl([H, P]); fim = tl([H, P])
    tt(den[:], ar[:], ar[:], ALU.mult, "den", "ar", "ar")
    tt(tmp[0][:], ai[:], ai[:], ALU.mult, "tmp0", "ai", "ai")
    tt(den[:], den[:], tmp[0][:], ALU.add, "den", "den", "tmp0")
    dve(lambda e: e.reciprocal(out=den[:], in_=den[:]), ["den"], ["den"])
    dve(lambda e: e.tensor_scalar(out=xr_[:], in0=pw(1, 0), scalar1=-1.0, scalar2=None, op0=ALU.add),
        ["pw1_0"], ["xr"])
    tt(tmp[0][:], xr_[:], ar[:], ALU.mult, "tmp0", "xr", "ar")
    tt(tmp[1][:], pw(1, 1), ai[:], ALU.mult, "tmp1", "pw1_1", "ai")
    tt(fre[:], tmp[0][:], tmp[1][:], ALU.add, "fre", "tmp0", "tmp1")
    tt(fre[:], fre[:], den[:], ALU.mult, "fre", "fre", "den")
    tt(tmp[2][:], pw(1, 1), ar[:], ALU.mult, "tmp2", "pw1_1", "ar")
    tt(tmp[3][:], xr_[:], ai[:], ALU.mult, "tmp3", "xr", "ai")
    tt(fim[:], tmp[2][:], tmp[3][:], ALU.subtract, "fim", "tmp2", "tmp3")
    tt(fim[:], fim[:], den[:], ALU.mult, "fim", "fim", "den")
    Pst = tl([P, P, 8]); Qst = tl([P, P, 8])
    for s_ in range(8):
        cmul(Pst[0:H, :, s_], Qst[H:P, :, s_], pw(7 - s_, 0), pw(7 - s_, 1), fre[:], fim[:],
             ["Pst_lo", "Qst_hi"], pk(7 - s_) + ["fre", "fim"])
    S.op("act", lambda e: e.activation(out=Pst[H:P, :, :], in_=Pst[0:H, :, :], func=AF.Copy),
         reads=["Pst_lo"], writes=["Pst_hi"])
    S.op("act", lambda e: e.activation(out=Qst[0:H, :, :], in_=Qst[H:P, :, :], func=AF.Copy, scale=-1.0),
         reads=["Qst_hi"], writes=["Qst_lo"])
    Pv = tl([P, 16, P]); Qv = tl([P, 16, P])
    S.op("act", lambda e: e.activation(out=Pv[0:H], in_=PW[:, :, 0, :], func=AF.Copy), reads=allpw, writes=["Pv_lo"])
    S.op("act", lambda e: e.activation(out=Pv[H:P], in_=PW[:, :, 0, :], func=AF.Copy), reads=allpw, writes=["Pv_hi"])
    S.op("act", lambda e: e.activation(out=Qv[0:H], in_=PW[:, :, 1, :], func=AF.Copy, scale=-1.0), reads=allpw,
         writes=["Qv_lo"])
    S.op("act", lambda e: e.activation(out=Qv[H:P], in_=PW[:, :, 1, :], func=AF.Copy, scale=-1.0), reads=allpw,
         writes=["Qv_hi"])
    t1 = tl([P, 32, 128]); t2 = tl([P, 32, 128])
    raw = t1[:].rearrange("p a b -> p (a b)")
    R = tl([P, P, 16]); Sx = tl([P, P, 16]); Rp = tl([P, P, 16]); Sp = tl([P, P, 16])
    srcs = (("b_re", 0), ("b_im", 1), ("c_re", 2), ("c_im", 3))
    for nm, idx in srcs:
        S.dma("sp", raw[:, idx * 1024:(idx + 1) * 1024], prm[nm], key="raw%d" % idx, writes=["raw%d" % idx])
    cnt = [0]
    for nm, idx in srcs:
        rv = raw[:, idx * 1024:(idx + 1) * 1024]
        for q4 in range(4):
            pb_ = pA if cnt[0] % 2 == 0 else pA2
            pkey = "pA" if cnt[0] % 2 == 0 else "pA2"
            cnt[0] += 1
            for j in range(4):
                qq = q4 * 4 + j
                if idx < 2:
                    src = rv.rearrange("p (n q) -> p q n", q=16)[:, qq, :]
                else:
                    src = rv[:, qq * 64:(qq + 1) * 64]
                S.op("pe", lambda e, pb_=pb_, j=j, src=src: e.transpose(pb_[0:H, j, :], src, ident[:]),
                     reads=["raw%d" % idx, "ident"], writes=[pkey])
            qs = slice(q4 * 4, q4 * 4 + 4)
            pin = pb_[0:H, :, :]

            def outv(tile_, lo):
                v = tile_[0:H] if lo else tile_[H:P]
                return v.rearrange("p g q -> p q g")[:, qs, :]
            if idx == 0:
                dsts = ((R, True, 1.0), (Sx, False, 1.0))
            elif idx == 1:
                dsts = ((R, False, 1.0), (Sx, True, 1.0))
            elif idx == 2:
                dsts = ((Rp, True, 1.0), (Sp, False, 1.0))
            else:
                dsts = ((Rp, False, -1.0), (Sp, True, 1.0))
            for (tile_, lo, sc) in dsts:
                ov = outv(tile_, lo)
                S.op("act", lambda e, ov=ov, pin=pin, sc=sc: e.activation(out=ov, in_=pin, func=AF.Copy, scale=sc),
                     reads=[pkey], writes=["tab%d_%d_%d" % (id(tile_) % 997, lo, q4)])
    tabkeys = None
    X7c = tl([P, 32, 128], BF16); Ypc = tl([P, 32, 128], BF16); Ec = tl([P, 32, 128], BF16)
    Mc = tl([P, 32, 128], BF16); Gc = tl([P, 32, 128], BF16)
    pM = [A.ps("q_pM%d" % i, [P, 4, P]) for i in range(2)]
    pG = [A.ps("q_pG%d" % i, [P, 8, P], BF16) for i in range(2)]
    anytab = [k for k in S.last_w.keys() if isinstance(k, str) and k.startswith("tab")]
    for ch in range(4):
        gs = slice(ch * 32, ch * 32 + 32)
        t1v = t1[:].rearrange("p g (s q) -> p g s q", q=16)
        t2v = t2[:].rearrange("p g (s q) -> p g s q", q=16)

        def build(dst, dkey, Pt, Qt, Rt, St, pkeys):
            dve(lambda e: e.tensor_tensor(out=t1v, in0=Pt.unsqueeze(3).to_broadcast([P, 32, 8, 16]),
                                          in1=Rt.unsqueeze(2).to_broadcast([P, 32, 8, 16]), op=ALU.mult),
                pkeys + anytab + ["raw0", "raw1", "raw2", "raw3"], ["t1"])
            S.op("pool", lambda e: e.tensor_tensor(out=t2v, in0=Qt.unsqueeze(3).to_broadcast([P, 32, 8, 16]),
                                                   in1=St.unsqueeze(2).to_broadcast([P, 32, 8, 16]), op=ALU.mult),
                 reads=pkeys + anytab, writes=["t2"])
            dve(lambda e: e.tensor_tensor(out=dst[:], in0=t1[:], in1=t2[:], op=ALU.add), ["t1", "t2"], [dkey])
        build(X7c, "X7c", Pst[:, gs, :], Qst[:, gs, :], R[:, gs, :], Sx[:, gs, :],
              ["Pst_lo", "Pst_hi", "Qst_lo", "Qst_hi"])
        pvk = ["Pv_lo", "Pv_hi", "Qv_lo", "Qv_hi"]
        build(Ypc, "Ypc", Pv[:, 0:8, gs].rearrange("p e g -> p g e"), Qv[:, 0:8, gs].rearrange("p e g -> p g e"),
              Rp[:, gs, :], Sp[:, gs, :], pvk)
        build(Ec, "Ec", Pv[:, 8:16, gs].rearrange("p e g -> p g e"), Qv[:, 8:16, gs].rearrange("p e g -> p g e"),
              Rp[:, gs, :], Sp[:, gs, :], pvk)
        for g4 in range(8):
            b = g4 % 2
            for j in range(4):
                g = g4 * 4 + j
                S.op("pe", lambda e, b=b, j=j, g=g: e.matmul(pM[b][:, j, :], lhsT=X7c[:, g, :], rhs=Ypc[:, g, :],
                                                             start=True, stop=True),
                     reads=["X7c", "Ypc"], writes=[("pM", b)])
            dve(lambda e, b=b, g4=g4: e.tensor_tensor(
                out=Mc[:, g4 * 4:(g4 + 1) * 4, :], in0=pM[b][:],
                in1=maskM[:].unsqueeze(1).to_broadcast([P, 4, P]), op=ALU.mult),
                [("pM", b), "maskM"], [("Mc", g4)])
        for g8 in range(4):
            b = g8 % 2
            for j in range(8):
                g = g8 * 8 + j
                S.op("pe", lambda e, b=b, j=j, g=g: e.transpose(pG[b][:, j, :], X7c[:, g, :], identb[:]),
                     reads=["X7c", "identb"], writes=[("pG", b)])
            S.op("act", lambda e, b=b, g8=g8: e.activation(out=Gc[:, g8 * 8:(g8 + 1) * 8, :], in_=pG[b][:],
                                                           func=AF.Copy),
                 reads=[("pG", b)], writes=[("Gc", g8)])
        S.dma("sp", SM[:, gs, :], Mc[:], key="SMst", reads=[("Mc", i) for i in range(8)], is_output=True)
        S.dma("sp", SG[:, gs, :], Gc[:], key="SGst", reads=[("Gc", i) for i in range(4)], is_output=True)
        S.dma("sp", SE[:, gs, :], Ec[:], key="SEst", reads=["Ec"], is_output=True)


def s5main_body(S, A, zo, zp, SM, SG, SE, A8S, selm_ap, seqm_ap, ident_ap, s5in, YS, s5p_out, s5s_out):
    H = 64
    ident = A.sb("m_ident", [P, P], F32)
    S.dma("sp", ident[:], ident_ap, key="ident", writes=["ident"])
    selm = A.sb("m_selm", [P, 8], F32)
    S.dma("sp", selm[:], selm_ap, key="selm", writes=["selm"])
    bsf = A.sb("m_bsf", [P, 16], F32)
    bsel = A.sb("m_bsel", [P, 16], BF16)
    S.dma("sp", bsf[:], seqm_ap, key="bsf", writes=["bsf"])
    S.op("dve", lambda e: e.tensor_copy(out=bsel[:], in_=bsf[:]), reads=["bsf"], writes=["bsel"])
    a8 = A.sb("m_a8", [H, 2, P], F32)
    S.dma("sp", a8[:], A8S, key="a8", writes=["a8"])
    AA = A.sb("m_AA", [H, 2, P], F32)
    AB = A.sb("m_AB", [H, 2, P], F32)
    S.op("dve", lambda e: e.tensor_copy(out=AA[:, 0, :], in_=a8[:, 0, :]), reads=["a8"], writes=["AA0"])
    S.op("dve", lambda e: e.tensor_copy(out=AA[:, 1, :], in_=a8[:, 0, :]), reads=["a8"], writes=["AA1"])
    S.op("dve", lambda e: e.tensor_scalar(out=AB[:, 0, :], in0=a8[:, 1, :], scalar1=-1.0, scalar2=None,
                                          op0=ALU.mult), reads=["a8"], writes=["AB0"])
    S.op("dve", lambda e: e.tensor_copy(out=AB[:, 1, :], in_=a8[:, 1, :]), reads=["a8"], writes=["AB1"])
    AK = ["AA0", "AA1", "AB0", "AB1"]
    Gm = A.sb("m_G", [P, P, P], BF16)
    for ch in range(4):
        S.dma("sp", Gm[:, ch * 32:(ch + 1) * 32, :], SG[:, ch * 32:(ch + 1) * 32, :], key=("Gl", ch),
              writes=[("G", ch)])
    MEc = [A.sb("m_ME%d" % i, [P, 32, 2, P], BF16) for i in range(2)]
    uin = [A.sb("m_uin%d" % i, [P, 2048], F32) for i in range(2)]
    urep = A.sb("m_urep", [P, 32, 128], BF16)
    Uts = [A.sb("m_Ut%d" % i, [P, P, 16], BF16) for i in range(2)]
    VH = A.sb("m_VH", [H, 2, P, 17], F32)
    Hbfs = [A.sb("m_Hbf%d" % i, [P, P, 16], BF16) for i in range(2)]
    ysbs = [A.sb("m_ysb%d" % i, [16, 32, 128], F32) for i in range(2)]
    P1 = A.sb("m_P1", [H, 2, P], F32)
    P2 = A.sb("m_P2", [H, 2, P], F32)
    Vs = A.sb("m_Vs", [H, 2, P, 16], F32)
    H0s = A.sb("m_H0s", [H, 2, P, 16], F32)
    psU = [A.ps("m_psU%d" % i, [P, 32, 16]) for i in range(2)]
    psV = [A.ps("m_psV%d" % i, [P, 32, 16]) for i in range(2)]
    psY = [A.ps("m_psY%d" % i, [P, 4, P]) for i in range(2)]
    ptr = A.ps("m_ptr", [P, 4, P])
    S.op("pool", lambda e: e.memset(VH[:], 0.0), writes=["VH"])
    cn = {"u": 0, "U": 0, "V": 0, "Y": 0, "me": 0, "ys": 0}

    def make_U(src_ap, ub):
        Ut = Uts[ub]
        i = cn["u"] % 2
        cn["u"] += 1
        S.dma("sp", uin[i][:], src_ap, key=("uin", i), writes=[("uin", i)])
        for ch in range(4):
            uv = uin[i][:, ch * 512:(ch + 1) * 512].rearrange("p (g q) -> p g q", q=16)
            S.op("dve", lambda e, uv=uv: e.tensor_tensor(
                out=urep[:].rearrange("p g (s q) -> p g s q", q=16),
                in0=uv.unsqueeze(2).to_broadcast([P, 32, 8, 16]),
                in1=selm[:].unsqueeze(1).unsqueeze(3).to_broadcast([P, 32, 8, 16]), op=ALU.mult),
                reads=[("uin", i), "selm"], writes=["urep"])
            b = cn["U"] % 2
            cn["U"] += 1
            for j in range(32):
                S.op("pe", lambda e, b=b, j=j: e.matmul(psU[b][:, j, :], lhsT=urep[:, j, :], rhs=bsel[:],
                                                        start=True, stop=True),
                     reads=["urep", "bsel"], writes=[("psU", b)])
            S.op("act", lambda e, b=b, ch=ch: e.activation(out=Ut[:, ch * 32:(ch + 1) * 32, :], in_=psU[b][:],
                                                           func=AF.Copy),
                 reads=[("psU", b)], writes=[("Ut", ub, ch)])

    def make_V(dst_fn, dkey, ub):
        Ut = Uts[ub]
        for ch in range(4):
            b = cn["V"] % 2
            cn["V"] += 1
            for j in range(32):
                g = ch * 32 + j
                S.op("pe", lambda e, b=b, j=j, g=g: e.matmul(psV[b][:, j, :], lhsT=Gm[:, g, :], rhs=Ut[:, g, :],
                                                             start=True, stop=True),
                     reads=[("G", ch), ("Ut", ub, ch)], writes=[("psV", b)])
            for c in range(2):
                S.op("act", lambda e, b=b, c=c, ch=ch: e.activation(
                    out=dst_fn(c, ch), in_=psV[b][c * H:(c + 1) * H, :, :], func=AF.Copy),
                    reads=[("psV", b)], writes=[dkey])

    def cstep(Hj0, Hj1, Hj, Hn, key, w, extra=(), tk=("P1", "P2a", "P2b")):
        p1, p2 = w
        S.op("dve", lambda e: e.tensor_tensor(out=p1, in0=AA_v(Hj), in1=Hj, op=ALU.mult),
             reads=[key] + AK + list(extra), writes=[tk[0]])
        S.op("dve", lambda e: e.tensor_tensor(out=sub(p2, 0), in0=AB_v(Hj, 0), in1=Hj1, op=ALU.mult),
             reads=[key] + AK + list(extra), writes=[tk[1]])
        S.op("dve", lambda e: e.tensor_tensor(out=sub(p2, 1), in0=AB_v(Hj, 1), in1=Hj0, op=ALU.mult),
             reads=[key] + AK + list(extra), writes=[tk[2]])
        S.op("dve", lambda e: e.tensor_tensor(out=Hn, in0=Hn, in1=p1, op=ALU.add), reads=[key, tk[0]], writes=[key])
        S.op("dve", lambda e: e.tensor_tensor(out=Hn, in0=Hn, in1=p2, op=ALU.add),
             reads=[key, tk[1], tk[2]], writes=[key])

    def sub(ap, c):
        return ap[:, c]

    def AA_v(like):
        if len(like.shape) == 3:
            return AA[:]
        return AA[:].unsqueeze(3).to_broadcast([H, 2, P, like.shape[-1]])

    def AB_v(like, c):
        if len(like.shape) == 3:
            return AB[:, c, :]
        return AB[:, c, :].unsqueeze(2).to_broadcast([H, P, like.shape[-1]])

    def make_Hbf(src, skey, hb):
        Hbf = Hbfs[hb]
        S.op("act", lambda e: e.activation(out=Hbf[0:H], in_=src[:, 0, :, 0:16], func=AF.Copy),
             reads=[skey], writes=[("Hbf0", hb)])
        S.op("pool", lambda e: e.tensor_copy(out=Hbf[H:P], in_=src[:, 1, :, 0:16]),
             reads=[skey], writes=[("Hbf1", hb)])

    def make_Y(t, ub, hb):
        Ut = Uts[ub]
        Hbf = Hbfs[hb]
        for ch in range(4):
            ysi = cn["ys"] % 2
            cn["ys"] += 1
            ysb = ysbs[ysi]
            gs = slice(ch * 32, ch * 32 + 32)
            mi = cn["me"] % 2
            cn["me"] += 1
            S.dma("sp", MEc[mi][:, :, 0, :], SM[:, gs, :], key=("ME", mi), writes=[("ME", mi)])
            S.dma("sp", MEc[mi][:, :, 1, :], SE[:, gs, :], key=("ME", mi), writes=[("ME", mi)])
            for g4 in range(8):
                b = cn["Y"] % 2
                cn["Y"] += 1
                for j in range(4):
                    gl = g4 * 4 + j
                    g = ch * 32 + gl
                    S.op("pe", lambda e, b=b, j=j, g=g, gl=gl, mi=mi: e.matmul(
                        psY[b][0:16, j, :], lhsT=Ut[:, g, :], rhs=MEc[mi][:, gl, 0, :], start=True, stop=False),
                        reads=[("Ut", ub, ch), ("ME", mi)], writes=[("psY", b)])
                    S.op("pe", lambda e, b=b, j=j, g=g, gl=gl, mi=mi: e.matmul(
                        psY[b][0:16, j, :], lhsT=Hbf[:, g, :], rhs=MEc[mi][:, gl, 1, :], start=False, stop=True),
                        reads=[("Hbf0", hb), ("Hbf1", hb), ("ME", mi)], writes=[("psY", b)])
                S.op("act", lambda e, b=b, g4=g4, ysb=ysb: e.activation(out=ysb[:, g4 * 4:(g4 + 1) * 4, :],
                                                                        in_=psY[b][0:16, :, :], func=AF.Copy),
                     reads=[("psY", b)], writes=[("ysb", ysi)])
            dst = YS[t * P:(t + 1) * P, ch * 512:(ch + 1) * 512].rearrange("(b i) (g p) -> b i g p", i=8, p=16)
            for i_ in range(8):
                S.dma("act", dst[:, i_], ysb[:, :, i_ * 16:(i_ + 1) * 16], key=("ysb_st", ysi),
                      reads=[("ysb", ysi)], is_output=True)

    def vh_dst(c, ch):
        return VH[:, c, ch * 32:(ch + 1) * 32, 1:17]

    tiles = [(zp[t * P:(t + 1) * P, 6144:8192], t, False) for t in range(8)] + \
            [(zo[t * P:(t + 1) * P, 12288:14336], t, True) for t in range(8)]
    make_U(tiles[0][0], 0)
    make_V(vh_dst, "VH", 0)
    for idx, (src_ap, t, own) in enumerate(tiles):
        cur = idx % 2
        nxt = idx + 1 < len(tiles)
        if nxt:
            make_U(tiles[idx + 1][0], 1 - cur)
        for j in range(16):
            cstep(VH[:, 0, :, j], VH[:, 1, :, j], VH[:, :, :, j], VH[:, :, :, j + 1], "VH", (P1[:], P2[:]))
        if own:
            make_Hbf(VH, "VH", cur)
        S.op("dve", lambda e: e.tensor_copy(out=VH[:, :, :, 0], in_=VH[:, :, :, 16]), reads=["VH"], writes=["VH"])
        if nxt:
            make_V(vh_dst, "VH", 1 - cur)
        if own:
            make_Y(t, cur, cur)
    hout = A.sb("m_hout", [P, 16, H], F32)
    for c in range(2):
        S.op("pe", lambda e, c=c: e.transpose(ptr[:, c, 0:H], VH[:, c, :, 0], ident[0:H, 0:H]),
             reads=["VH", "ident"], writes=["ptr"])
    S.op("act", lambda e: e.activation(out=hout[:, 0:2, :], in_=ptr[:, 0:2, 0:H], func=AF.Copy),
         reads=["ptr"], writes=["hout"])
    S.dma("sp", s5p_out.rearrange("c g n -> g c n"), hout[:, 0:2, :], key="s5p_st", reads=["hout"],
          writes=["s5p_dram"], is_output=True)

    hraw = A.sb("m_hraw", [P, 16, H], F32)
    for c in range(2):
        S.dma("sp", hraw[:], s5in[c].rearrange("s g n -> g s n"), key="hraw", writes=["hraw"])
        for s4 in range(4):
            for j in range(4):
                s_ = s4 * 4 + j
                S.op("pe", lambda e, j=j, s_=s_: e.transpose(ptr[0:H, j, :], hraw[:, s_, :], ident[:]),
                     reads=["hraw", "ident"], writes=["ptr"])
            S.op("act", lambda e, c=c, s4=s4: e.activation(
                out=H0s[:, c, :, s4 * 4:(s4 + 1) * 4].rearrange("p g s -> p s g"), in_=ptr[0:H, :, :],
                func=AF.Copy), reads=["ptr"], writes=["H0s"])
    make_U(zo[1024:1152, 12288:14336], 0)
    make_V(lambda c, ch: Vs[:, c, ch * 32:(ch + 1) * 32, :], "Vs", 0)
    make_Hbf(H0s, "H0s", 0)
    make_Y(8, 0, 0)
    P1s = uin[0][0:H, :].rearrange("p (c g s) -> p c g s", c=2, g=P)
    P2s = uin[1][0:H, :].rearrange("p (c g s) -> p c g s", c=2, g=P)
    for hh_ in range(2):
        hs = slice(hh_ * 8, hh_ * 8 + 8)
        cstep(H0s[:, 0, :, hs], H0s[:, 1, :, hs], H0s[:, :, :, hs], Vs[:, :, :, hs], "Vs", (P1s, P2s),
              extra=["H0s"], tk=(("uin", 0), ("uin", 1), ("uin", 1)))
    for c in range(2):
        for s8 in range(2):
            for j in range(8):
                s_ = s8 * 8 + j
                S.op("pe", lambda e, c=c, j=j, s_=s_: e.transpose(
                    ptr[:, j // 2, (j % 2) * H:(j % 2 + 1) * H], Vs[:, c, :, s_], ident[0:H, 0:H]),
                    reads=["Vs", "ident"], writes=["ptr"])
            S.op("act", lambda e, s8=s8: e.activation(
                out=hout[:, s8 * 8:(s8 + 1) * 8, :], in_=ptr[:].rearrange("p a (b n) -> p (a b) n", n=H),
                func=AF.Copy), reads=["ptr"], writes=["hout"])
        S.dma("sp", s5s_out[c].rearrange("s g n -> g s n"), hout[:], key="s5s_st", reads=["hout"],
              writes=["s5s_dram"], is_output=True)


def build_program(debug=False, stages=None, scr_in=()):
    nc = bass.Bass("TRN2", target_bir_lowering=False)
    NT = NTOK_OWN // P

    def din(name, shape, dt=F32):
        return nc.dram_tensor(name, list(shape), dt, kind="ExternalInput").ap()

    def dout(name, shape, dt=F32):
        return nc.dram_tensor(name, list(shape), dt, kind="ExternalOutput").ap()

    def dscr(name, shape, dt=F32):
        kind = "ExternalOutput" if debug else "Internal"
        if name in scr_in:
            kind = "ExternalInput"
        return nc.dram_tensor(name, list(shape), dt, kind=kind).ap()

    xo = din("xo", [NTOK_OWN, D_MODEL])
    xp = din("xp", [NTOK_PRE, D_MODEL])
    mem = din("mem", [256, D_MODEL])
    w_in = din("w_in", [D_MODEL, IN_WIDTH])
    w_mem_kv = din("w_mem_kv", [D_MODEL, 4096])
    ident = din("ident", [P, P])

    memkv = dout("memkv", [256, 4096])
    zo = dscr("zo", [NTOK_OWN, IN_WIDTH])
    zp = dscr("zp", [NTOK_PRE, 8192])

    def st_mem(S, A):
        blocks = [(c0, AF.Copy, (lambda t, c0=c0: memkv[t * P:(t + 1) * P, c0:c0 + 256]))
                  for c0 in range(0, 4096, 256)]
        gemm_body(S, A, "m", mem, 2, w_mem_kv, blocks, ident)
    if stages is None or 'mem' in stages:
        run_stage(nc, st_mem)

    def st_pre(S, A):
        blocks = []
        for (src0, n, dst0) in ((2048, 2048, 0), (4096, 4096, 2048), (12288, 2048, 6144)):
            for c in range(0, n, 256):
                blocks.append((src0 + c, AF.Copy,
                               (lambda t, d=dst0 + c: zp[t * P:(t + 1) * P, d:d + 256])))
        gemm_body(S, A, "p", xp, NTOK_PRE // P, w_in, blocks, ident)
    if stages is None or 'pre' in stages:
        run_stage(nc, st_pre)

    def st_own(S, A):
        blocks = [(c0, col_func(c0), (lambda t, c0=c0: zo[t * P:(t + 1) * P, c0:c0 + 256]))
                  for c0 in range(0, IN_WIDTH, 256)]
        gemm_body(S, A, "o", xo, NTOK_OWN // P, w_in, blocks, ident)
    if stages is None or 'own' in stages:
        run_stage(nc, st_own)

    rq = din("rq", [9, P, 2, 16, 64])
    rk_own = din("rk_own", [9, P, 2, 16, 64])
    rk_pre = din("rk_pre", [8, P, 2, 16, 64])
    tabs = {"rq": rq, "rk_own": rk_own, "rk_pre": rk_pre,
            "mask_p": din("mask_p", [P, P]), "mask_s": din("mask_s", [P, P]),
            "seqm": din("seqm", [P, 16]), "seqmT": din("seqmT", [P, 16, P])}
    sret_in = din("sret_in", [16, 16, P, 256])
    sret_out = dout("sret_out", [16, 16, P, 256])
    sretp_out = dout("sretp_out", [16, P, 256])
    OT = dscr("OT", [9, P, 64, P], BF16)

    def st_ret(S, A):
        retention_body(S, A, zo, zp, tabs, sret_in, sret_out, sretp_out, OT, ident)
    if stages is None or 'ret' in stages:
        run_stage(nc, st_ret)

    cmk = din("cmk", [16, 256, 2048])
    cmv = din("cmv", [16, 256, 2048])

    def st_x(S, A):
        xattn_body(S, A, zo, memkv, cmk, cmv, tabs["seqmT"], OT, ident)
    if stages is None or 'x' in stages:
        run_stage(nc, st_x)

    prm = {"a_re": din("s5_a_re", [P, 64]), "a_im": din("s5_a_im", [P, 64]), "log_step": din("s5_log_step", [1, P]),
           "b_re": din("s5_b_re", [P, 1024]), "b_im": din("s5_b_im", [P, 1024]),
           "c_re": din("s5_c_re", [P, 1024]), "c_im": din("s5_c_im", [P, 1024])}
    maskM = din("maskM", [P, P])
    SM = dscr("SM", [P, P, P], BF16)
    SG = dscr("SG", [P, P, P], BF16)
    SE = dscr("SE", [P, P, P], BF16)
    A8S = dscr("A8S", [64, 2, P])

    def st_s5prep(S, A):
        s5prep_body(S, A, prm, ident, maskM, SM, SG, SE, A8S)
    if stages is None or 's5prep' in stages:
        run_stage(nc, st_s5prep)

    selm = din("selm", [P, 8])
    s5in = din("s5in", [2, 16, P, 64])
    YS = dscr("YS", [NTOK_OWN, 2048])
    s5p_out = dout("s5p_out", [2, P, 64])
    s5s_out = dout("s5s_out", [2, 16, P, 64])

    def st_s5main(S, A):
        s5main_body(S, A, zo, zp, SM, SG, SE, A8S, selm, tabs["seqm"], ident, s5in, YS, s5p_out, s5s_out)
    if stages is None or 's5main' in stages:
        run_stage(nc, st_s5main)

    s5d = din("s5_d", [1, 2048])
    w_glu = din("w_glu", [2048, 4096])
    GL = dscr("GL", [NTOK_OWN, 2048])
    GAB = dscr("GAB", [NTOK_OWN, 4096])

    def st_gelu(S, A):
        db = A.sb("g_db", [P, 2048], F32)
        S.dma("sp", db[:], s5d.to_broadcast([P, 2048]), key="db", writes=["db"])
        yb = [A.sb("g_y%d" % i, [P, 2048], F32) for i in range(2)]
        ub = [A.sb("g_u%d" % i, [P, 2048], F32) for i in range(2)]
        tb = [A.sb("g_t%d" % i, [P, 2048], F32) for i in range(2)]
        for t in range(NT):
            i = t % 2
            y, u, tt_ = yb[i], ub[i], tb[i]
            S.dma("sp", y[:], YS[t * P:(t + 1) * P, :], key=("y", i), writes=[("y", i)])
            S.dma("sp", u[:], zo[t * P:(t + 1) * P, 12288:14336], key=("u", i), writes=[("u", i)])
            S.op("pool", lambda e, u=u: e.tensor_tensor(out=u[:], in0=u[:], in1=db[:], op=ALU.mult),
                 reads=[("u", i), "db"], writes=[("u", i)])
            S.op("dve", lambda e, y=y, u=u: e.tensor_tensor(out=y[:], in0=y[:], in1=u[:], op=ALU.add),
                 reads=[("y", i), ("u", i)], writes=[("y", i)])
            S.op("pool", lambda e, y=y, tt_=tt_: e.tensor_tensor(out=tt_[:], in0=y[:], in1=y[:], op=ALU.mult),
                 reads=[("y", i)], writes=[("t", i)])
            S.op("dve", lambda e, tt_=tt_: e.tensor_scalar(out=tt_[:], in0=tt_[:], scalar1=0.044715, scalar2=1.0,
                                                           op0=ALU.mult, op1=ALU.add),
                 reads=[("t", i)], writes=[("t", i)])
            S.op("dve", lambda e, y=y, tt_=tt_: e.tensor_tensor(out=tt_[:], in0=tt_[:], in1=y[:], op=ALU.mult),
                 reads=[("t", i), ("y", i)], writes=[("t", i)])
            S.op("act", lambda e, tt_=tt_: e.activation(out=tt_[:], in_=tt_[:], func=AF.Sigmoid,
                                                        scale=1.5957691216057308),
                 reads=[("t", i)], writes=[("t", i)])
            S.op("dve", lambda e, y=y, tt_=tt_: e.tensor_tensor(out=y[:], in0=y[:], in1=tt_[:], op=ALU.mult),
                 reads=[("t", i), ("y", i)], writes=[("y", i)])
            S.dma("sp", GL[t * P:(t + 1) * P, :], y[:], key=("gl", i), reads=[("y", i)], is_output=True)

    def st_glu(S, A):
        blocks = [(c0, (AF.Copy if c0 < 2048 else AF.Sigmoid),
                   (lambda t, c0=c0: GAB[t * P:(t + 1) * P, c0:c0 + 256])) for c0 in range(0, 4096, 256)]
        gemm_body(S, A, "g", GL, NT, w_glu, blocks, ident, nkt=16)

    def st_s5fin(S, A):
        identf = A.sb("f_ident", [P, P], F32)
        identb = A.sb("f_identb", [P, P], BF16)
        S.dma("sp", identf[:], ident, key="ident", writes=["ident"])
        S.op("dve", lambda e: e.tensor_copy(out=identb[:], in_=identf[:]), reads=["ident"], writes=["identb"])
        ab = [A.sb("f_ab%d" % i, [P, 4096], F32) for i in range(2)]
        gg = [A.sb("f_g%d" % i, [P, 2048], F32) for i in range(2)]
        ob = [A.sb("f_ob%d" % i, [P, 16, P], BF16) for i in range(2)]
        oT = [A.sb("f_oT%d" % i, [P, 16, P], BF16) for i in range(2)]
        ptr = [A.ps("f_ptr%d" % i, [P, 8, P], BF16) for i in range(2)]
        cnt = 0
        for t in range(NT):
            i = t % 2
            S.dma("sp", ab[i][:], GAB[t * P:(t + 1) * P, :], key=("ab", i), writes=[("ab", i)])
            S.dma("sp", gg[i][:], zo[t * P:(t + 1) * P, 14336:16384], key=("gg", i), writes=[("gg", i)])
            S.op("pool", lambda e, i=i: e.tensor_tensor(out=gg[i][:], in0=gg[i][:], in1=ab[i][:, 2048:4096],
                                                        op=ALU.mult),
                 reads=[("gg", i), ("ab", i)], writes=[("gg", i)])
            S.op("dve", lambda e, i=i: e.tensor_tensor(out=ob[i][:].rearrange("p a b -> p (a b)"),
                                                       in0=ab[i][:, 0:2048], in1=gg[i][:], op=ALU.mult),
                 reads=[("gg", i), ("ab", i)], writes=[("ob", i)])
            for half in range(2):
                b = cnt % 2
                cnt += 1
                for j in range(8):
                    S.op("pe", lambda e, b=b, j=j, i=i, half=half: e.transpose(
                        ptr[b][:, j, :], ob[i][:, half * 8 + j, :], identb[:]),
                        reads=[("ob", i), "identb"], writes=[("ptr", b)])
                S.op("act", lambda e, b=b, i=i, half=half: e.activation(
                    out=oT[i][:, half * 8:(half + 1) * 8, :], in_=ptr[b][:], func=AF.Copy),
                    reads=[("ptr", b)], writes=[("oT", i, half)])
            S.dma("act", OT[t, :, 32:48, :], oT[i][:], key=("oTs", i), reads=[("oT", i, 0), ("oT", i, 1)],
                  is_output=True)
    if stages is None or 's5post' in stages:
        run_stage(nc, st_gelu)
        run_stage(nc, st_glu)
        run_stage(nc, st_s5fin)

    w_pa = din("w_proj_a", [4096, D_MODEL])
    w_pb = din("w_proj_b", [2048, D_MODEL])
    w_pc = din("w_proj_c", [2048, D_MODEL])
    w_o = din("w_out", [D_MODEL, D_MODEL])
    ln_g = din("ln_g", [1, D_MODEL])
    ln_b = din("ln_b", [1, D_MODEL])
    y_out = dout("y_out", [NTOK_OWN, D_MODEL])
    PR = [dscr("PR%d" % i, [NTOK_OWN, D_MODEL]) for i in range(3)]
    HP = dscr("HP", [NTOK_OWN, D_MODEL])
    NT = NTOK_OWN // P

    def proj_stage(i, w_ap, ft0, nkt, gate0):
        def body(S, A):
            gt = [A.sb("gt%d_%d" % (i, j), [P, NT, 256], F32) for j in range(2)]

            def pre_block(S, bi):
                c0 = bi * 256
                S.dma("sp", gt[bi % 2][:], zo[:, gate0 + c0:gate0 + c0 + 256].rearrange("(t p) c -> p t c", p=P),
                      key=("gt", bi % 2), writes=[("gt", bi % 2)])

            def epi(S, t, bi, ps_ap, pkey, ob_t, okey):
                j = bi % 2
                S.op("dve", lambda e, j=j, t=t: e.tensor_tensor(out=ob_t[:], in0=ps_ap, in1=gt[j][:, t, :],
                                                                op=ALU.mult),
                     reads=[pkey, ("gt", j)], writes=[okey])
            blocks = [(c0, None, (lambda t, c0=c0: PR[i][t * P:(t + 1) * P, c0:c0 + 256]))
                      for c0 in range(0, D_MODEL, 256)]
            gemm_body(S, A, "j%d" % i, None, NT, w_ap, blocks, ident, nkt=nkt, a_T=(OT, ft0), epi=epi,
                      pre_block=pre_block)
        return body
    if stages is None or 'tail' in stages:
        run_stage(nc, proj_stage(0, w_pa, 0, 32, 20480))
        run_stage(nc, proj_stage(1, w_pb, 32, 16, 24576))
        run_stage(nc, proj_stage(2, w_pc, 48, 16, 28672))

    def st_out(S, A):
        xr = [A.sb("xr%d" % j, [P, NT, 256], F32) for j in range(2)]
        alpha = float((2.0 * 1) ** 0.25)

        def pre_block(S, bi):
            c0 = bi * 256
            S.dma("sp", xr[bi % 2][:], xo[:, c0:c0 + 256].rearrange("(t p) c -> p t c", p=P),
                  key=("xr", bi % 2), writes=[("xr", bi % 2)])

        def epi(S, t, bi, ps_ap, pkey, ob_t, okey):
            j = bi % 2
            S.op("dve", lambda e, j=j, t=t: e.scalar_tensor_tensor(out=ob_t[:], in0=xr[j][:, t, :], scalar=alpha,
                                                                   in1=ps_ap, op0=ALU.mult, op1=ALU.add),
                 reads=[pkey, ("xr", j)], writes=[okey])
        blocks = [(c0, None, (lambda t, c0=c0: HP[t * P:(t + 1) * P, c0:c0 + 256]))
                  for c0 in range(0, D_MODEL, 256)]
        gemm_body(S, A, "w", PR[0], NT, w_o, blocks, ident, x_sum=[PR[1], PR[2]], epi=epi, pre_block=pre_block)
    if stages is None or 'tail' in stages or 'out' in stages:
        run_stage(nc, st_out)

    def st_ln(S, A):
        gb = A.sb("ln_gb", [P, D_MODEL], F32)
        bb = A.sb("ln_bb", [P, D_MODEL], F32)
        S.dma("sp", gb[:], ln_g.to_broadcast([P, D_MODEL]), key="gb", writes=["gb"])
        S.dma("sp", bb[:], ln_b.to_broadcast([P, D_MODEL]), key="bb", writes=["bb"])
        hb = [A.sb("ln_h%d" % j, [P, D_MODEL], F32) for j in range(2)]
        st = A.sb("ln_st", [P, 8, 6], F32)
        mv = A.sb("ln_mv", [P, 2], F32)
        nb = A.sb("ln_nb", [P, 1], F32)
        for t in range(NT):
            j = t % 2
            h = hb[j]
            S.dma("sp", h[:], HP[t * P:(t + 1) * P, :], key=("h", j), writes=[("h", j)])
            for c in range(8):
                S.op("dve", lambda e, c=c, h=h: e.bn_stats(out=st[:, c, :], in_=h[:, c * 512:(c + 1) * 512]),
                     reads=[("h", j)], writes=[("st", c)])
            S.op("dve", lambda e: e.bn_aggr(out=mv[:], in_=st[:].rearrange("p a b -> p (a b)")),
                 reads=[("st", c) for c in range(8)], writes=["mv"])
            S.op("dve", lambda e: e.tensor_scalar(out=mv[:, 1:2], in0=mv[:, 1:2], scalar1=1e-5, scalar2=None,
                                                  op0=ALU.add), reads=["mv"], writes=["mv"])
            S.op("act", lambda e: e.activation(out=mv[:, 1:2], in_=mv[:, 1:2], func=AF.Sqrt),
                 reads=["mv"], writes=["mv"])
            S.op("dve", lambda e: e.reciprocal(out=mv[:, 1:2], in_=mv[:, 1:2]), reads=["mv"], writes=["mv"])
            S.op("dve", lambda e: e.scalar_tensor_tensor(out=nb[:], in0=mv[:, 0:1], scalar=-1.0, in1=mv[:, 1:2],
                                                         op0=ALU.mult, op1=ALU.mult),
                 reads=["mv"], writes=["nb"])
            S.op("act", lambda e, h=h: e.activation(out=h[:], in_=h[:], func=AF.Identity, bias=nb[:],
                                                    scale=mv[:, 1:2]),
                 reads=[("h", j), "mv", "nb"], writes=[("h", j)])
            S.op("dve", lambda e, h=h: e.tensor_tensor(out=h[:], in0=h[:], in1=gb[:], op=ALU.mult),
                 reads=[("h", j), "gb"], writes=[("h", j)])
            S.op("pool", lambda e, h=h: e.tensor_tensor(out=h[:], in0=h[:], in1=bb[:], op=ALU.add),
                 reads=[("h", j), "bb"], writes=[("h", j)])
            S.dma("sp", y_out[t * P:(t + 1) * P, :], h[:], key=("hout", j), reads=[("h", j)], is_output=True)
    if stages is None or 'tail' in stages or 'ln' in stages:
        run_stage(nc, st_ln)

    return nc


def host_tables(hf):
    f = np.float32
    inv = (1.0 / (np.float32(10000.0) ** (np.arange(64, dtype=f) / np.float32(64)))).astype(f)
    g = np.array(RET_G, dtype=np.float64)
    i = np.arange(P)

    def tab(pos, il, kind):
        ang = pos.astype(f)[:, None] * inv[None, :]
        c, s_ = np.cos(ang).astype(f), np.sin(ang).astype(f)
        if kind == "q":
            sc = g[None, :] ** (il[:, None] + 1.0)
        else:
            sc = g[None, :] ** (-(il[:, None] + 1.0)) * (128.0 ** -0.5)
        out = np.empty((P, 2, 16, 64), f)
        out[:, 0] = (c[:, None, :] * sc[:, :, None]).astype(f)
        out[:, 1] = (s_[:, None, :] * sc[:, :, None]).astype(f)
        return out
    rq = np.stack([tab(hf * 1024 + t * P + i, i, "q") for t in range(8)] + [tab(16384 + (i % 8), i % 8, "q")])
    rk_own = np.stack([tab(hf * 1024 + t * P + i, i, "k") for t in range(8)] + [tab(16384 + (i % 8), i % 8, "k")])
    rk_pre = np.stack([tab(t * P + i, i, "k") for t in range(8)])
    mask_p = (i[None, :] >= i[:, None]).astype(f)
    same = (i[None, :] // 8) == (i[:, None] // 8)
    mask_s = (mask_p * same).astype(f)
    seqm = (i[:, None] // 8 == np.arange(16)[None, :]).astype(f)
    seqmT = np.ascontiguousarray(np.broadcast_to(seqm.T[None, :, :], (P, 16, P))).astype(f)
    maskM = ((i[None, :] // 16) >= (i[:, None] // 16)).astype(f)
    selm = (i[:, None] % 8 == np.arange(8)[None, :]).astype(f)
    return {"rq": rq, "rk_own": rk_own, "rk_pre": rk_pre, "mask_p": mask_p, "mask_s": mask_s,
            "seqm": seqm, "seqmT": seqmT, "maskM": maskM, "selm": selm}


_PROGRAM = None


def kernel(x_prompt, x_sample, mem_prompt, state_ret, state_s5_re, state_s5_im, cache_mem_k, cache_mem_v,
           w_in, w_mem_kv, s5_a_re, s5_a_im, s5_log_step, s5_b_re, s5_b_im, s5_c_re, s5_c_im, s5_d, w_glu,
           w_proj_a, w_proj_b, w_proj_c, w_out, ln_g, ln_b):
    global _PROGRAM
    if _PROGRAM is None:
        _PROGRAM = build_program()
    nc = _PROGRAM
    f = np.float32
    x_prompt = np.asarray(x_prompt, f)
    x_sample = np.asarray(x_sample, f)
    w_in0 = np.ascontiguousarray(np.asarray(w_in, f)[0])
    w_mem0 = np.ascontiguousarray(np.asarray(w_mem_kv, f)[0])
    ident = np.eye(P, dtype=f)
    wpa = np.ascontiguousarray(np.asarray(w_proj_a, f)[0])
    wpb = np.ascontiguousarray(np.asarray(w_proj_b, f)[0])
    wpc = np.ascontiguousarray(np.asarray(w_proj_c, f)[0])
    wout = np.ascontiguousarray(np.asarray(w_out, f)[0])
    lng = np.ascontiguousarray(np.asarray(ln_g, f).reshape(1, D_MODEL))
    lnb = np.ascontiguousarray(np.asarray(ln_b, f).reshape(1, D_MODEL))
    s5p = {"a_re": np.ascontiguousarray(np.asarray(s5_a_re, f)[0]), "a_im": np.ascontiguousarray(np.asarray(s5_a_im, f)[0]),
           "log_step": np.ascontiguousarray(np.asarray(s5_log_step, f).reshape(1, P)),
           "b_re": np.ascontiguousarray(np.asarray(s5_b_re, f)[0].reshape(P, 1024)),
           "b_im": np.ascontiguousarray(np.asarray(s5_b_im, f)[0].reshape(P, 1024)),
           "c_re": np.ascontiguousarray(np.asarray(s5_c_re, f)[0].reshape(P, 1024)),
           "c_im": np.ascontiguousarray(np.asarray(s5_c_im, f)[0].reshape(P, 1024)),
           "d": np.ascontiguousarray(np.asarray(s5_d, f).reshape(1, 2048))}
    wglu = np.ascontiguousarray(np.asarray(w_glu, f)[0])
    in_maps = []
    for c in range(NCORES):
        b, hf = c // 2, c % 2
        xo = np.concatenate([x_prompt[b, hf * 1024:(hf + 1) * 1024],
                             x_sample[16 * c:16 * c + 16].reshape(128, D_MODEL)], axis=0)
        xp = x_prompt[b, 0:1024] if hf == 1 else np.zeros((1024, D_MODEL), f)
        in_maps.append({
            "xo": np.ascontiguousarray(xo), "xp": np.ascontiguousarray(xp),
            "mem": np.ascontiguousarray(np.asarray(mem_prompt, f)[b]),
            "w_in": w_in0, "w_mem_kv": w_mem0, "ident": ident,
            "sret_in": np.ascontiguousarray(np.asarray(state_ret, f)[0, 16 * c:16 * c + 16]),
            "cmk": np.ascontiguousarray(np.asarray(cache_mem_k, f)[0, 16 * c:16 * c + 16]).reshape(16, 256, 2048),
            "cmv": np.ascontiguousarray(np.asarray(cache_mem_v, f)[0, 16 * c:16 * c + 16]).reshape(16, 256, 2048),
            "w_proj_a": wpa, "w_proj_b": wpb, "w_proj_c": wpc, "w_out": wout, "ln_g": lng, "ln_b": lnb,
            "s5_a_re": s5p["a_re"], "s5_a_im": s5p["a_im"], "s5_log_step": s5p["log_step"],
            "s5_b_re": s5p["b_re"], "s5_b_im": s5p["b_im"], "s5_c_re": s5p["c_re"], "s5_c_im": s5p["c_im"],
            "s5_d": s5p["d"], "w_glu": wglu,
            "s5in": np.ascontiguousarray(np.stack([np.asarray(state_s5_re, f)[0, 16 * c:16 * c + 16],
                                                   np.asarray(state_s5_im, f)[0, 16 * c:16 * c + 16]])),
        })
        in_maps[-1].update(host_tables(hf))
    res = run_bass_kernel_spmd(nc, in_maps, core_ids=list(range(NCORES)))
    R = res.results
    memk = np.stack([R[2 * b]["memkv"][:, 0:2048].reshape(256, 4, 512) for b in range(4)])[None]
    memv = np.stack([R[2 * b]["memkv"][:, 2048:4096].reshape(256, 4, 512) for b in range(4)])[None]
    y_p = np.stack([np.concatenate([R[2 * b]["y_out"][:1024], R[2 * b + 1]["y_out"][:1024]], axis=0)
                    for b in range(4)])
    y_s = np.concatenate([R[c]["y_out"][1024:] for c in range(NCORES)], axis=0).reshape(128, 8, D_MODEL)
    sretp = np.stack([R[2 * b + 1]["sretp_out"] for b in range(4)])[None]
    srets = np.concatenate([R[c]["sret_out"] for c in range(NCORES)], axis=0)[None]
    s5p_re = np.stack([R[2 * b + 1]["s5p_out"][0] for b in range(4)])[None]
    s5p_im = np.stack([R[2 * b + 1]["s5p_out"][1] for b in range(4)])[None]
    s5s_re = np.concatenate([R[c]["s5s_out"][0] for c in range(NCORES)], axis=0)[None]
    s5s_im = np.concatenate([R[c]["s5s_out"][1] for c in range(NCORES)], axis=0)[None]
    return (y_p, y_s, sretp, s5p_re, s5p_im, memk, memv, srets, s5s_re, s5s_im)
```

```python
import math
from contextlib import ExitStack

import numpy as np
import concourse.bass as bass
import concourse.mybir as mybir
from concourse.bass_utils import run_bass_kernel_spmd

F32 = mybir.dt.float32
BF16 = mybir.dt.bfloat16
AF = mybir.ActivationFunctionType
ALU = mybir.AluOpType
P = 128
NCORES = 8

D_MODEL = 4096
IN_WIDTH = 32768
NTOK_OWN = 1152
NTOK_PRE = 1024

ENGS = ("pe", "act", "dve", "pool", "sp")
SEM_LIMIT = 20000


class Ins:
    __slots__ = ("eng", "fn", "deps", "is_dma", "key", "need_inc", "semref")

    def __init__(self, eng, fn, is_dma=False, key=None):
        self.eng = eng
        self.fn = fn
        self.deps = []
        self.is_dma = is_dma
        self.key = key
        self.need_inc = False
        self.semref = None


class Sched:
    _stage = 0

    def __init__(self, nc):
        Sched._stage += 1
        self.sid = Sched._stage
        self.nc = nc
        self.ins = []
        self.last_w = {}
        self.readers = {}
        self.dma_count = {}
        self.out_keys = set()

    def _add(self, ins, reads, writes):
        deps = set()
        for k in reads:
            w = self.last_w.get(k)
            if w is not None:
                deps.add(w)
        for k in writes:
            w = self.last_w.get(k)
            if w is not None:
                deps.add(w)
            for r in self.readers.get(k, ()):
                deps.add(r)
        deps.discard(ins)
        for d in deps:
            if d.is_dma:
                ins.deps.append((d, 16 * self.dma_count[d.key]))
            elif d.eng == ins.eng and not ins.is_dma:
                if ins.eng != "pe":
                    ins.deps.append((d, 0))
                    d.need_inc = True
            else:
                ins.deps.append((d, 0))
                d.need_inc = True
        for k in reads:
            self.readers.setdefault(k, []).append(ins)
        for k in writes:
            self.last_w[k] = ins
            self.readers[k] = []
        self.ins.append(ins)
        return ins

    def op(self, eng, fn, reads=(), writes=()):
        return self._add(Ins(eng, fn), list(reads), list(writes))

    def dma(self, eng, out, in_, key, reads=(), writes=(), is_output=False):
        ins = Ins(eng, lambda e: e.dma_start(out=out, in_=in_), is_dma=True, key=key)
        if is_output:
            self.out_keys.add(key)
        self.dma_count.setdefault(key, 0)
        self._add(ins, list(reads), list(writes))
        self.dma_count[key] += 1
        return ins

    def emit(self):
        nc = self.nc
        sem_names = []
        cur = {}
        for ins in self.ins:
            if ins.is_dma or not ins.need_inc:
                continue
            c = cur.get(ins.eng)
            if c is None or c[1] >= SEM_LIMIT:
                c = [len(sem_names), 0]
                sem_names.append("c%d_%s_%d" % (self.sid, ins.eng, len(sem_names)))
                cur[ins.eng] = c
            c[1] += 1
            ins.semref = (c[0], c[1])
        dma_keys = sorted(self.dma_count.keys(), key=str)
        csem = [nc.alloc_semaphore(name=n) for n in sem_names]
        dsem = {k: nc.alloc_semaphore(name="d%d_%d" % (self.sid, i)) for i, k in enumerate(dma_keys)}
        streams = {e: [i for i in self.ins if i.eng == e] for e in ENGS}
        final_dma = dict((k, 16 * v) for k, v in self.dma_count.items())
        out_keys = self.out_keys

        def run(engname, e):
            waited = {}
            for ins in streams[engname]:
                need = {}
                for d, dv in ins.deps:
                    if d.is_dma:
                        sk = ("d", d.key)
                        v = dv
                    else:
                        sk = ("c", d.semref[0])
                        v = d.semref[1]
                    if v > need.get(sk, 0):
                        need[sk] = v
                for sk, v in need.items():
                    if waited.get(sk, 0) >= v:
                        continue
                    waited[sk] = v
                    sem = dsem[sk[1]] if sk[0] == "d" else csem[sk[1]]
                    e.wait_ge(sem, v)
                r = ins.fn(e)
                if ins.is_dma:
                    r.then_inc(dsem[ins.key], 16)
                elif ins.need_inc:
                    r.then_inc(csem[ins.semref[0]], 1)
            if engname == "sp":
                for k in sorted(out_keys, key=str):
                    e.wait_ge(dsem[k], final_dma[k])

        with nc.Block() as block:
            @block.tensor
            def _(e):
                run("pe", e)

            @block.scalar
            def _(e):
                run("act", e)

            @block.vector
            def _(e):
                run("dve", e)

            @block.gpsimd
            def _(e):
                run("pool", e)

            @block.sync
            def _(e):
                run("sp", e)

        if not getattr(Sched, "NOCLEAR", False):
            nc.clear_and_free_semaphores(csem + list(dsem.values()))
        if not getattr(Sched, "NOCLEAR", False):
            nc.all_engine_barrier()


class Alloc:
    def __init__(self, nc, st):
        self.nc = nc
        self.st = st

    def sb(self, name, shape, dt):
        return self.st.enter_context(self.nc.sbuf_tensor(name, list(shape), dt))

    def ps(self, name, shape, dt=F32):
        return self.st.enter_context(self.nc.psum_tensor(name, list(shape), dt))


def run_stage(nc, body):
    with ExitStack() as st:
        S = Sched(nc)
        A = Alloc(nc, st)
        body(S, A)
        S.emit()


def gemm_body(S, A, uid, x_ap, ntile, w_ap, blocks, ident_ap, nkt=32, a_T=None, x_sum=None, epi=None,
              store_eng="act", pre_block=None):
    xT = A.sb("xT" + uid, [P, ntile, nkt, P], BF16)
    ident = A.sb("ident" + uid, [P, P], F32)
    S.dma("sp", ident[:], ident_ap, key="ident", writes=["ident"])
    pT = [A.ps("pT%d%s" % (i, uid), [P, 4, P]) for i in range(2)]
    cnt = 0
    if a_T is not None:
        OTd, ft0 = a_T
        for t in range(ntile):
            S.dma("sp", xT[:, t, :, :], OTd[t, :, ft0:ft0 + nkt, :], key=("xTl", t % 4),
                  writes=[("xT", t, kq) for kq in range(nkt // 4)])
    else:
        xin = [A.sb("xin%d%s" % (i, uid), [P, nkt * P], F32) for i in range(2)]
        if x_sum:
            xad = A.sb("xad" + uid, [P, nkt * P], F32)
    for t in range(ntile if a_T is None else 0):
        xi = t % 2
        S.dma("sp", xin[xi][:], x_ap[t * P:(t + 1) * P, :], key=("xin", xi), writes=[("xin", xi)])
        for extra in (x_sum or ()):
            S.dma("sp", xad[:], extra[t * P:(t + 1) * P, :], key="xad", writes=["xad"])
            S.op("dve", lambda e, xi=xi: e.tensor_tensor(out=xin[xi][:], in0=xin[xi][:], in1=xad[:], op=ALU.add),
                 reads=[("xin", xi), "xad"], writes=[("xin", xi)])
        for kq in range(nkt // 4):
            b = cnt % 2
            cnt += 1
            for j in range(4):
                kt = kq * 4 + j
                S.op("pe", lambda e, b=b, j=j, xi=xi, kt=kt: e.transpose(
                    pT[b][:, j, :], xin[xi][:, kt * P:(kt + 1) * P], ident[:]),
                    reads=[("xin", xi), "ident"], writes=[("pT", b)])
            if kq % 2 == 0:
                S.op("act", lambda e, b=b, t=t, kq=kq: e.activation(
                    out=xT[:, t, kq * 4:(kq + 1) * 4, :], in_=pT[b][:], func=AF.Copy),
                    reads=[("pT", b)], writes=[("xT", t, kq)])
            else:
                S.op("dve", lambda e, b=b, t=t, kq=kq: e.tensor_copy(
                    out=xT[:, t, kq * 4:(kq + 1) * 4, :], in_=pT[b][:]),
                    reads=[("pT", b)], writes=[("xT", t, kq)])

    stg = [A.sb("stg%d%s" % (i, uid), [P, 8, 256], F32) for i in range(3)]
    wb = [A.sb("wb%d%s" % (i, uid), [P, nkt, 256], BF16) for i in range(2)]
    pz = [A.ps("pz%d%s" % (i, uid), [P, 512]) for i in range(4)]
    ob = [A.sb("ob%d%s" % (i, uid), [P, 256], F32) for i in range(4)]
    w_view = w_ap.rearrange("(kt p) c -> p kt c", p=P)
    ctr = {"stg": 0, "pz": 0}

    def load_block(bi):
        c0 = blocks[bi][0]
        if pre_block is not None:
            pre_block(S, bi)
        for c in range(nkt // 8):
            s = ctr["stg"] % 3
            ctr["stg"] += 1
            S.dma("sp", stg[s][:], w_view[:, c * 8:(c + 1) * 8, c0:c0 + 256],
                  key=("stg", s), writes=[("stg", s)])
            if c % 2 == 0:
                S.op("dve", lambda e, bi=bi, c=c, s=s: e.tensor_copy(
                    out=wb[bi % 2][:, c * 8:(c + 1) * 8, :], in_=stg[s][:]),
                    reads=[("stg", s)], writes=[("wb", bi % 2, c)])
            else:
                S.op("pool", lambda e, bi=bi, c=c, s=s: e.tensor_copy(
                    out=wb[bi % 2][:, c * 8:(c + 1) * 8, :], in_=stg[s][:]),
                    reads=[("stg", s)], writes=[("wb", bi % 2, c)])

    nb = len(blocks)
    if nb:
        load_block(0)
    for bi in range(nb):
        if bi + 1 < nb:
            load_block(bi + 1)
        _, func, out_fn = blocks[bi]
        for t in range(ntile if not getattr(Sched, 'NOMM', False) else 0):
            pb = ctr["pz"] % 4
            ctr["pz"] += 1
            for kt in range(nkt):
                S.op("pe", lambda e, pb=pb, t=t, kt=kt, bi=bi: e.matmul(
                    pz[pb][:, 0:256], lhsT=xT[:, t, kt, :], rhs=wb[bi % 2][:, kt, :],
                    start=(kt == 0), stop=(kt == nkt - 1)),
                    reads=[("xT", t, kt // 4), ("wb", bi % 2, kt // 8)], writes=[("pz", pb)])
            if epi is not None:
                epi(S, t, bi, pz[pb][:, 0:256], ("pz", pb), ob[pb], ("ob", pb))
            else:
                S.op("act", lambda e, pb=pb, func=func: e.activation(
                    out=ob[pb][:], in_=pz[pb][:, 0:256], func=func),
                    reads=[("pz", pb)], writes=[("ob", pb)])
            S.dma(store_eng, out_fn(t), ob[pb][:], key=("ob", pb), reads=[("ob", pb)], is_output=True)


def col_func(c0):
    if 8192 <= c0 < 12288 or 14336 <= c0 < 16384 or 18432 <= c0 < 20480:
        return AF.Silu
    if c0 >= 20480:
        return AF.Sigmoid
    return AF.Copy


RET_G = [1.0 - 2.0 ** (-5.0 - h) for h in range(16)]


def retention_body(S, A, zo, zp, tabs, sret_in, sret_out, sretp_out, OT, ident_ap):
    ident = A.sb("r_ident", [P, P], F32)
    identb = A.sb("r_identb", [P, P], BF16)
    S.dma("sp", ident[:], ident_ap, key="ident", writes=["ident"])
    S.op("dve", lambda e: e.tensor_copy(out=identb[:], in_=ident[:]), reads=["ident"], writes=["identb"])
    maskp = A.sb("r_maskp", [P, P], F32)
    masks = A.sb("r_masks", [P, P], F32)
    seqm = A.sb("r_seqm", [P, 16], F32)
    seqmT = A.sb("r_seqmT", [P, 16, P], F32)
    S.dma("sp", maskp[:], tabs["mask_p"], key="maskp", writes=["maskp"])
    S.dma("sp", masks[:], tabs["mask_s"], key="masks", writes=["masks"])
    S.dma("sp", seqm[:], tabs["seqm"], key="seqm", writes=["seqm"])
    S.dma("sp", seqmT[:], tabs["seqmT"], key="seqmT", writes=["seqmT"])

    St = A.sb("r_S", [P, 16, 256], F32)
    Sb = A.sb("r_Sb", [P, 16, 256], BF16)
    S.op("pool", lambda e: e.memset(St[:], 0.0), writes=["S"])
    S.op("pool", lambda e: e.memset(Sb[:], 0.0), writes=["Sb"])

    qins = [A.sb("r_qin", [P, 2048], F32)] * 2
    kins = [A.sb("r_kin", [P, 2048], F32)] * 2
    vins = [A.sb("r_vin%d" % i, [P, 4096], F32) for i in range(2)]
    gins = [A.sb("r_gin%d" % i, [P, 4096], F32) for i in range(2)]
    cur = {"i": 0}
    rt = A.sb("r_rt", [P, 2, 16, 64], F32)
    t1 = A.sb("r_t1", [P, 16, 64], F32)
    t2 = A.sb("r_t2", [P, 16, 64], F32)
    qt = A.sb("r_qt", [P, 16, 128], BF16)
    kt_ = A.sb("r_kt", [P, 16, 128], BF16)
    vb = A.sb("r_vb", [P, 16, 256], BF16)
    qT = A.sb("r_qT", [P, 16, 128], BF16)
    kT = A.sb("r_kT", [P, 16, 128], BF16)
    scs = A.sb("r_scs", [P, 16, 128], BF16)
    osb = A.sb("r_osb", [P, 16, 256], F32)
    sq = vin.rearrange("p (h e) -> p h e", h=16) if False else None
    og = A.sb("r_og", [P, 16, 256], BF16)
    oT = A.sb("r_oT", [P, 32, 128], BF16)
    st1 = A.sb("r_st1", [P, 16], F32)
    st2 = A.sb("r_st2", [P, 16], F32)
    st3 = A.sb("r_st3", [P, 16], F32)
    dtmp = A.sb("r_dtmp", [P, 2, 256], F32)
    ptr = [A.ps("r_ptr%d" % i, [P, 8, 128], BF16) for i in range(2)]
    psc = [A.ps("r_psc%d" % i, [P, 4, 128]) for i in range(2)]
    po = [A.ps("r_po%d" % i, [P, 2, 256]) for i in range(2)]
    pd = [A.ps("r_pd%d" % i, [P, 2, 256]) for i in range(2)]
    cn = {"tr": 0, "sc": 0, "o": 0, "d": 0}

    def rotary(src, dst, rt_ap, rkey, skey, dkey):
        S.dma("sp", rt[:], rt_ap, key="rt", writes=["rt"])
        sv = src[:].rearrange("p (h j two) -> p h j two", h=16, two=2)
        dv = dst[:].rearrange("p h (j two) -> p h j two", two=2)
        S.op("dve", lambda e: e.tensor_tensor(out=t1[:], in0=sv[:, :, :, 0], in1=rt[:, 0], op=ALU.mult),
             reads=[skey, "rt"], writes=["t1"])
        S.op("pool", lambda e: e.tensor_tensor(out=t2[:], in0=sv[:, :, :, 1], in1=rt[:, 1], op=ALU.mult),
             reads=[skey, "rt"], writes=["t2"])
        S.op("dve", lambda e: e.tensor_tensor(out=dv[:, :, :, 0], in0=t1[:], in1=t2[:], op=ALU.subtract),
             reads=["t1", "t2"], writes=[dkey + "0"])
        S.op("dve", lambda e: e.tensor_tensor(out=t1[:], in0=sv[:, :, :, 0], in1=rt[:, 1], op=ALU.mult),
             reads=[skey, "rt", dkey + "0"], writes=["t1"])
        S.op("pool", lambda e: e.tensor_tensor(out=t2[:], in0=sv[:, :, :, 1], in1=rt[:, 0], op=ALU.mult),
             reads=[skey, "rt", dkey + "0"], writes=["t2"])
        S.op("dve", lambda e: e.tensor_tensor(out=dv[:, :, :, 1], in0=t1[:], in1=t2[:], op=ALU.add),
             reads=["t1", "t2"], writes=[dkey + "1"])

    def transpose16(src, dst, skeys, dkey):
        for half in range(2):
            b = cn["tr"] % 2
            cn["tr"] += 1
            for j in range(8):
                h = half * 8 + j
                S.op("pe", lambda e, b=b, j=j, h=h: e.transpose(ptr[b][:, j, :], src[:, h, :], identb[:]),
                     reads=list(skeys) + ["identb"], writes=[("ptr", b)])
            S.op("act", lambda e, b=b, half=half: e.activation(
                out=dst[:, half * 8:(half + 1) * 8, :], in_=ptr[b][:], func=AF.Copy),
                reads=[("ptr", b)], writes=[(dkey, half)])

    def state_update(g, sample_head=None):
        pass

    def chunk(kind, t):
        own = kind != "pre"
        cur["n"] = cur.get("n", -1) + 1
        ci = cur["n"] % 2
        cur["i"] = ci
        qin, kin, vin, gin = qins[ci], kins[ci], vins[ci], gins[ci]
        KI, VI, QI, GI = "kin", ("vin", ci), "qin", ("gin", ci)
        z = zo if own else zp
        r0 = t * P
        kcol = 2048 if own else 0
        vcol = 4096 if own else 2048
        S.dma("sp", kin[:], z[r0:r0 + P, kcol:kcol + 2048], key=KI, writes=[KI])
        S.dma("sp", vin[:], z[r0:r0 + P, vcol:vcol + 4096], key=VI, writes=[VI])
        S.op("act", lambda e: e.activation(out=vb[:].rearrange("p h e -> p (h e)"), in_=vin[:], func=AF.Copy),
             reads=[VI], writes=["vb"])
        rotary(kin, kt_, (tabs["rk_own"] if own else tabs["rk_pre"])[t], "rk", KI, "kt")
        if own:
            S.dma("sp", qin[:], z[r0:r0 + P, 0:2048], key=QI, writes=[QI])
            S.dma("sp", gin[:], z[r0:r0 + P, 8192:12288], key=GI, writes=[GI])
            rotary(qin, qt, tabs["rq"][t], "rq", QI, "qt")
            transpose16(qt, qT, ["qt0", "qt1"], "qT")
            transpose16(kt_, kT, ["kt0", "kt1"], "kT")
        return own

    def scores_and_out(mask, sample):
        for hq in range(4):
            b = cn["sc"] % 2
            cn["sc"] += 1
            for j in range(4):
                h = hq * 4 + j
                S.op("pe", lambda e, b=b, j=j, h=h: e.matmul(psc[b][:, j, :], lhsT=kT[:, h, :], rhs=qT[:, h, :],
                                                             start=True, stop=True),
                     reads=[("kT", h // 8), ("qT", h // 8)], writes=[("psc", b)])
            S.op("dve", lambda e, b=b, hq=hq: e.tensor_tensor(
                out=scs[:, hq * 4:(hq + 1) * 4, :], in0=psc[b][:],
                in1=mask[:].unsqueeze(1).to_broadcast([P, 4, P]), op=ALU.mult),
                reads=[("psc", b), "maskp", "masks"], writes=[("scs", hq)])

    def finish_out(t):
        ci = cur["i"]
        vin, gin = vins[ci], gins[ci]
        VI, GI = ("vin", ci), ("gin", ci)
        S.op("dve", lambda e: e.tensor_reduce(out=st1[:], in_=osb[:], op=ALU.add, axis=mybir.AxisListType.X),
             reads=["osb"], writes=["st1"])
        sqv = vin[:].rearrange("p (h e) -> p h e", h=16)
        S.op("act", lambda e: e.activation(out=sqv, in_=osb[:], func=AF.Square),
             reads=["osb"], writes=[VI])
        S.op("dve", lambda e: e.tensor_reduce(out=st2[:], in_=sqv, op=ALU.add, axis=mybir.AxisListType.X),
             reads=[VI], writes=["st2"])
        S.op("dve", lambda e: e.tensor_scalar(out=st1[:], in0=st1[:], scalar1=1.0 / 256, scalar2=None, op0=ALU.mult),
             reads=["st1"], writes=["st1"])
        S.op("dve", lambda e: e.tensor_tensor(out=st3[:], in0=st1[:], in1=st1[:], op=ALU.mult),
             reads=["st1"], writes=["st3"])
        S.op("dve", lambda e: e.scalar_tensor_tensor(out=st2[:], in0=st2[:], scalar=1.0 / 256, in1=st3[:],
                                                     op0=ALU.mult, op1=ALU.subtract),
             reads=["st2", "st3"], writes=["st2"])
        S.op("dve", lambda e: e.tensor_scalar(out=st2[:], in0=st2[:], scalar1=1e-5, scalar2=None, op0=ALU.add),
             reads=["st2"], writes=["st2"])
        S.op("act", lambda e: e.activation(out=st2[:], in_=st2[:], func=AF.Sqrt), reads=["st2"], writes=["st2"])
        S.op("dve", lambda e: e.reciprocal(out=st2[:], in_=st2[:]), reads=["st2"], writes=["st2"])
        S.op("dve", lambda e: e.tensor_tensor(out=osb[:], in0=osb[:],
                                              in1=st1[:].unsqueeze(2).to_broadcast([P, 16, 256]), op=ALU.subtract),
             reads=["osb", "st1"], writes=["osb"])
        S.op("dve", lambda e: e.tensor_tensor(out=osb[:], in0=osb[:],
                                              in1=st2[:].unsqueeze(2).to_broadcast([P, 16, 256]), op=ALU.mult),
             reads=["osb", "st2"], writes=["osb"])
        S.op("dve", lambda e: e.tensor_tensor(out=og[:].rearrange("p h e -> p (h e)"),
                                              in0=osb[:].rearrange("p h e -> p (h e)"), in1=gin[:], op=ALU.mult),
             reads=["osb", GI], writes=["og"])
        ogv = og[:].rearrange("p h (two e) -> p (h two) e", two=2)
        for q4 in range(4):
            b = cn["tr"] % 2
            cn["tr"] += 1
            for j in range(8):
                ft = q4 * 8 + j
                S.op("pe", lambda e, b=b, j=j, ft=ft: e.transpose(ptr[b][:, j, :], ogv[:, ft, :], identb[:]),
                     reads=["og", "identb"], writes=[("ptr", b)])
            S.op("act", lambda e, b=b, q4=q4: e.activation(out=oT[:, q4 * 8:(q4 + 1) * 8, :], in_=ptr[b][:],
                                                           func=AF.Copy),
                 reads=[("ptr", b)], writes=[("oT", q4)])
        S.dma("act", OT[t, :, 0:32, :], oT[:], key="oT", reads=[("oT", q) for q in range(4)], is_output=True)

    def prompt_state_update():
        for hp in range(8):
            b = cn["d"] % 2
            cn["d"] += 1
            for j in range(2):
                h = hp * 2 + j
                S.op("pe", lambda e, b=b, j=j, h=h: e.matmul(pd[b][:, j, :], lhsT=kt_[:, h, :], rhs=vb[:, h, :],
                                                             start=True, stop=True),
                     reads=["kt0", "kt1", "vb"], writes=[("pd", b)])
            for j in range(2):
                h = hp * 2 + j
                g = float(RET_G[h] ** 128)
                S.op("act", lambda e, b=b, j=j, g=g: e.activation(out=dtmp[:, j, :], in_=pd[b][:, j, :],
                                                                  func=AF.Copy, scale=g),
                     reads=[("pd", b)], writes=[("dtmp", j)])
                S.op("dve", lambda e, h=h, j=j, g=g: e.scalar_tensor_tensor(
                    out=St[:, h, :], in0=St[:, h, :], scalar=g, in1=dtmp[:, j, :], op0=ALU.mult, op1=ALU.add),
                    reads=[("dtmp", j), "S"], writes=["S"])
        S.op("act", lambda e: e.activation(out=Sb[:], in_=St[:], func=AF.Copy), reads=["S"], writes=["Sb"])

    for t in range(8):
        chunk("pre", t)
        prompt_state_update()

    for t in range(8):
        chunk("own", t)
        scores_and_out(maskp, False)
        for hp in range(8):
            b = cn["o"] % 2
            cn["o"] += 1
            for j in range(2):
                h = hp * 2 + j
                S.op("pe", lambda e, b=b, j=j, h=h: e.matmul(po[b][:, j, :], lhsT=scs[:, h, :], rhs=vb[:, h, :],
                                                             start=True, stop=False),
                     reads=[("scs", h // 4), "vb"], writes=[("po", b)])
                S.op("pe", lambda e, b=b, j=j, h=h: e.matmul(po[b][:, j, :], lhsT=qT[:, h, :], rhs=Sb[:, h, :],
                                                             start=False, stop=True),
                     reads=[("qT", h // 8), "Sb"], writes=[("po", b)])
            S.op("act", lambda e, b=b, hp=hp: e.activation(out=osb[:, hp * 2:hp * 2 + 2, :], in_=po[b][:],
                                                           func=AF.Copy),
                 reads=[("po", b)], writes=["osb"])
        prompt_state_update()
        finish_out(t)
    S.dma("sp", sretp_out.rearrange("h d e -> d h e"), St[:], key="St_out", reads=["S"], is_output=True)

    t = 8
    chunk("own", t)
    scores_and_out(masks, True)
    qTm = oT[:, 0:16, :]
    ktm = oT[:, 16:32, :]
    Ss_b = [vins[i][:].rearrange("p (h e) -> p h e", h=16) for i in range(2)]
    Ss_k = [("vin", i) for i in range(2)]
    Ssb_b = [og, A.sb("r_Ssb1", [P, 16, 256], BF16)]
    Ssb_k = ["og", "Ssb1"]

    def load_state(h):
        i = h % 2
        S.dma("sp", Ss_b[i], sret_in[:, h].rearrange("s d e -> d s e"), key=("Ss", i), writes=[Ss_k[i]])
    load_state(0)
    for h in range(16):
        bi_ = h % 2
        Ss, VI, Ssb, SBK = Ss_b[bi_], Ss_k[bi_], Ssb_b[bi_], Ssb_k[bi_]
        g8 = float(RET_G[h] ** 8)
        if h + 1 < 16:
            load_state(h + 1)
        S.op("act", lambda e, Ssb=Ssb, Ss=Ss: e.activation(out=Ssb[:], in_=Ss, func=AF.Copy), reads=[VI], writes=[SBK])
        S.op("dve", lambda e, h=h: e.tensor_tensor(
            out=qTm, in0=qT[:, h, :].unsqueeze(1).to_broadcast([P, 16, P]), in1=seqmT[:], op=ALU.mult),
            reads=[("qT", h // 8), "seqmT"], writes=[("oT", 0), ("oT", 1)])
        S.op("dve", lambda e, h=h: e.tensor_tensor(
            out=ktm, in0=kt_[:, h, :].unsqueeze(1).to_broadcast([P, 16, P]),
            in1=seqm[:].unsqueeze(2).to_broadcast([P, 16, P]), op=ALU.mult),
            reads=["kt0", "kt1", "seqm"], writes=[("oT", 2), ("oT", 3)])
        b = cn["o"] % 2
        cn["o"] += 1
        S.op("pe", lambda e, b=b, h=h: e.matmul(po[b][:, 0, :], lhsT=scs[:, h, :], rhs=vb[:, h, :],
                                                start=True, stop=False),
             reads=[("scs", h // 4), "vb"], writes=[("po", b)])
        for s_ in range(16):
            S.op("pe", lambda e, b=b, s_=s_, Ssb=Ssb: e.matmul(po[b][:, 0, :], lhsT=qTm[:, s_, :],
                                                               rhs=Ssb[:, s_, :], start=False, stop=(s_ == 15)),
                 reads=[("oT", 0), ("oT", 1), SBK], writes=[("po", b)])
        S.op("act", lambda e, b=b, h=h: e.activation(out=osb[:, h, :], in_=po[b][:, 0, :], func=AF.Copy),
             reads=[("po", b)], writes=["osb"])
        for sp_ in range(8):
            b2 = cn["d"] % 2
            cn["d"] += 1
            for j in range(2):
                s_ = sp_ * 2 + j
                S.op("pe", lambda e, b2=b2, j=j, s_=s_, h=h: e.matmul(
                    pd[b2][:, j, :], lhsT=ktm[:, s_, :], rhs=vb[:, h, :], start=True, stop=True),
                    reads=[("oT", 2), ("oT", 3), "vb"], writes=[("pd", b2)])
            for j in range(2):
                s_ = sp_ * 2 + j
                S.op("act", lambda e, b2=b2, j=j, g8=g8: e.activation(out=dtmp[:, j, :], in_=pd[b2][:, j, :],
                                                                      func=AF.Copy, scale=g8),
                     reads=[("pd", b2)], writes=[("dtmp", j)])
                S.op("dve", lambda e, s_=s_, j=j, g8=g8, Ss=Ss: e.scalar_tensor_tensor(
                    out=Ss[:, s_, :], in0=Ss[:, s_, :], scalar=g8, in1=dtmp[:, j, :], op0=ALU.mult, op1=ALU.add),
                    reads=[("dtmp", j), VI, SBK], writes=[VI])
        S.dma("sp", sret_out[:, h].rearrange("s d e -> d s e"), Ss, key=("Ss_out", bi_), reads=[VI],
              is_output=True)
    finish_out(8)


def xattn_body(S, A, zo, memkv, cmk, cmv, seqmT_ap, OT, ident_ap):
    X = mybir.AxisListType.X
    scale = 512.0 ** -0.5
    ident = A.sb("x_ident", [P, P], F32)
    identb = A.sb("x_identb", [P, P], BF16)
    S.dma("sp", ident[:], ident_ap, key="ident", writes=["ident"])
    S.op("dve", lambda e: e.tensor_copy(out=identb[:], in_=ident[:]), reads=["ident"], writes=["identb"])
    seqmT = A.sb("x_seqmT", [P, 16, P], F32)
    S.dma("sp", seqmT[:], seqmT_ap, key="seqmT", writes=["seqmT"])

    qin = A.sb("x_qin", [P, 2048], F32)
    gin = A.sb("x_gin", [P, 2048], F32)
    qb = A.sb("x_qb", [P, 16, 128], BF16)
    qT = A.sb("x_qT", [P, 16, 128], BF16)
    kvin = [A.sb("x_kvin%d" % i, [P, 2, 2048], F32) for i in range(2)]
    kb = A.sb("x_kb", [P, 2, 16, 128], BF16)
    KT = A.sb("x_KT", [P, 16, 256], BF16)
    Vb = A.sb("x_Vb", [P, 2, 2048], BF16)
    pb = A.sb("x_pb", [P, 4, 256], BF16)
    pT = A.sb("x_pT", [P, 4, 2, 128], BF16)
    pTm = A.sb("x_pTm", [P, 16, 128], BF16)
    qTm = A.sb("x_qTm", [P, 16, 128], BF16)
    ob = A.sb("x_ob", [P, 16, 128], BF16)
    oT = A.sb("x_oT", [P, 16, 128], BF16)
    mx = A.sb("x_mx", [P, 4], F32)
    sm = A.sb("x_sm", [P, 4], F32)
    ptr = [A.ps("x_ptr%d" % i, [P, 8, 128], BF16) for i in range(2)]
    pso = [A.ps("x_pso%d" % i, [P, 512]) for i in range(4)]
    cn = {"tr": 0, "kv": 0}

    def tr_group(srcs, dst_ap, skeys, dkey):
        b = cn["tr"] % 2
        cn["tr"] += 1
        for j, src in enumerate(srcs):
            S.op("pe", lambda e, b=b, j=j, src=src: e.transpose(ptr[b][:, j, :], src, identb[:]),
                 reads=list(skeys) + ["identb"], writes=[("ptr", b)])
        n = len(srcs)
        S.op("act", lambda e, b=b, n=n: e.activation(out=dst_ap, in_=ptr[b][:, 0:n, :], func=AF.Copy),
             reads=[("ptr", b)], writes=[dkey])

    def load_q(t):
        r0 = t * P
        S.dma("sp", qin[:], zo[r0:r0 + P, 16384:18432], key="qin", writes=["qin"])
        S.dma("sp", gin[:], zo[r0:r0 + P, 18432:20480], key="gin", writes=["gin"])
        S.op("dve", lambda e: e.tensor_copy(out=qb[:].rearrange("p a b -> p (a b)"), in_=qin[:]),
             reads=["qin"], writes=["qb"])
        for half in range(2):
            tr_group([qb[:, half * 8 + j, :] for j in range(8)], qT[:, half * 8:(half + 1) * 8, :],
                     ["qb"], ("qT", half))

    def load_kv(src_ap, which):
        i = cn["kv"] % 2
        cn["kv"] += 1
        S.dma("sp", kvin[i][:], src_ap.rearrange("(mt p) c -> p mt c", p=P), key=("kvin", i),
              writes=[("kvin", i)])
        return i

    def make_KT(i):
        S.op("pool", lambda e: e.tensor_copy(out=kb[:].rearrange("p m a b -> p m (a b)"), in_=kvin[i][:]),
             reads=[("kvin", i)], writes=["kb"])
        for mt in range(2):
            for half in range(2):
                tr_group([kb[:, mt, half * 8 + j, :] for j in range(8)],
                         KT[:, half * 8:(half + 1) * 8, mt * P:(mt + 1) * P], ["kb"], ("KT", mt, half))

    def make_V(i):
        S.op("pool", lambda e: e.tensor_copy(out=Vb[:], in_=kvin[i][:]), reads=[("kvin", i)], writes=["Vb"])

    KT_keys = [("KT", mt, half) for mt in range(2) for half in range(2)]

    def sc_loc(h, sample):
        return pso[h][:, 0:256], ("pso", h)

    def score_mm(lhs, lkeys, first, last, sample=False):
        for h in range(4):
            for dt in range(4):
                k = h * 4 + dt
                loc, lk = sc_loc(h, sample)
                S.op("pe", lambda e, loc=loc, k=k, dt=dt: e.matmul(
                    loc, lhsT=lhs[:, k, :], rhs=KT[:, k, :],
                    start=(first and dt == 0), stop=(last and dt == 3)),
                    reads=list(lkeys) + KT_keys, writes=[lk])

    def softmax(sample=False):
        for h in range(4):
            sv, lk = sc_loc(h, sample)
            S.op("dve", lambda e, h=h, sv=sv: e.tensor_reduce(out=mx[:, h:h + 1], in_=sv, op=ALU.max, axis=X),
                 reads=[lk], writes=[("mx", h)])
            S.op("dve", lambda e, h=h: e.tensor_scalar(out=mx[:, h:h + 1], in0=mx[:, h:h + 1], scalar1=-scale,
                                                       scalar2=None, op0=ALU.mult),
                 reads=[("mx", h)], writes=[("mx", h)])
            S.op("act", lambda e, h=h, sv=sv: e.activation(out=pb[:, h, :], in_=sv, func=AF.Exp,
                                                           bias=mx[:, h:h + 1], scale=scale,
                                                           accum_out=sm[:, h:h + 1]),
                 reads=[lk, ("mx", h)], writes=[("pb", h), ("sm", h)])
            S.op("dve", lambda e, h=h: e.reciprocal(out=sm[:, h:h + 1], in_=sm[:, h:h + 1]),
                 reads=[("sm", h)], writes=[("sm", h)])
        tr_group([pb[:, h, mt * P:(mt + 1) * P] for h in range(4) for mt in range(2)],
                 pT[:].rearrange("p h m l -> p (h m) l"), [("pb", h) for h in range(4)], "pT")

    def finish(t):
        for h in range(4):
            S.op("dve", lambda e, h=h: e.scalar_tensor_tensor(
                out=ob[:, h * 4:(h + 1) * 4, :].rearrange("p a b -> p (a b)"), in0=pso[h][:],
                scalar=sm[:, h:h + 1], in1=gin[:, h * 512:(h + 1) * 512], op0=ALU.mult, op1=ALU.mult),
                reads=[("pso", h), ("sm", h), "gin"], writes=[("ob", h)])
        for half in range(2):
            tr_group([ob[:, half * 8 + j, :] for j in range(8)], oT[:, half * 8:(half + 1) * 8, :],
                     [("ob", h) for h in range(4)], ("oT", half))
        S.dma("act", OT[t, :, 48:64, :], oT[:], key="oT", reads=[("oT", 0), ("oT", 1)], is_output=True)

    i = load_kv(memkv[:, 0:2048], "k")
    make_KT(i)
    i = load_kv(memkv[:, 2048:4096], "v")
    make_V(i)
    for t in range(8):
        load_q(t)
        score_mm(qT, [("qT", 0), ("qT", 1)], True, True)
        softmax()
        for h in range(4):
            for mt in range(2):
                S.op("pe", lambda e, h=h, mt=mt: e.matmul(pso[h][:], lhsT=pT[:, h, mt, :],
                                                          rhs=Vb[:, mt, h * 512:(h + 1) * 512],
                                                          start=(mt == 0), stop=(mt == 1)),
                     reads=["pT", "Vb"], writes=[("pso", h)])
        finish(t)

    load_q(8)
    for s_ in range(16):
        i = load_kv(cmk[s_], "k")
        make_KT(i)
        S.op("dve", lambda e, s_=s_: e.tensor_tensor(
            out=qTm[:], in0=qT[:], in1=seqmT[:, s_, :].unsqueeze(1).to_broadcast([P, 16, P]), op=ALU.mult),
            reads=[("qT", 0), ("qT", 1), "seqmT"], writes=["qTm"])
        score_mm(qTm, ["qTm"], s_ == 0, s_ == 15, sample=True)
    softmax(sample=True)
    for s_ in range(16):
        i = load_kv(cmv[s_], "v")
        make_V(i)
        for h in range(4):
            S.op("dve", lambda e, h=h, s_=s_: e.tensor_tensor(
                out=pTm[:, h * 2:(h + 1) * 2, :], in0=pT[:, h, :, :],
                in1=seqmT[:, s_, :].unsqueeze(1).to_broadcast([P, 2, P]), op=ALU.mult),
                reads=["pT", "seqmT"], writes=[("pTm", h)])
            for mt in range(2):
                S.op("pe", lambda e, h=h, mt=mt, s_=s_: e.matmul(
                    pso[h][:], lhsT=pTm[:, h * 2 + mt, :], rhs=Vb[:, mt, h * 512:(h + 1) * 512],
                    start=(s_ == 0 and mt == 0), stop=(s_ == 15 and mt == 1)),
                    reads=[("pTm", h), "Vb"], writes=[("pso", h)])
    finish(8)


def s5prep_body(S, A, prm, ident_ap, maskM_ap, SM, SG, SE, A8S):
    PI = math.pi
    ident = A.sb("q_ident", [P, P], F32)
    identb = A.sb("q_identb", [P, P], BF16)
    S.dma("sp", ident[:], ident_ap, key="ident", writes=["ident"])
    S.op("dve", lambda e: e.tensor_copy(out=identb[:], in_=ident[:]), reads=["ident"], writes=["identb"])
    maskM = A.sb("q_maskM", [P, P], F32)
    S.dma("sp", maskM[:], maskM_ap, key="maskM", writes=["maskM"])
    H = 64
    uid = [0]

    def tl(shape, dt=F32):
        uid[0] += 1
        return A.sb("q_t%d" % uid[0], shape, dt)

    def dve(fn, reads, writes):
        S.op("dve", fn, reads=reads, writes=writes)

    def tt(out, a, b, op, okey, akey, bkey):
        dve(lambda e: e.tensor_tensor(out=out, in0=a, in1=b, op=op), [akey, bkey], [okey])

    araw = tl([P, 2, 64])
    S.dma("sp", araw[:, 0, :], prm["a_re"], key="araw0", writes=["araw"])
    S.dma("sp", araw[:, 1, :], prm["a_im"], key="araw1", writes=["araw"])
    pA = A.ps("q_pA", [P, 4, P])
    pA2 = A.ps("q_pA2", [P, 4, P])
    ar = tl([H, P]); ai = tl([H, P])
    for j in range(2):
        S.op("pe", lambda e, j=j: e.transpose(pA[0:H, j, :], araw[:, j, :], ident[:]), reads=["araw", "ident"],
             writes=["pA"])
    dve(lambda e: e.tensor_copy(out=ar[:], in_=pA[0:H, 0, :]), ["pA"], ["ar"])
    dve(lambda e: e.tensor_copy(out=ai[:], in_=pA[0:H, 1, :]), ["pA"], ["ai"])
    dtb = tl([H, P])
    S.dma("sp", dtb[:], prm["log_step"].to_broadcast([H, P]), key="dtb", writes=["dtb"])
    S.op("act", lambda e: e.activation(out=dtb[:], in_=dtb[:], func=AF.Exp), reads=["dtb"], writes=["dtb"])
    dtar = tl([H, P]); dtai = tl([H, P]); mag = tl([H, P])
    tt(dtar[:], dtb[:], ar[:], ALU.mult, "dtar", "dtb", "ar")
    tt(dtai[:], dtb[:], ai[:], ALU.mult, "dtai", "dtb", "ai")
    kq = tl([H, P]); ki = tl([H, P], mybir.dt.int32); rr = tl([H, P])
    dve(lambda e: e.tensor_scalar(out=kq[:], in0=dtai[:], scalar1=1.0 / (2 * PI), scalar2=None, op0=ALU.mult),
        ["dtai"], ["kq"])
    dve(lambda e: e.tensor_copy(out=ki[:], in_=kq[:]), ["kq"], ["ki"])
    dve(lambda e: e.tensor_copy(out=kq[:], in_=ki[:]), ["ki"], ["kq"])
    dve(lambda e: e.scalar_tensor_tensor(out=rr[:], in0=kq[:], scalar=-2 * PI, in1=dtai[:], op0=ALU.mult,
                                         op1=ALU.add), ["kq", "dtai"], ["rr"])
    rs = tl([H, P]); rc = tl([H, P]); sn = tl([H, P]); cs = tl([H, P])
    msk = tl([H, P])
    for t_, k_, sh in ((rs, "rs", 0.0), (rc, "rc", PI / 2)):
        dve(lambda e, t_=t_, sh=sh: e.tensor_scalar(out=t_[:], in0=rr[:], scalar1=sh, scalar2=None, op0=ALU.add),
            ["rr"], [k_])
        dve(lambda e, t_=t_: e.tensor_scalar(out=msk[:], in0=t_[:], scalar1=PI, scalar2=None, op0=ALU.is_gt),
            [k_], ["msk"])
        dve(lambda e, t_=t_: e.scalar_tensor_tensor(out=t_[:], in0=msk[:], scalar=-2 * PI, in1=t_[:],
                                                    op0=ALU.mult, op1=ALU.add), ["msk", k_], [k_])
        dve(lambda e, t_=t_: e.tensor_scalar(out=msk[:], in0=t_[:], scalar1=-PI, scalar2=None, op0=ALU.is_lt),
            [k_], ["msk"])
        dve(lambda e, t_=t_: e.scalar_tensor_tensor(out=t_[:], in0=msk[:], scalar=2 * PI, in1=t_[:],
                                                    op0=ALU.mult, op1=ALU.add), ["msk", k_], [k_])
    for t_, k_ in ((rs, "rs"), (rc, "rc")):
        dve(lambda e, t_=t_: e.tensor_scalar(out=t_[:], in0=t_[:], scalar1=3.1415925, scalar2=-3.1415925,
                                             op0=ALU.min, op1=ALU.max), [k_], [k_])
    hh = tl([H, P]); x2 = tl([H, P]); sh_ = tl([H, P]); ch_ = tl([H, P])
    dve(lambda e: e.tensor_scalar(out=hh[:], in0=rs[:], scalar1=0.5, scalar2=None, op0=ALU.mult), ["rs"], ["hh"])
    tt(x2[:], hh[:], hh[:], ALU.mult, "x2", "hh", "hh")
    sc_ = [(-1.0) ** k / math.factorial(2 * k + 1) for k in range(9)]
    cc_ = [(-1.0) ** k / math.factorial(2 * k) for k in range(9)]

    def horner(dst, dkey, co):
        dve(lambda e: e.tensor_scalar(out=dst[:], in0=x2[:], scalar1=co[-1], scalar2=None, op0=ALU.mult),
            ["x2"], [dkey])
        for c_ in co[-2:0:-1]:
            dve(lambda e, c_=c_: e.scalar_tensor_tensor(out=dst[:], in0=dst[:], scalar=c_, in1=x2[:],
                                                        op0=ALU.add, op1=ALU.mult), [dkey, "x2"], [dkey])
        dve(lambda e: e.tensor_scalar(out=dst[:], in0=dst[:], scalar1=co[0], scalar2=None, op0=ALU.add),
            [dkey], [dkey])
    horner(sh_, "sh", sc_)
    tt(sh_[:], sh_[:], hh[:], ALU.mult, "sh", "sh", "hh")
    horner(ch_, "ch", cc_)
    dve(lambda e: e.scalar_tensor_tensor(out=sn[:], in0=sh_[:], scalar=2.0, in1=ch_[:], op0=ALU.mult,
                                         op1=ALU.mult), ["sh", "ch"], ["sn"])
    tt(cs[:], sh_[:], sh_[:], ALU.mult, "cs", "sh", "sh")
    dve(lambda e: e.tensor_scalar(out=cs[:], in0=cs[:], scalar1=-2.0, scalar2=1.0, op0=ALU.mult, op1=ALU.add),
        ["cs"], ["cs"])
    ec_ = [1.0 / math.factorial(k) for k in range(9)]
    dve(lambda e: e.tensor_scalar(out=mag[:], in0=dtar[:], scalar1=ec_[-1], scalar2=None, op0=ALU.mult),
        ["dtar"], ["mag"])
    for c_ in ec_[-2:0:-1]:
        dve(lambda e, c_=c_: e.scalar_tensor_tensor(out=mag[:], in0=mag[:], scalar=c_, in1=dtar[:],
                                                    op0=ALU.add, op1=ALU.mult), ["mag", "dtar"], ["mag"])
    dve(lambda e: e.tensor_scalar(out=mag[:], in0=mag[:], scalar1=1.0, scalar2=None, op0=ALU.add),
        ["mag"], ["mag"])
    PW = tl([H, 16, 2, P])
    tmp = [tl([H, P]) for _ in range(4)]
    cm = [0]

    def cmul(ore, oim, xr, xi, yr, yi, okeys, ikeys):
        cm[0] += 1
        k = ["cm%d_%d" % (cm[0], i) for i in range(4)]
        dve(lambda e: e.tensor_tensor(out=tmp[0][:], in0=xr, in1=yr, op=ALU.mult), ikeys, ["tmp0"])
        dve(lambda e: e.tensor_tensor(out=tmp[1][:], in0=xi, in1=yi, op=ALU.mult), ikeys, ["tmp1"])
        dve(lambda e: e.tensor_tensor(out=tmp[2][:], in0=xr, in1=yi, op=ALU.mult), ikeys, ["tmp2"])
        dve(lambda e: e.tensor_tensor(out=tmp[3][:], in0=xi, in1=yr, op=ALU.mult), ikeys, ["tmp3"])
        dve(lambda e: e.tensor_tensor(out=ore, in0=tmp[0][:], in1=tmp[1][:], op=ALU.subtract),
            ["tmp0", "tmp1"], [okeys[0]])
        dve(lambda e: e.tensor_tensor(out=oim, in0=tmp[2][:], in1=tmp[3][:], op=ALU.add),
            ["tmp2", "tmp3"], [okeys[1]])

    def pw(e_, c):
        return PW[:, e_ + 7, c, :]

    def pk(e_):
        return ["pw%d_0" % e_, "pw%d_1" % e_]
    S.op("pool", lambda e: e.memset(pw(0, 0), 1.0), writes=["pw0_0"])
    S.op("pool", lambda e: e.memset(pw(0, 1), 0.0), writes=["pw0_1"])
    tt(pw(1, 0), mag[:], cs[:], ALU.mult, "pw1_0", "mag", "cs")
    tt(pw(1, 1), mag[:], sn[:], ALU.mult, "pw1_1", "mag", "sn")
    for e_ in range(2, 9):
        cmul(pw(e_, 0), pw(e_, 1), pw(e_ - 1, 0), pw(e_ - 1, 1), pw(1, 0), pw(1, 1), pk(e_), pk(e_ - 1) + pk(1))
    im2 = tl([H, P])
    tt(im2[:], mag[:], mag[:], ALU.mult, "im2", "mag", "mag")
    dve(lambda e: e.reciprocal(out=im2[:], in_=im2[:]), ["im2"], ["im2"])
    tt(pw(-1, 0), pw(1, 0), im2[:], ALU.mult, "pw-1_0", "pw1_0", "im2")
    dve(lambda e: e.scalar_tensor_tensor(out=pw(-1, 1), in0=pw(1, 1), scalar=-1.0, in1=im2[:], op0=ALU.mult,
                                         op1=ALU.mult), ["pw1_1", "im2"], ["pw-1_1"])
    for e_ in range(2, 8):
        cmul(pw(-e_, 0), pw(-e_, 1), pw(-e_ + 1, 0), pw(-e_ + 1, 1), pw(-1, 0), pw(-1, 1),
             pk(-e_), pk(-e_ + 1) + pk(-1))
    allpw = [k for e_ in range(-7, 9) for k in pk(e_)]
    S.dma("sp", A8S, PW[:, 15, :, :], key="a8s", reads=pk(8), is_output=True)
    den = tl([H, P]); xr_ = tl([H, P]); fre = tl([H, P]); fim = tl([H, P])
    tt(den[:], ar[:], ar[:], ALU.mult, "den", "ar", "ar")
    tt(tmp[0][:], ai[:], ai[:], ALU.mult, "tmp0", "ai", "ai")
    tt(den[:], den[:], tmp[0][:], ALU.add, "den", "den", "tmp0")
    dve(lambda e: e.reciprocal(out=den[:], in_=den[:]), ["den"], ["den"])
    dve(lambda e: e.tensor_scalar(out=xr_[:], in0=pw(1, 0), scalar1=-1.0, scalar2=None, op0=ALU.add),
        ["pw1_0"], ["xr"])
    tt(tmp[0][:], xr_[:], ar[:], ALU.mult, "tmp0", "xr", "ar")
    tt(tmp[1][:], pw(1, 1), ai[:], ALU.mult, "tmp1", "pw1_1", "ai")
    tt(fre[:], tmp[0][:], tmp[1][:], ALU.add, "fre", "tmp0", "tmp1")
    tt(fre[:], fre[:], den[:], ALU.mult, "fre", "fre", "den")
    tt(tmp[2][:], pw(1, 1), ar[:], ALU.mult, "tmp2", "pw1_1", "ar")
    tt(tmp[3][:], xr_[:], ai[:], ALU.mult, "tmp3", "xr", "ai")
    tt(fim[:], tmp[2][:], tmp[3][:], ALU.subtract, "fim", "tmp2", "tmp3")
    tt(fim[:], fim[:], den[:], ALU.mult, "fim", "fim", "den")
    Pst = tl([P, P, 8]); Qst = tl([P, P, 8])
    for s_ in range(8):
        cmul(Pst[0:H, :, s_], Qst[H:P, :, s_], pw(7 - s_, 0), pw(7 - s_, 1), fre[:], fim[:],
             ["Pst_lo", "Qst_hi"], pk(7 - s_) + ["fre", "fim"])
    S.op("act", lambda e: e.activation(out=Pst[H:P, :, :], in_=Pst[0:H, :, :], func=AF.Copy),
         reads=["Pst_lo"], writes=["Pst_hi"])
    S.op("act", lambda e: e.activation(out=Qst[0:H, :, :], in_=Qst[H:P, :, :], func=AF.Copy, scale=-1.0),
         reads=["Qst_hi"], writes=["Qst_lo"])
    Pv = tl([P, 16, P]); Qv = tl([P, 16, P])
    S.op("act", lambda e: e.activation(out=Pv[0:H], in_=PW[:, :, 0, :], func=AF.Copy), reads=allpw, writes=["Pv_lo"])
    S.op("act", lambda e: e.activation(out=Pv[H:P], in_=PW[:, :, 0, :], func=AF.Copy), reads=allpw, writes=["Pv_hi"])
    S.op("act", lambda e: e.activation(out=Qv[0:H], in_=PW[:, :, 1, :], func=AF.Copy, scale=-1.0), reads=allpw,
         writes=["Qv_lo"])
    S.op("act", lambda e: e.activation(out=Qv[H:P], in_=PW[:, :, 1, :], func=AF.Copy, scale=-1.0), reads=allpw,
         writes=["Qv_hi"])
    t1 = tl([P, 32, 128]); t2 = tl([P, 32, 128])
    raw = t1[:].rearrange("p a b -> p (a b)")
    R = tl([P, P, 16]); Sx = tl([P, P, 16]); Rp = tl([P, P, 16]); Sp = tl([P, P, 16])
    srcs = (("b_re", 0), ("b_im", 1), ("c_re", 2), ("c_im", 3))
    for nm, idx in srcs:
        S.dma("sp", raw[:, idx * 1024:(idx + 1) * 1024], prm[nm], key="raw%d" % idx, writes=["raw%d" % idx])
    cnt = [0]
    for nm, idx in srcs:
        rv = raw[:, idx * 1024:(idx + 1) * 1024]
        for q4 in range(4):
            pb_ = pA if cnt[0] % 2 == 0 else pA2
            pkey = "pA" if cnt[0] % 2 == 0 else "pA2"
            cnt[0] += 1
            for j in range(4):
                qq = q4 * 4 + j
                if idx < 2:
                    src = rv.rearrange("p (n q) -> p q n", q=16)[:, qq, :]
                else:
                    src = rv[:, qq * 64:(qq + 1) * 64]
                S.op("pe", lambda e, pb_=pb_, j=j, src=src: e.transpose(pb_[0:H, j, :], src, ident[:]),
                     reads=["raw%d" % idx, "ident"], writes=[pkey])
            qs = slice(q4 * 4, q4 * 4 + 4)
            pin = pb_[0:H, :, :]

            def outv(tile_, lo):
                v = tile_[0:H] if lo else tile_[H:P]
                return v.rearrange("p g q -> p q g")[:, qs, :]
            if idx == 0:
                dsts = ((R, True, 1.0), (Sx, False, 1.0))
            elif idx == 1:
                dsts = ((R, False, 1.0), (Sx, True, 1.0))
            elif idx == 2:
                dsts = ((Rp, True, 1.0), (Sp, False, 1.0))
            else:
                dsts = ((Rp, False, -1.0), (Sp, True, 1.0))
            for (tile_, lo, sc) in dsts:
                ov = outv(tile_, lo)
                S.op("act", lambda e, ov=ov, pin=pin, sc=sc: e.activation(out=ov, in_=pin, func=AF.Copy, scale=sc),
                     reads=[pkey], writes=["tab%d_%d_%d" % (id(tile_) % 997, lo, q4)])
    tabkeys = None
    X7c = tl([P, 32, 128], BF16); Ypc = tl([P, 32, 128], BF16); Ec = tl([P, 32, 128], BF16)
    Mc = tl([P, 32, 128], BF16); Gc = tl([P, 32, 128], BF16)
    pM = [A.ps("q_pM%d" % i, [P, 4, P]) for i in range(2)]
    pG = [A.ps("q_pG%d" % i, [P, 8, P], BF16) for i in range(2)]
    anytab = [k for k in S.last_w.keys() if isinstance(k, str) and k.startswith("tab")]
    for ch in range(4):
        gs = slice(ch * 32, ch * 32 + 32)
        t1v = t1[:].rearrange("p g (s q) -> p g s q", q=16)
        t2v = t2[:].rearrange("p g (s q) -> p g s q", q=16)

        def build(dst, dkey, Pt, Qt, Rt, St, pkeys):
            dve(lambda e: e.tensor_tensor(out=t1v, in0=Pt.unsqueeze(3).to_broadcast([P, 32, 8, 16]),
                                          in1=Rt.unsqueeze(2).to_broadcast([P, 32, 8, 16]), op=ALU.mult),
                pkeys + anytab + ["raw0", "raw1", "raw2", "raw3"], ["t1"])
            S.op("pool", lambda e: e.tensor_tensor(out=t2v, in0=Qt.unsqueeze(3).to_broadcast([P, 32, 8, 16]),
                                                   in1=St.unsqueeze(2).to_broadcast([P, 32, 8, 16]), op=ALU.mult),
                 reads=pkeys + anytab, writes=["t2"])
            dve(lambda e: e.tensor_tensor(out=dst[:], in0=t1[:], in1=t2[:], op=ALU.add), ["t1", "t2"], [dkey])
        build(X7c, "X7c", Pst[:, gs, :], Qst[:, gs, :], R[:, gs, :], Sx[:, gs, :],
              ["Pst_lo", "Pst_hi", "Qst_lo", "Qst_hi"])
        pvk = ["Pv_lo", "Pv_hi", "Qv_lo", "Qv_hi"]
        build(Ypc, "Ypc", Pv[:, 0:8, gs].rearrange("p e g -> p g e"), Qv[:, 0:8, gs].rearrange("p e g -> p g e"),
              Rp[:, gs, :], Sp[:, gs, :], pvk)
        build(Ec, "Ec", Pv[:, 8:16, gs].rearrange("p e g -> p g e"), Qv[:, 8:16, gs].rearrange("p e g -> p g e"),
              Rp[:, gs, :], Sp[:, gs, :], pvk)
        for g4 in range(8):
            b = g4 % 2
            for j in range(4):
                g = g4 * 4 + j
                S.op("pe", lambda e, b=b, j=j, g=g: e.matmul(pM[b][:, j, :], lhsT=X7c[:, g, :], rhs=Ypc[:, g, :],
                                                             start=True, stop=True),
                     reads=["X7c", "Ypc"], writes=[("pM", b)])
            dve(lambda e, b=b, g4=g4: e.tensor_tensor(
                out=Mc[:, g4 * 4:(g4 + 1) * 4, :], in0=pM[b][:],
                in1=maskM[:].unsqueeze(1).to_broadcast([P, 4, P]), op=ALU.mult),
                [("pM", b), "maskM"], [("Mc", g4)])
        for g8 in range(4):
            b = g8 % 2
            for j in range(8):
                g = g8 * 8 + j
                S.op("pe", lambda e, b=b, j=j, g=g: e.transpose(pG[b][:, j, :], X7c[:, g, :], identb[:]),
                     reads=["X7c", "identb"], writes=[("pG", b)])
            S.op("act", lambda e, b=b, g8=g8: e.activation(out=Gc[:, g8 * 8:(g8 + 1) * 8, :], in_=pG[b][:],
                                                           func=AF.Copy),
                 reads=[("pG", b)], writes=[("Gc", g8)])
        S.dma("sp", SM[:, gs, :], Mc[:], key="SMst", reads=[("Mc", i) for i in range(8)], is_output=True)
        S.dma("sp", SG[:, gs, :], Gc[:], key="SGst", reads=[("Gc", i) for i in range(4)], is_output=True)
        S.dma("sp", SE[:, gs, :], Ec[:], key="SEst", reads=["Ec"], is_output=True)


def s5main_body(S, A, zo, zp, SM, SG, SE, A8S, selm_ap, seqm_ap, ident_ap, s5in, YS, s5p_out, s5s_out):
    H = 64
    ident = A.sb("m_ident", [P, P], F32)
    S.dma("sp", ident[:], ident_ap, key="ident", writes=["ident"])
    selm = A.sb("m_selm", [P, 8], F32)
    S.dma("sp", selm[:], selm_ap, key="selm", writes=["selm"])
    bsf = A.sb("m_bsf", [P, 16], F32)
    bsel = A.sb("m_bsel", [P, 16], BF16)
    S.dma("sp", bsf[:], seqm_ap, key="bsf", writes=["bsf"])
    S.op("dve", lambda e: e.tensor_copy(out=bsel[:], in_=bsf[:]), reads=["bsf"], writes=["bsel"])
    a8 = A.sb("m_a8", [H, 2, P], F32)
    S.dma("sp", a8[:], A8S, key="a8", writes=["a8"])
    AA = A.sb("m_AA", [H, 2, P], F32)
    AB = A.sb("m_AB", [H, 2, P], F32)
    S.op("dve", lambda e: e.tensor_copy(out=AA[:, 0, :], in_=a8[:, 0, :]), reads=["a8"], writes=["AA0"])
    S.op("dve", lambda e: e.tensor_copy(out=AA[:, 1, :], in_=a8[:, 0, :]), reads=["a8"], writes=["AA1"])
    S.op("dve", lambda e: e.tensor_scalar(out=AB[:, 0, :], in0=a8[:, 1, :], scalar1=-1.0, scalar2=None,
                                          op0=ALU.mult), reads=["a8"], writes=["AB0"])
    S.op("dve", lambda e: e.tensor_copy(out=AB[:, 1, :], in_=a8[:, 1, :]), reads=["a8"], writes=["AB1"])
    AK = ["AA0", "AA1", "AB0", "AB1"]
    Gm = A.sb("m_G", [P, P, P], BF16)
    for ch in range(4):
        S.dma("sp", Gm[:, ch * 32:(ch + 1) * 32, :], SG[:, ch * 32:(ch + 1) * 32, :], key=("Gl", ch),
              writes=[("G", ch)])
    MEc = [A.sb("m_ME%d" % i, [P, 32, 2, P], BF16) for i in range(2)]
    uin = [A.sb("m_uin%d" % i, [P, 2048], F32) for i in range(2)]
    urep = A.sb("m_urep", [P, 32, 128], BF16)
    Uts = [A.sb("m_Ut%d" % i, [P, P, 16], BF16) for i in range(2)]
    VH = A.sb("m_VH", [H, 2, P, 17], F32)
    Hbfs = [A.sb("m_Hbf%d" % i, [P, P, 16], BF16) for i in range(2)]
    ysbs = [A.sb("m_ysb%d" % i, [16, 32, 128], F32) for i in range(2)]
    P1 = A.sb("m_P1", [H, 2, P], F32)
    P2 = A.sb("m_P2", [H, 2, P], F32)
    Vs = A.sb("m_Vs", [H, 2, P, 16], F32)
    H0s = A.sb("m_H0s", [H, 2, P, 16], F32)
    psU = [A.ps("m_psU%d" % i, [P, 32, 16]) for i in range(2)]
    psV = [A.ps("m_psV%d" % i, [P, 32, 16]) for i in range(2)]
    psY = [A.ps("m_psY%d" % i, [P, 4, P]) for i in range(2)]
    ptr = A.ps("m_ptr", [P, 4, P])
    S.op("pool", lambda e: e.memset(VH[:], 0.0), writes=["VH"])
    cn = {"u": 0, "U": 0, "V": 0, "Y": 0, "me": 0, "ys": 0}

    def make_U(src_ap, ub):
        Ut = Uts[ub]
        i = cn["u"] % 2
        cn["u"] += 1
        S.dma("sp", uin[i][:], src_ap, key=("uin", i), writes=[("uin", i)])
        for ch in range(4):
            uv = uin[i][:, ch * 512:(ch + 1) * 512].rearrange("p (g q) -> p g q", q=16)
            S.op("dve", lambda e, uv=uv: e.tensor_tensor(
                out=urep[:].rearrange("p g (s q) -> p g s q", q=16),
                in0=uv.unsqueeze(2).to_broadcast([P, 32, 8, 16]),
                in1=selm[:].unsqueeze(1).unsqueeze(3).to_broadcast([P, 32, 8, 16]), op=ALU.mult),
                reads=[("uin", i), "selm"], writes=["urep"])
            b = cn["U"] % 2
            cn["U"] += 1
            for j in range(32):
                S.op("pe", lambda e, b=b, j=j: e.matmul(psU[b][:, j, :], lhsT=urep[:, j, :], rhs=bsel[:],
                                                        start=True, stop=True),
                     reads=["urep", "bsel"], writes=[("psU", b)])
            S.op("act", lambda e, b=b, ch=ch: e.activation(out=Ut[:, ch * 32:(ch + 1) * 32, :], in_=psU[b][:],
                                                           func=AF.Copy),
                 reads=[("psU", b)], writes=[("Ut", ub, ch)])

    def make_V(dst_fn, dkey, ub):
        Ut = Uts[ub]
        for ch in range(4):
            b = cn["V"] % 2
            cn["V"] += 1
            for j in range(32):
                g = ch * 32 + j
                S.op("pe", lambda e, b=b, j=j, g=g: e.matmul(psV[b][:, j, :], lhsT=Gm[:, g, :], rhs=Ut[:, g, :],
                                                             start=True, stop=True),
                     reads=[("G", ch), ("Ut", ub, ch)], writes=[("psV", b)])
            for c in range(2):
                S.op("act", lambda e, b=b, c=c, ch=ch: e.activation(
                    out=dst_fn(c, ch), in_=psV[b][c * H:(c + 1) * H, :, :], func=AF.Copy),
                    reads=[("psV", b)], writes=[dkey])

    def cstep(Hj0, Hj1, Hj, Hn, key, w, extra=(), tk=("P1", "P2a", "P2b")):
        p1, p2 = w
        S.op("dve", lambda e: e.tensor_tensor(out=p1, in0=AA_v(Hj), in1=Hj, op=ALU.mult),
             reads=[key] + AK + list(extra), writes=[tk[0]])
        S.op("dve", lambda e: e.tensor_tensor(out=sub(p2, 0), in0=AB_v(Hj, 0), in1=Hj1, op=ALU.mult),
             reads=[key] + AK + list(extra), writes=[tk[1]])
        S.op("dve", lambda e: e.tensor_tensor(out=sub(p2, 1), in0=AB_v(Hj, 1), in1=Hj0, op=ALU.mult),
             reads=[key] + AK + list(extra), writes=[tk[2]])
        S.op("dve", lambda e: e.tensor_tensor(out=Hn, in0=Hn, in1=p1, op=ALU.add), reads=[key, tk[0]], writes=[key])
        S.op("dve", lambda e: e.tensor_tensor(out=Hn, in0=Hn, in1=p2, op=ALU.add),
             reads=[key, tk[1], tk[2]], writes=[key])

    def sub(ap, c):
        return ap[:, c]

    def AA_v(like):
        if len(like.shape) == 3:
            return AA[:]
        return AA[:].unsqueeze(3).to_broadcast([H, 2, P, like.shape[-1]])

    def AB_v(like, c):
        if len(like.shape) == 3:
            return AB[:, c, :]
        return AB[:, c, :].unsqueeze(2).to_broadcast([H, P, like.shape[-1]])

    def make_Hbf(src, skey, hb):
        Hbf = Hbfs[hb]
        S.op("act", lambda e: e.activation(out=Hbf[0:H], in_=src[:, 0, :, 0:16], func=AF.Copy),
             reads=[skey], writes=[("Hbf0", hb)])
        S.op("pool", lambda e: e.tensor_copy(out=Hbf[H:P], in_=src[:, 1, :, 0:16]),
             reads=[skey], writes=[("Hbf1", hb)])

    def make_Y(t, ub, hb):
        Ut = Uts[ub]
        Hbf = Hbfs[hb]
        for ch in range(4):
            ysi = cn["ys"] % 2
            cn["ys"] += 1
            ysb = ysbs[ysi]
            gs = slice(ch * 32, ch * 32 + 32)
            mi = cn["me"] % 2
            cn["me"] += 1
            S.dma("sp", MEc[mi][:, :, 0, :], SM[:, gs, :], key=("ME", mi), writes=[("ME", mi)])
            S.dma("sp", MEc[mi][:, :, 1, :], SE[:, gs, :], key=("ME", mi), writes=[("ME", mi)])
            for g4 in range(8):
                b = cn["Y"] % 2
                cn["Y"] += 1
                for j in range(4):
                    gl = g4 * 4 + j
                    g = ch * 32 + gl
                    S.op("pe", lambda e, b=b, j=j, g=g, gl=gl, mi=mi: e.matmul(
                        psY[b][0:16, j, :], lhsT=Ut[:, g, :], rhs=MEc[mi][:, gl, 0, :], start=True, stop=False),
                        reads=[("Ut", ub, ch), ("ME", mi)], writes=[("psY", b)])
                    S.op("pe", lambda e, b=b, j=j, g=g, gl=gl, mi=mi: e.matmul(
                        psY[b][0:16, j, :], lhsT=Hbf[:, g, :], rhs=MEc[mi][:, gl, 1, :], start=False, stop=True),
                        reads=[("Hbf0", hb), ("Hbf1", hb), ("ME", mi)], writes=[("psY", b)])
                S.op("act", lambda e, b=b, g4=g4, ysb=ysb: e.activation(out=ysb[:, g4 * 4:(g4 + 1) * 4, :],
                                                                        in_=psY[b][0:16, :, :], func=AF.Copy),
                     reads=[("psY", b)], writes=[("ysb", ysi)])
            dst = YS[t * P:(t + 1) * P, ch * 512:(ch + 1) * 512].rearrange("(b i) (g p) -> b i g p", i=8, p=16)
            for i_ in range(8):
                S.dma("act", dst[:, i_], ysb[:, :, i_ * 16:(i_ + 1) * 16], key=("ysb_st", ysi),
                      reads=[("ysb", ysi)], is_output=True)

    def vh_dst(c, ch):
        return VH[:, c, ch * 32:(ch + 1) * 32, 1:17]

    tiles = [(zp[t * P:(t + 1) * P, 6144:8192], t, False) for t in range(8)] + \
            [(zo[t * P:(t + 1) * P, 12288:14336], t, True) for t in range(8)]
    make_U(tiles[0][0], 0)
    make_V(vh_dst, "VH", 0)
    for idx, (src_ap, t, own) in enumerate(tiles):
        cur = idx % 2
        nxt = idx + 1 < len(tiles)
        if nxt:
            make_U(tiles[idx + 1][0], 1 - cur)
        for j in range(16):
            cstep(VH[:, 0, :, j], VH[:, 1, :, j], VH[:, :, :, j], VH[:, :, :, j + 1], "VH", (P1[:], P2[:]))
        if own:
            make_Hbf(VH, "VH", cur)
        S.op("dve", lambda e: e.tensor_copy(out=VH[:, :, :, 0], in_=VH[:, :, :, 16]), reads=["VH"], writes=["VH"])
        if nxt:
            make_V(vh_dst, "VH", 1 - cur)
        if own:
            make_Y(t, cur, cur)
    hout = A.sb("m_hout", [P, 16, H], F32)
    for c in range(2):
        S.op("pe", lambda e, c=c: e.transpose(ptr[:, c, 0:H], VH[:, c, :, 0], ident[0:H, 0:H]),
             reads=["VH", "ident"], writes=["ptr"])
    S.op("act", lambda e: e.activation(out=hout[:, 0:2, :], in_=ptr[:, 0:2, 0:H], func=AF.Copy),
         reads=["ptr"], writes=["hout"])
    S.dma("sp", s5p_out.rearrange("c g n -> g c n"), hout[:, 0:2, :], key="s5p_st", reads=["hout"],
          writes=["s5p_dram"], is_output=True)

    hraw = A.sb("m_hraw", [P, 16, H], F32)
    for c in range(2):
        S.dma("sp", hraw[:], s5in[c].rearrange("s g n -> g s n"), key="hraw", writes=["hraw"])
        for s4 in range(4):
            for j in range(4):
                s_ = s4 * 4 + j
                S.op("pe", lambda e, j=j, s_=s_: e.transpose(ptr[0:H, j, :], hraw[:, s_, :], ident[:]),
                     reads=["hraw", "ident"], writes=["ptr"])
            S.op("act", lambda e, c=c, s4=s4: e.activation(
                out=H0s[:, c, :, s4 * 4:(s4 + 1) * 4].rearrange("p g s -> p s g"), in_=ptr[0:H, :, :],
                func=AF.Copy), reads=["ptr"], writes=["H0s"])
    make_U(zo[1024:1152, 12288:14336], 0)
    make_V(lambda c, ch: Vs[:, c, ch * 32:(ch + 1) * 32, :], "Vs", 0)
    make_Hbf(H0s, "H0s", 0)
    make_Y(8, 0, 0)
    P1s = uin[0][0:H, :].rearrange("p (c g s) -> p c g s", c=2, g=P)
    P2s = uin[1][0:H, :].rearrange("p (c g s) -> p c g s", c=2, g=P)
    for hh_ in range(2):
        hs = slice(hh_ * 8, hh_ * 8 + 8)
        cstep(H0s[:, 0, :, hs], H0s[:, 1, :, hs], H0s[:, :, :, hs], Vs[:, :, :, hs], "Vs", (P1s, P2s),
              extra=["H0s"], tk=(("uin", 0), ("uin", 1), ("uin", 1)))
    for c in range(2):
        for s8 in range(2):
            for j in range(8):
                s_ = s8 * 8 + j
                S.op("pe", lambda e, c=c, j=j, s_=s_: e.transpose(
                    ptr[:, j // 2, (j % 2) * H:(j % 2 + 1) * H], Vs[:, c, :, s_], ident[0:H, 0:H]),
                    reads=["Vs", "ident"], writes=["ptr"])
            S.op("act", lambda e, s8=s8: e.activation(
                out=hout[:, s8 * 8:(s8 + 1) * 8, :], in_=ptr[:].rearrange("p a (b n) -> p (a b) n", n=H),
                func=AF.Copy), reads=["ptr"], writes=["hout"])
        S.dma("sp", s5s_out[c].rearrange("s g n -> g s n"), hout[:], key="s5s_st", reads=["hout"],
              writes=["s5s_dram"], is_output=True)


def build_program(debug=False, stages=None, scr_in=()):
    nc = bass.Bass("TRN2", target_bir_lowering=False)
    NT = NTOK_OWN // P

    def din(name, shape, dt=F32):
        return nc.dram_tensor(name, list(shape), dt, kind="ExternalInput").ap()

    def dout(name, shape, dt=F32):
        return nc.dram_tensor(name, list(shape), dt, kind="ExternalOutput").ap()

    def dscr(name, shape, dt=F32):
        kind = "ExternalOutput" if debug else "Internal"
        if name in scr_in:
            kind = "ExternalInput"
        return nc.dram_tensor(name, list(shape), dt, kind=kind).ap()

    xo = din("xo", [NTOK_OWN, D_MODEL])
    xp = din("xp", [NTOK_PRE, D_MODEL])
    mem = din("mem", [256, D_MODEL])
    w_in = din("w_in", [D_MODEL, IN_WIDTH])
    w_mem_kv = din("w_mem_kv", [D_MODEL, 4096])
    ident = din("ident", [P, P])

    memkv = dout("memkv", [256, 4096])
    zo = dscr("zo", [NTOK_OWN, IN_WIDTH])
    zp = dscr("zp", [NTOK_PRE, 8192])

    def st_mem(S, A):
        blocks = [(c0, AF.Copy, (lambda t, c0=c0: memkv[t * P:(t + 1) * P, c0:c0 + 256]))
                  for c0 in range(0, 4096, 256)]
        gemm_body(S, A, "m", mem, 2, w_mem_kv, blocks, ident)
    if stages is None or 'mem' in stages:
        run_stage(nc, st_mem)

    def st_pre(S, A):
        blocks = []
        for (src0, n, dst0) in ((2048, 2048, 0), (4096, 4096, 2048), (12288, 2048, 6144)):
            for c in range(0, n, 256):
                blocks.append((src0 + c, AF.Copy,
                               (lambda t, d=dst0 + c: zp[t * P:(t + 1) * P, d:d + 256])))
        gemm_body(S, A, "p", xp, NTOK_PRE // P, w_in, blocks, ident)
    if stages is None or 'pre' in stages:
        run_stage(nc, st_pre)

    def st_own(S, A):
        blocks = [(c0, col_func(c0), (lambda t, c0=c0: zo[t * P:(t + 1) * P, c0:c0 + 256]))
                  for c0 in range(0, IN_WIDTH, 256)]
        gemm_body(S, A, "o", xo, NTOK_OWN // P, w_in, blocks, ident)
    if stages is None or 'own' in stages:
        run_stage(nc, st_own)

    rq = din("rq", [9, P, 2, 16, 64])
    rk_own = din("rk_own", [9, P, 2, 16, 64])
    rk_pre = din("rk_pre", [8, P, 2, 16, 64])
    tabs = {"rq": rq, "rk_own": rk_own, "rk_pre": rk_pre,
            "mask_p": din("mask_p", [P, P]), "mask_s": din("mask_s", [P, P]),
            "seqm": din("seqm", [P, 16]), "seqmT": din("seqmT", [P, 16, P])}
    sret_in = din("sret_in", [16, 16, P, 256])
    sret_out = dout("sret_out", [16, 16, P, 256])
    sretp_out = dout("sretp_out", [16, P, 256])
    OT = dscr("OT", [9, P, 64, P], BF16)

    def st_ret(S, A):
        retention_body(S, A, zo, zp, tabs, sret_in, sret_out, sretp_out, OT, ident)
    if stages is None or 'ret' in stages:
        run_stage(nc, st_ret)

    cmk = din("cmk", [16, 256, 2048])
    cmv = din("cmv", [16, 256, 2048])

    def st_x(S, A):
        xattn_body(S, A, zo, memkv, cmk, cmv, tabs["seqmT"], OT, ident)
    if stages is None or 'x' in stages:
        run_stage(nc, st_x)

    prm = {"a_re": din("s5_a_re", [P, 64]), "a_im": din("s5_a_im", [P, 64]), "log_step": din("s5_log_step", [1, P]),
           "b_re": din("s5_b_re", [P, 1024]), "b_im": din("s5_b_im", [P, 1024]),
           "c_re": din("s5_c_re", [P, 1024]), "c_im": din("s5_c_im", [P, 1024])}
    maskM = din("maskM", [P, P])
    SM = dscr("SM", [P, P, P], BF16)
    SG = dscr("SG", [P, P, P], BF16)
    SE = dscr("SE", [P, P, P], BF16)
    A8S = dscr("A8S", [64, 2, P])

    def st_s5prep(S, A):
        s5prep_body(S, A, prm, ident, maskM, SM, SG, SE, A8S)
    if stages is None or 's5prep' in stages:
        run_stage(nc, st_s5prep)

    selm = din("selm", [P, 8])
    s5in = din("s5in", [2, 16, P, 64])
    YS = dscr("YS", [NTOK_OWN, 2048])
    s5p_out = dout("s5p_out", [2, P, 64])
    s5s_out = dout("s5s_out", [2, 16, P, 64])

    def st_s5main(S, A):
        s5main_body(S, A, zo, zp, SM, SG, SE, A8S, selm, tabs["seqm"], ident, s5in, YS, s5p_out, s5s_out)
    if stages is None or 's5main' in stages:
        run_stage(nc, st_s5main)

    s5d = din("s5_d", [1, 2048])
    w_glu = din("w_glu", [2048, 4096])
    GL = dscr("GL", [NTOK_OWN, 2048])
    GAB = dscr("GAB", [NTOK_OWN, 4096])

    def st_gelu(S, A):
        db = A.sb("g_db", [P, 2048], F32)
        S.dma("sp", db[:], s5d.to_broadcast([P, 2048]), key="db", writes=["db"])
        yb = [A.sb("g_y%d" % i, [P, 2048], F32) for i in range(2)]
        ub = [A.sb("g_u%d" % i, [P, 2048], F32) for i in range(2)]
        tb = [A.sb("g_t%d" % i, [P, 2048], F32) for i in range(2)]
        for t in range(NT):
            i = t % 2
            y, u, tt_ = yb[i], ub[i], tb[i]
            S.dma("sp", y[:], YS[t * P:(t + 1) * P, :], key=("y", i), writes=[("y", i)])
            S.dma("sp", u[:], zo[t * P:(t + 1) * P, 12288:14336], key=("u", i), writes=[("u", i)])
            S.op("pool", lambda e, u=u: e.tensor_tensor(out=u[:], in0=u[:], in1=db[:], op=ALU.mult),
                 reads=[("u", i), "db"], writes=[("u", i)])
            S.op("dve", lambda e, y=y, u=u: e.tensor_tensor(out=y[:], in0=y[:], in1=u[:], op=ALU.add),
                 reads=[("y", i), ("u", i)], writes=[("y", i)])
            S.op("pool", lambda e, y=y, tt_=tt_: e.tensor_tensor(out=tt_[:], in0=y[:], in1=y[:], op=ALU.mult),
                 reads=[("y", i)], writes=[("t", i)])
            S.op("dve", lambda e, tt_=tt_: e.tensor_scalar(out=tt_[:], in0=tt_[:], scalar1=0.044715, scalar2=1.0,
                                                           op0=ALU.mult, op1=ALU.add),
                 reads=[("t", i)], writes=[("t", i)])
            S.op("dve", lambda e, y=y, tt_=tt_: e.tensor_tensor(out=tt_[:], in0=tt_[:], in1=y[:], op=ALU.mult),
                 reads=[("t", i), ("y", i)], writes=[("t", i)])
            S.op("act", lambda e, tt_=tt_: e.activation(out=tt_[:], in_=tt_[:], func=AF.Sigmoid,
                                                        scale=1.5957691216057308),
                 reads=[("t", i)], writes=[("t", i)])
            S.op("dve", lambda e, y=y, tt_=tt_: e.tensor_tensor(out=y[:], in0=y[:], in1=tt_[:], op=ALU.mult),
                 reads=[("t", i), ("y", i)], writes=[("y", i)])
            S.dma("sp", GL[t * P:(t + 1) * P, :], y[:], key=("gl", i), reads=[("y", i)], is_output=True)

    def st_glu(S, A):
        blocks = [(c0, (AF.Copy if c0 < 2048 else AF.Sigmoid),
                   (lambda t, c0=c0: GAB[t * P:(t + 1) * P, c0:c0 + 256])) for c0 in range(0, 4096, 256)]
        gemm_body(S, A, "g", GL, NT, w_glu, blocks, ident, nkt=16)

    def st_s5fin(S, A):
        identf = A.sb("f_ident", [P, P], F32)
        identb = A.sb("f_identb", [P, P], BF16)
        S.dma("sp", identf[:], ident, key="ident", writes=["ident"])
        S.op("dve", lambda e: e.tensor_copy(out=identb[:], in_=identf[:]), reads=["ident"], writes=["identb"])
        ab = [A.sb("f_ab%d" % i, [P, 4096], F32) for i in range(2)]
        gg = [A.sb("f_g%d" % i, [P, 2048], F32) for i in range(2)]
        ob = [A.sb("f_ob%d" % i, [P, 16, P], BF16) for i in range(2)]
        oT = [A.sb("f_oT%d" % i, [P, 16, P], BF16) for i in range(2)]
        ptr = [A.ps("f_ptr%d" % i, [P, 8, P], BF16) for i in range(2)]
        cnt = 0
        for t in range(NT):
            i = t % 2
            S.dma("sp", ab[i][:], GAB[t * P:(t + 1) * P, :], key=("ab", i), writes=[("ab", i)])
            S.dma("sp", gg[i][:], zo[t * P:(t + 1) * P, 14336:16384], key=("gg", i), writes=[("gg", i)])
            S.op("pool", lambda e, i=i: e.tensor_tensor(out=gg[i][:], in0=gg[i][:], in1=ab[i][:, 2048:4096],
                                                        op=ALU.mult),
                 reads=[("gg", i), ("ab", i)], writes=[("gg", i)])
            S.op("dve", lambda e, i=i: e.tensor_tensor(out=ob[i][:].rearrange("p a b -> p (a b)"),
                                                       in0=ab[i][:, 0:2048], in1=gg[i][:], op=ALU.mult),
                 reads=[("gg", i), ("ab", i)], writes=[("ob", i)])
            for half in range(2):
                b = cnt % 2
                cnt += 1
                for j in range(8):
                    S.op("pe", lambda e, b=b, j=j, i=i, half=half: e.transpose(
                        ptr[b][:, j, :], ob[i][:, half * 8 + j, :], identb[:]),
                        reads=[("ob", i), "identb"], writes=[("ptr", b)])
                S.op("act", lambda e, b=b, i=i, half=half: e.activation(
                    out=oT[i][:, half * 8:(half + 1) * 8, :], in_=ptr[b][:], func=AF.Copy),
                    reads=[("ptr", b)], writes=[("oT", i, half)])
            S.dma("act", OT[t, :, 32:48, :], oT[i][:], key=("oTs", i), reads=[("oT", i, 0), ("oT", i, 1)],
                  is_output=True)
    if stages is None or 's5post' in stages:
        run_stage(nc, st_gelu)
        run_stage(nc, st_glu)
        run_stage(nc, st_s5fin)

    w_pa = din("w_proj_a", [4096, D_MODEL])
    w_pb = din("w_proj_b", [2048, D_MODEL])
    w_pc = din("w_proj_c", [2048, D_MODEL])
    w_o = din("w_out", [D_MODEL, D_MODEL])
    ln_g = din("ln_g", [1, D_MODEL])
    ln_b = din("ln_b", [1, D_MODEL])
    y_out = dout("y_out", [NTOK_OWN, D_MODEL])
    PR = [dscr("PR%d" % i, [NTOK_OWN, D_MODEL]) for i in range(3)]
    HP = dscr("HP", [NTOK_OWN, D_MODEL])
    NT = NTOK_OWN // P

    def proj_stage(i, w_ap, ft0, nkt, gate0):
        def body(S, A):
            gt = [A.sb("gt%d_%d" % (i, j), [P, NT, 256], F32) for j in range(2)]

            def pre_block(S, bi):
                c0 = bi * 256
                S.dma("sp", gt[bi % 2][:], zo[:, gate0 + c0:gate0 + c0 + 256].rearrange("(t p) c -> p t c", p=P),
                      key=("gt", bi % 2), writes=[("gt", bi % 2)])

            def epi(S, t, bi, ps_ap, pkey, ob_t, okey):
                j = bi % 2
                S.op("dve", lambda e, j=j, t=t: e.tensor_tensor(out=ob_t[:], in0=ps_ap, in1=gt[j][:, t, :],
                                                                op=ALU.mult),
                     reads=[pkey, ("gt", j)], writes=[okey])
            blocks = [(c0, None, (lambda t, c0=c0: PR[i][t * P:(t + 1) * P, c0:c0 + 256]))
                      for c0 in range(0, D_MODEL, 256)]
            gemm_body(S, A, "j%d" % i, None, NT, w_ap, blocks, ident, nkt=nkt, a_T=(OT, ft0), epi=epi,
                      pre_block=pre_block)
        return body
    if stages is None or 'tail' in stages:
        run_stage(nc, proj_stage(0, w_pa, 0, 32, 20480))
        run_stage(nc, proj_stage(1, w_pb, 32, 16, 24576))
        run_stage(nc, proj_stage(2, w_pc, 48, 16, 28672))

    def st_out(S, A):
        xr = [A.sb("xr%d" % j, [P, NT, 256], F32) for j in range(2)]
        alpha = float((2.0 * 1) ** 0.25)

        def pre_block(S, bi):
            c0 = bi * 256
            S.dma("sp", xr[bi % 2][:], xo[:, c0:c0 + 256].rearrange("(t p) c -> p t c", p=P),
                  key=("xr", bi % 2), writes=[("xr", bi % 2)])

        def epi(S, t, bi, ps_ap, pkey, ob_t, okey):
            j = bi % 2
            S.op("dve", lambda e, j=j, t=t: e.scalar_tensor_tensor(out=ob_t[:], in0=xr[j][:, t, :], scalar=alpha,
                                                                   in1=ps_ap, op0=ALU.mult, op1=ALU.add),
                 reads=[pkey, ("xr", j)], writes=[okey])
        blocks = [(c0, None, (lambda t, c0=c0: HP[t * P:(t + 1) * P, c0:c0 + 256]))
                  for c0 in range(0, D_MODEL, 256)]
        gemm_body(S, A, "w", PR[0], NT, w_o, blocks, ident, x_sum=[PR[1], PR[2]], epi=epi, pre_block=pre_block)
    if stages is None or 'tail' in stages or 'out' in stages:
        run_stage(nc, st_out)

    def st_ln(S, A):
        gb = A.sb("ln_gb", [P, D_MODEL], F32)
        bb = A.sb("ln_bb", [P, D_MODEL], F32)
        S.dma("sp", gb[:], ln_g.to_broadcast([P, D_MODEL]), key="gb", writes=["gb"])
        S.dma("sp", bb[:], ln_b.to_broadcast([P, D_MODEL]), key="bb", writes=["bb"])
        hb = [A.sb("ln_h%d" % j, [P, D_MODEL], F32) for j in range(2)]
        st = A.sb("ln_st", [P, 8, 6], F32)
        mv = A.sb("ln_mv", [P, 2], F32)
        nb = A.sb("ln_nb", [P, 1], F32)
        for t in range(NT):
            j = t % 2
            h = hb[j]
            S.dma("sp", h[:], HP[t * P:(t + 1) * P, :], key=("h", j), writes=[("h", j)])
            for c in range(8):
                S.op("dve", lambda e, c=c, h=h: e.bn_stats(out=st[:, c, :], in_=h[:, c * 512:(c + 1) * 512]),
                     reads=[("h", j)], writes=[("st", c)])
            S.op("dve", lambda e: e.bn_aggr(out=mv[:], in_=st[:].rearrange("p a b -> p (a b)")),
                 reads=[("st", c) for c in range(8)], writes=["mv"])
            S.op("dve", lambda e: e.tensor_scalar(out=mv[:, 1:2], in0=mv[:, 1:2], scalar1=1e-5, scalar2=None,
                                                  op0=ALU.add), reads=["mv"], writes=["mv"])
            S.op("act", lambda e: e.activation(out=mv[:, 1:2], in_=mv[:, 1:2], func=AF.Sqrt),
                 reads=["mv"], writes=["mv"])
            S.op("dve", lambda e: e.reciprocal(out=mv[:, 1:2], in_=mv[:, 1:2]), reads=["mv"], writes=["mv"])
            S.op("dve", lambda e: e.scalar_tensor_tensor(out=nb[:], in0=mv[:, 0:1], scalar=-1.0, in1=mv[:, 1:2],
                                                         op0=ALU.mult, op1=ALU.mult),
                 reads=["mv"], writes=["nb"])
            S.op("act", lambda e, h=h: e.activation(out=h[:], in_=h[:], func=AF.Identity, bias=nb[:],
                                                    scale=mv[:, 1:2]),
                 reads=[("h", j), "mv", "nb"], writes=[("h", j)])
            S.op("dve", lambda e, h=h: e.tensor_tensor(out=h[:], in0=h[:], in1=gb[:], op=ALU.mult),
                 reads=[("h", j), "gb"], writes=[("h", j)])
            S.op("pool", lambda e, h=h: e.tensor_tensor(out=h[:], in0=h[:], in1=bb[:], op=ALU.add),
                 reads=[("h", j), "bb"], writes=[("h", j)])
            S.dma("sp", y_out[t * P:(t + 1) * P, :], h[:], key=("hout", j), reads=[("h", j)], is_output=True)
    if stages is None or 'tail' in stages or 'ln' in stages:
        run_stage(nc, st_ln)

    return nc


def host_tables(hf):
    f = np.float32
    inv = (1.0 / (np.float32(10000.0) ** (np.arange(64, dtype=f) / np.float32(64)))).astype(f)
    g = np.array(RET_G, dtype=np.float64)
    i = np.arange(P)

    def tab(pos, il, kind):
        ang = pos.astype(f)[:, None] * inv[None, :]
        c, s_ = np.cos(ang).astype(f), np.sin(ang).astype(f)
        if kind == "q":
            sc = g[None, :] ** (il[:, None] + 1.0)
        else:
            sc = g[None, :] ** (-(il[:, None] + 1.0)) * (128.0 ** -0.5)
        out = np.empty((P, 2, 16, 64), f)
        out[:, 0] = (c[:, None, :] * sc[:, :, None]).astype(f)
        out[:, 1] = (s_[:, None, :] * sc[:, :, None]).astype(f)
        return out
    rq = np.stack([tab(hf * 1024 + t * P + i, i, "q") for t in range(8)] + [tab(16384 + (i % 8), i % 8, "q")])
    rk_own = np.stack([tab(hf * 1024 + t * P + i, i, "k") for t in range(8)] + [tab(16384 + (i % 8), i % 8, "k")])
    rk_pre = np.stack([tab(t * P + i, i, "k") for t in range(8)])
    mask_p = (i[None, :] >= i[:, None]).astype(f)
    same = (i[None, :] // 8) == (i[:, None] // 8)
    mask_s = (mask_p * same).astype(f)
    seqm = (i[:, None] // 8 == np.arange(16)[None, :]).astype(f)
    seqmT = np.ascontiguousarray(np.broadcast_to(seqm.T[None, :, :], (P, 16, P))).astype(f)
    maskM = ((i[None, :] // 16) >= (i[:, None] // 16)).astype(f)
    selm = (i[:, None] % 8 == np.arange(8)[None, :]).astype(f)
    return {"rq": rq, "rk_own": rk_own, "rk_pre": rk_pre, "mask_p": mask_p, "mask_s": mask_s,
            "seqm": seqm, "seqmT": seqmT, "maskM": maskM, "selm": selm}


_PROGRAM = None


def kernel(x_prompt, x_sample, mem_prompt, state_ret, state_s5_re, state_s5_im, cache_mem_k, cache_mem_v,
           w_in, w_mem_kv, s5_a_re, s5_a_im, s5_log_step, s5_b_re, s5_b_im, s5_c_re, s5_c_im, s5_d, w_glu,
           w_proj_a, w_proj_b, w_proj_c, w_out, ln_g, ln_b):
    global _PROGRAM
    if _PROGRAM is None:
        _PROGRAM = build_program()
    nc = _PROGRAM
    f = np.float32
    x_prompt = np.asarray(x_prompt, f)
    x_sample = np.asarray(x_sample, f)
    w_in0 = np.ascontiguousarray(np.asarray(w_in, f)[0])
    w_mem0 = np.ascontiguousarray(np.asarray(w_mem_kv, f)[0])
    ident = np.eye(P, dtype=f)
    wpa = np.ascontiguousarray(np.asarray(w_proj_a, f)[0])
    wpb = np.ascontiguousarray(np.asarray(w_proj_b, f)[0])
    wpc = np.ascontiguousarray(np.asarray(w_proj_c, f)[0])
    wout = np.ascontiguousarray(np.asarray(w_out, f)[0])
    lng = np.ascontiguousarray(np.asarray(ln_g, f).reshape(1, D_MODEL))
    lnb = np.ascontiguousarray(np.asarray(ln_b, f).reshape(1, D_MODEL))
    s5p = {"a_re": np.ascontiguousarray(np.asarray(s5_a_re, f)[0]), "a_im": np.ascontiguousarray(np.asarray(s5_a_im, f)[0]),
           "log_step": np.ascontiguousarray(np.asarray(s5_log_step, f).reshape(1, P)),
           "b_re": np.ascontiguousarray(np.asarray(s5_b_re, f)[0].reshape(P, 1024)),
           "b_im": np.ascontiguousarray(np.asarray(s5_b_im, f)[0].reshape(P, 1024)),
           "c_re": np.ascontiguousarray(np.asarray(s5_c_re, f)[0].reshape(P, 1024)),
           "c_im": np.ascontiguousarray(np.asarray(s5_c_im, f)[0].reshape(P, 1024)),
           "d": np.ascontiguousarray(np.asarray(s5_d, f).reshape(1, 2048))}
    wglu = np.ascontiguousarray(np.asarray(w_glu, f)[0])
    in_maps = []
    for c in range(NCORES):
        b, hf = c // 2, c % 2
        xo = np.concatenate([x_prompt[b, hf * 1024:(hf + 1) * 1024],
                             x_sample[16 * c:16 * c + 16].reshape(128, D_MODEL)], axis=0)
        xp = x_prompt[b, 0:1024] if hf == 1 else np.zeros((1024, D_MODEL), f)
        in_maps.append({
            "xo": np.ascontiguousarray(xo), "xp": np.ascontiguousarray(xp),
            "mem": np.ascontiguousarray(np.asarray(mem_prompt, f)[b]),
            "w_in": w_in0, "w_mem_kv": w_mem0, "ident": ident,
            "sret_in": np.ascontiguousarray(np.asarray(state_ret, f)[0, 16 * c:16 * c + 16]),
            "cmk": np.ascontiguousarray(np.asarray(cache_mem_k, f)[0, 16 * c:16 * c + 16]).reshape(16, 256, 2048),
            "cmv": np.ascontiguousarray(np.asarray(cache_mem_v, f)[0, 16 * c:16 * c + 16]).reshape(16, 256, 2048),
            "w_proj_a": wpa, "w_proj_b": wpb, "w_proj_c": wpc, "w_out": wout, "ln_g": lng, "ln_b": lnb,
            "s5_a_re": s5p["a_re"], "s5_a_im": s5p["a_im"], "s5_log_step": s5p["log_step"],
            "s5_b_re": s5p["b_re"], "s5_b_im": s5p["b_im"], "s5_c_re": s5p["c_re"], "s5_c_im": s5p["c_im"],
            "s5_d": s5p["d"], "w_glu": wglu,
            "s5in": np.ascontiguousarray(np.stack([np.asarray(state_s5_re, f)[0, 16 * c:16 * c + 16],
                                                   np.asarray(state_s5_im, f)[0, 16 * c:16 * c + 16]])),
        })
        in_maps[-1].update(host_tables(hf))
    res = run_bass_kernel_spmd(nc, in_maps, core_ids=list(range(NCORES)))
    R = res.results
    memk = np.stack([R[2 * b]["memkv"][:, 0:2048].reshape(256, 4, 512) for b in range(4)])[None]
    memv = np.stack([R[2 * b]["memkv"][:, 2048:4096].reshape(256, 4, 512) for b in range(4)])[None]
    y_p = np.stack([np.concatenate([R[2 * b]["y_out"][:1024], R[2 * b + 1]["y_out"][:1024]], axis=0)
                    for b in range(4)])
    y_s = np.concatenate([R[c]["y_out"][1024:] for c in range(NCORES)], axis=0).reshape(128, 8, D_MODEL)
    sretp = np.stack([R[2 * b + 1]["sretp_out"] for b in range(4)])[None]
    srets = np.concatenate([R[c]["sret_out"] for c in range(NCORES)], axis=0)[None]
    s5p_re = np.stack([R[2 * b + 1]["s5p_out"][0] for b in range(4)])[None]
    s5p_im = np.stack([R[2 * b + 1]["s5p_out"][1] for b in range(4)])[None]
    s5s_re = np.concatenate([R[c]["s5s_out"][0] for c in range(NCORES)], axis=0)[None]
    s5s_im = np.concatenate([R[c]["s5s_out"][1] for c in range(NCORES)], axis=0)[None]
    return (y_p, y_s, sretp, s5p_re, s5p_im, memk, memv, srets, s5s_re, s5s_im)
```

```python
import math
from contextlib import ExitStack

import numpy as np
import concourse.bass as bass
import concourse.mybir as mybir
from concourse.bass_utils import run_bass_kernel_spmd

F32 = mybir.dt.float32
BF16 = mybir.dt.bfloat16
AF = mybir.ActivationFunctionType
ALU = mybir.AluOpType
P = 128
NCORES = 8

D_MODEL = 4096
IN_WIDTH = 32768
NTOK_OWN = 1152
NTOK_PRE = 1024

ENGS = ("pe", "act", "dve", "pool", "sp")
SEM_LIMIT = 20000


class Ins:
    __slots__ = ("eng", "fn", "deps", "is_dma", "key", "need_inc", "semref")

    def __init__(self, eng, fn, is_dma=False, key=None):
        self.eng = eng
        self.fn = fn
        self.deps = []
        self.is_dma = is_dma
        self.key = key
        self.need_inc = False
        self.semref = None


class Sched:
    _stage = 0

    def __init__(self, nc):
        Sched._stage += 1
        self.sid = Sched._stage
        self.nc = nc
        self.ins = []
        self.last_w = {}
        self.readers = {}
        self.dma_count = {}
        self.out_keys = set()

    def _add(self, ins, reads, writes):
        deps = set()
        for k in reads:
            w = self.last_w.get(k)
            if w is not None:
                deps.add(w)
        for k in writes:
            w = self.last_w.get(k)
            if w is not None:
                deps.add(w)
            for r in self.readers.get(k, ()):
                deps.add(r)
        deps.discard(ins)
        for d in deps:
            if d.is_dma:
                ins.deps.append((d, 16 * self.dma_count[d.key]))
            elif d.eng == ins.eng and not ins.is_dma:
                if ins.eng != "pe":
                    ins.deps.append((d, 0))
                    d.need_inc = True
            else:
                ins.deps.append((d, 0))
                d.need_inc = True
        for k in reads:
            self.readers.setdefault(k, []).append(ins)
        for k in writes:
            self.last_w[k] = ins
            self.readers[k] = []
        self.ins.append(ins)
        return ins

    def op(self, eng, fn, reads=(), writes=()):
        return self._add(Ins(eng, fn), list(reads), list(writes))

    def dma(self, eng, out, in_, key, reads=(), writes=(), is_output=False):
        ins = Ins(eng, lambda e: e.dma_start(out=out, in_=in_), is_dma=True, key=key)
        if is_output:
            self.out_keys.add(key)
        self.dma_count.setdefault(key, 0)
        self._add(ins, list(reads), list(writes))
        self.dma_count[key] += 1
        return ins

    def emit(self):
        nc = self.nc
        sem_names = []
        cur = {}
        for ins in self.ins:
            if ins.is_dma or not ins.need_inc:
                continue
            c = cur.get(ins.eng)
            if c is None or c[1] >= SEM_LIMIT:
                c = [len(sem_names), 0]
                sem_names.append("c%d_%s_%d" % (self.sid, ins.eng, len(sem_names)))
                cur[ins.eng] = c
            c[1] += 1
            ins.semref = (c[0], c[1])
        dma_keys = sorted(self.dma_count.keys(), key=str)
        csem = [nc.alloc_semaphore(name=n) for n in sem_names]
        dsem = {k: nc.alloc_semaphore(name="d%d_%d" % (self.sid, i)) for i, k in enumerate(dma_keys)}
        streams = {e: [i for i in self.ins if i.eng == e] for e in ENGS}
        final_dma = dict((k, 16 * v) for k, v in self.dma_count.items())
        out_keys = self.out_keys

        def run(engname, e):
            waited = {}
            for ins in streams[engname]:
                need = {}
                for d, dv in ins.deps:
                    if d.is_dma:
                        sk = ("d", d.key)
                        v = dv
                    else:
                        sk = ("c", d.semref[0])
                        v = d.semref[1]
                    if v > need.get(sk, 0):
                        need[sk] = v
                for sk, v in need.items():
                    if waited.get(sk, 0) >= v:
                        continue
                    waited[sk] = v
                    sem = dsem[sk[1]] if sk[0] == "d" else csem[sk[1]]
                    e.wait_ge(sem, v)
                r = ins.fn(e)
                if ins.is_dma:
                    r.then_inc(dsem[ins.key], 16)
                elif ins.need_inc:
                    r.then_inc(csem[ins.semref[0]], 1)
            if engname == "sp":
                for k in sorted(out_keys, key=str):
                    e.wait_ge(dsem[k], final_dma[k])

        with nc.Block() as block:
            @block.tensor
            def _(e):
                run("pe", e)

            @block.scalar
            def _(e):
                run("act", e)

            @block.vector
            def _(e):
                run("dve", e)

            @block.gpsimd
            def _(e):
                run("pool", e)

            @block.sync
            def _(e):
                run("sp", e)

        if not getattr(Sched, "NOCLEAR", False):
            nc.clear_and_free_semaphores(csem + list(dsem.values()))
        if not getattr(Sched, "NOCLEAR", False):
            nc.all_engine_barrier()


class Alloc:
    def __init__(self, nc, st):
        self.nc = nc
        self.st = st

    def sb(self, name, shape, dt):
        return self.st.enter_context(self.nc.sbuf_tensor(name, list(shape), dt))

    def ps(self, name, shape, dt=F32):
        return self.st.enter_context(self.nc.psum_tensor(name, list(shape), dt))


def run_stage(nc, body):
    with ExitStack() as st:
        S = Sched(nc)
        A = Alloc(nc, st)
        body(S, A)
        S.emit()


def gemm_body(S, A, uid, x_ap, ntile, w_ap, blocks, ident_ap, nkt=32, a_T=None, x_sum=None, epi=None,
              store_eng="act", pre_block=None):
    xT = A.sb("xT" + uid, [P, ntile, nkt, P], BF16)
    ident = A.sb("ident" + uid, [P, P], F32)
    S.dma("sp", ident[:], ident_ap, key="ident", writes=["ident"])
    pT = [A.ps("pT%d%s" % (i, uid), [P, 4, P]) for i in range(2)]
    cnt = 0
    if a_T is not None:
        OTd, ft0 = a_T
        for t in range(ntile):
            S.dma("sp", xT[:, t, :, :], OTd[t, :, ft0:ft0 + nkt, :], key=("xTl", t % 4),
                  writes=[("xT", t, kq) for kq in range(nkt // 4)])
    else:
        xin = [A.sb("xin%d%s" % (i, uid), [P, nkt * P], F32) for i in range(2)]
        if x_sum:
            xad = A.sb("xad" + uid, [P, nkt * P], F32)
    for t in range(ntile if a_T is None else 0):
        xi = t % 2
        S.dma("sp", xin[xi][:], x_ap[t * P:(t + 1) * P, :], key=("xin", xi), writes=[("xin", xi)])
        for extra in (x_sum or ()):
            S.dma("sp", xad[:], extra[t * P:(t + 1) * P, :], key="xad", writes=["xad"])
            S.op("dve", lambda e, xi=xi: e.tensor_tensor(out=xin[xi][:], in0=xin[xi][:], in1=xad[:], op=ALU.add),
                 reads=[("xin", xi), "xad"], writes=[("xin", xi)])
        for kq in range(nkt // 4):
            b = cnt % 2
            cnt += 1
            for j in range(4):
                kt = kq * 4 + j
                S.op("pe", lambda e, b=b, j=j, xi=xi, kt=kt: e.transpose(
                    pT[b][:, j, :], xin[xi][:, kt * P:(kt + 1) * P], ident[:]),
                    reads=[("xin", xi), "ident"], writes=[("pT", b)])
            if kq % 2 == 0:
                S.op("act", lambda e, b=b, t=t, kq=kq: e.activation(
                    out=xT[:, t, kq * 4:(kq + 1) * 4, :], in_=pT[b][:], func=AF.Copy),
                    reads=[("pT", b)], writes=[("xT", t, kq)])
            else:
                S.op("dve", lambda e, b=b, t=t, kq=kq: e.tensor_copy(
                    out=xT[:, t, kq * 4:(kq + 1) * 4, :], in_=pT[b][:]),
                    reads=[("pT", b)], writes=[("xT", t, kq)])

    stg = [A.sb("stg%d%s" % (i, uid), [P, 8, 256], F32) for i in range(3)]
    wb = [A.sb("wb%d%s" % (i, uid), [P, nkt, 256], BF16) for i in range(2)]
    pz = [A.ps("pz%d%s" % (i, uid), [P, 512]) for i in range(4)]
    ob = [A.sb("ob%d%s" % (i, uid), [P, 256], F32) for i in range(4)]
    w_view = w_ap.rearrange("(kt p) c -> p kt c", p=P)
    ctr = {"stg": 0, "pz": 0}

    def load_block(bi):
        c0 = blocks[bi][0]
        if pre_block is not None:
            pre_block(S, bi)
        for c in range(nkt // 8):
            s = ctr["stg"] % 3
            ctr["stg"] += 1
            S.dma("sp", stg[s][:], w_view[:, c * 8:(c + 1) * 8, c0:c0 + 256],
                  key=("stg", s), writes=[("stg", s)])
            if c % 2 == 0:
                S.op("dve", lambda e, bi=bi, c=c, s=s: e.tensor_copy(
                    out=wb[bi % 2][:, c * 8:(c + 1) * 8, :], in_=stg[s][:]),
                    reads=[("stg", s)], writes=[("wb", bi % 2, c)])
            else:
                S.op("act", lambda e, bi=bi, c=c, s=s: e.activation(
                    out=wb[bi % 2][:, c * 8:(c + 1) * 8, :], in_=stg[s][:], func=AF.Copy),
                    reads=[("stg", s)], writes=[("wb", bi % 2, c)])

    nb = len(blocks)
    if nb:
        load_block(0)
    for bi in range(nb):
        if bi + 1 < nb:
            load_block(bi + 1)
        _, func, out_fn = blocks[bi]
        for t in range(ntile if not getattr(Sched, 'NOMM', False) else 0):
            pb = ctr["pz"] % 4
            ctr["pz"] += 1
            for kt in range(nkt):
                S.op("pe", lambda e, pb=pb, t=t, kt=kt, bi=bi: e.matmul(
                    pz[pb][:, 0:256], lhsT=xT[:, t, kt, :], rhs=wb[bi % 2][:, kt, :],
                    start=(kt == 0), stop=(kt == nkt - 1)),
                    reads=[("xT", t, kt // 4), ("wb", bi % 2, kt // 8)], writes=[("pz", pb)])
            if epi is not None:
                epi(S, t, bi, pz[pb][:, 0:256], ("pz", pb), ob[pb], ("ob", pb))
            else:
                S.op("act", lambda e, pb=pb, func=func: e.activation(
                    out=ob[pb][:], in_=pz[pb][:, 0:256], func=func),
                    reads=[("pz", pb)], writes=[("ob", pb)])
            S.dma(store_eng, out_fn(t), ob[pb][:], key=("ob", pb), reads=[("ob", pb)], is_output=True)


def col_func(c0):
    if 8192 <= c0 < 12288 or 14336 <= c0 < 16384 or 18432 <= c0 < 20480:
        return AF.Silu
    if c0 >= 20480:
        return AF.Sigmoid
    return AF.Copy


RET_G = [1.0 - 2.0 ** (-5.0 - h) for h in range(16)]


def retention_body(S, A, zo, zp, tabs, sret_in, sret_out, sretp_out, OT, ident_ap):
    ident = A.sb("r_ident", [P, P], F32)
    identb = A.sb("r_identb", [P, P], BF16)
    S.dma("sp", ident[:], ident_ap, key="ident", writes=["ident"])
    S.op("dve", lambda e: e.tensor_copy(out=identb[:], in_=ident[:]), reads=["ident"], writes=["identb"])
    maskp = A.sb("r_maskp", [P, P], F32)
    masks = A.sb("r_masks", [P, P], F32)
    seqm = A.sb("r_seqm", [P, 16], F32)
    seqmT = A.sb("r_seqmT", [P, 16, P], F32)
    S.dma("sp", maskp[:], tabs["mask_p"], key="maskp", writes=["maskp"])
    S.dma("sp", masks[:], tabs["mask_s"], key="masks", writes=["masks"])
    S.dma("sp", seqm[:], tabs["seqm"], key="seqm", writes=["seqm"])
    S.dma("sp", seqmT[:], tabs["seqmT"], key="seqmT", writes=["seqmT"])

    St = A.sb("r_S", [P, 16, 256], F32)
    Sb = A.sb("r_Sb", [P, 16, 256], BF16)
    S.op("pool", lambda e: e.memset(St[:], 0.0), writes=["S"])
    S.op("pool", lambda e: e.memset(Sb[:], 0.0), writes=["Sb"])

    qins = [A.sb("r_qin", [P, 2048], F32)] * 2
    kins = [A.sb("r_kin", [P, 2048], F32)] * 2
    vins = [A.sb("r_vin%d" % i, [P, 4096], F32) for i in range(2)]
    gins = [A.sb("r_gin%d" % i, [P, 4096], F32) for i in range(2)]
    cur = {"i": 0}
    rt = A.sb("r_rt", [P, 2, 16, 64], F32)
    t1 = A.sb("r_t1", [P, 16, 64], F32)
    t2 = A.sb("r_t2", [P, 16, 64], F32)
    qt = A.sb("r_qt", [P, 16, 128], BF16)
    kt_ = A.sb("r_kt", [P, 16, 128], BF16)
    vb = A.sb("r_vb", [P, 16, 256], BF16)
    qT = A.sb("r_qT", [P, 16, 128], BF16)
    kT = A.sb("r_kT", [P, 16, 128], BF16)
    scs = A.sb("r_scs", [P, 16, 128], BF16)
    osb = A.sb("r_osb", [P, 16, 256], F32)
    sq = vin.rearrange("p (h e) -> p h e", h=16) if False else None
    og = A.sb("r_og", [P, 16, 256], BF16)
    oT = A.sb("r_oT", [P, 32, 128], BF16)
    st1 = A.sb("r_st1", [P, 16], F32)
    st2 = A.sb("r_st2", [P, 16], F32)
    st3 = A.sb("r_st3", [P, 16], F32)
    dtmp = A.sb("r_dtmp", [P, 2, 256], F32)
    ptr = [A.ps("r_ptr%d" % i, [P, 8, 128], BF16) for i in range(2)]
    psc = [A.ps("r_psc%d" % i, [P, 4, 128]) for i in range(2)]
    po = [A.ps("r_po%d" % i, [P, 2, 256]) for i in range(2)]
    pd = [A.ps("r_pd%d" % i, [P, 2, 256]) for i in range(2)]
    cn = {"tr": 0, "sc": 0, "o": 0, "d": 0}

    def rotary(src, dst, rt_ap, rkey, skey, dkey):
        S.dma("sp", rt[:], rt_ap, key="rt", writes=["rt"])
        sv = src[:].rearrange("p (h j two) -> p h j two", h=16, two=2)
        dv = dst[:].rearrange("p h (j two) -> p h j two", two=2)
        S.op("dve", lambda e: e.tensor_tensor(out=t1[:], in0=sv[:, :, :, 0], in1=rt[:, 0], op=ALU.mult),
             reads=[skey, "rt"], writes=["t1"])
        S.op("pool", lambda e: e.tensor_tensor(out=t2[:], in0=sv[:, :, :, 1], in1=rt[:, 1], op=ALU.mult),
             reads=[skey, "rt"], writes=["t2"])
        S.op("dve", lambda e: e.tensor_tensor(out=dv[:, :, :, 0], in0=t1[:], in1=t2[:], op=ALU.subtract),
             reads=["t1", "t2"], writes=[dkey + "0"])
        S.op("dve", lambda e: e.tensor_tensor(out=t1[:], in0=sv[:, :, :, 0], in1=rt[:, 1], op=ALU.mult),
             reads=[skey, "rt", dkey + "0"], writes=["t1"])
        S.op("pool", lambda e: e.tensor_tensor(out=t2[:], in0=sv[:, :, :, 1], in1=rt[:, 0], op=ALU.mult),
             reads=[skey, "rt", dkey + "0"], writes=["t2"])
        S.op("dve", lambda e: e.tensor_tensor(out=dv[:, :, :, 1], in0=t1[:], in1=t2[:], op=ALU.add),
             reads=["t1", "t2"], writes=[dkey + "1"])

    def transpose16(src, dst, skeys, dkey):
        for half in range(2):
            b = cn["tr"] % 2
            cn["tr"] += 1
            for j in range(8):
                h = half * 8 + j
                S.op("pe", lambda e, b=b, j=j, h=h: e.transpose(ptr[b][:, j, :], src[:, h, :], identb[:]),
                     reads=list(skeys) + ["identb"], writes=[("ptr", b)])
            S.op("act", lambda e, b=b, half=half: e.activation(
                out=dst[:, half * 8:(half + 1) * 8, :], in_=ptr[b][:], func=AF.Copy),
                reads=[("ptr", b)], writes=[(dkey, half)])

    def state_update(g, sample_head=None):
        pass

    def chunk(kind, t):
        own = kind != "pre"
        cur["n"] = cur.get("n", -1) + 1
        ci = cur["n"] % 2
        cur["i"] = ci
        qin, kin, vin, gin = qins[ci], kins[ci], vins[ci], gins[ci]
        KI, VI, QI, GI = "kin", ("vin", ci), "qin", ("gin", ci)
        z = zo if own else zp
        r0 = t * P
        kcol = 2048 if own else 0
        vcol = 4096 if own else 2048
        S.dma("sp", kin[:], z[r0:r0 + P, kcol:kcol + 2048], key=KI, writes=[KI])
        S.dma("sp", vin[:], z[r0:r0 + P, vcol:vcol + 4096], key=VI, writes=[VI])
        S.op("act", lambda e: e.activation(out=vb[:].rearrange("p h e -> p (h e)"), in_=vin[:], func=AF.Copy),
             reads=[VI], writes=["vb"])
        rotary(kin, kt_, (tabs["rk_own"] if own else tabs["rk_pre"])[t], "rk", KI, "kt")
        if own:
            S.dma("sp", qin[:], z[r0:r0 + P, 0:2048], key=QI, writes=[QI])
            S.dma("sp", gin[:], z[r0:r0 + P, 8192:12288], key=GI, writes=[GI])
            rotary(qin, qt, tabs["rq"][t], "rq", QI, "qt")
            transpose16(qt, qT, ["qt0", "qt1"], "qT")
            transpose16(kt_, kT, ["kt0", "kt1"], "kT")
        return own

    def scores_and_out(mask, sample):
        for hq in range(4):
            b = cn["sc"] % 2
            cn["sc"] += 1
            for j in range(4):
                h = hq * 4 + j
                S.op("pe", lambda e, b=b, j=j, h=h: e.matmul(psc[b][:, j, :], lhsT=kT[:, h, :], rhs=qT[:, h, :],
                                                             start=True, stop=True),
                     reads=[("kT", h // 8), ("qT", h // 8)], writes=[("psc", b)])
            S.op("dve", lambda e, b=b, hq=hq: e.tensor_tensor(
                out=scs[:, hq * 4:(hq + 1) * 4, :], in0=psc[b][:],
                in1=mask[:].unsqueeze(1).to_broadcast([P, 4, P]), op=ALU.mult),
                reads=[("psc", b), "maskp", "masks"], writes=[("scs", hq)])

    def finish_out(t):
        ci = cur["i"]
        vin, gin = vins[ci], gins[ci]
        VI, GI = ("vin", ci), ("gin", ci)
        S.op("dve", lambda e: e.tensor_reduce(out=st1[:], in_=osb[:], op=ALU.add, axis=mybir.AxisListType.X),
             reads=["osb"], writes=["st1"])
        sqv = vin[:].rearrange("p (h e) -> p h e", h=16)
        S.op("act", lambda e: e.activation(out=sqv, in_=osb[:], func=AF.Square),
             reads=["osb"], writes=[VI])
        S.op("dve", lambda e: e.tensor_reduce(out=st2[:], in_=sqv, op=ALU.add, axis=mybir.AxisListType.X),
             reads=[VI], writes=["st2"])
        S.op("dve", lambda e: e.tensor_scalar(out=st1[:], in0=st1[:], scalar1=1.0 / 256, scalar2=None, op0=ALU.mult),
             reads=["st1"], writes=["st1"])
        S.op("dve", lambda e: e.tensor_tensor(out=st3[:], in0=st1[:], in1=st1[:], op=ALU.mult),
             reads=["st1"], writes=["st3"])
        S.op("dve", lambda e: e.scalar_tensor_tensor(out=st2[:], in0=st2[:], scalar=1.0 / 256, in1=st3[:],
                                                     op0=ALU.mult, op1=ALU.subtract),
             reads=["st2", "st3"], writes=["st2"])
        S.op("dve", lambda e: e.tensor_scalar(out=st2[:], in0=st2[:], scalar1=1e-5, scalar2=None, op0=ALU.add),
             reads=["st2"], writes=["st2"])
        S.op("act", lambda e: e.activation(out=st2[:], in_=st2[:], func=AF.Sqrt), reads=["st2"], writes=["st2"])
        S.op("dve", lambda e: e.reciprocal(out=st2[:], in_=st2[:]), reads=["st2"], writes=["st2"])
        S.op("dve", lambda e: e.tensor_tensor(out=osb[:], in0=osb[:],
                                              in1=st1[:].unsqueeze(2).to_broadcast([P, 16, 256]), op=ALU.subtract),
             reads=["osb", "st1"], writes=["osb"])
        S.op("dve", lambda e: e.tensor_tensor(out=osb[:], in0=osb[:],
                                              in1=st2[:].unsqueeze(2).to_broadcast([P, 16, 256]), op=ALU.mult),
             reads=["osb", "st2"], writes=["osb"])
        S.op("dve", lambda e: e.tensor_tensor(out=og[:].rearrange("p h e -> p (h e)"),
                                              in0=osb[:].rearrange("p h e -> p (h e)"), in1=gin[:], op=ALU.mult),
             reads=["osb", GI], writes=["og"])
        ogv = og[:].rearrange("p h (two e) -> p (h two) e", two=2)
        for q4 in range(4):
            b = cn["tr"] % 2
            cn["tr"] += 1
            for j in range(8):
                ft = q4 * 8 + j
                S.op("pe", lambda e, b=b, j=j, ft=ft: e.transpose(ptr[b][:, j, :], ogv[:, ft, :], identb[:]),
                     reads=["og", "identb"], writes=[("ptr", b)])
            S.op("act", lambda e, b=b, q4=q4: e.activation(out=oT[:, q4 * 8:(q4 + 1) * 8, :], in_=ptr[b][:],
                                                           func=AF.Copy),
                 reads=[("ptr", b)], writes=[("oT", q4)])
        S.dma("act", OT[t, :, 0:32, :], oT[:], key="oT", reads=[("oT", q) for q in range(4)], is_output=True)

    def prompt_state_update():
        for hp in range(8):
            b = cn["d"] % 2
            cn["d"] += 1
            for j in range(2):
                h = hp * 2 + j
                S.op("pe", lambda e, b=b, j=j, h=h: e.matmul(pd[b][:, j, :], lhsT=kt_[:, h, :], rhs=vb[:, h, :],
                                                             start=True, stop=True),
                     reads=["kt0", "kt1", "vb"], writes=[("pd", b)])
            for j in range(2):
                h = hp * 2 + j
                g = float(RET_G[h] ** 128)
                S.op("act", lambda e, b=b, j=j, g=g: e.activation(out=dtmp[:, j, :], in_=pd[b][:, j, :],
                                                                  func=AF.Copy, scale=g),
                     reads=[("pd", b)], writes=[("dtmp", j)])
                S.op("dve", lambda e, h=h, j=j, g=g: e.scalar_tensor_tensor(
                    out=St[:, h, :], in0=St[:, h, :], scalar=g, in1=dtmp[:, j, :], op0=ALU.mult, op1=ALU.add),
                    reads=[("dtmp", j), "S"], writes=["S"])
        S.op("act", lambda e: e.activation(out=Sb[:], in_=St[:], func=AF.Copy), reads=["S"], writes=["Sb"])

    for t in range(8):
        chunk("pre", t)
        prompt_state_update()

    for t in range(8):
        chunk("own", t)
        scores_and_out(maskp, False)
        for hp in range(8):
            b = cn["o"] % 2
            cn["o"] += 1
            for j in range(2):
                h = hp * 2 + j
                S.op("pe", lambda e, b=b, j=j, h=h: e.matmul(po[b][:, j, :], lhsT=scs[:, h, :], rhs=vb[:, h, :],
                                                             start=True, stop=False),
                     reads=[("scs", h // 4), "vb"], writes=[("po", b)])
                S.op("pe", lambda e, b=b, j=j, h=h: e.matmul(po[b][:, j, :], lhsT=qT[:, h, :], rhs=Sb[:, h, :],
                                                             start=False, stop=True),
                     reads=[("qT", h // 8), "Sb"], writes=[("po", b)])
            S.op("act", lambda e, b=b, hp=hp: e.activation(out=osb[:, hp * 2:hp * 2 + 2, :], in_=po[b][:],
                                                           func=AF.Copy),
                 reads=[("po", b)], writes=["osb"])
        prompt_state_update()
        finish_out(t)
    S.dma("sp", sretp_out.rearrange("h d e -> d h e"), St[:], key="St_out", reads=["S"], is_output=True)

    t = 8
    chunk("own", t)
    scores_and_out(masks, True)
    qTm = oT[:, 0:16, :]
    ktm = oT[:, 16:32, :]
    Ss_b = [vins[i][:].rearrange("p (h e) -> p h e", h=16) for i in range(2)]
    Ss_k = [("vin", i) for i in range(2)]
    Ssb_b = [og, A.sb("r_Ssb1", [P, 16, 256], BF16)]
    Ssb_k = ["og", "Ssb1"]

    def load_state(h):
        i = h % 2
        S.dma("sp", Ss_b[i], sret_in[:, h].rearrange("s d e -> d s e"), key=("Ss", i), writes=[Ss_k[i]])
    load_state(0)
    for h in range(16):
        bi_ = h % 2
        Ss, VI, Ssb, SBK = Ss_b[bi_], Ss_k[bi_], Ssb_b[bi_], Ssb_k[bi_]
        g8 = float(RET_G[h] ** 8)
        if h + 1 < 16:
            load_state(h + 1)
        S.op("act", lambda e, Ssb=Ssb, Ss=Ss: e.activation(out=Ssb[:], in_=Ss, func=AF.Copy), reads=[VI], writes=[SBK])
        S.op("dve", lambda e, h=h: e.tensor_tensor(
            out=qTm, in0=qT[:, h, :].unsqueeze(1).to_broadcast([P, 16, P]), in1=seqmT[:], op=ALU.mult),
            reads=[("qT", h // 8), "seqmT"], writes=[("oT", 0), ("oT", 1)])
        S.op("dve", lambda e, h=h: e.tensor_tensor(
            out=ktm, in0=kt_[:, h, :].unsqueeze(1).to_broadcast([P, 16, P]),
            in1=seqm[:].unsqueeze(2).to_broadcast([P, 16, P]), op=ALU.mult),
            reads=["kt0", "kt1", "seqm"], writes=[("oT", 2), ("oT", 3)])
        b = cn["o"] % 2
        cn["o"] += 1
        S.op("pe", lambda e, b=b, h=h: e.matmul(po[b][:, 0, :], lhsT=scs[:, h, :], rhs=vb[:, h, :],
                                                start=True, stop=False),
             reads=[("scs", h // 4), "vb"], writes=[("po", b)])
        for s_ in range(16):
            S.op("pe", lambda e, b=b, s_=s_, Ssb=Ssb: e.matmul(po[b][:, 0, :], lhsT=qTm[:, s_, :],
                                                               rhs=Ssb[:, s_, :], start=False, stop=(s_ == 15)),
                 reads=[("oT", 0), ("oT", 1), SBK], writes=[("po", b)])
        S.op("act", lambda e, b=b, h=h: e.activation(out=osb[:, h, :], in_=po[b][:, 0, :], func=AF.Copy),
             reads=[("po", b)], writes=["osb"])
        for sp_ in range(8):
            b2 = cn["d"] % 2
            cn["d"] += 1
            for j in range(2):
                s_ = sp_ * 2 + j
                S.op("pe", lambda e, b2=b2, j=j, s_=s_, h=h: e.matmul(
                    pd[b2][:, j, :], lhsT=ktm[:, s_, :], rhs=vb[:, h, :], start=True, stop=True),
                    reads=[("oT", 2), ("oT", 3), "vb"], writes=[("pd", b2)])
            for j in range(2):
                s_ = sp_ * 2 + j
                S.op("act", lambda e, b2=b2, j=j, g8=g8: e.activation(out=dtmp[:, j, :], in_=pd[b2][:, j, :],
                                                                      func=AF.Copy, scale=g8),
                     reads=[("pd", b2)], writes=[("dtmp", j)])
                S.op("dve", lambda e, s_=s_, j=j, g8=g8, Ss=Ss: e.scalar_tensor_tensor(
                    out=Ss[:, s_, :], in0=Ss[:, s_, :], scalar=g8, in1=dtmp[:, j, :], op0=ALU.mult, op1=ALU.add),
                    reads=[("dtmp", j), VI, SBK], writes=[VI])
        S.dma("sp", sret_out[:, h].rearrange("s d e -> d s e"), Ss, key=("Ss_out", bi_), reads=[VI],
              is_output=True)
    finish_out(8)


def xattn_body(S, A, zo, memkv, cmk, cmv, seqmT_ap, OT, ident_ap):
    X = mybir.AxisListType.X
    scale = 512.0 ** -0.5
    ident = A.sb("x_ident", [P, P], F32)
    identb = A.sb("x_identb", [P, P], BF16)
    S.dma("sp", ident[:], ident_ap, key="ident", writes=["ident"])
    S.op("dve", lambda e: e.tensor_copy(out=identb[:], in_=ident[:]), reads=["ident"], writes=["identb"])
    seqmT = A.sb("x_seqmT", [P, 16, P], F32)
    S.dma("sp", seqmT[:], seqmT_ap, key="seqmT", writes=["seqmT"])

    qin = A.sb("x_qin", [P, 2048], F32)
    gin = A.sb("x_gin", [P, 2048], F32)
    qb = A.sb("x_qb", [P, 16, 128], BF16)
    qT = A.sb("x_qT", [P, 16, 128], BF16)
    kvin = [A.sb("x_kvin%d" % i, [P, 2, 2048], F32) for i in range(2)]
    kb = A.sb("x_kb", [P, 2, 16, 128], BF16)
    KT = A.sb("x_KT", [P, 16, 256], BF16)
    Vb = A.sb("x_Vb", [P, 2, 2048], BF16)
    pb = A.sb("x_pb", [P, 4, 256], BF16)
    pT = A.sb("x_pT", [P, 4, 2, 128], BF16)
    pTm = A.sb("x_pTm", [P, 16, 128], BF16)
    qTm = A.sb("x_qTm", [P, 16, 128], BF16)
    ob = A.sb("x_ob", [P, 16, 128], BF16)
    oT = A.sb("x_oT", [P, 16, 128], BF16)
    mx = A.sb("x_mx", [P, 4], F32)
    sm = A.sb("x_sm", [P, 4], F32)
    ptr = [A.ps("x_ptr%d" % i, [P, 8, 128], BF16) for i in range(2)]
    pso = [A.ps("x_pso%d" % i, [P, 512]) for i in range(4)]
    cn = {"tr": 0, "kv": 0}

    def tr_group(srcs, dst_ap, skeys, dkey):
        b = cn["tr"] % 2
        cn["tr"] += 1
        for j, src in enumerate(srcs):
            S.op("pe", lambda e, b=b, j=j, src=src: e.transpose(ptr[b][:, j, :], src, identb[:]),
                 reads=list(skeys) + ["identb"], writes=[("ptr", b)])
        n = len(srcs)
        S.op("act", lambda e, b=b, n=n: e.activation(out=dst_ap, in_=ptr[b][:, 0:n, :], func=AF.Copy),
             reads=[("ptr", b)], writes=[dkey])

    def load_q(t):
        r0 = t * P
        S.dma("sp", qin[:], zo[r0:r0 + P, 16384:18432], key="qin", writes=["qin"])
        S.dma("sp", gin[:], zo[r0:r0 + P, 18432:20480], key="gin", writes=["gin"])
        S.op("dve", lambda e: e.tensor_copy(out=qb[:].rearrange("p a b -> p (a b)"), in_=qin[:]),
             reads=["qin"], writes=["qb"])
        for half in range(2):
            tr_group([qb[:, half * 8 + j, :] for j in range(8)], qT[:, half * 8:(half + 1) * 8, :],
                     ["qb"], ("qT", half))

    def load_kv(src_ap, which):
        i = cn["kv"] % 2
        cn["kv"] += 1
        S.dma("sp", kvin[i][:], src_ap.rearrange("(mt p) c -> p mt c", p=P), key=("kvin", i),
              writes=[("kvin", i)])
        return i

    def make_KT(i):
        S.op("dve", lambda e: e.tensor_copy(out=kb[:].rearrange("p m a b -> p m (a b)"), in_=kvin[i][:]),
             reads=[("kvin", i)], writes=["kb"])
        for mt in range(2):
            for half in range(2):
                tr_group([kb[:, mt, half * 8 + j, :] for j in range(8)],
                         KT[:, half * 8:(half + 1) * 8, mt * P:(mt + 1) * P], ["kb"], ("KT", mt, half))

    def make_V(i):
        S.op("act", lambda e: e.activation(out=Vb[:], in_=kvin[i][:], func=AF.Copy), reads=[("kvin", i)],
             writes=["Vb"])

    KT_keys = [("KT", mt, half) for mt in range(2) for half in range(2)]

    def sc_loc(h, sample):
        return pso[h][:, 0:256], ("pso", h)

    def score_mm(lhs, lkeys, first, last, sample=False):
        for h in range(4):
            for dt in range(4):
                k = h * 4 + dt
                loc, lk = sc_loc(h, sample)
                S.op("pe", lambda e, loc=loc, k=k, dt=dt: e.matmul(
                    loc, lhsT=lhs[:, k, :], rhs=KT[:, k, :],
                    start=(first and dt == 0), stop=(last and dt == 3)),
                    reads=list(lkeys) + KT_keys, writes=[lk])

    def softmax(sample=False):
        for h in range(4):
            sv, lk = sc_loc(h, sample)
            S.op("dve", lambda e, h=h, sv=sv: e.tensor_reduce(out=mx[:, h:h + 1], in_=sv, op=ALU.max, axis=X),
                 reads=[lk], writes=[("mx", h)])
            S.op("dve", lambda e, h=h: e.tensor_scalar(out=mx[:, h:h + 1], in0=mx[:, h:h + 1], scalar1=-scale,
                                                       scalar2=None, op0=ALU.mult),
                 reads=[("mx", h)], writes=[("mx", h)])
            S.op("act", lambda e, h=h, sv=sv: e.activation(out=pb[:, h, :], in_=sv, func=AF.Exp,
                                                           bias=mx[:, h:h + 1], scale=scale,
                                                           accum_out=sm[:, h:h + 1]),
                 reads=[lk, ("mx", h)], writes=[("pb", h), ("sm", h)])
            S.op("dve", lambda e, h=h: e.reciprocal(out=sm[:, h:h + 1], in_=sm[:, h:h + 1]),
                 reads=[("sm", h)], writes=[("sm", h)])
        tr_group([pb[:, h, mt * P:(mt + 1) * P] for h in range(4) for mt in range(2)],
                 pT[:].rearrange("p h m l -> p (h m) l"), [("pb", h) for h in range(4)], "pT")

    def finish(t):
        for h in range(4):
            S.op("dve", lambda e, h=h: e.scalar_tensor_tensor(
                out=ob[:, h * 4:(h + 1) * 4, :].rearrange("p a b -> p (a b)"), in0=pso[h][:],
                scalar=sm[:, h:h + 1], in1=gin[:, h * 512:(h + 1) * 512], op0=ALU.mult, op1=ALU.mult),
                reads=[("pso", h), ("sm", h), "gin"], writes=[("ob", h)])
        for half in range(2):
            tr_group([ob[:, half * 8 + j, :] for j in range(8)], oT[:, half * 8:(half + 1) * 8, :],
                     [("ob", h) for h in range(4)], ("oT", half))
        S.dma("act", OT[t, :, 48:64, :], oT[:], key="oT", reads=[("oT", 0), ("oT", 1)], is_output=True)

    i = load_kv(memkv[:, 0:2048], "k")
    make_KT(i)
    i = load_kv(memkv[:, 2048:4096], "v")
    make_V(i)
    for t in range(8):
        load_q(t)
        score_mm(qT, [("qT", 0), ("qT", 1)], True, True)
        softmax()
        for h in range(4):
            for mt in range(2):
                S.op("pe", lambda e, h=h, mt=mt: e.matmul(pso[h][:], lhsT=pT[:, h, mt, :],
                                                          rhs=Vb[:, mt, h * 512:(h + 1) * 512],
                                                          start=(mt == 0), stop=(mt == 1)),
                     reads=["pT", "Vb"], writes=[("pso", h)])
        finish(t)

    load_q(8)
    for s_ in range(16):
        i = load_kv(cmk[s_], "k")
        make_KT(i)
        S.op("dve", lambda e, s_=s_: e.tensor_tensor(
            out=qTm[:], in0=qT[:], in1=seqmT[:, s_, :].unsqueeze(1).to_broadcast([P, 16, P]), op=ALU.mult),
            reads=[("qT", 0), ("qT", 1), "seqmT"], writes=["qTm"])
        score_mm(qTm, ["qTm"], s_ == 0, s_ == 15, sample=True)
    softmax(sample=True)
    for s_ in range(16):
        i = load_kv(cmv[s_], "v")
        make_V(i)
        for h in range(4):
            S.op("dve", lambda e, h=h, s_=s_: e.tensor_tensor(
                out=pTm[:, h * 2:(h + 1) * 2, :], in0=pT[:, h, :, :],
                in1=seqmT[:, s_, :].unsqueeze(1).to_broadcast([P, 2, P]), op=ALU.mult),
                reads=["pT", "seqmT"], writes=[("pTm", h)])
            for mt in range(2):
                S.op("pe", lambda e, h=h, mt=mt, s_=s_: e.matmul(
                    pso[h][:], lhsT=pTm[:, h * 2 + mt, :], rhs=Vb[:, mt, h * 512:(h + 1) * 512],
                    start=(s_ == 0 and mt == 0), stop=(s_ == 15 and mt == 1)),
                    reads=[("pTm", h), "Vb"], writes=[("pso", h)])
    finish(8)


def s5prep_body(S, A, prm, ident_ap, maskM_ap, SM, SG, SE, A8S):
    PI = math.pi
    ident = A.sb("q_ident", [P, P], F32)
    identb = A.sb("q_identb", [P, P], BF16)
    S.dma("sp", ident[:], ident_ap, key="ident", writes=["ident"])
    S.op("dve", lambda e: e.tensor_copy(out=identb[:], in_=ident[:]), reads=["ident"], writes=["identb"])
    maskM = A.sb("q_maskM", [P, P], F32)
    S.dma("sp", maskM[:], maskM_ap, key="maskM", writes=["maskM"])
    H = 64
    uid = [0]

    def tl(shape, dt=F32):
        uid[0] += 1
        return A.sb("q_t%d" % uid[0], shape, dt)

    def dve(fn, reads, writes):
        S.op("dve", fn, reads=reads, writes=writes)

    def tt(out, a, b, op, okey, akey, bkey):
        dve(lambda e: e.tensor_tensor(out=out, in0=a, in1=b, op=op), [akey, bkey], [okey])

    araw = tl([P, 2, 64])
    S.dma("sp", araw[:, 0, :], prm["a_re"], key="araw0", writes=["araw"])
    S.dma("sp", araw[:, 1, :], prm["a_im"], key="araw1", writes=["araw"])
    pA = A.ps("q_pA", [P, 4, P])
    pA2 = A.ps("q_pA2", [P, 4, P])
    ar = tl([H, P]); ai = tl([H, P])
    for j in range(2):
        S.op("pe", lambda e, j=j: e.transpose(pA[0:H, j, :], araw[:, j, :], ident[:]), reads=["araw", "ident"],
             writes=["pA"])
    dve(lambda e: e.tensor_copy(out=ar[:], in_=pA[0:H, 0, :]), ["pA"], ["ar"])
    dve(lambda e: e.tensor_copy(out=ai[:], in_=pA[0:H, 1, :]), ["pA"], ["ai"])
    dtb = tl([H, P])
    S.dma("sp", dtb[:], prm["log_step"].to_broadcast([H, P]), key="dtb", writes=["dtb"])
    S.op("act", lambda e: e.activation(out=dtb[:], in_=dtb[:], func=AF.Exp), reads=["dtb"], writes=["dtb"])
    dtar = tl([H, P]); dtai = tl([H, P]); mag = tl([H, P])
    tt(dtar[:], dtb[:], ar[:], ALU.mult, "dtar", "dtb", "ar")
    tt(dtai[:], dtb[:], ai[:], ALU.mult, "dtai", "dtb", "ai")
    kq = tl([H, P]); ki = tl([H, P], mybir.dt.int32); rr = tl([H, P])
    dve(lambda e: e.tensor_scalar(out=kq[:], in0=dtai[:], scalar1=1.0 / (2 * PI), scalar2=None, op0=ALU.mult),
        ["dtai"], ["kq"])
    dve(lambda e: e.tensor_copy(out=ki[:], in_=kq[:]), ["kq"], ["ki"])
    dve(lambda e: e.tensor_copy(out=kq[:], in_=ki[:]), ["ki"], ["kq"])
    dve(lambda e: e.scalar_tensor_tensor(out=rr[:], in0=kq[:], scalar=-2 * PI, in1=dtai[:], op0=ALU.mult,
                                         op1=ALU.add), ["kq", "dtai"], ["rr"])
    rs = tl([H, P]); rc = tl([H, P]); sn = tl([H, P]); cs = tl([H, P])
    msk = tl([H, P])
    for t_, k_, sh in ((rs, "rs", 0.0), (rc, "rc", PI / 2)):
        dve(lambda e, t_=t_, sh=sh: e.tensor_scalar(out=t_[:], in0=rr[:], scalar1=sh, scalar2=None, op0=ALU.add),
            ["rr"], [k_])
        dve(lambda e, t_=t_: e.tensor_scalar(out=msk[:], in0=t_[:], scalar1=PI, scalar2=None, op0=ALU.is_gt),
            [k_], ["msk"])
        dve(lambda e, t_=t_: e.scalar_tensor_tensor(out=t_[:], in0=msk[:], scalar=-2 * PI, in1=t_[:],
                                                    op0=ALU.mult, op1=ALU.add), ["msk", k_], [k_])
        dve(lambda e, t_=t_: e.tensor_scalar(out=msk[:], in0=t_[:], scalar1=-PI, scalar2=None, op0=ALU.is_lt),
            [k_], ["msk"])
        dve(lambda e, t_=t_: e.scalar_tensor_tensor(out=t_[:], in0=msk[:], scalar=2 * PI, in1=t_[:],
                                                    op0=ALU.mult, op1=ALU.add), ["msk", k_], [k_])
    for t_, k_ in ((rs, "rs"), (rc, "rc")):
        dve(lambda e, t_=t_: e.tensor_scalar(out=t_[:], in0=t_[:], scalar1=3.1415925, scalar2=-3.1415925,
                                             op0=ALU.min, op1=ALU.max), [k_], [k_])
    hh = tl([H, P]); x2 = tl([H, P]); sh_ = tl([H, P]); ch_ = tl([H, P])
    dve(lambda e: e.tensor_scalar(out=hh[:], in0=rs[:], scalar1=0.5, scalar2=None, op0=ALU.mult), ["rs"], ["hh"])
    tt(x2[:], hh[:], hh[:], ALU.mult, "x2", "hh", "hh")
    sc_ = [(-1.0) ** k / math.factorial(2 * k + 1) for k in range(9)]
    cc_ = [(-1.0) ** k / math.factorial(2 * k) for k in range(9)]

    def horner(dst, dkey, co):
        dve(lambda e: e.tensor_scalar(out=dst[:], in0=x2[:], scalar1=co[-1], scalar2=None, op0=ALU.mult),
            ["x2"], [dkey])
        for c_ in co[-2:0:-1]:
            dve(lambda e, c_=c_: e.scalar_tensor_tensor(out=dst[:], in0=dst[:], scalar=c_, in1=x2[:],
                                                        op0=ALU.add, op1=ALU.mult), [dkey, "x2"], [dkey])
        dve(lambda e: e.tensor_scalar(out=dst[:], in0=dst[:], scalar1=co[0], scalar2=None, op0=ALU.add),
            [dkey], [dkey])
    horner(sh_, "sh", sc_)
    tt(sh_[:], sh_[:], hh[:], ALU.mult, "sh", "sh", "hh")
    horner(ch_, "ch", cc_)
    dve(lambda e: e.scalar_tensor_tensor(out=sn[:], in0=sh_[:], scalar=2.0, in1=ch_[:], op0=ALU.mult,
                                         op1=ALU.mult), ["sh", "ch"], ["sn"])
    tt(cs[:], sh_[:], sh_[:], ALU.mult, "cs", "sh", "sh")
    dve(lambda e: e.tensor_scalar(out=cs[:], in0=cs[:], scalar1=-2.0, scalar2=1.0, op0=ALU.mult, op1=ALU.add),
        ["cs"], ["cs"])
    ec_ = [1.0 / math.factorial(k) for k in range(9)]
    dve(lambda e: e.tensor_scalar(out=mag[:], in0=dtar[:], scalar1=ec_[-1], scalar2=None, op0=ALU.mult),
        ["dtar"], ["mag"])
    for c_ in ec_[-2:0:-1]:
        dve(lambda e, c_=c_: e.scalar_tensor_tensor(out=mag[:], in0=mag[:], scalar=c_, in1=dtar[:],
                                                    op0=ALU.add, op1=ALU.mult), ["mag", "dtar"], ["mag"])
    dve(lambda e: e.tensor_scalar(out=mag[:], in0=mag[:], scalar1=1.0, scalar2=None, op0=ALU.add),
        ["mag"], ["mag"])
    PW = tl([H, 16, 2, P])
    tmp = [tl([H, P]) for _ in range(4)]
    cm = [0]

    def cmul(ore, oim, xr, xi, yr, yi, okeys, ikeys):
        cm[0] += 1
        k = ["cm%d_%d" % (cm[0], i) for i in range(4)]
        dve(lambda e: e.tensor_tensor(out=tmp[0][:], in0=xr, in1=yr, op=ALU.mult), ikeys, ["tmp0"])
        dve(lambda e: e.tensor_tensor(out=tmp[1][:], in0=xi, in1=yi, op=ALU.mult), ikeys, ["tmp1"])
        dve(lambda e: e.tensor_tensor(out=tmp[2][:], in0=xr, in1=yi, op=ALU.mult), ikeys, ["tmp2"])
        dve(lambda e: e.tensor_tensor(out=tmp[3][:], in0=xi, in1=yr, op=ALU.mult), ikeys, ["tmp3"])
        dve(lambda e: e.tensor_tensor(out=ore, in0=tmp[0][:], in1=tmp[1][:], op=ALU.subtract),
            ["tmp0", "tmp1"], [okeys[0]])
        dve(lambda e: e.tensor_tensor(out=oim, in0=tmp[2][:], in1=tmp[3][:], op=ALU.add),
            ["tmp2", "tmp3"], [okeys[1]])

    def pw(e_, c):
        return PW[:, e_ + 7, c, :]

    def pk(e_):
        return ["pw%d_0" % e_, "pw%d_1" % e_]
    S.op("pool", lambda e: e.memset(pw(0, 0), 1.0), writes=["pw0_0"])
    S.op("pool", lambda e: e.memset(pw(0, 1), 0.0), writes=["pw0_1"])
    tt(pw(1, 0), mag[:], cs[:], ALU.mult, "pw1_0", "mag", "cs")
    tt(pw(1, 1), mag[:], sn[:], ALU.mult, "pw1_1", "mag", "sn")
    for e_ in range(2, 9):
        cmul(pw(e_, 0), pw(e_, 1), pw(e_ - 1, 0), pw(e_ - 1, 1), pw(1, 0), pw(1, 1), pk(e_), pk(e_ - 1) + pk(1))
    im2 = tl([H, P])
    tt(im2[:], mag[:], mag[:], ALU.mult, "im2", "mag", "mag")
    dve(lambda e: e.reciprocal(out=im2[:], in_=im2[:]), ["im2"], ["im2"])
    tt(pw(-1, 0), pw(1, 0), im2[:], ALU.mult, "pw-1_0", "pw1_0", "im2")
    dve(lambda e: e.scalar_tensor_tensor(out=pw(-1, 1), in0=pw(1, 1), scalar=-1.0, in1=im2[:], op0=ALU.mult,
                                         op1=ALU.mult), ["pw1_1", "im2"], ["pw-1_1"])
    for e_ in range(2, 8):
        cmul(pw(-e_, 0), pw(-e_, 1), pw(-e_ + 1, 0), pw(-e_ + 1, 1), pw(-1, 0), pw(-1, 1),
             pk(-e_), pk(-e_ + 1) + pk(-1))
    allpw = [k for e_ in range(-7, 9) for k in pk(e_)]
    S.dma("sp", A8S, PW[:, 15, :, :], key="a8s", reads=pk(8), is_output=True)
    den = tl([H, P]); xr_ = tl([H, P]); fre = tl([H, P]); fim = tl([H, P])
    tt(den[:], ar[:], ar[:], ALU.mult, "den", "ar", "ar")
    tt(tmp[0][:], ai[:], ai[:], ALU.mult, "tmp0", "ai", "ai")
    tt(den[:], den[:], tmp[0][:], ALU.add, "den", "den", "tmp0")
    dve(lambda e: e.reciprocal(out=den[:], in_=den[:]), ["den"], ["den"])
    dve(lambda e: e.tensor_scalar(out=xr_[:], in0=pw(1, 0), scalar1=-1.0, scalar2=None, op0=ALU.add),
        ["pw1_0"], ["xr"])
    tt(tmp[0][:], xr_[:], ar[:], ALU.mult, "tmp0", "xr", "ar")
    tt(tmp[1][:], pw(1, 1), ai[:], ALU.mult, "tmp1", "pw1_1", "ai")
    tt(fre[:], tmp[0][:], tmp[1][:], ALU.add, "fre", "tmp0", "tmp1")
    tt(fre[:], fre[:], den[:], ALU.mult, "fre", "fre", "den")
    tt(tmp[2][:], pw(1, 1), ar[:], ALU.mult, "tmp2", "pw1_1", "ar")
    tt(tmp[3][:], xr_[:], ai[:], ALU.mult, "tmp3", "xr", "ai")
    tt(fim[:], tmp[2][:], tmp[3][:], ALU.subtract, "fim", "tmp2", "tmp3")
    tt(fim[:], fim[:], den[:], ALU.mult, "fim", "fim", "den")
    Pst = tl([P, P, 8]); Qst = tl([P, P, 8])
    for s_ in range(8):
        cmul(Pst[0:H, :, s_], Qst[H:P, :, s_], pw(7 - s_, 0), pw(7 - s_, 1), fre[:], fim[:],
             ["Pst_lo", "Qst_hi"], pk(7 - s_) + ["fre", "fim"])
    S.op("act", lambda e: e.activation(out=Pst[H:P, :, :], in_=Pst[0:H, :, :], func=AF.Copy),
         reads=["Pst_lo"], writes=["Pst_hi"])
    S.op("act", lambda e: e.activation(out=Qst[0:H, :, :], in_=Qst[H:P, :, :], func=AF.Copy, scale=-1.0),
         reads=["Qst_hi"], writes=["Qst_lo"])
    Pv = tl([P, 16, P]); Qv = tl([P, 16, P])
    S.op("act", lambda e: e.activation(out=Pv[0:H], in_=PW[:, :, 0, :], func=AF.Copy), reads=allpw, writes=["Pv_lo"])
    S.op("act", lambda e: e.activation(out=Pv[H:P], in_=PW[:, :, 0, :], func=AF.Copy), reads=allpw, writes=["Pv_hi"])
    S.op("act", lambda e: e.activation(out=Qv[0:H], in_=PW[:, :, 1, :], func=AF.Copy, scale=-1.0), reads=allpw,
         writes=["Qv_lo"])
    S.op("act", lambda e: e.activation(out=Qv[H:P], in_=PW[:, :, 1, :], func=AF.Copy, scale=-1.0), reads=allpw,
         writes=["Qv_hi"])
    t1 = tl([P, 32, 128]); t2 = tl([P, 32, 128])
    raw = t1[:].rearrange("p a b -> p (a b)")
    R = tl([P, P, 16]); Sx = tl([P, P, 16]); Rp = tl([P, P, 16]); Sp = tl([P, P, 16])
    srcs = (("b_re", 0), ("b_im", 1), ("c_re", 2), ("c_im", 3))
    for nm, idx in srcs:
        S.dma("sp", raw[:, idx * 1024:(idx + 1) * 1024], prm[nm], key="raw%d" % idx, writes=["raw%d" % idx])
    cnt = [0]
    for nm, idx in srcs:
        rv = raw[:, idx * 1024:(idx + 1) * 1024]
        for q4 in range(4):
            pb_ = pA if cnt[0] % 2 == 0 else pA2
            pkey = "pA" if cnt[0] % 2 == 0 else "pA2"
            cnt[0] += 1
            for j in range(4):
                qq = q4 * 4 + j
                if idx < 2:
                    src = rv.rearrange("p (n q) -> p q n", q=16)[:, qq, :]
                else:
                    src = rv[:, qq * 64:(qq + 1) * 64]
                S.op("pe", lambda e, pb_=pb_, j=j, src=src: e.transpose(pb_[0:H, j, :], src, ident[:]),
                     reads=["raw%d" % idx, "ident"], writes=[pkey])
            qs = slice(q4 * 4, q4 * 4 + 4)
            pin = pb_[0:H, :, :]

            def outv(tile_, lo):
                v = tile_[0:H] if lo else tile_[H:P]
                return v.rearrange("p g q -> p q g")[:, qs, :]
            if idx == 0:
                dsts = ((R, True, 1.0), (Sx, False, 1.0))
            elif idx == 1:
                dsts = ((R, False, 1.0), (Sx, True, 1.0))
            elif idx == 2:
                dsts = ((Rp, True, 1.0), (Sp, False, 1.0))
            else:
                dsts = ((Rp, False, -1.0), (Sp, True, 1.0))
            for (tile_, lo, sc) in dsts:
                ov = outv(tile_, lo)
                S.op("act", lambda e, ov=ov, pin=pin, sc=sc: e.activation(out=ov, in_=pin, func=AF.Copy, scale=sc),
                     reads=[pkey], writes=["tab%d_%d_%d" % (id(tile_) % 997, lo, q4)])
    tabkeys = None
    X7c = tl([P, 32, 128], BF16); Ypc = tl([P, 32, 128], BF16); Ec = tl([P, 32, 128], BF16)
    Mc = tl([P, 32, 128], BF16); Gc = tl([P, 32, 128], BF16)
    pM = [A.ps("q_pM%d" % i, [P, 4, P]) for i in range(2)]
    pG = [A.ps("q_pG%d" % i, [P, 8, P], BF16) for i in range(2)]
    anytab = [k for k in S.last_w.keys() if isinstance(k, str) and k.startswith("tab")]
    for ch in range(4):
        gs = slice(ch * 32, ch * 32 + 32)
        t1v = t1[:].rearrange("p g (s q) -> p g s q", q=16)
        t2v = t2[:].rearrange("p g (s q) -> p g s q", q=16)

        def build(dst, dkey, Pt, Qt, Rt, St, pkeys):
            dve(lambda e: e.tensor_tensor(out=t1v, in0=Pt.unsqueeze(3).to_broadcast([P, 32, 8, 16]),
                                          in1=Rt.unsqueeze(2).to_broadcast([P, 32, 8, 16]), op=ALU.mult),
                pkeys + anytab + ["raw0", "raw1", "raw2", "raw3"], ["t1"])
            S.op("pool", lambda e: e.tensor_tensor(out=t2v, in0=Qt.unsqueeze(3).to_broadcast([P, 32, 8, 16]),
                                                   in1=St.unsqueeze(2).to_broadcast([P, 32, 8, 16]), op=ALU.mult),
                 reads=pkeys + anytab, writes=["t2"])
            dve(lambda e: e.tensor_tensor(out=dst[:], in0=t1[:], in1=t2[:], op=ALU.add), ["t1", "t2"], [dkey])
        build(X7c, "X7c", Pst[:, gs, :], Qst[:, gs, :], R[:, gs, :], Sx[:, gs, :],
              ["Pst_lo", "Pst_hi", "Qst_lo", "Qst_hi"])
        pvk = ["Pv_lo", "Pv_hi", "Qv_lo", "Qv_hi"]
        build(Ypc, "Ypc", Pv[:, 0:8, gs].rearrange("p e g -> p g e"), Qv[:, 0:8, gs].rearrange("p e g -> p g e"),
              Rp[:, gs, :], Sp[:, gs, :], pvk)
        build(Ec, "Ec", Pv[:, 8:16, gs].rearrange("p e g -> p g e"), Qv[:, 8:16, gs].rearrange("p e g -> p g e"),
              Rp[:, gs, :], Sp[:, gs, :], pvk)
        for g4 in range(8):
            b = g4 % 2
            for j in range(4):
                g = g4 * 4 + j
                S.op("pe", lambda e, b=b, j=j, g=g: e.matmul(pM[b][:, j, :], lhsT=X7c[:, g, :], rhs=Ypc[:, g, :],
                                                             start=True, stop=True),
                     reads=["X7c", "Ypc"], writes=[("pM", b)])
            dve(lambda e, b=b, g4=g4: e.tensor_tensor(
                out=Mc[:, g4 * 4:(g4 + 1) * 4, :], in0=pM[b][:],
                in1=maskM[:].unsqueeze(1).to_broadcast([P, 4, P]), op=ALU.mult),
                [("pM", b), "maskM"], [("Mc", g4)])
        for g8 in range(4):
            b = g8 % 2
            for j in range(8):
                g = g8 * 8 + j
                S.op("pe", lambda e, b=b, j=j, g=g: e.transpose(pG[b][:, j, :], X7c[:, g, :], identb[:]),
                     reads=["X7c", "identb"], writes=[("pG", b)])
            S.op("act", lambda e, b=b, g8=g8: e.activation(out=Gc[:, g8 * 8:(g8 + 1) * 8, :], in_=pG[b][:],
                                                           func=AF.Copy),
                 reads=[("pG", b)], writes=[("Gc", g8)])
        S.dma("sp", SM[:, gs, :], Mc[:], key="SMst", reads=[("Mc", i) for i in range(8)], is_output=True)
        S.dma("sp", SG[:, gs, :], Gc[:], key="SGst", reads=[("Gc", i) for i in range(4)], is_output=True)
        S.dma("sp", SE[:, gs, :], Ec[:], key="SEst", reads=["Ec"], is_output=True)


def s5main_body(S, A, zo, zp, SM, SG, SE, A8S, selm_ap, seqm_ap, ident_ap, s5in, YS, s5p_out, s5s_out):
    H = 64
    ident = A.sb("m_ident", [P, P], F32)
    S.dma("sp", ident[:], ident_ap, key="ident", writes=["ident"])
    selm = A.sb("m_selm", [P, 8], F32)
    S.dma("sp", selm[:], selm_ap, key="selm", writes=["selm"])
    bsf = A.sb("m_bsf", [P, 16], F32)
    bsel = A.sb("m_bsel", [P, 16], BF16)
    S.dma("sp", bsf[:], seqm_ap, key="bsf", writes=["bsf"])
    S.op("dve", lambda e: e.tensor_copy(out=bsel[:], in_=bsf[:]), reads=["bsf"], writes=["bsel"])
    a8 = A.sb("m_a8", [H, 2, P], F32)
    S.dma("sp", a8[:], A8S, key="a8", writes=["a8"])
    AA = A.sb("m_AA", [H, 2, P], F32)
    AB = A.sb("m_AB", [H, 2, P], F32)
    S.op("dve", lambda e: e.tensor_copy(out=AA[:, 0, :], in_=a8[:, 0, :]), reads=["a8"], writes=["AA0"])
    S.op("dve", lambda e: e.tensor_copy(out=AA[:, 1, :], in_=a8[:, 0, :]), reads=["a8"], writes=["AA1"])
    S.op("dve", lambda e: e.tensor_scalar(out=AB[:, 0, :], in0=a8[:, 1, :], scalar1=-1.0, scalar2=None,
                                          op0=ALU.mult), reads=["a8"], writes=["AB0"])
    S.op("dve", lambda e: e.tensor_copy(out=AB[:, 1, :], in_=a8[:, 1, :]), reads=["a8"], writes=["AB1"])
    AK = ["AA0", "AA1", "AB0", "AB1"]
    Gm = A.sb("m_G", [P, P, P], BF16)
    for ch in range(4):
        S.dma("sp", Gm[:, ch * 32:(ch + 1) * 32, :], SG[:, ch * 32:(ch + 1) * 32, :], key=("Gl", ch),
              writes=[("G", ch)])
    MEc = [A.sb("m_ME%d" % i, [P, 32, 2, P], BF16) for i in range(2)]
    uin = [A.sb("m_uin%d" % i, [P, 2048], F32) for i in range(2)]
    urep = A.sb("m_urep", [P, 32, 128], BF16)
    Uts = [A.sb("m_Ut%d" % i, [P, P, 16], BF16) for i in range(2)]
    VH = A.sb("m_VH", [H, 2, P, 17], F32)
    Hbfs = [A.sb("m_Hbf%d" % i, [P, P, 16], BF16) for i in range(2)]
    ysbs = [A.sb("m_ysb%d" % i, [16, 32, 128], F32) for i in range(2)]
    P1 = A.sb("m_P1", [H, 2, P], F32)
    P2 = A.sb("m_P2", [H, 2, P], F32)
    Vs = A.sb("m_Vs", [H, 2, P, 16], F32)
    H0s = A.sb("m_H0s", [H, 2, P, 16], F32)
    psU = [A.ps("m_psU%d" % i, [P, 32, 16]) for i in range(2)]
    psV = [A.ps("m_psV%d" % i, [P, 32, 16]) for i in range(2)]
    psY = [A.ps("m_psY%d" % i, [P, 4, P]) for i in range(2)]
    ptr = A.ps("m_ptr", [P, 4, P])
    S.op("pool", lambda e: e.memset(VH[:], 0.0), writes=["VH"])
    cn = {"u": 0, "U": 0, "V": 0, "Y": 0, "me": 0, "ys": 0}

    def make_U(src_ap, ub):
        Ut = Uts[ub]
        i = cn["u"] % 2
        cn["u"] += 1
        S.dma("sp", uin[i][:], src_ap, key=("uin", i), writes=[("uin", i)])
        for ch in range(4):
            uv = uin[i][:, ch * 512:(ch + 1) * 512].rearrange("p (g q) -> p g q", q=16)
            S.op("dve", lambda e, uv=uv: e.tensor_tensor(
                out=urep[:].rearrange("p g (s q) -> p g s q", q=16),
                in0=uv.unsqueeze(2).to_broadcast([P, 32, 8, 16]),
                in1=selm[:].unsqueeze(1).unsqueeze(3).to_broadcast([P, 32, 8, 16]), op=ALU.mult),
                reads=[("uin", i), "selm"], writes=["urep"])
            b = cn["U"] % 2
            cn["U"] += 1
            for j in range(32):
                S.op("pe", lambda e, b=b, j=j: e.matmul(psU[b][:, j, :], lhsT=urep[:, j, :], rhs=bsel[:],
                                                        start=True, stop=True),
                     reads=["urep", "bsel"], writes=[("psU", b)])
            S.op("act", lambda e, b=b, ch=ch: e.activation(out=Ut[:, ch * 32:(ch + 1) * 32, :], in_=psU[b][:],
                                                           func=AF.Copy),
                 reads=[("psU", b)], writes=[("Ut", ub, ch)])

    def make_V(dst_fn, dkey, ub):
        Ut = Uts[ub]
        for ch in range(4):
            b = cn["V"] % 2
            cn["V"] += 1
            for j in range(32):
                g = ch * 32 + j
                S.op("pe", lambda e, b=b, j=j, g=g: e.matmul(psV[b][:, j, :], lhsT=Gm[:, g, :], rhs=Ut[:, g, :],
                                                             start=True, stop=True),
                     reads=[("G", ch), ("Ut", ub, ch)], writes=[("psV", b)])
            for c in range(2):
                S.op("act", lambda e, b=b, c=c, ch=ch: e.activation(
                    out=dst_fn(c, ch), in_=psV[b][c * H:(c + 1) * H, :, :], func=AF.Copy),
                    reads=[("psV", b)], writes=[dkey])

    def cstep(Hj0, Hj1, Hj, Hn, key, w, extra=(), tk=("P1", "P2a", "P2b")):
        p1, p2 = w
        S.op("dve", lambda e: e.tensor_tensor(out=p1, in0=AA_v(Hj), in1=Hj, op=ALU.mult),
             reads=[key] + AK + list(extra), writes=[tk[0]])
        S.op("dve", lambda e: e.tensor_tensor(out=sub(p2, 0), in0=AB_v(Hj, 0), in1=Hj1, op=ALU.mult),
             reads=[key] + AK + list(extra), writes=[tk[1]])
        S.op("dve", lambda e: e.tensor_tensor(out=sub(p2, 1), in0=AB_v(Hj, 1), in1=Hj0, op=ALU.mult),
             reads=[key] + AK + list(extra), writes=[tk[2]])
        S.op("dve", lambda e: e.tensor_tensor(out=Hn, in0=Hn, in1=p1, op=ALU.add), reads=[key, tk[0]], writes=[key])
        S.op("dve", lambda e: e.tensor_tensor(out=Hn, in0=Hn, in1=p2, op=ALU.add),
             reads=[key, tk[1], tk[2]], writes=[key])

    def sub(ap, c):
        return ap[:, c]

    def AA_v(like):
        if len(like.shape) == 3:
            return AA[:]
        return AA[:].unsqueeze(3).to_broadcast([H, 2, P, like.shape[-1]])

    def AB_v(like, c):
        if len(like.shape) == 3:
            return AB[:, c, :]
        return AB[:, c, :].unsqueeze(2).to_broadcast([H, P, like.shape[-1]])

    def make_Hbf(src, skey, hb):
        Hbf = Hbfs[hb]
        S.op("act", lambda e: e.activation(out=Hbf[0:H], in_=src[:, 0, :, 0:16], func=AF.Copy),
             reads=[skey], writes=[("Hbf0", hb)])
        S.op("pool", lambda e: e.tensor_copy(out=Hbf[H:P], in_=src[:, 1, :, 0:16]),
             reads=[skey], writes=[("Hbf1", hb)])

    def make_Y(t, ub, hb):
        Ut = Uts[ub]
        Hbf = Hbfs[hb]
        for ch in range(4):
            ysi = cn["ys"] % 2
            cn["ys"] += 1
            ysb = ysbs[ysi]
            gs = slice(ch * 32, ch * 32 + 32)
            mi = cn["me"] % 2
            cn["me"] += 1
            S.dma("sp", MEc[mi][:, :, 0, :], SM[:, gs, :], key=("ME", mi), writes=[("ME", mi)])
            S.dma("sp", MEc[mi][:, :, 1, :], SE[:, gs, :], key=("ME", mi), writes=[("ME", mi)])
            for g4 in range(8):
                b = cn["Y"] % 2
                cn["Y"] += 1
                for j in range(4):
                    gl = g4 * 4 + j
                    g = ch * 32 + gl
                    S.op("pe", lambda e, b=b, j=j, g=g, gl=gl, mi=mi: e.matmul(
                        psY[b][0:16, j, :], lhsT=Ut[:, g, :], rhs=MEc[mi][:, gl, 0, :], start=True, stop=False),
                        reads=[("Ut", ub, ch), ("ME", mi)], writes=[("psY", b)])
                    S.op("pe", lambda e, b=b, j=j, g=g, gl=gl, mi=mi: e.matmul(
                        psY[b][0:16, j, :], lhsT=Hbf[:, g, :], rhs=MEc[mi][:, gl, 1, :], start=False, stop=True),
                        reads=[("Hbf0", hb), ("Hbf1", hb), ("ME", mi)], writes=[("psY", b)])
                S.op("act", lambda e, b=b, g4=g4, ysb=ysb: e.activation(out=ysb[:, g4 * 4:(g4 + 1) * 4, :],
                                                                        in_=psY[b][0:16, :, :], func=AF.Copy),
                     reads=[("psY", b)], writes=[("ysb", ysi)])
            dst = YS[t * P:(t + 1) * P, ch * 512:(ch + 1) * 512].rearrange("(b i) (g p) -> b i g p", i=8, p=16)
            for i_ in range(8):
                S.dma("act", dst[:, i_], ysb[:, :, i_ * 16:(i_ + 1) * 16], key=("ysb_st", ysi),
                      reads=[("ysb", ysi)], is_output=True)

    def vh_dst(c, ch):
        return VH[:, c, ch * 32:(ch + 1) * 32, 1:17]

    tiles = [(zp[t * P:(t + 1) * P, 6144:8192], t, False) for t in range(8)] + \
            [(zo[t * P:(t + 1) * P, 12288:14336], t, True) for t in range(8)]
    make_U(tiles[0][0], 0)
    make_V(vh_dst, "VH", 0)
    for idx, (src_ap, t, own) in enumerate(tiles):
        cur = idx % 2
        nxt = idx + 1 < len(tiles)
        if nxt:
            make_U(tiles[idx + 1][0], 1 - cur)
        for j in range(16):
            cstep(VH[:, 0, :, j], VH[:, 1, :, j], VH[:, :, :, j], VH[:, :, :, j + 1], "VH", (P1[:], P2[:]))
        if own:
            make_Hbf(VH, "VH", cur)
        S.op("dve", lambda e: e.tensor_copy(out=VH[:, :, :, 0], in_=VH[:, :, :, 16]), reads=["VH"], writes=["VH"])
        if nxt:
            make_V(vh_dst, "VH", 1 - cur)
        if own:
            make_Y(t, cur, cur)
    hout = A.sb("m_hout", [P, 16, H], F32)
    for c in range(2):
        S.op("pe", lambda e, c=c: e.transpose(ptr[:, c, 0:H], VH[:, c, :, 0], ident[0:H, 0:H]),
             reads=["VH", "ident"], writes=["ptr"])
    S.op("act", lambda e: e.activation(out=hout[:, 0:2, :], in_=ptr[:, 0:2, 0:H], func=AF.Copy),
         reads=["ptr"], writes=["hout"])
    S.dma("sp", s5p_out.rearrange("c g n -> g c n"), hout[:, 0:2, :], key="s5p_st", reads=["hout"],
          writes=["s5p_dram"], is_output=True)

    hraw = A.sb("m_hraw", [P, 16, H], F32)
    for c in range(2):
        S.dma("sp", hraw[:], s5in[c].rearrange("s g n -> g s n"), key="hraw", writes=["hraw"])
        for s4 in range(4):
            for j in range(4):
                s_ = s4 * 4 + j
                S.op("pe", lambda e, j=j, s_=s_: e.transpose(ptr[0:H, j, :], hraw[:, s_, :], ident[:]),
                     reads=["hraw", "ident"], writes=["ptr"])
            S.op("act", lambda e, c=c, s4=s4: e.activation(
                out=H0s[:, c, :, s4 * 4:(s4 + 1) * 4].rearrange("p g s -> p s g"), in_=ptr[0:H, :, :],
                func=AF.Copy), reads=["ptr"], writes=["H0s"])
    make_U(zo[1024:1152, 12288:14336], 0)
    make_V(lambda c, ch: Vs[:, c, ch * 32:(ch + 1) * 32, :], "Vs", 0)
    make_Hbf(H0s, "H0s", 0)
    make_Y(8, 0, 0)
    P1s = uin[0][0:H, :].rearrange("p (c g s) -> p c g s", c=2, g=P)
    P2s = uin[1][0:H, :].rearrange("p (c g s) -> p c g s", c=2, g=P)
    for hh_ in range(2):
        hs = slice(hh_ * 8, hh_ * 8 + 8)
        cstep(H0s[:, 0, :, hs], H0s[:, 1, :, hs], H0s[:, :, :, hs], Vs[:, :, :, hs], "Vs", (P1s, P2s),
              extra=["H0s"], tk=(("uin", 0), ("uin", 1), ("uin", 1)))
    for c in range(2):
        for s8 in range(2):
            for j in range(8):
                s_ = s8 * 8 + j
                S.op("pe", lambda e, c=c, j=j, s_=s_: e.transpose(
                    ptr[:, j // 2, (j % 2) * H:(j % 2 + 1) * H], Vs[:, c, :, s_], ident[0:H, 0:H]),
                    reads=["Vs", "ident"], writes=["ptr"])
            S.op("act", lambda e, s8=s8: e.activation(
                out=hout[:, s8 * 8:(s8 + 1) * 8, :], in_=ptr[:].rearrange("p a (b n) -> p (a b) n", n=H),
                func=AF.Copy), reads=["ptr"], writes=["hout"])
        S.dma("sp", s5s_out[c].rearrange("s g n -> g s n"), hout[:], key="s5s_st", reads=["hout"],
              writes=["s5s_dram"], is_output=True)


def build_program(debug=False, stages=None, scr_in=()):
    nc = bass.Bass("TRN2", target_bir_lowering=False)
    NT = NTOK_OWN // P

    def din(name, shape, dt=F32):
        return nc.dram_tensor(name, list(shape), dt, kind="ExternalInput").ap()

    def dout(name, shape, dt=F32):
        return nc.dram_tensor(name, list(shape), dt, kind="ExternalOutput").ap()

    def dscr(name, shape, dt=F32):
        kind = "ExternalOutput" if debug else "Internal"
        if name in scr_in:
            kind = "ExternalInput"
        return nc.dram_tensor(name, list(shape), dt, kind=kind).ap()

    xo = din("xo", [NTOK_OWN, D_MODEL])
    xp = din("xp", [NTOK_PRE, D_MODEL])
    mem = din("mem", [256, D_MODEL])
    w_in = din("w_in", [D_MODEL, IN_WIDTH])
    w_mem_kv = din("w_mem_kv", [D_MODEL, 4096])
    ident = din("ident", [P, P])

    memkv = dout("memkv", [256, 4096])
    zo = dscr("zo", [NTOK_OWN, IN_WIDTH])
    zp = dscr("zp", [NTOK_PRE, 8192])

    def st_mem(S, A):
        blocks = [(c0, AF.Copy, (lambda t, c0=c0: memkv[t * P:(t + 1) * P, c0:c0 + 256]))
                  for c0 in range(0, 4096, 256)]
        gemm_body(S, A, "m", mem, 2, w_mem_kv, blocks, ident)
    if stages is None or 'mem' in stages:
        run_stage(nc, st_mem)

    def st_pre(S, A):
        blocks = []
        for (src0, n, dst0) in ((2048, 2048, 0), (4096, 4096, 2048), (12288, 2048, 6144)):
            for c in range(0, n, 256):
                blocks.append((src0 + c, AF.Copy,
                               (lambda t, d=dst0 + c: zp[t * P:(t + 1) * P, d:d + 256])))
        gemm_body(S, A, "p", xp, NTOK_PRE // P, w_in, blocks, ident)
    if stages is None or 'pre' in stages:
        run_stage(nc, st_pre)

    def st_own(S, A):
        blocks = [(c0, col_func(c0), (lambda t, c0=c0: zo[t * P:(t + 1) * P, c0:c0 + 256]))
                  for c0 in range(0, IN_WIDTH, 256)]
        gemm_body(S, A, "o", xo, NTOK_OWN // P, w_in, blocks, ident)
    if stages is None or 'own' in stages:
        run_stage(nc, st_own)

    rq = din("rq", [9, P, 2, 16, 64])
    rk_own = din("rk_own", [9, P, 2, 16, 64])
    rk_pre = din("rk_pre", [8, P, 2, 16, 64])
    tabs = {"rq": rq, "rk_own": rk_own, "rk_pre": rk_pre,
            "mask_p": din("mask_p", [P, P]), "mask_s": din("mask_s", [P, P]),
            "seqm": din("seqm", [P, 16]), "seqmT": din("seqmT", [P, 16, P])}
    sret_in = din("sret_in", [16, 16, P, 256])
    sret_out = dout("sret_out", [16, 16, P, 256])
    sretp_out = dout("sretp_out", [16, P, 256])
    OT = dscr("OT", [9, P, 64, P], BF16)

    def st_ret(S, A):
        retention_body(S, A, zo, zp, tabs, sret_in, sret_out, sretp_out, OT, ident)
    if stages is None or 'ret' in stages:
        run_stage(nc, st_ret)

    cmk = din("cmk", [16, 256, 2048])
    cmv = din("cmv", [16, 256, 2048])

    def st_x(S, A):
        xattn_body(S, A, zo, memkv, cmk, cmv, tabs["seqmT"], OT, ident)
    if stages is None or 'x' in stages:
        run_stage(nc, st_x)

    prm = {"a_re": din("s5_a_re", [P, 64]), "a_im": din("s5_a_im", [P, 64]), "log_step": din("s5_log_step", [1, P]),
           "b_re": din("s5_b_re", [P, 1024]), "b_im": din("s5_b_im", [P, 1024]),
           "c_re": din("s5_c_re", [P, 1024]), "c_im": din("s5_c_im", [P, 1024])}
    maskM = din("maskM", [P, P])
    SM = dscr("SM", [P, P, P], BF16)
    SG = dscr("SG", [P, P, P], BF16)
    SE = dscr("SE", [P, P, P], BF16)
    A8S = dscr("A8S", [64, 2, P])

    def st_s5prep(S, A):
        s5prep_body(S, A, prm, ident, maskM, SM, SG, SE, A8S)
    if stages is None or 's5prep' in stages:
        run_stage(nc, st_s5prep)

    selm = din("selm", [P, 8])
    s5in = din("s5in", [2, 16, P, 64])
    YS = dscr("YS", [NTOK_OWN, 2048])
    s5p_out = dout("s5p_out", [2, P, 64])
    s5s_out = dout("s5s_out", [2, 16, P, 64])

    def st_s5main(S, A):
        s5main_body(S, A, zo, zp, SM, SG, SE, A8S, selm, tabs["seqm"], ident, s5in, YS, s5p_out, s5s_out)
    if stages is None or 's5main' in stages:
        run_stage(nc, st_s5main)

    s5d = din("s5_d", [1, 2048])
    w_glu = din("w_glu", [2048, 4096])
    GL = dscr("GL", [NTOK_OWN, 2048])
    GAB = dscr("GAB", [NTOK_OWN, 4096])

    def st_gelu(S, A):
        db = A.sb("g_db", [P, 2048], F32)
        S.dma("sp", db[:], s5d.to_broadcast([P, 2048]), key="db", writes=["db"])
        yb = [A.sb("g_y%d" % i, [P, 2048], F32) for i in range(2)]
        ub = [A.sb("g_u%d" % i, [P, 2048], F32) for i in range(2)]
        tb = [A.sb("g_t%d" % i, [P, 2048], F32) for i in range(2)]
        for t in range(NT):
            i = t % 2
            y, u, tt_ = yb[i], ub[i], tb[i]
            S.dma("sp", y[:], YS[t * P:(t + 1) * P, :], key=("y", i), writes=[("y", i)])
            S.dma("sp", u[:], zo[t * P:(t + 1) * P, 12288:14336], key=("u", i), writes=[("u", i)])
            S.op("pool", lambda e, u=u: e.tensor_tensor(out=u[:], in0=u[:], in1=db[:], op=ALU.mult),
                 reads=[("u", i), "db"], writes=[("u", i)])
            S.op("dve", lambda e, y=y, u=u: e.tensor_tensor(out=y[:], in0=y[:], in1=u[:], op=ALU.add),
                 reads=[("y", i), ("u", i)], writes=[("y", i)])
            S.op("pool", lambda e, y=y, tt_=tt_: e.tensor_tensor(out=tt_[:], in0=y[:], in1=y[:], op=ALU.mult),
                 reads=[("y", i)], writes=[("t", i)])
            S.op("dve", lambda e, tt_=tt_: e.tensor_scalar(out=tt_[:], in0=tt_[:], scalar1=0.044715, scalar2=1.0,
                                                           op0=ALU.mult, op1=ALU.add),
                 reads=[("t", i)], writes=[("t", i)])
            S.op("dve", lambda e, y=y, tt_=tt_: e.tensor_tensor(out=tt_[:], in0=tt_[:], in1=y[:], op=ALU.mult),
                 reads=[("t", i), ("y", i)], writes=[("t", i)])
            S.op("act", lambda e, tt_=tt_: e.activation(out=tt_[:], in_=tt_[:], func=AF.Sigmoid,
                                                        scale=1.5957691216057308),
                 reads=[("t", i)], writes=[("t", i)])
            S.op("dve", lambda e, y=y, tt_=tt_: e.tensor_tensor(out=y[:], in0=y[:], in1=tt_[:], op=ALU.mult),
                 reads=[("t", i), ("y", i)], writes=[("y", i)])
            S.dma("sp", GL[t * P:(t + 1) * P, :], y[:], key=("gl", i), reads=[("y", i)], is_output=True)

    def st_glu(S, A):
        blocks = [(c0, (AF.Copy if c0 < 2048 else AF.Sigmoid),
                   (lambda t, c0=c0: GAB[t * P:(t + 1) * P, c0:c0 + 256])) for c0 in range(0, 4096, 256)]
        gemm_body(S, A, "g", GL, NT, w_glu, blocks, ident, nkt=16)

    def st_s5fin(S, A):
        identf = A.sb("f_ident", [P, P], F32)
        identb = A.sb("f_identb", [P, P], BF16)
        S.dma("sp", identf[:], ident, key="ident", writes=["ident"])
        S.op("dve", lambda e: e.tensor_copy(out=identb[:], in_=identf[:]), reads=["ident"], writes=["identb"])
        ab = [A.sb("f_ab%d" % i, [P, 4096], F32) for i in range(2)]
        gg = [A.sb("f_g%d" % i, [P, 2048], F32) for i in range(2)]
        ob = [A.sb("f_ob%d" % i, [P, 16, P], BF16) for i in range(2)]
        oT = [A.sb("f_oT%d" % i, [P, 16, P], BF16) for i in range(2)]
        ptr = [A.ps("f_ptr%d" % i, [P, 8, P], BF16) for i in range(2)]
        cnt = 0
        for t in range(NT):
            i = t % 2
            S.dma("sp", ab[i][:], GAB[t * P:(t + 1) * P, :], key=("ab", i), writes=[("ab", i)])
            S.dma("sp", gg[i][:], zo[t * P:(t + 1) * P, 14336:16384], key=("gg", i), writes=[("gg", i)])
            S.op("pool", lambda e, i=i: e.tensor_tensor(out=gg[i][:], in0=gg[i][:], in1=ab[i][:, 2048:4096],
                                                        op=ALU.mult),
                 reads=[("gg", i), ("ab", i)], writes=[("gg", i)])
            S.op("dve", lambda e, i=i: e.tensor_tensor(out=ob[i][:].rearrange("p a b -> p (a b)"),
                                                       in0=ab[i][:, 0:2048], in1=gg[i][:], op=ALU.mult),
                 reads=[("gg", i), ("ab", i)], writes=[("ob", i)])
            for half in range(2):
                b = cnt % 2
                cnt += 1
                for j in range(8):
                    S.op("pe", lambda e, b=b, j=j, i=i, half=half: e.transpose(
                        ptr[b][:, j, :], ob[i][:, half * 8 + j, :], identb[:]),
                        reads=[("ob", i), "identb"], writes=[("ptr", b)])
                S.op("act", lambda e, b=b, i=i, half=half: e.activation(
                    out=oT[i][:, half * 8:(half + 1) * 8, :], in_=ptr[b][:], func=AF.Copy),
                    reads=[("ptr", b)], writes=[("oT", i, half)])
            S.dma("act", OT[t, :, 32:48, :], oT[i][:], key=("oTs", i), reads=[("oT", i, 0), ("oT", i, 1)],
                  is_output=True)
    if stages is None or 's5post' in stages:
        run_stage(nc, st_gelu)
        run_stage(nc, st_glu)
        run_stage(nc, st_s5fin)

    w_pa = din("w_proj_a", [4096, D_MODEL])
    w_pb = din("w_proj_b", [2048, D_MODEL])
    w_pc = din("w_proj_c", [2048, D_MODEL])
    w_o = din("w_out", [D_MODEL, D_MODEL])
    ln_g = din("ln_g", [1, D_MODEL])
    ln_b = din("ln_b", [1, D_MODEL])
    y_out = dout("y_out", [NTOK_OWN, D_MODEL])
    PR = [dscr("PR%d" % i, [NTOK_OWN, D_MODEL]) for i in range(3)]
    HP = dscr("HP", [NTOK_OWN, D_MODEL])
    NT = NTOK_OWN // P

    def proj_stage(i, w_ap, ft0, nkt, gate0):
        def body(S, A):
            gt = [A.sb("gt%d_%d" % (i, j), [P, NT, 256], F32) for j in range(2)]

            def pre_block(S, bi):
                c0 = bi * 256
                S.dma("sp", gt[bi % 2][:], zo[:, gate0 + c0:gate0 + c0 + 256].rearrange("(t p) c -> p t c", p=P),
                      key=("gt", bi % 2), writes=[("gt", bi % 2)])

            def epi(S, t, bi, ps_ap, pkey, ob_t, okey):
                j = bi % 2
                S.op("dve", lambda e, j=j, t=t: e.tensor_tensor(out=ob_t[:], in0=ps_ap, in1=gt[j][:, t, :],
                                                                op=ALU.mult),
                     reads=[pkey, ("gt", j)], writes=[okey])
            blocks = [(c0, None, (lambda t, c0=c0: PR[i][t * P:(t + 1) * P, c0:c0 + 256]))
                      for c0 in range(0, D_MODEL, 256)]
            gemm_body(S, A, "j%d" % i, None, NT, w_ap, blocks, ident, nkt=nkt, a_T=(OT, ft0), epi=epi,
                      pre_block=pre_block)
        return body
    if stages is None or 'tail' in stages:
        run_stage(nc, proj_stage(0, w_pa, 0, 32, 20480))
        run_stage(nc, proj_stage(1, w_pb, 32, 16, 24576))
        run_stage(nc, proj_stage(2, w_pc, 48, 16, 28672))

    def st_out(S, A):
        xr = [A.sb("xr%d" % j, [P, NT, 256], F32) for j in range(2)]
        alpha = float((2.0 * 1) ** 0.25)

        def pre_block(S, bi):
            c0 = bi * 256
            S.dma("sp", xr[bi % 2][:], xo[:, c0:c0 + 256].rearrange("(t p) c -> p t c", p=P),
                  key=("xr", bi % 2), writes=[("xr", bi % 2)])

        def epi(S, t, bi, ps_ap, pkey, ob_t, okey):
            j = bi % 2
            S.op("dve", lambda e, j=j, t=t: e.scalar_tensor_tensor(out=ob_t[:], in0=xr[j][:, t, :], scalar=alpha,
                                                                   in1=ps_ap, op0=ALU.mult, op1=ALU.add),
                 reads=[pkey, ("xr", j)], writes=[okey])
        blocks = [(c0, None, (lambda t, c0=c0: HP[t * P:(t + 1) * P, c0:c0 + 256]))
                  for c0 in range(0, D_MODEL, 256)]
        gemm_body(S, A, "w", PR[0], NT, w_o, blocks, ident, x_sum=[PR[1], PR[2]], epi=epi, pre_block=pre_block)
    if stages is None or 'tail' in stages or 'out' in stages:
        run_stage(nc, st_out)

    def st_ln(S, A):
        gb = A.sb("ln_gb", [P, D_MODEL], F32)
        bb = A.sb("ln_bb", [P, D_MODEL], F32)
        S.dma("sp", gb[:], ln_g.to_broadcast([P, D_MODEL]), key="gb", writes=["gb"])
        S.dma("sp", bb[:], ln_b.to_broadcast([P, D_MODEL]), key="bb", writes=["bb"])
        hb = [A.sb("ln_h%d" % j, [P, D_MODEL], F32) for j in range(2)]
        st = A.sb("ln_st", [P, 8, 6], F32)
        mv = A.sb("ln_mv", [P, 2], F32)
        nb = A.sb("ln_nb", [P, 1], F32)
        for t in range(NT):
            j = t % 2
            h = hb[j]
            S.dma("sp", h[:], HP[t * P:(t + 1) * P, :], key=("h", j), writes=[("h", j)])
            for c in range(8):
                S.op("dve", lambda e, c=c, h=h: e.bn_stats(out=st[:, c, :], in_=h[:, c * 512:(c + 1) * 512]),
                     reads=[("h", j)], writes=[("st", c)])
            S.op("dve", lambda e: e.bn_aggr(out=mv[:], in_=st[:].rearrange("p a b -> p (a b)")),
                 reads=[("st", c) for c in range(8)], writes=["mv"])
            S.op("dve", lambda e: e.tensor_scalar(out=mv[:, 1:2], in0=mv[:, 1:2], scalar1=1e-5, scalar2=None,
                                                  op0=ALU.add), reads=["mv"], writes=["mv"])
            S.op("act", lambda e: e.activation(out=mv[:, 1:2], in_=mv[:, 1:2], func=AF.Sqrt),
                 reads=["mv"], writes=["mv"])
            S.op("dve", lambda e: e.reciprocal(out=mv[:, 1:2], in_=mv[:, 1:2]), reads=["mv"], writes=["mv"])
            S.op("dve", lambda e: e.scalar_tensor_tensor(out=nb[:], in0=mv[:, 0:1], scalar=-1.0, in1=mv[:, 1:2],
                                                         op0=ALU.mult, op1=ALU.mult),
                 reads=["mv"], writes=["nb"])
            S.op("act", lambda e, h=h: e.activation(out=h[:], in_=h[:], func=AF.Identity, bias=nb[:],
                                                    scale=mv[:, 1:2]),
                 reads=[("h", j), "mv", "nb"], writes=[("h", j)])
            S.op("dve", lambda e, h=h: e.tensor_tensor(out=h[:], in0=h[:], in1=gb[:], op=ALU.mult),
                 reads=[("h", j), "gb"], writes=[("h", j)])
            S.op("dve", lambda e, h=h: e.tensor_tensor(out=h[:], in0=h[:], in1=bb[:], op=ALU.add),
                 reads=[("h", j), "bb"], writes=[("h", j)])
            S.dma("sp", y_out[t * P:(t + 1) * P, :], h[:], key=("hout", j), reads=[("h", j)], is_output=True)
    if stages is None or 'tail' in stages or 'ln' in stages:
        run_stage(nc, st_ln)

    return nc


def host_tables(hf):
    f = np.float32
    inv = (1.0 / (np.float32(10000.0) ** (np.arange(64, dtype=f) / np.float32(64)))).astype(f)
    g = np.array(RET_G, dtype=np.float64)
    i = np.arange(P)

    def tab(pos, il, kind):
        ang = pos.astype(f)[:, None] * inv[None, :]
        c, s_ = np.cos(ang).astype(f), np.sin(ang).astype(f)
        if kind == "q":
            sc = g[None, :] ** (il[:, None] + 1.0)
        else:
            sc = g[None, :] ** (-(il[:, None] + 1.0)) * (128.0 ** -0.5)
        out = np.empty((P, 2, 16, 64), f)
        out[:, 0] = (c[:, None, :] * sc[:, :, None]).astype(f)
        out[:, 1] = (s_[:, None, :] * sc[:, :, None]).astype(f)
        return out
    rq = np.stack([tab(hf * 1024 + t * P + i, i, "q") for t in range(8)] + [tab(16384 + (i % 8), i % 8, "q")])
    rk_own = np.stack([tab(hf * 1024 + t * P + i, i, "k") for t in range(8)] + [tab(16384 + (i % 8), i % 8, "k")])
    rk_pre = np.stack([tab(t * P + i, i, "k") for t in range(8)])
    mask_p = (i[None, :] >= i[:, None]).astype(f)
    same = (i[None, :] // 8) == (i[:, None] // 8)
    mask_s = (mask_p * same).astype(f)
    seqm = (i[:, None] // 8 == np.arange(16)[None, :]).astype(f)
    seqmT = np.ascontiguousarray(np.broadcast_to(seqm.T[None, :, :], (P, 16, P))).astype(f)
    maskM = ((i[None, :] // 16) >= (i[:, None] // 16)).astype(f)
    selm = (i[:, None] % 8 == np.arange(8)[None, :]).astype(f)
    return {"rq": rq, "rk_own": rk_own, "rk_pre": rk_pre, "mask_p": mask_p, "mask_s": mask_s,
            "seqm": seqm, "seqmT": seqmT, "maskM": maskM, "selm": selm}


_PROGRAM = None


def kernel(x_prompt, x_sample, mem_prompt, state_ret, state_s5_re, state_s5_im, cache_mem_k, cache_mem_v,
           w_in, w_mem_kv, s5_a_re, s5_a_im, s5_log_step, s5_b_re, s5_b_im, s5_c_re, s5_c_im, s5_d, w_glu,
           w_proj_a, w_proj_b, w_proj_c, w_out, ln_g, ln_b):
    global _PROGRAM
    if _PROGRAM is None:
        _PROGRAM = build_program()
    nc = _PROGRAM
    f = np.float32
    x_prompt = np.asarray(x_prompt, f)
    x_sample = np.asarray(x_sample, f)
    w_in0 = np.ascontiguousarray(np.asarray(w_in, f)[0])
    w_mem0 = np.ascontiguousarray(np.asarray(w_mem_kv, f)[0])
    ident = np.eye(P, dtype=f)
    wpa = np.ascontiguousarray(np.asarray(w_proj_a, f)[0])
    wpb = np.ascontiguousarray(np.asarray(w_proj_b, f)[0])
    wpc = np.ascontiguousarray(np.asarray(w_proj_c, f)[0])
    wout = np.ascontiguousarray(np.asarray(w_out, f)[0])
    lng = np.ascontiguousarray(np.asarray(ln_g, f).reshape(1, D_MODEL))
    lnb = np.ascontiguousarray(np.asarray(ln_b, f).reshape(1, D_MODEL))
    s5p = {"a_re": np.ascontiguousarray(np.asarray(s5_a_re, f)[0]), "a_im": np.ascontiguousarray(np.asarray(s5_a_im, f)[0]),
           "log_step": np.ascontiguousarray(np.asarray(s5_log_step, f).reshape(1, P)),
           "b_re": np.ascontiguousarray(np.asarray(s5_b_re, f)[0].reshape(P, 1024)),
           "b_im": np.ascontiguousarray(np.asarray(s5_b_im, f)[0].reshape(P, 1024)),
           "c_re": np.ascontiguousarray(np.asarray(s5_c_re, f)[0].reshape(P, 1024)),
           "c_im": np.ascontiguousarray(np.asarray(s5_c_im, f)[0].reshape(P, 1024)),
           "d": np.ascontiguousarray(np.asarray(s5_d, f).reshape(1, 2048))}
    wglu = np.ascontiguousarray(np.asarray(w_glu, f)[0])
    in_maps = []
    for c in range(NCORES):
        b, hf = c // 2, c % 2
        xo = np.concatenate([x_prompt[b, hf * 1024:(hf + 1) * 1024],
                             x_sample[16 * c:16 * c + 16].reshape(128, D_MODEL)], axis=0)
        xp = x_prompt[b, 0:1024] if hf == 1 else np.zeros((1024, D_MODEL), f)
        in_maps.append({
            "xo": np.ascontiguousarray(xo), "xp": np.ascontiguousarray(xp),
            "mem": np.ascontiguousarray(np.asarray(mem_prompt, f)[b]),
            "w_in": w_in0, "w_mem_kv": w_mem0, "ident": ident,
            "sret_in": np.ascontiguousarray(np.asarray(state_ret, f)[0, 16 * c:16 * c + 16]),
            "cmk": np.ascontiguousarray(np.asarray(cache_mem_k, f)[0, 16 * c:16 * c + 16]).reshape(16, 256, 2048),
            "cmv": np.ascontiguousarray(np.asarray(cache_mem_v, f)[0, 16 * c:16 * c + 16]).reshape(16, 256, 2048),
            "w_proj_a": wpa, "w_proj_b": wpb, "w_proj_c": wpc, "w_out": wout, "ln_g": lng, "ln_b": lnb,
            "s5_a_re": s5p["a_re"], "s5_a_im": s5p["a_im"], "s5_log_step": s5p["log_step"],
            "s5_b_re": s5p["b_re"], "s5_b_im": s5p["b_im"], "s5_c_re": s5p["c_re"], "s5_c_im": s5p["c_im"],
            "s5_d": s5p["d"], "w_glu": wglu,
            "s5in": np.ascontiguousarray(np.stack([np.asarray(state_s5_re, f)[0, 16 * c:16 * c + 16],
                                                   np.asarray(state_s5_im, f)[0, 16 * c:16 * c + 16]])),
        })
        in_maps[-1].update(host_tables(hf))
    res = run_bass_kernel_spmd(nc, in_maps, core_ids=list(range(NCORES)))
    R = res.results
    memk = np.stack([R[2 * b]["memkv"][:, 0:2048].reshape(256, 4, 512) for b in range(4)])[None]
    memv = np.stack([R[2 * b]["memkv"][:, 2048:4096].reshape(256, 4, 512) for b in range(4)])[None]
    y_p = np.stack([np.concatenate([R[2 * b]["y_out"][:1024], R[2 * b + 1]["y_out"][:1024]], axis=0)
                    for b in range(4)])
    y_s = np.concatenate([R[c]["y_out"][1024:] for c in range(NCORES)], axis=0).reshape(128, 8, D_MODEL)
    sretp = np.stack([R[2 * b + 1]["sretp_out"] for b in range(4)])[None]
    srets = np.concatenate([R[c]["sret_out"] for c in range(NCORES)], axis=0)[None]
    s5p_re = np.stack([R[2 * b + 1]["s5p_out"][0] for b in range(4)])[None]
    s5p_im = np.stack([R[2 * b + 1]["s5p_out"][1] for b in range(4)])[None]
    s5s_re = np.concatenate([R[c]["s5s_out"][0] for c in range(NCORES)], axis=0)[None]
    s5s_im = np.concatenate([R[c]["s5s_out"][1] for c in range(NCORES)], axis=0)[None]
    return (y_p, y_s, sretp, s5p_re, s5p_im, memk, memv, srets, s5s_re, s5s_im)
```

```python
import math
from contextlib import ExitStack

import numpy as np
import concourse.bass as bass
import concourse.mybir as mybir
from concourse.bass_utils import run_bass_kernel_spmd

F32 = mybir.dt.float32
BF16 = mybir.dt.bfloat16
AF = mybir.ActivationFunctionType
ALU = mybir.AluOpType
P = 128
NCORES = 8

D_MODEL = 4096
IN_WIDTH = 32768
NTOK_OWN = 1152
NTOK_PRE = 1024

ENGS = ("pe", "act", "dve", "pool", "sp")
SEM_LIMIT = 20000


class Ins:
    __slots__ = ("eng", "fn", "deps", "is_dma", "key", "need_inc", "semref")

    def __init__(self, eng, fn, is_dma=False, key=None):
        self.eng = eng
        self.fn = fn
        self.deps = []
        self.is_dma = is_dma
        self.key = key
        self.need_inc = False
        self.semref = None


class Sched:
    _stage = 0

    def __init__(self, nc):
        Sched._stage += 1
        self.sid = Sched._stage
        self.nc = nc
        self.ins = []
        self.last_w = {}
        self.readers = {}
        self.dma_count = {}
        self.out_keys = set()

    def _add(self, ins, reads, writes):
        deps = set()
        for k in reads:
            w = self.last_w.get(k)
            if w is not None:
                deps.add(w)
        for k in writes:
            w = self.last_w.get(k)
            if w is not None:
                deps.add(w)
            for r in self.readers.get(k, ()):
                deps.add(r)
        deps.discard(ins)
        for d in deps:
            if d.is_dma:
                ins.deps.append((d, 16 * self.dma_count[d.key]))
            elif d.eng == ins.eng and not ins.is_dma:
                if ins.eng != "pe":
                    ins.deps.append((d, 0))
                    d.need_inc = True
            else:
                ins.deps.append((d, 0))
                d.need_inc = True
        for k in reads:
            self.readers.setdefault(k, []).append(ins)
        for k in writes:
            self.last_w[k] = ins
            self.readers[k] = []
        self.ins.append(ins)
        return ins

    def op(self, eng, fn, reads=(), writes=()):
        return self._add(Ins(eng, fn), list(reads), list(writes))

    def dma(self, eng, out, in_, key, reads=(), writes=(), is_output=False):
        ins = Ins(eng, lambda e: e.dma_start(out=out, in_=in_), is_dma=True, key=key)
        if is_output:
            self.out_keys.add(key)
        self.dma_count.setdefault(key, 0)
        self._add(ins, list(reads), list(writes))
        self.dma_count[key] += 1
        return ins

    def emit(self):
        nc = self.nc
        sem_names = []
        cur = {}
        for ins in self.ins:
            if ins.is_dma or not ins.need_inc:
                continue
            c = cur.get(ins.eng)
            if c is None or c[1] >= SEM_LIMIT:
                c = [len(sem_names), 0]
                sem_names.append("c%d_%s_%d" % (self.sid, ins.eng, len(sem_names)))
                cur[ins.eng] = c
            c[1] += 1
            ins.semref = (c[0], c[1])
        dma_keys = sorted(self.dma_count.keys(), key=str)
        csem = [nc.alloc_semaphore(name=n) for n in sem_names]
        dsem = {k: nc.alloc_semaphore(name="d%d_%d" % (self.sid, i)) for i, k in enumerate(dma_keys)}
        streams = {e: [i for i in self.ins if i.eng == e] for e in ENGS}
        final_dma = dict((k, 16 * v) for k, v in self.dma_count.items())
        out_keys = self.out_keys

        def run(engname, e):
            waited = {}
            for ins in streams[engname]:
                need = {}
                for d, dv in ins.deps:
                    if d.is_dma:
                        sk = ("d", d.key)
                        v = dv
                    else:
                        sk = ("c", d.semref[0])
                        v = d.semref[1]
                    if v > need.get(sk, 0):
                        need[sk] = v
                for sk, v in need.items():
                    if waited.get(sk, 0) >= v:
                        continue
                    waited[sk] = v
                    sem = dsem[sk[1]] if sk[0] == "d" else csem[sk[1]]
                    e.wait_ge(sem, v)
                r = ins.fn(e)
                if ins.is_dma:
                    r.then_inc(dsem[ins.key], 16)
                elif ins.need_inc:
                    r.then_inc(csem[ins.semref[0]], 1)
            if engname == "sp":
                for k in sorted(out_keys, key=str):
                    e.wait_ge(dsem[k], final_dma[k])

        with nc.Block() as block:
            @block.tensor
            def _(e):
                run("pe", e)

            @block.scalar
            def _(e):
                run("act", e)

            @block.vector
            def _(e):
                run("dve", e)

            @block.gpsimd
            def _(e):
                run("pool", e)

            @block.sync
            def _(e):
                run("sp", e)

        if not getattr(Sched, "NOCLEAR", False):
            nc.clear_and_free_semaphores(csem + list(dsem.values()))
        if not getattr(Sched, "NOCLEAR", False):
            nc.all_engine_barrier()


class Alloc:
    def __init__(self, nc, st):
        self.nc = nc
        self.st = st

    def sb(self, name, shape, dt):
        return self.st.enter_context(self.nc.sbuf_tensor(name, list(shape), dt))

    def ps(self, name, shape, dt=F32):
        return self.st.enter_context(self.nc.psum_tensor(name, list(shape), dt))


def run_stage(nc, body):
    with ExitStack() as st:
        S = Sched(nc)
        A = Alloc(nc, st)
        body(S, A)
        S.emit()


def gemm_body(S, A, uid, x_ap, ntile, w_ap, blocks, ident_ap, nkt=32, a_T=None, x_sum=None, epi=None,
              store_eng="act", pre_block=None):
    xT = A.sb("xT" + uid, [P, ntile, nkt, P], BF16)
    ident = A.sb("ident" + uid, [P, P], F32)
    S.dma("sp", ident[:], ident_ap, key="ident", writes=["ident"])
    pT = [A.ps("pT%d%s" % (i, uid), [P, 4, P]) for i in range(2)]
    cnt = 0
    if a_T is not None:
        OTd, ft0 = a_T
        for t in range(ntile):
            S.dma("sp", xT[:, t, :, :], OTd[t, :, ft0:ft0 + nkt, :], key=("xTl", t % 4),
                  writes=[("xT", t, kq) for kq in range(nkt // 4)])
    else:
        xin = [A.sb("xin%d%s" % (i, uid), [P, nkt * P], F32) for i in range(2)]
        if x_sum:
            xad = A.sb("xad" + uid, [P, nkt * P], F32)
    for t in range(ntile if a_T is None else 0):
        xi = t % 2
        S.dma("sp", xin[xi][:], x_ap[t * P:(t + 1) * P, :], key=("xin", xi), writes=[("xin", xi)])
        for extra in (x_sum or ()):
            S.dma("sp", xad[:], extra[t * P:(t + 1) * P, :], key="xad", writes=["xad"])
            S.op("dve", lambda e, xi=xi: e.tensor_tensor(out=xin[xi][:], in0=xin[xi][:], in1=xad[:], op=ALU.add),
                 reads=[("xin", xi), "xad"], writes=[("xin", xi)])
        for kq in range(nkt // 4):
            b = cnt % 2
            cnt += 1
            for j in range(4):
                kt = kq * 4 + j
                S.op("pe", lambda e, b=b, j=j, xi=xi, kt=kt: e.transpose(
                    pT[b][:, j, :], xin[xi][:, kt * P:(kt + 1) * P], ident[:]),
                    reads=[("xin", xi), "ident"], writes=[("pT", b)])
            if kq % 2 == 0:
                S.op("act", lambda e, b=b, t=t, kq=kq: e.activation(
                    out=xT[:, t, kq * 4:(kq + 1) * 4, :], in_=pT[b][:], func=AF.Copy),
                    reads=[("pT", b)], writes=[("xT", t, kq)])
            else:
                S.op("dve", lambda e, b=b, t=t, kq=kq: e.tensor_copy(
                    out=xT[:, t, kq * 4:(kq + 1) * 4, :], in_=pT[b][:]),
                    reads=[("pT", b)], writes=[("xT", t, kq)])

    stg = [A.sb("stg%d%s" % (i, uid), [P, 8, 256], F32) for i in range(3)]
    wb = [A.sb("wb%d%s" % (i, uid), [P, nkt, 256], BF16) for i in range(2)]
    pz = [A.ps("pz%d%s" % (i, uid), [P, 512]) for i in range(4)]
    ob = [A.sb("ob%d%s" % (i, uid), [P, 256], F32) for i in range(4)]
    w_view = w_ap.rearrange("(kt p) c -> p kt c", p=P)
    ctr = {"stg": 0, "pz": 0}

    def load_block(bi):
        c0 = blocks[bi][0]
        if pre_block is not None:
            pre_block(S, bi)
        for c in range(nkt // 8):
            s = ctr["stg"] % 3
            ctr["stg"] += 1
            S.dma("sp", stg[s][:], w_view[:, c * 8:(c + 1) * 8, c0:c0 + 256],
                  key=("stg", s), writes=[("stg", s)])
            if c % 2 == 0:
                S.op("dve", lambda e, bi=bi, c=c, s=s: e.tensor_copy(
                    out=wb[bi % 2][:, c * 8:(c + 1) * 8, :], in_=stg[s][:]),
                    reads=[("stg", s)], writes=[("wb", bi % 2, c)])
            else:
                S.op("act", lambda e, bi=bi, c=c, s=s: e.activation(
                    out=wb[bi % 2][:, c * 8:(c + 1) * 8, :], in_=stg[s][:], func=AF.Copy),
                    reads=[("stg", s)], writes=[("wb", bi % 2, c)])

    nb = len(blocks)
    if nb:
        load_block(0)
    for bi in range(nb):
        if bi + 1 < nb:
            load_block(bi + 1)
        _, func, out_fn = blocks[bi]
        for t in range(ntile if not getattr(Sched, 'NOMM', False) else 0):
            pb = ctr["pz"] % 4
            ctr["pz"] += 1
            for kt in range(nkt):
                S.op("pe", lambda e, pb=pb, t=t, kt=kt, bi=bi: e.matmul(
                    pz[pb][:, 0:256], lhsT=xT[:, t, kt, :], rhs=wb[bi % 2][:, kt, :],
                    start=(kt == 0), stop=(kt == nkt - 1)),
                    reads=[("xT", t, kt // 4), ("wb", bi % 2, kt // 8)], writes=[("pz", pb)])
            if epi is not None:
                epi(S, t, bi, pz[pb][:, 0:256], ("pz", pb), ob[pb], ("ob", pb))
            else:
                S.op("act", lambda e, pb=pb, func=func: e.activation(
                    out=ob[pb][:], in_=pz[pb][:, 0:256], func=func),
                    reads=[("pz", pb)], writes=[("ob", pb)])
            S.dma(store_eng, out_fn(t), ob[pb][:], key=("ob", pb), reads=[("ob", pb)], is_output=True)


def col_func(c0):
    if 8192 <= c0 < 12288 or 14336 <= c0 < 16384 or 18432 <= c0 < 20480:
        return AF.Silu
    if c0 >= 20480:
        return AF.Sigmoid
    return AF.Copy


RET_G = [1.0 - 2.0 ** (-5.0 - h) for h in range(16)]


def retention_body(S, A, zo, zp, tabs, sret_in, sret_out, sretp_out, OT, ident_ap):
    ident = A.sb("r_ident", [P, P], F32)
    identb = A.sb("r_identb", [P, P], BF16)
    S.dma("sp", ident[:], ident_ap, key="ident", writes=["ident"])
    S.op("dve", lambda e: e.tensor_copy(out=identb[:], in_=ident[:]), reads=["ident"], writes=["identb"])
    maskp = A.sb("r_maskp", [P, P], F32)
    masks = A.sb("r_masks", [P, P], F32)
    seqm = A.sb("r_seqm", [P, 16], F32)
    seqmT = A.sb("r_seqmT", [P, 16, P], F32)
    S.dma("sp", maskp[:], tabs["mask_p"], key="maskp", writes=["maskp"])
    S.dma("sp", masks[:], tabs["mask_s"], key="masks", writes=["masks"])
    S.dma("sp", seqm[:], tabs["seqm"], key="seqm", writes=["seqm"])
    S.dma("sp", seqmT[:], tabs["seqmT"], key="seqmT", writes=["seqmT"])

    St = A.sb("r_S", [P, 16, 256], F32)
    Sb = A.sb("r_Sb", [P, 16, 256], BF16)
    S.op("pool", lambda e: e.memset(St[:], 0.0), writes=["S"])
    S.op("pool", lambda e: e.memset(Sb[:], 0.0), writes=["Sb"])

    qins = [A.sb("r_qin", [P, 2048], F32)] * 2
    kins = [A.sb("r_kin", [P, 2048], F32)] * 2
    vins = [A.sb("r_vin%d" % i, [P, 4096], F32) for i in range(2)]
    gins = [A.sb("r_gin%d" % i, [P, 4096], F32) for i in range(2)]
    cur = {"i": 0}
    rt = A.sb("r_rt", [P, 2, 16, 64], F32)
    t1 = A.sb("r_t1", [P, 16, 64], F32)
    t2 = A.sb("r_t2", [P, 16, 64], F32)
    qt = A.sb("r_qt", [P, 16, 128], BF16)
    kt_ = A.sb("r_kt", [P, 16, 128], BF16)
    vb = A.sb("r_vb", [P, 16, 256], BF16)
    qT = A.sb("r_qT", [P, 16, 128], BF16)
    kT = A.sb("r_kT", [P, 16, 128], BF16)
    scs = A.sb("r_scs", [P, 16, 128], BF16)
    osb = A.sb("r_osb", [P, 16, 256], F32)
    sq = vin.rearrange("p (h e) -> p h e", h=16) if False else None
    og = A.sb("r_og", [P, 16, 256], BF16)
    oT = A.sb("r_oT", [P, 32, 128], BF16)
    st1 = A.sb("r_st1", [P, 16], F32)
    st2 = A.sb("r_st2", [P, 16], F32)
    st3 = A.sb("r_st3", [P, 16], F32)
    dtmp = A.sb("r_dtmp", [P, 2, 256], F32)
    ptr = [A.ps("r_ptr%d" % i, [P, 8, 128], BF16) for i in range(2)]
    psc = [A.ps("r_psc%d" % i, [P, 4, 128]) for i in range(2)]
    po = [A.ps("r_po%d" % i, [P, 2, 256]) for i in range(2)]
    pd = [A.ps("r_pd%d" % i, [P, 2, 256]) for i in range(2)]
    cn = {"tr": 0, "sc": 0, "o": 0, "d": 0}

    def rotary(src, dst, rt_ap, rkey, skey, dkey):
        S.dma("sp", rt[:], rt_ap, key="rt", writes=["rt"])
        sv = src[:].rearrange("p (h j two) -> p h j two", h=16, two=2)
        dv = dst[:].rearrange("p h (j two) -> p h j two", two=2)
        S.op("dve", lambda e: e.tensor_tensor(out=t1[:], in0=sv[:, :, :, 0], in1=rt[:, 0], op=ALU.mult),
             reads=[skey, "rt"], writes=["t1"])
        S.op("pool", lambda e: e.tensor_tensor(out=t2[:], in0=sv[:, :, :, 1], in1=rt[:, 1], op=ALU.mult),
             reads=[skey, "rt"], writes=["t2"])
        S.op("dve", lambda e: e.tensor_tensor(out=dv[:, :, :, 0], in0=t1[:], in1=t2[:], op=ALU.subtract),
             reads=["t1", "t2"], writes=[dkey + "0"])
        S.op("dve", lambda e: e.tensor_tensor(out=t1[:], in0=sv[:, :, :, 0], in1=rt[:, 1], op=ALU.mult),
             reads=[skey, "rt", dkey + "0"], writes=["t1"])
        S.op("pool", lambda e: e.tensor_tensor(out=t2[:], in0=sv[:, :, :, 1], in1=rt[:, 0], op=ALU.mult),
             reads=[skey, "rt", dkey + "0"], writes=["t2"])
        S.op("dve", lambda e: e.tensor_tensor(out=dv[:, :, :, 1], in0=t1[:], in1=t2[:], op=ALU.add),
             reads=["t1", "t2"], writes=[dkey + "1"])

    def transpose16(src, dst, skeys, dkey):
        for half in range(2):
            b = cn["tr"] % 2
            cn["tr"] += 1
            for j in range(8):
                h = half * 8 + j
                S.op("pe", lambda e, b=b, j=j, h=h: e.transpose(ptr[b][:, j, :], src[:, h, :], identb[:]),
                     reads=list(skeys) + ["identb"], writes=[("ptr", b)])
            S.op("act", lambda e, b=b, half=half: e.activation(
                out=dst[:, half * 8:(half + 1) * 8, :], in_=ptr[b][:], func=AF.Copy),
                reads=[("ptr", b)], writes=[(dkey, half)])

    def state_update(g, sample_head=None):
        pass

    def chunk(kind, t):
        own = kind != "pre"
        cur["n"] = cur.get("n", -1) + 1
        ci = cur["n"] % 2
        cur["i"] = ci
        qin, kin, vin, gin = qins[ci], kins[ci], vins[ci], gins[ci]
        KI, VI, QI, GI = "kin", ("vin", ci), "qin", ("gin", ci)
        z = zo if own else zp
        r0 = t * P
        kcol = 2048 if own else 0
        vcol = 4096 if own else 2048
        S.dma("sp", kin[:], z[r0:r0 + P, kcol:kcol + 2048], key=KI, writes=[KI])
        S.dma("sp", vin[:], z[r0:r0 + P, vcol:vcol + 4096], key=VI, writes=[VI])
        S.op("act", lambda e: e.activation(out=vb[:].rearrange("p h e -> p (h e)"), in_=vin[:], func=AF.Copy),
             reads=[VI], writes=["vb"])
        rotary(kin, kt_, (tabs["rk_own"] if own else tabs["rk_pre"])[t], "rk", KI, "kt")
        if own:
            S.dma("sp", qin[:], z[r0:r0 + P, 0:2048], key=QI, writes=[QI])
            S.dma("sp", gin[:], z[r0:r0 + P, 8192:12288], key=GI, writes=[GI])
            rotary(qin, qt, tabs["rq"][t], "rq", QI, "qt")
            transpose16(qt, qT, ["qt0", "qt1"], "qT")
            transpose16(kt_, kT, ["kt0", "kt1"], "kT")
        return own

    def scores_and_out(mask, sample):
        for hq in range(4):
            b = cn["sc"] % 2
            cn["sc"] += 1
            for j in range(4):
                h = hq * 4 + j
                S.op("pe", lambda e, b=b, j=j, h=h: e.matmul(psc[b][:, j, :], lhsT=kT[:, h, :], rhs=qT[:, h, :],
                                                             start=True, stop=True),
                     reads=[("kT", h // 8), ("qT", h // 8)], writes=[("psc", b)])
            S.op("dve", lambda e, b=b, hq=hq: e.tensor_tensor(
                out=scs[:, hq * 4:(hq + 1) * 4, :], in0=psc[b][:],
                in1=mask[:].unsqueeze(1).to_broadcast([P, 4, P]), op=ALU.mult),
                reads=[("psc", b), "maskp", "masks"], writes=[("scs", hq)])

    def finish_out(t):
        ci = cur["i"]
        vin, gin = vins[ci], gins[ci]
        VI, GI = ("vin", ci), ("gin", ci)
        S.op("dve", lambda e: e.tensor_reduce(out=st1[:], in_=osb[:], op=ALU.add, axis=mybir.AxisListType.X),
             reads=["osb"], writes=["st1"])
        sqv = vin[:].rearrange("p (h e) -> p h e", h=16)
        S.op("act", lambda e: e.activation(out=sqv, in_=osb[:], func=AF.Square),
             reads=["osb"], writes=[VI])
        S.op("dve", lambda e: e.tensor_reduce(out=st2[:], in_=sqv, op=ALU.add, axis=mybir.AxisListType.X),
             reads=[VI], writes=["st2"])
        S.op("dve", lambda e: e.tensor_scalar(out=st1[:], in0=st1[:], scalar1=1.0 / 256, scalar2=None, op0=ALU.mult),
             reads=["st1"], writes=["st1"])
        S.op("dve", lambda e: e.tensor_tensor(out=st3[:], in0=st1[:], in1=st1[:], op=ALU.mult),
             reads=["st1"], writes=["st3"])
        S.op("dve", lambda e: e.scalar_tensor_tensor(out=st2[:], in0=st2[:], scalar=1.0 / 256, in1=st3[:],
                                                     op0=ALU.mult, op1=ALU.subtract),
             reads=["st2", "st3"], writes=["st2"])
        S.op("dve", lambda e: e.tensor_scalar(out=st2[:], in0=st2[:], scalar1=1e-5, scalar2=None, op0=ALU.add),
             reads=["st2"], writes=["st2"])
        S.op("act", lambda e: e.activation(out=st2[:], in_=st2[:], func=AF.Sqrt), reads=["st2"], writes=["st2"])
        S.op("dve", lambda e: e.reciprocal(out=st2[:], in_=st2[:]), reads=["st2"], writes=["st2"])
        S.op("dve", lambda e: e.tensor_tensor(out=osb[:], in0=osb[:],
                                              in1=st1[:].unsqueeze(2).to_broadcast([P, 16, 256]), op=ALU.subtract),
             reads=["osb", "st1"], writes=["osb"])
        S.op("dve", lambda e: e.tensor_tensor(out=osb[:], in0=osb[:],
                                              in1=st2[:].unsqueeze(2).to_broadcast([P, 16, 256]), op=ALU.mult),
             reads=["osb", "st2"], writes=["osb"])
        S.op("dve", lambda e: e.tensor_tensor(out=og[:].rearrange("p h e -> p (h e)"),
                                              in0=osb[:].rearrange("p h e -> p (h e)"), in1=gin[:], op=ALU.mult),
             reads=["osb", GI], writes=["og"])
        ogv = og[:].rearrange("p h (two e) -> p (h two) e", two=2)
        for q4 in range(4):
            b = cn["tr"] % 2
            cn["tr"] += 1
            for j in range(8):
                ft = q4 * 8 + j
                S.op("pe", lambda e, b=b, j=j, ft=ft: e.transpose(ptr[b][:, j, :], ogv[:, ft, :], identb[:]),
                     reads=["og", "identb"], writes=[("ptr", b)])
            S.op("act", lambda e, b=b, q4=q4: e.activation(out=oT[:, q4 * 8:(q4 + 1) * 8, :], in_=ptr[b][:],
                                                           func=AF.Copy),
                 reads=[("ptr", b)], writes=[("oT", q4)])
        S.dma("act", OT[t, :, 0:32, :], oT[:], key="oT", reads=[("oT", q) for q in range(4)], is_output=True)

    def prompt_state_update():
        for hp in range(8):
            b = cn["d"] % 2
            cn["d"] += 1
            for j in range(2):
                h = hp * 2 + j
                S.op("pe", lambda e, b=b, j=j, h=h: e.matmul(pd[b][:, j, :], lhsT=kt_[:, h, :], rhs=vb[:, h, :],
                                                             start=True, stop=True),
                     reads=["kt0", "kt1", "vb"], writes=[("pd", b)])
            for j in range(2):
                h = hp * 2 + j
                g = float(RET_G[h] ** 128)
                S.op("act", lambda e, b=b, j=j, g=g: e.activation(out=dtmp[:, j, :], in_=pd[b][:, j, :],
                                                                  func=AF.Copy, scale=g),
                     reads=[("pd", b)], writes=[("dtmp", j)])
                S.op("dve", lambda e, h=h, j=j, g=g: e.scalar_tensor_tensor(
                    out=St[:, h, :], in0=St[:, h, :], scalar=g, in1=dtmp[:, j, :], op0=ALU.mult, op1=ALU.add),
                    reads=[("dtmp", j), "S"], writes=["S"])
        S.op("act", lambda e: e.activation(out=Sb[:], in_=St[:], func=AF.Copy), reads=["S"], writes=["Sb"])

    for t in range(8):
        chunk("pre", t)
        prompt_state_update()

    for t in range(8):
        chunk("own", t)
        scores_and_out(maskp, False)
        for hp in range(8):
            b = cn["o"] % 2
            cn["o"] += 1
            for j in range(2):
                h = hp * 2 + j
                S.op("pe", lambda e, b=b, j=j, h=h: e.matmul(po[b][:, j, :], lhsT=scs[:, h, :], rhs=vb[:, h, :],
                                                             start=True, stop=False),
                     reads=[("scs", h // 4), "vb"], writes=[("po", b)])
                S.op("pe", lambda e, b=b, j=j, h=h: e.matmul(po[b][:, j, :], lhsT=qT[:, h, :], rhs=Sb[:, h, :],
                                                             start=False, stop=True),
                     reads=[("qT", h // 8), "Sb"], writes=[("po", b)])
            S.op("act", lambda e, b=b, hp=hp: e.activation(out=osb[:, hp * 2:hp * 2 + 2, :], in_=po[b][:],
                                                           func=AF.Copy),
                 reads=[("po", b)], writes=["osb"])
        prompt_state_update()
        finish_out(t)
    S.dma("sp", sretp_out.rearrange("h d e -> d h e"), St[:], key="St_out", reads=["S"], is_output=True)

    t = 8
    chunk("own", t)
    scores_and_out(masks, True)
    qTm = oT[:, 0:16, :]
    ktm = oT[:, 16:32, :]
    Ss_b = [vins[i][:].rearrange("p (h e) -> p h e", h=16) for i in range(2)]
    Ss_k = [("vin", i) for i in range(2)]
    Ssb_b = [og, A.sb("r_Ssb1", [P, 16, 256], BF16)]
    Ssb_k = ["og", "Ssb1"]

    def load_state(h):
        i = h % 2
        S.dma("sp", Ss_b[i], sret_in[:, h].rearrange("s d e -> d s e"), key=("Ss", i), writes=[Ss_k[i]])
    load_state(0)
    for h in range(16):
        bi_ = h % 2
        Ss, VI, Ssb, SBK = Ss_b[bi_], Ss_k[bi_], Ssb_b[bi_], Ssb_k[bi_]
        g8 = float(RET_G[h] ** 8)
        if h + 1 < 16:
            load_state(h + 1)
        S.op("act", lambda e, Ssb=Ssb, Ss=Ss: e.activation(out=Ssb[:], in_=Ss, func=AF.Copy), reads=[VI], writes=[SBK])
        S.op("dve", lambda e, h=h: e.tensor_tensor(
            out=qTm, in0=qT[:, h, :].unsqueeze(1).to_broadcast([P, 16, P]), in1=seqmT[:], op=ALU.mult),
            reads=[("qT", h // 8), "seqmT"], writes=[("oT", 0), ("oT", 1)])
        S.op("dve", lambda e, h=h: e.tensor_tensor(
            out=ktm, in0=kt_[:, h, :].unsqueeze(1).to_broadcast([P, 16, P]),
            in1=seqm[:].unsqueeze(2).to_broadcast([P, 16, P]), op=ALU.mult),
            reads=["kt0", "kt1", "seqm"], writes=[("oT", 2), ("oT", 3)])
        b = cn["o"] % 2
        cn["o"] += 1
        S.op("pe", lambda e, b=b, h=h: e.matmul(po[b][:, 0, :], lhsT=scs[:, h, :], rhs=vb[:, h, :],
                                                start=True, stop=False),
             reads=[("scs", h // 4), "vb"], writes=[("po", b)])
        for s_ in range(16):
            S.op("pe", lambda e, b=b, s_=s_, Ssb=Ssb: e.matmul(po[b][:, 0, :], lhsT=qTm[:, s_, :],
                                                               rhs=Ssb[:, s_, :], start=False, stop=(s_ == 15)),
                 reads=[("oT", 0), ("oT", 1), SBK], writes=[("po", b)])
        S.op("act", lambda e, b=b, h=h: e.activation(out=osb[:, h, :], in_=po[b][:, 0, :], func=AF.Copy),
             reads=[("po", b)], writes=["osb"])
        for sp_ in range(8):
            b2 = cn["d"] % 2
            cn["d"] += 1
            for j in range(2):
                s_ = sp_ * 2 + j
                S.op("pe", lambda e, b2=b2, j=j, s_=s_, h=h: e.matmul(
                    pd[b2][:, j, :], lhsT=ktm[:, s_, :], rhs=vb[:, h, :], start=True, stop=True),
                    reads=[("oT", 2), ("oT", 3), "vb"], writes=[("pd", b2)])
            for j in range(2):
                s_ = sp_ * 2 + j
                S.op("act", lambda e, b2=b2, j=j, g8=g8: e.activation(out=dtmp[:, j, :], in_=pd[b2][:, j, :],
                                                                      func=AF.Copy, scale=g8),
                     reads=[("pd", b2)], writes=[("dtmp", j)])
                S.op("dve", lambda e, s_=s_, j=j, g8=g8, Ss=Ss: e.scalar_tensor_tensor(
                    out=Ss[:, s_, :], in0=Ss[:, s_, :], scalar=g8, in1=dtmp[:, j, :], op0=ALU.mult, op1=ALU.add),
                    reads=[("dtmp", j), VI, SBK], writes=[VI])
        S.dma("sp", sret_out[:, h].rearrange("s d e -> d s e"), Ss, key=("Ss_out", bi_), reads=[VI],
              is_output=True)
    finish_out(8)


def xattn_body(S, A, zo, memkv, cmk, cmv, seqmT_ap, OT, ident_ap):
    X = mybir.AxisListType.X
    scale = 512.0 ** -0.5
    ident = A.sb("x_ident", [P, P], F32)
    identb = A.sb("x_identb", [P, P], BF16)
    S.dma("sp", ident[:], ident_ap, key="ident", writes=["ident"])
    S.op("dve", lambda e: e.tensor_copy(out=identb[:], in_=ident[:]), reads=["ident"], writes=["identb"])
    seqmT = A.sb("x_seqmT", [P, 16, P], F32)
    S.dma("sp", seqmT[:], seqmT_ap, key="seqmT", writes=["seqmT"])

    qin = A.sb("x_qin", [P, 2048], F32)
    gin = A.sb("x_gin", [P, 2048], F32)
    qb = A.sb("x_qb", [P, 16, 128], BF16)
    qT = A.sb("x_qT", [P, 16, 128], BF16)
    kvin = [A.sb("x_kvin%d" % i, [P, 2, 2048], F32) for i in range(2)]
    kb = A.sb("x_kb", [P, 2, 16, 128], BF16)
    KT = A.sb("x_KT", [P, 16, 256], BF16)
    Vb = A.sb("x_Vb", [P, 2, 2048], BF16)
    pb = A.sb("x_pb", [P, 4, 256], BF16)
    pT = A.sb("x_pT", [P, 4, 2, 128], BF16)
    pTm = A.sb("x_pTm", [P, 16, 128], BF16)
    qTm = A.sb("x_qTm", [P, 16, 128], BF16)
    ob = A.sb("x_ob", [P, 16, 128], BF16)
    oT = A.sb("x_oT", [P, 16, 128], BF16)
    mx = A.sb("x_mx", [P, 4], F32)
    sm = A.sb("x_sm", [P, 4], F32)
    ptr = [A.ps("x_ptr%d" % i, [P, 8, 128], BF16) for i in range(2)]
    pso = [A.ps("x_pso%d" % i, [P, 512]) for i in range(4)]
    cn = {"tr": 0, "kv": 0}

    def tr_group(srcs, dst_ap, skeys, dkey):
        b = cn["tr"] % 2
        cn["tr"] += 1
        for j, src in enumerate(srcs):
            S.op("pe", lambda e, b=b, j=j, src=src: e.transpose(ptr[b][:, j, :], src, identb[:]),
                 reads=list(skeys) + ["identb"], writes=[("ptr", b)])
        n = len(srcs)
        S.op("act", lambda e, b=b, n=n: e.activation(out=dst_ap, in_=ptr[b][:, 0:n, :], func=AF.Copy),
             reads=[("ptr", b)], writes=[dkey])

    def load_q(t):
        r0 = t * P
        S.dma("sp", qin[:], zo[r0:r0 + P, 16384:18432], key="qin", writes=["qin"])
        S.dma("sp", gin[:], zo[r0:r0 + P, 18432:20480], key="gin", writes=["gin"])
        S.op("dve", lambda e: e.tensor_copy(out=qb[:].rearrange("p a b -> p (a b)"), in_=qin[:]),
             reads=["qin"], writes=["qb"])
        for half in range(2):
            tr_group([qb[:, half * 8 + j, :] for j in range(8)], qT[:, half * 8:(half + 1) * 8, :],
                     ["qb"], ("qT", half))

    def load_kv(src_ap, which):
        i = cn["kv"] % 2
        cn["kv"] += 1
        S.dma("sp", kvin[i][:], src_ap.rearrange("(mt p) c -> p mt c", p=P), key=("kvin", i),
              writes=[("kvin", i)])
        return i

    def make_KT(i):
        S.op("dve", lambda e: e.tensor_copy(out=kb[:].rearrange("p m a b -> p m (a b)"), in_=kvin[i][:]),
             reads=[("kvin", i)], writes=["kb"])
        for mt in range(2):
            for half in range(2):
                tr_group([kb[:, mt, half * 8 + j, :] for j in range(8)],
                         KT[:, half * 8:(half + 1) * 8, mt * P:(mt + 1) * P], ["kb"], ("KT", mt, half))

    def make_V(i):
        S.op("act", lambda e: e.activation(out=Vb[:], in_=kvin[i][:], func=AF.Copy), reads=[("kvin", i)],
             writes=["Vb"])

    KT_keys = [("KT", mt, half) for mt in range(2) for half in range(2)]

    def sc_loc(h, sample):
        return pso[h][:, 0:256], ("pso", h)

    def score_mm(lhs, lkeys, first, last, sample=False):
        for h in range(4):
            for dt in range(4):
                k = h * 4 + dt
                loc, lk = sc_loc(h, sample)
                S.op("pe", lambda e, loc=loc, k=k, dt=dt: e.matmul(
                    loc, lhsT=lhs[:, k, :], rhs=KT[:, k, :],
                    start=(first and dt == 0), stop=(last and dt == 3)),
                    reads=list(lkeys) + KT_keys, writes=[lk])

    def softmax(sample=False):
        for h in range(4):
            sv, lk = sc_loc(h, sample)
            S.op("dve", lambda e, h=h, sv=sv: e.tensor_reduce(out=mx[:, h:h + 1], in_=sv, op=ALU.max, axis=X),
                 reads=[lk], writes=[("mx", h)])
            S.op("dve", lambda e, h=h: e.tensor_scalar(out=mx[:, h:h + 1], in0=mx[:, h:h + 1], scalar1=-scale,
                                                       scalar2=None, op0=ALU.mult),
                 reads=[("mx", h)], writes=[("mx", h)])
            S.op("act", lambda e, h=h, sv=sv: e.activation(out=pb[:, h, :], in_=sv, func=AF.Exp,
                                                           bias=mx[:, h:h + 1], scale=scale,
                                                           accum_out=sm[:, h:h + 1]),
                 reads=[lk, ("mx", h)], writes=[("pb", h), ("sm", h)])
            S.op("dve", lambda e, h=h: e.reciprocal(out=sm[:, h:h + 1], in_=sm[:, h:h + 1]),
                 reads=[("sm", h)], writes=[("sm", h)])
        tr_group([pb[:, h, mt * P:(mt + 1) * P] for h in range(4) for mt in range(2)],
                 pT[:].rearrange("p h m l -> p (h m) l"), [("pb", h) for h in range(4)], "pT")

    def finish(t):
        for h in range(4):
            S.op("dve", lambda e, h=h: e.scalar_tensor_tensor(
                out=ob[:, h * 4:(h + 1) * 4, :].rearrange("p a b -> p (a b)"), in0=pso[h][:],
                scalar=sm[:, h:h + 1], in1=gin[:, h * 512:(h + 1) * 512], op0=ALU.mult, op1=ALU.mult),
                reads=[("pso", h), ("sm", h), "gin"], writes=[("ob", h)])
        for half in range(2):
            tr_group([ob[:, half * 8 + j, :] for j in range(8)], oT[:, half * 8:(half + 1) * 8, :],
                     [("ob", h) for h in range(4)], ("oT", half))
        S.dma("act", OT[t, :, 48:64, :], oT[:], key="oT", reads=[("oT", 0), ("oT", 1)], is_output=True)

    i = load_kv(memkv[:, 0:2048], "k")
    make_KT(i)
    i = load_kv(memkv[:, 2048:4096], "v")
    make_V(i)
    for t in range(8):
        load_q(t)
        score_mm(qT, [("qT", 0), ("qT", 1)], True, True)
        softmax()
        for h in range(4):
            for mt in range(2):
                S.op("pe", lambda e, h=h, mt=mt: e.matmul(pso[h][:], lhsT=pT[:, h, mt, :],
                                                          rhs=Vb[:, mt, h * 512:(h + 1) * 512],
                                                          start=(mt == 0), stop=(mt == 1)),
                     reads=["pT", "Vb"], writes=[("pso", h)])
        finish(t)

    load_q(8)
    for s_ in range(16):
        i = load_kv(cmk[s_], "k")
        make_KT(i)
        S.op("dve", lambda e, s_=s_: e.tensor_tensor(
            out=qTm[:], in0=qT[:], in1=seqmT[:, s_, :].unsqueeze(1).to_broadcast([P, 16, P]), op=ALU.mult),
            reads=[("qT", 0), ("qT", 1), "seqmT"], writes=["qTm"])
        score_mm(qTm, ["qTm"], s_ == 0, s_ == 15, sample=True)
    softmax(sample=True)
    for s_ in range(16):
        i = load_kv(cmv[s_], "v")
        make_V(i)
        for h in range(4):
            S.op("dve", lambda e, h=h, s_=s_: e.tensor_tensor(
                out=pTm[:, h * 2:(h + 1) * 2, :], in0=pT[:, h, :, :],
                in1=seqmT[:, s_, :].unsqueeze(1).to_broadcast([P, 2, P]), op=ALU.mult),
                reads=["pT", "seqmT"], writes=[("pTm", h)])
            for mt in range(2):
                S.op("pe", lambda e, h=h, mt=mt, s_=s_: e.matmul(
                    pso[h][:], lhsT=pTm[:, h * 2 + mt, :], rhs=Vb[:, mt, h * 512:(h + 1) * 512],
                    start=(s_ == 0 and mt == 0), stop=(s_ == 15 and mt == 1)),
                    reads=[("pTm", h), "Vb"], writes=[("pso", h)])
    finish(8)


def s5prep_body(S, A, prm, ident_ap, maskM_ap, SM, SG, SE, A8S):
    PI = math.pi
    ident = A.sb("q_ident", [P, P], F32)
    identb = A.sb("q_identb", [P, P], BF16)
    S.dma("sp", ident[:], ident_ap, key="ident", writes=["ident"])
    S.op("dve", lambda e: e.tensor_copy(out=identb[:], in_=ident[:]), reads=["ident"], writes=["identb"])
    maskM = A.sb("q_maskM", [P, P], F32)
    S.dma("sp", maskM[:], maskM_ap, key="maskM", writes=["maskM"])
    H = 64
    uid = [0]

    def tl(shape, dt=F32):
        uid[0] += 1
        return A.sb("q_t%d" % uid[0], shape, dt)

    def dve(fn, reads, writes):
        S.op("dve", fn, reads=reads, writes=writes)

    def tt(out, a, b, op, okey, akey, bkey):
        dve(lambda e: e.tensor_tensor(out=out, in0=a, in1=b, op=op), [akey, bkey], [okey])

    araw = tl([P, 2, 64])
    S.dma("sp", araw[:, 0, :], prm["a_re"], key="araw0", writes=["araw"])
    S.dma("sp", araw[:, 1, :], prm["a_im"], key="araw1", writes=["araw"])
    pA = A.ps("q_pA", [P, 4, P])
    pA2 = A.ps("q_pA2", [P, 4, P])
    ar = tl([H, P]); ai = tl([H, P])
    for j in range(2):
        S.op("pe", lambda e, j=j: e.transpose(pA[0:H, j, :], araw[:, j, :], ident[:]), reads=["araw", "ident"],
             writes=["pA"])
    dve(lambda e: e.tensor_copy(out=ar[:], in_=pA[0:H, 0, :]), ["pA"], ["ar"])
    dve(lambda e: e.tensor_copy(out=ai[:], in_=pA[0:H, 1, :]), ["pA"], ["ai"])
    dtb = tl([H, P])
    S.dma("sp", dtb[:], prm["log_step"].to_broadcast([H, P]), key="dtb", writes=["dtb"])
    S.op("act", lambda e: e.activation(out=dtb[:], in_=dtb[:], func=AF.Exp), reads=["dtb"], writes=["dtb"])
    dtar = tl([H, P]); dtai = tl([H, P]); mag = tl([H, P])
    tt(dtar[:], dtb[:], ar[:], ALU.mult, "dtar", "dtb", "ar")
    tt(dtai[:], dtb[:], ai[:], ALU.mult, "dtai", "dtb", "ai")
    kq = tl([H, P]); ki = tl([H, P], mybir.dt.int32); rr = tl([H, P])
    dve(lambda e: e.tensor_scalar(out=kq[:], in0=dtai[:], scalar1=1.0 / (2 * PI), scalar2=None, op0=ALU.mult),
        ["dtai"], ["kq"])
    dve(lambda e: e.tensor_copy(out=ki[:], in_=kq[:]), ["kq"], ["ki"])
    dve(lambda e: e.tensor_copy(out=kq[:], in_=ki[:]), ["ki"], ["kq"])
    dve(lambda e: e.scalar_tensor_tensor(out=rr[:], in0=kq[:], scalar=-2 * PI, in1=dtai[:], op0=ALU.mult,
                                         op1=ALU.add), ["kq", "dtai"], ["rr"])
    rs = tl([H, P]); rc = tl([H, P]); sn = tl([H, P]); cs = tl([H, P])
    msk = tl([H, P])
    for t_, k_, sh in ((rs, "rs", 0.0), (rc, "rc", PI / 2)):
        dve(lambda e, t_=t_, sh=sh: e.tensor_scalar(out=t_[:], in0=rr[:], scalar1=sh, scalar2=None, op0=ALU.add),
            ["rr"], [k_])
        dve(lambda e, t_=t_: e.tensor_scalar(out=msk[:], in0=t_[:], scalar1=PI, scalar2=None, op0=ALU.is_gt),
            [k_], ["msk"])
        dve(lambda e, t_=t_: e.scalar_tensor_tensor(out=t_[:], in0=msk[:], scalar=-2 * PI, in1=t_[:],
                                                    op0=ALU.mult, op1=ALU.add), ["msk", k_], [k_])
        dve(lambda e, t_=t_: e.tensor_scalar(out=msk[:], in0=t_[:], scalar1=-PI, scalar2=None, op0=ALU.is_lt),
            [k_], ["msk"])
        dve(lambda e, t_=t_: e.scalar_tensor_tensor(out=t_[:], in0=msk[:], scalar=2 * PI, in1=t_[:],
                                                    op0=ALU.mult, op1=ALU.add), ["msk", k_], [k_])
    for t_, k_ in ((rs, "rs"), (rc, "rc")):
        dve(lambda e, t_=t_: e.tensor_scalar(out=t_[:], in0=t_[:], scalar1=3.1415925, scalar2=-3.1415925,
                                             op0=ALU.min, op1=ALU.max), [k_], [k_])
    hh = tl([H, P]); x2 = tl([H, P]); sh_ = tl([H, P]); ch_ = tl([H, P])
    dve(lambda e: e.tensor_scalar(out=hh[:], in0=rs[:], scalar1=0.5, scalar2=None, op0=ALU.mult), ["rs"], ["hh"])
    tt(x2[:], hh[:], hh[:], ALU.mult, "x2", "hh", "hh")
    sc_ = [(-1.0) ** k / math.factorial(2 * k + 1) for k in range(9)]
    cc_ = [(-1.0) ** k / math.factorial(2 * k) for k in range(9)]

    def horner(dst, dkey, co):
        dve(lambda e: e.tensor_scalar(out=dst[:], in0=x2[:], scalar1=co[-1], scalar2=None, op0=ALU.mult),
            ["x2"], [dkey])
        for c_ in co[-2:0:-1]:
            dve(lambda e, c_=c_: e.scalar_tensor_tensor(out=dst[:], in0=dst[:], scalar=c_, in1=x2[:],
                                                        op0=ALU.add, op1=ALU.mult), [dkey, "x2"], [dkey])
        dve(lambda e: e.tensor_scalar(out=dst[:], in0=dst[:], scalar1=co[0], scalar2=None, op0=ALU.add),
            [dkey], [dkey])
    horner(sh_, "sh", sc_)
    tt(sh_[:], sh_[:], hh[:], ALU.mult, "sh", "sh", "hh")
    horner(ch_, "ch", cc_)
    dve(lambda e: e.scalar_tensor_tensor(out=sn[:], in0=sh_[:], scalar=2.0, in1=ch_[:], op0=ALU.mult,
                                         op1=ALU.mult), ["sh", "ch"], ["sn"])
    tt(cs[:], sh_[:], sh_[:], ALU.mult, "cs", "sh", "sh")
    dve(lambda e: e.tensor_scalar(out=cs[:], in0=cs[:], scalar1=-2.0, scalar2=1.0, op0=ALU.mult, op1=ALU.add),
        ["cs"], ["cs"])
    ec_ = [1.0 / math.factorial(k) for k in range(9)]
    dve(lambda e: e.tensor_scalar(out=mag[:], in0=dtar[:], scalar1=ec_[-1], scalar2=None, op0=ALU.mult),
        ["dtar"], ["mag"])
    for c_ in ec_[-2:0:-1]:
        dve(lambda e, c_=c_: e.scalar_tensor_tensor(out=mag[:], in0=mag[:], scalar=c_, in1=dtar[:],
                                                    op0=ALU.add, op1=ALU.mult), ["mag", "dtar"], ["mag"])
    dve(lambda e: e.tensor_scalar(out=mag[:], in0=mag[:], scalar1=1.0, scalar2=None, op0=ALU.add),
        ["mag"], ["mag"])
    PW = tl([H, 16, 2, P])
    tmp = [tl([H, P]) for _ in range(4)]
    cm = [0]

    def cmul(ore, oim, xr, xi, yr, yi, okeys, ikeys):
        cm[0] += 1
        k = ["cm%d_%d" % (cm[0], i) for i in range(4)]
        dve(lambda e: e.tensor_tensor(out=tmp[0][:], in0=xr, in1=yr, op=ALU.mult), ikeys, ["tmp0"])
        dve(lambda e: e.tensor_tensor(out=tmp[1][:], in0=xi, in1=yi, op=ALU.mult), ikeys, ["tmp1"])
        dve(lambda e: e.tensor_tensor(out=tmp[2][:], in0=xr, in1=yi, op=ALU.mult), ikeys, ["tmp2"])
        dve(lambda e: e.tensor_tensor(out=tmp[3][:], in0=xi, in1=yr, op=ALU.mult), ikeys, ["tmp3"])
        dve(lambda e: e.tensor_tensor(out=ore, in0=tmp[0][:], in1=tmp[1][:], op=ALU.subtract),
            ["tmp0", "tmp1"], [okeys[0]])
        dve(lambda e: e.tensor_tensor(out=oim, in0=tmp[2][:], in1=tmp[3][:], op=ALU.add),
            ["tmp2", "tmp3"], [okeys[1]])

    def pw(e_, c):
        return PW[:, e_ + 7, c, :]

    def pk(e_):
        return ["pw%d_0" % e_, "pw%d_1" % e_]
    S.op("pool", lambda e: e.memset(pw(0, 0), 1.0), writes=["pw0_0"])
    S.op("pool", lambda e: e.memset(pw(0, 1), 0.0), writes=["pw0_1"])
    tt(pw(1, 0), mag[:], cs[:], ALU.mult, "pw1_0", "mag", "cs")
    tt(pw(1, 1), mag[:], sn[:], ALU.mult, "pw1_1", "mag", "sn")
    for e_ in range(2, 9):
        cmul(pw(e_, 0), pw(e_, 1), pw(e_ - 1, 0), pw(e_ - 1, 1), pw(1, 0), pw(1, 1), pk(e_), pk(e_ - 1) + pk(1))
    im2 = tl([H, P])
    tt(im2[:], mag[:], mag[:], ALU.mult, "im2", "mag", "mag")
    dve(lambda e: e.reciprocal(out=im2[:], in_=im2[:]), ["im2"], ["im2"])
    tt(pw(-1, 0), pw(1, 0), im2[:], ALU.mult, "pw-1_0", "pw1_0", "im2")
    dve(lambda e: e.scalar_tensor_tensor(out=pw(-1, 1), in0=pw(1, 1), scalar=-1.0, in1=im2[:], op0=ALU.mult,
                                         op1=ALU.mult), ["pw1_1", "im2"], ["pw-1_1"])
    for e_ in range(2, 8):
        cmul(pw(-e_, 0), pw(-e_, 1), pw(-e_ + 1, 0), pw(-e_ + 1, 1), pw(-1, 0), pw(-1, 1),
             pk(-e_), pk(-e_ + 1) + pk(-1))
    allpw = [k for e_ in range(-7, 9) for k in pk(e_)]
    S.dma("sp", A8S, PW[:, 15, :, :], key="a8s", reads=pk(8), is_output=True)
    den = tl([H, P]); xr_ = tl([H, P]); fre = tl([H, P]); fim = tl([H, P])
    tt(den[:], ar[:], ar[:], ALU.mult, "den", "ar", "ar")
    tt(tmp[0][:], ai[:], ai[:], ALU.mult, "tmp0", "ai", "ai")
    tt(den[:], den[:], tmp[0][:], ALU.add, "den", "den", "tmp0")
    dve(lambda e: e.reciprocal(out=den[:], in_=den[:]), ["den"], ["den"])
    dve(lambda e: e.tensor_scalar(out=xr_[:], in0=pw(1, 0), scalar1=-1.0, scalar2=None, op0=ALU.add),
        ["pw1_0"], ["xr"])
    tt(tmp[0][:], xr_[:], ar[:], ALU.mult, "tmp0", "xr", "ar")
    tt(tmp[1][:], pw(1, 1), ai[:], ALU.mult, "tmp1", "pw1_1", "ai")
    tt(fre[:], tmp[0][:], tmp[1][:], ALU.add, "fre", "tmp0", "tmp1")
    tt(fre[:], fre[:], den[:], ALU.mult, "fre", "fre", "den")
    tt(tmp[2][:], pw(1, 1), ar[:], ALU.mult, "tmp2", "pw1_1", "ar")
    tt(tmp[3][:], xr_[:], ai[:], ALU.mult, "tmp3", "xr", "ai")
    tt(fim[:], tmp[2][:], tmp[3][:], ALU.subtract, "fim", "tmp2", "tmp3")
    tt(fim[:], fim[:], den[:], ALU.mult, "fim", "fim", "den")
    Pst = tl([P, P, 8]); Qst = tl([P, P, 8])
    for s_ in range(8):
        cmul(Pst[0:H, :, s_], Qst[H:P, :, s_], pw(7 - s_, 0), pw(7 - s_, 1), fre[:], fim[:],
             ["Pst_lo", "Qst_hi"], pk(7 - s_) + ["fre", "fim"])
    S.op("act", lambda e: e.activation(out=Pst[H:P, :, :], in_=Pst[0:H, :, :], func=AF.Copy),
         reads=["Pst_lo"], writes=["Pst_hi"])
    S.op("act", lambda e: e.activation(out=Qst[0:H, :, :], in_=Qst[H:P, :, :], func=AF.Copy, scale=-1.0),
         reads=["Qst_hi"], writes=["Qst_lo"])
    Pv = tl([P, 16, P]); Qv = tl([P, 16, P])
    S.op("act", lambda e: e.activation(out=Pv[0:H], in_=PW[:, :, 0, :], func=AF.Copy), reads=allpw, writes=["Pv_lo"])
    S.op("act", lambda e: e.activation(out=Pv[H:P], in_=PW[:, :, 0, :], func=AF.Copy), reads=allpw, writes=["Pv_hi"])
    S.op("act", lambda e: e.activation(out=Qv[0:H], in_=PW[:, :, 1, :], func=AF.Copy, scale=-1.0), reads=allpw,
         writes=["Qv_lo"])
    S.op("act", lambda e: e.activation(out=Qv[H:P], in_=PW[:, :, 1, :], func=AF.Copy, scale=-1.0), reads=allpw,
         writes=["Qv_hi"])
    t1 = tl([P, 32, 128]); t2 = tl([P, 32, 128])
    raw = t1[:].rearrange("p a b -> p (a b)")
    R = tl([P, P, 16]); Sx = tl([P, P, 16]); Rp = tl([P, P, 16]); Sp = tl([P, P, 16])
    srcs = (("b_re", 0), ("b_im", 1), ("c_re", 2), ("c_im", 3))
    for nm, idx in srcs:
        S.dma("sp", raw[:, idx * 1024:(idx + 1) * 1024], prm[nm], key="raw%d" % idx, writes=["raw%d" % idx])
    cnt = [0]
    for nm, idx in srcs:
        rv = raw[:, idx * 1024:(idx + 1) * 1024]
        for q4 in range(4):
            pb_ = pA if cnt[0] % 2 == 0 else pA2
            pkey = "pA" if cnt[0] % 2 == 0 else "pA2"
            cnt[0] += 1
            for j in range(4):
                qq = q4 * 4 + j
                if idx < 2:
                    src = rv.rearrange("p (n q) -> p q n", q=16)[:, qq, :]
                else:
                    src = rv[:, qq * 64:(qq + 1) * 64]
                S.op("pe", lambda e, pb_=pb_, j=j, src=src: e.transpose(pb_[0:H, j, :], src, ident[:]),
                     reads=["raw%d" % idx, "ident"], writes=[pkey])
            qs = slice(q4 * 4, q4 * 4 + 4)
            pin = pb_[0:H, :, :]

            def outv(tile_, lo):
                v = tile_[0:H] if lo else tile_[H:P]
                return v.rearrange("p g q -> p q g")[:, qs, :]
            if idx == 0:
                dsts = ((R, True, 1.0), (Sx, False, 1.0))
            elif idx == 1:
                dsts = ((R, False, 1.0), (Sx, True, 1.0))
            elif idx == 2:
                dsts = ((Rp, True, 1.0), (Sp, False, 1.0))
            else:
                dsts = ((Rp, False, -1.0), (Sp, True, 1.0))
            for (tile_, lo, sc) in dsts:
                ov = outv(tile_, lo)
                S.op("act", lambda e, ov=ov, pin=pin, sc=sc: e.activation(out=ov, in_=pin, func=AF.Copy, scale=sc),
                     reads=[pkey], writes=["tab%d_%d_%d" % (id(tile_) % 997, lo, q4)])
    tabkeys = None
    X7c = tl([P, 32, 128], BF16); Ypc = tl([P, 32, 128], BF16); Ec = tl([P, 32, 128], BF16)
    Mc = tl([P, 32, 128], BF16); Gc = tl([P, 32, 128], BF16)
    pM = [A.ps("q_pM%d" % i, [P, 4, P]) for i in range(2)]
    pG = [A.ps("q_pG%d" % i, [P, 8, P], BF16) for i in range(2)]
    anytab = [k for k in S.last_w.keys() if isinstance(k, str) and k.startswith("tab")]
    for ch in range(4):
        gs = slice(ch * 32, ch * 32 + 32)
        t1v = t1[:].rearrange("p g (s q) -> p g s q", q=16)
        t2v = t2[:].rearrange("p g (s q) -> p g s q", q=16)

        def build(dst, dkey, Pt, Qt, Rt, St, pkeys):
            dve(lambda e: e.tensor_tensor(out=t1v, in0=Pt.unsqueeze(3).to_broadcast([P, 32, 8, 16]),
                                          in1=Rt.unsqueeze(2).to_broadcast([P, 32, 8, 16]), op=ALU.mult),
                pkeys + anytab + ["raw0", "raw1", "raw2", "raw3"], ["t1"])
            S.op("dve", lambda e: e.tensor_tensor(out=t2v, in0=Qt.unsqueeze(3).to_broadcast([P, 32, 8, 16]),
                                                  in1=St.unsqueeze(2).to_broadcast([P, 32, 8, 16]), op=ALU.mult),
                 reads=pkeys + anytab, writes=["t2"])
            dve(lambda e: e.tensor_tensor(out=dst[:], in0=t1[:], in1=t2[:], op=ALU.add), ["t1", "t2"], [dkey])
        build(X7c, "X7c", Pst[:, gs, :], Qst[:, gs, :], R[:, gs, :], Sx[:, gs, :],
              ["Pst_lo", "Pst_hi", "Qst_lo", "Qst_hi"])
        pvk = ["Pv_lo", "Pv_hi", "Qv_lo", "Qv_hi"]
        build(Ypc, "Ypc", Pv[:, 0:8, gs].rearrange("p e g -> p g e"), Qv[:, 0:8, gs].rearrange("p e g -> p g e"),
              Rp[:, gs, :], Sp[:, gs, :], pvk)
        build(Ec, "Ec", Pv[:, 8:16, gs].rearrange("p e g -> p g e"), Qv[:, 8:16, gs].rearrange("p e g -> p g e"),
              Rp[:, gs, :], Sp[:, gs, :], pvk)
        for g4 in range(8):
            b = g4 % 2
            for j in range(4):
                g = g4 * 4 + j
                S.op("pe", lambda e, b=b, j=j, g=g: e.matmul(pM[b][:, j, :], lhsT=X7c[:, g, :], rhs=Ypc[:, g, :],
                                                             start=True, stop=True),
                     reads=["X7c", "Ypc"], writes=[("pM", b)])
            dve(lambda e, b=b, g4=g4: e.tensor_tensor(
                out=Mc[:, g4 * 4:(g4 + 1) * 4, :], in0=pM[b][:],
                in1=maskM[:].unsqueeze(1).to_broadcast([P, 4, P]), op=ALU.mult),
                [("pM", b), "maskM"], [("Mc", g4)])
        for g8 in range(4):
            b = g8 % 2
            for j in range(8):
                g = g8 * 8 + j
                S.op("pe", lambda e, b=b, j=j, g=g: e.transpose(pG[b][:, j, :], X7c[:, g, :], identb[:]),
                     reads=["X7c", "identb"], writes=[("pG", b)])
            S.op("act", lambda e, b=b, g8=g8: e.activation(out=Gc[:, g8 * 8:(g8 + 1) * 8, :], in_=pG[b][:],
                                                           func=AF.Copy),
                 reads=[("pG", b)], writes=[("Gc", g8)])
        S.dma("sp", SM[:, gs, :], Mc[:], key="SMst", reads=[("Mc", i) for i in range(8)], is_output=True)
        S.dma("sp", SG[:, gs, :], Gc[:], key="SGst", reads=[("Gc", i) for i in range(4)], is_output=True)
        S.dma("sp", SE[:, gs, :], Ec[:], key="SEst", reads=["Ec"], is_output=True)


def s5main_body(S, A, zo, zp, SM, SG, SE, A8S, selm_ap, seqm_ap, ident_ap, s5in, YS, s5p_out, s5s_out):
    H = 64
    ident = A.sb("m_ident", [P, P], F32)
    S.dma("sp", ident[:], ident_ap, key="ident", writes=["ident"])
    selm = A.sb("m_selm", [P, 8], F32)
    S.dma("sp", selm[:], selm_ap, key="selm", writes=["selm"])
    bsf = A.sb("m_bsf", [P, 16], F32)
    bsel = A.sb("m_bsel", [P, 16], BF16)
    S.dma("sp", bsf[:], seqm_ap, key="bsf", writes=["bsf"])
    S.op("dve", lambda e: e.tensor_copy(out=bsel[:], in_=bsf[:]), reads=["bsf"], writes=["bsel"])
    a8 = A.sb("m_a8", [H, 2, P], F32)
    S.dma("sp", a8[:], A8S, key="a8", writes=["a8"])
    AA = A.sb("m_AA", [H, 2, P], F32)
    AB = A.sb("m_AB", [H, 2, P], F32)
    S.op("dve", lambda e: e.tensor_copy(out=AA[:, 0, :], in_=a8[:, 0, :]), reads=["a8"], writes=["AA0"])
    S.op("dve", lambda e: e.tensor_copy(out=AA[:, 1, :], in_=a8[:, 0, :]), reads=["a8"], writes=["AA1"])
    S.op("dve", lambda e: e.tensor_scalar(out=AB[:, 0, :], in0=a8[:, 1, :], scalar1=-1.0, scalar2=None,
                                          op0=ALU.mult), reads=["a8"], writes=["AB0"])
    S.op("dve", lambda e: e.tensor_copy(out=AB[:, 1, :], in_=a8[:, 1, :]), reads=["a8"], writes=["AB1"])
    AK = ["AA0", "AA1", "AB0", "AB1"]
    Gm = A.sb("m_G", [P, P, P], BF16)
    for ch in range(4):
        S.dma("sp", Gm[:, ch * 32:(ch + 1) * 32, :], SG[:, ch * 32:(ch + 1) * 32, :], key=("Gl", ch),
              writes=[("G", ch)])
    MEc = [A.sb("m_ME%d" % i, [P, 32, 2, P], BF16) for i in range(2)]
    uin = [A.sb("m_uin%d" % i, [P, 2048], F32) for i in range(2)]
    urep = A.sb("m_urep", [P, 32, 128], BF16)
    Uts = [A.sb("m_Ut%d" % i, [P, P, 16], BF16) for i in range(2)]
    VH = A.sb("m_VH", [H, 2, P, 17], F32)
    Hbfs = [A.sb("m_Hbf%d" % i, [P, P, 16], BF16) for i in range(2)]
    ysbs = [A.sb("m_ysb%d" % i, [16, 32, 128], F32) for i in range(2)]
    P1 = A.sb("m_P1", [H, 2, P], F32)
    P2 = A.sb("m_P2", [H, 2, P], F32)
    Vs = A.sb("m_Vs", [H, 2, P, 16], F32)
    H0s = A.sb("m_H0s", [H, 2, P, 16], F32)
    psU = [A.ps("m_psU%d" % i, [P, 32, 16]) for i in range(2)]
    psV = [A.ps("m_psV%d" % i, [P, 32, 16]) for i in range(2)]
    psY = [A.ps("m_psY%d" % i, [P, 4, P]) for i in range(2)]
    ptr = A.ps("m_ptr", [P, 4, P])
    S.op("pool", lambda e: e.memset(VH[:], 0.0), writes=["VH"])
    cn = {"u": 0, "U": 0, "V": 0, "Y": 0, "me": 0, "ys": 0}

    def make_U(src_ap, ub):
        Ut = Uts[ub]
        i = cn["u"] % 2
        cn["u"] += 1
        S.dma("sp", uin[i][:], src_ap, key=("uin", i), writes=[("uin", i)])
        for ch in range(4):
            uv = uin[i][:, ch * 512:(ch + 1) * 512].rearrange("p (g q) -> p g q", q=16)
            S.op("dve", lambda e, uv=uv: e.tensor_tensor(
                out=urep[:].rearrange("p g (s q) -> p g s q", q=16),
                in0=uv.unsqueeze(2).to_broadcast([P, 32, 8, 16]),
                in1=selm[:].unsqueeze(1).unsqueeze(3).to_broadcast([P, 32, 8, 16]), op=ALU.mult),
                reads=[("uin", i), "selm"], writes=["urep"])
            b = cn["U"] % 2
            cn["U"] += 1
            for j in range(32):
                S.op("pe", lambda e, b=b, j=j: e.matmul(psU[b][:, j, :], lhsT=urep[:, j, :], rhs=bsel[:],
                                                        start=True, stop=True),
                     reads=["urep", "bsel"], writes=[("psU", b)])
            S.op("act", lambda e, b=b, ch=ch: e.activation(out=Ut[:, ch * 32:(ch + 1) * 32, :], in_=psU[b][:],
                                                           func=AF.Copy),
                 reads=[("psU", b)], writes=[("Ut", ub, ch)])

    def make_V(dst_fn, dkey, ub):
        Ut = Uts[ub]
        for ch in range(4):
            b = cn["V"] % 2
            cn["V"] += 1
            for j in range(32):
                g = ch * 32 + j
                S.op("pe", lambda e, b=b, j=j, g=g: e.matmul(psV[b][:, j, :], lhsT=Gm[:, g, :], rhs=Ut[:, g, :],
                                                             start=True, stop=True),
                     reads=[("G", ch), ("Ut", ub, ch)], writes=[("psV", b)])
            for c in range(2):
                S.op("act", lambda e, b=b, c=c, ch=ch: e.activation(
                    out=dst_fn(c, ch), in_=psV[b][c * H:(c + 1) * H, :, :], func=AF.Copy),
                    reads=[("psV", b)], writes=[dkey])

    def cstep(Hj0, Hj1, Hj, Hn, key, w, extra=(), tk=("P1", "P2a", "P2b")):
        p1, p2 = w
        S.op("dve", lambda e: e.tensor_tensor(out=p1, in0=AA_v(Hj), in1=Hj, op=ALU.mult),
             reads=[key] + AK + list(extra), writes=[tk[0]])
        S.op("dve", lambda e: e.tensor_tensor(out=sub(p2, 0), in0=AB_v(Hj, 0), in1=Hj1, op=ALU.mult),
             reads=[key] + AK + list(extra), writes=[tk[1]])
        S.op("dve", lambda e: e.tensor_tensor(out=sub(p2, 1), in0=AB_v(Hj, 1), in1=Hj0, op=ALU.mult),
             reads=[key] + AK + list(extra), writes=[tk[2]])
        S.op("dve", lambda e: e.tensor_tensor(out=Hn, in0=Hn, in1=p1, op=ALU.add), reads=[key, tk[0]], writes=[key])
        S.op("dve", lambda e: e.tensor_tensor(out=Hn, in0=Hn, in1=p2, op=ALU.add),
             reads=[key, tk[1], tk[2]], writes=[key])

    def sub(ap, c):
        return ap[:, c]

    def AA_v(like):
        if len(like.shape) == 3:
            return AA[:]
        return AA[:].unsqueeze(3).to_broadcast([H, 2, P, like.shape[-1]])

    def AB_v(like, c):
        if len(like.shape) == 3:
            return AB[:, c, :]
        return AB[:, c, :].unsqueeze(2).to_broadcast([H, P, like.shape[-1]])

    def make_Hbf(src, skey, hb):
        Hbf = Hbfs[hb]
        S.op("act", lambda e: e.activation(out=Hbf[0:H], in_=src[:, 0, :, 0:16], func=AF.Copy),
             reads=[skey], writes=[("Hbf0", hb)])
        S.op("act", lambda e: e.activation(out=Hbf[H:P], in_=src[:, 1, :, 0:16], func=AF.Copy),
             reads=[skey], writes=[("Hbf1", hb)])

    def make_Y(t, ub, hb):
        Ut = Uts[ub]
        Hbf = Hbfs[hb]
        for ch in range(4):
            ysi = cn["ys"] % 2
            cn["ys"] += 1
            ysb = ysbs[ysi]
            gs = slice(ch * 32, ch * 32 + 32)
            mi = cn["me"] % 2
            cn["me"] += 1
            S.dma("sp", MEc[mi][:, :, 0, :], SM[:, gs, :], key=("ME", mi), writes=[("ME", mi)])
            S.dma("sp", MEc[mi][:, :, 1, :], SE[:, gs, :], key=("ME", mi), writes=[("ME", mi)])
            for g4 in range(8):
                b = cn["Y"] % 2
                cn["Y"] += 1
                for j in range(4):
                    gl = g4 * 4 + j
                    g = ch * 32 + gl
                    S.op("pe", lambda e, b=b, j=j, g=g, gl=gl, mi=mi: e.matmul(
                        psY[b][0:16, j, :], lhsT=Ut[:, g, :], rhs=MEc[mi][:, gl, 0, :], start=True, stop=False),
                        reads=[("Ut", ub, ch), ("ME", mi)], writes=[("psY", b)])
                    S.op("pe", lambda e, b=b, j=j, g=g, gl=gl, mi=mi: e.matmul(
                        psY[b][0:16, j, :], lhsT=Hbf[:, g, :], rhs=MEc[mi][:, gl, 1, :], start=False, stop=True),
                        reads=[("Hbf0", hb), ("Hbf1", hb), ("ME", mi)], writes=[("psY", b)])
                S.op("act", lambda e, b=b, g4=g4, ysb=ysb: e.activation(out=ysb[:, g4 * 4:(g4 + 1) * 4, :],
                                                                        in_=psY[b][0:16, :, :], func=AF.Copy),
                     reads=[("psY", b)], writes=[("ysb", ysi)])
            dst = YS[t * P:(t + 1) * P, ch * 512:(ch + 1) * 512].rearrange("(b i) (g p) -> b i g p", i=8, p=16)
            for i_ in range(8):
                S.dma("act", dst[:, i_], ysb[:, :, i_ * 16:(i_ + 1) * 16], key=("ysb_st", ysi),
                      reads=[("ysb", ysi)], is_output=True)

    def vh_dst(c, ch):
        return VH[:, c, ch * 32:(ch + 1) * 32, 1:17]

    tiles = [(zp[t * P:(t + 1) * P, 6144:8192], t, False) for t in range(8)] + \
            [(zo[t * P:(t + 1) * P, 12288:14336], t, True) for t in range(8)]
    make_U(tiles[0][0], 0)
    make_V(vh_dst, "VH", 0)
    for idx, (src_ap, t, own) in enumerate(tiles):
        cur = idx % 2
        nxt = idx + 1 < len(tiles)
        if nxt:
            make_U(tiles[idx + 1][0], 1 - cur)
        for j in range(16):
            cstep(VH[:, 0, :, j], VH[:, 1, :, j], VH[:, :, :, j], VH[:, :, :, j + 1], "VH", (P1[:], P2[:]))
        if own:
            make_Hbf(VH, "VH", cur)
        S.op("dve", lambda e: e.tensor_copy(out=VH[:, :, :, 0], in_=VH[:, :, :, 16]), reads=["VH"], writes=["VH"])
        if nxt:
            make_V(vh_dst, "VH", 1 - cur)
        if own:
            make_Y(t, cur, cur)
    hout = A.sb("m_hout", [P, 16, H], F32)
    for c in range(2):
        S.op("pe", lambda e, c=c: e.transpose(ptr[:, c, 0:H], VH[:, c, :, 0], ident[0:H, 0:H]),
             reads=["VH", "ident"], writes=["ptr"])
    S.op("act", lambda e: e.activation(out=hout[:, 0:2, :], in_=ptr[:, 0:2, 0:H], func=AF.Copy),
         reads=["ptr"], writes=["hout"])
    S.dma("sp", s5p_out.rearrange("c g n -> g c n"), hout[:, 0:2, :], key="s5p_st", reads=["hout"],
          writes=["s5p_dram"], is_output=True)

    hraw = A.sb("m_hraw", [P, 16, H], F32)
    for c in range(2):
        S.dma("sp", hraw[:], s5in[c].rearrange("s g n -> g s n"), key="hraw", writes=["hraw"])
        for s4 in range(4):
            for j in range(4):
                s_ = s4 * 4 + j
                S.op("pe", lambda e, j=j, s_=s_: e.transpose(ptr[0:H, j, :], hraw[:, s_, :], ident[:]),
                     reads=["hraw", "ident"], writes=["ptr"])
            S.op("act", lambda e, c=c, s4=s4: e.activation(
                out=H0s[:, c, :, s4 * 4:(s4 + 1) * 4].rearrange("p g s -> p s g"), in_=ptr[0:H, :, :],
                func=AF.Copy), reads=["ptr"], writes=["H0s"])
    make_U(zo[1024:1152, 12288:14336], 0)
    make_V(lambda c, ch: Vs[:, c, ch * 32:(ch + 1) * 32, :], "Vs", 0)
    make_Hbf(H0s, "H0s", 0)
    make_Y(8, 0, 0)
    P1s = uin[0][0:H, :].rearrange("p (c g s) -> p c g s", c=2, g=P)
    P2s = uin[1][0:H, :].rearrange("p (c g s) -> p c g s", c=2, g=P)
    for hh_ in range(2):
        hs = slice(hh_ * 8, hh_ * 8 + 8)
        cstep(H0s[:, 0, :, hs], H0s[:, 1, :, hs], H0s[:, :, :, hs], Vs[:, :, :, hs], "Vs", (P1s, P2s),
              extra=["H0s"], tk=(("uin", 0), ("uin", 1), ("uin", 1)))
    for c in range(2):
        for s8 in range(2):
            for j in range(8):
                s_ = s8 * 8 + j
                S.op("pe", lambda e, c=c, j=j, s_=s_: e.transpose(
                    ptr[:, j // 2, (j % 2) * H:(j % 2 + 1) * H], Vs[:, c, :, s_], ident[0:H, 0:H]),
                    reads=["Vs", "ident"], writes=["ptr"])
            S.op("act", lambda e, s8=s8: e.activation(
                out=hout[:, s8 * 8:(s8 + 1) * 8, :], in_=ptr[:].rearrange("p a (b n) -> p (a b) n", n=H),
                func=AF.Copy), reads=["ptr"], writes=["hout"])
        S.dma("sp", s5s_out[c].rearrange("s g n -> g s n"), hout[:], key="s5s_st", reads=["hout"],
              writes=["s5s_dram"], is_output=True)


def build_program(debug=False, stages=None, scr_in=()):
    nc = bass.Bass("TRN2", target_bir_lowering=False)
    NT = NTOK_OWN // P

    def din(name, shape, dt=F32):
        return nc.dram_tensor(name, list(shape), dt, kind="ExternalInput").ap()

    def dout(name, shape, dt=F32):
        return nc.dram_tensor(name, list(shape), dt, kind="ExternalOutput").ap()

    def dscr(name, shape, dt=F32):
        kind = "ExternalOutput" if debug else "Internal"
        if name in scr_in:
            kind = "ExternalInput"
        return nc.dram_tensor(name, list(shape), dt, kind=kind).ap()

    xo = din("xo", [NTOK_OWN, D_MODEL])
    xp = din("xp", [NTOK_PRE, D_MODEL])
    mem = din("mem", [256, D_MODEL])
    w_in = din("w_in", [D_MODEL, IN_WIDTH])
    w_mem_kv = din("w_mem_kv", [D_MODEL, 4096])
    ident = din("ident", [P, P])

    memkv = dout("memkv", [256, 4096])
    zo = dscr("zo", [NTOK_OWN, IN_WIDTH])
    zp = dscr("zp", [NTOK_PRE, 8192])

    def st_mem(S, A):
        blocks = [(c0, AF.Copy, (lambda t, c0=c0: memkv[t * P:(t + 1) * P, c0:c0 + 256]))
                  for c0 in range(0, 4096, 256)]
        gemm_body(S, A, "m", mem, 2, w_mem_kv, blocks, ident)
    if stages is None or 'mem' in stages:
        run_stage(nc, st_mem)

    def st_pre(S, A):
        blocks = []
        for (src0, n, dst0) in ((2048, 2048, 0), (4096, 4096, 2048), (12288, 2048, 6144)):
            for c in range(0, n, 256):
                blocks.append((src0 + c, AF.Copy,
                               (lambda t, d=dst0 + c: zp[t * P:(t + 1) * P, d:d + 256])))
        gemm_body(S, A, "p", xp, NTOK_PRE // P, w_in, blocks, ident)
    if stages is None or 'pre' in stages:
        run_stage(nc, st_pre)

    def st_own(S, A):
        blocks = [(c0, col_func(c0), (lambda t, c0=c0: zo[t * P:(t + 1) * P, c0:c0 + 256]))
                  for c0 in range(0, IN_WIDTH, 256)]
        gemm_body(S, A, "o", xo, NTOK_OWN // P, w_in, blocks, ident)
    if stages is None or 'own' in stages:
        run_stage(nc, st_own)

    rq = din("rq", [9, P, 2, 16, 64])
    rk_own = din("rk_own", [9, P, 2, 16, 64])
    rk_pre = din("rk_pre", [8, P, 2, 16, 64])
    tabs = {"rq": rq, "rk_own": rk_own, "rk_pre": rk_pre,
            "mask_p": din("mask_p", [P, P]), "mask_s": din("mask_s", [P, P]),
            "seqm": din("seqm", [P, 16]), "seqmT": din("seqmT", [P, 16, P])}
    sret_in = din("sret_in", [16, 16, P, 256])
    sret_out = dout("sret_out", [16, 16, P, 256])
    sretp_out = dout("sretp_out", [16, P, 256])
    OT = dscr("OT", [9, P, 64, P], BF16)

    def st_ret(S, A):
        retention_body(S, A, zo, zp, tabs, sret_in, sret_out, sretp_out, OT, ident)
    if stages is None or 'ret' in stages:
        run_stage(nc, st_ret)

    cmk = din("cmk", [16, 256, 2048])
    cmv = din("cmv", [16, 256, 2048])

    def st_x(S, A):
        xattn_body(S, A, zo, memkv, cmk, cmv, tabs["seqmT"], OT, ident)
    if stages is None or 'x' in stages:
        run_stage(nc, st_x)

    prm = {"a_re": din("s5_a_re", [P, 64]), "a_im": din("s5_a_im", [P, 64]), "log_step": din("s5_log_step", [1, P]),
           "b_re": din("s5_b_re", [P, 1024]), "b_im": din("s5_b_im", [P, 1024]),
           "c_re": din("s5_c_re", [P, 1024]), "c_im": din("s5_c_im", [P, 1024])}
    maskM = din("maskM", [P, P])
    SM = dscr("SM", [P, P, P], BF16)
    SG = dscr("SG", [P, P, P], BF16)
    SE = dscr("SE", [P, P, P], BF16)
    A8S = dscr("A8S", [64, 2, P])

    def st_s5prep(S, A):
        s5prep_body(S, A, prm, ident, maskM, SM, SG, SE, A8S)
    if stages is None or 's5prep' in stages:
        run_stage(nc, st_s5prep)

    selm = din("selm", [P, 8])
    s5in = din("s5in", [2, 16, P, 64])
    YS = dscr("YS", [NTOK_OWN, 2048])
    s5p_out = dout("s5p_out", [2, P, 64])
    s5s_out = dout("s5s_out", [2, 16, P, 64])

    def st_s5main(S, A):
        s5main_body(S, A, zo, zp, SM, SG, SE, A8S, selm, tabs["seqm"], ident, s5in, YS, s5p_out, s5s_out)
    if stages is None or 's5main' in stages:
        run_stage(nc, st_s5main)

    s5d = din("s5_d", [1, 2048])
    w_glu = din("w_glu", [2048, 4096])
    GL = dscr("GL", [NTOK_OWN, 2048])
    GAB = dscr("GAB", [NTOK_OWN, 4096])

    def st_gelu(S, A):
        db = A.sb("g_db", [P, 2048], F32)
        S.dma("sp", db[:], s5d.to_broadcast([P, 2048]), key="db", writes=["db"])
        yb = [A.sb("g_y%d" % i, [P, 2048], F32) for i in range(2)]
        ub = [A.sb("g_u%d" % i, [P, 2048], F32) for i in range(2)]
        tb = [A.sb("g_t%d" % i, [P, 2048], F32) for i in range(2)]
        for t in range(NT):
            i = t % 2
            y, u, tt_ = yb[i], ub[i], tb[i]
            S.dma("sp", y[:], YS[t * P:(t + 1) * P, :], key=("y", i), writes=[("y", i)])
            S.dma("sp", u[:], zo[t * P:(t + 1) * P, 12288:14336], key=("u", i), writes=[("u", i)])
            S.op("dve", lambda e, u=u: e.tensor_tensor(out=u[:], in0=u[:], in1=db[:], op=ALU.mult),
                 reads=[("u", i), "db"], writes=[("u", i)])
            S.op("dve", lambda e, y=y, u=u: e.tensor_tensor(out=y[:], in0=y[:], in1=u[:], op=ALU.add),
                 reads=[("y", i), ("u", i)], writes=[("y", i)])
            S.op("act", lambda e, y=y, tt_=tt_: e.activation(out=tt_[:], in_=y[:], func=AF.Square),
                 reads=[("y", i)], writes=[("t", i)])
            S.op("dve", lambda e, tt_=tt_: e.tensor_scalar(out=tt_[:], in0=tt_[:], scalar1=0.044715, scalar2=1.0,
                                                           op0=ALU.mult, op1=ALU.add),
                 reads=[("t", i)], writes=[("t", i)])
            S.op("dve", lambda e, y=y, tt_=tt_: e.tensor_tensor(out=tt_[:], in0=tt_[:], in1=y[:], op=ALU.mult),
                 reads=[("t", i), ("y", i)], writes=[("t", i)])
            S.op("act", lambda e, tt_=tt_: e.activation(out=tt_[:], in_=tt_[:], func=AF.Sigmoid,
                                                        scale=1.5957691216057308),
                 reads=[("t", i)], writes=[("t", i)])
            S.op("dve", lambda e, y=y, tt_=tt_: e.tensor_tensor(out=y[:], in0=y[:], in1=tt_[:], op=ALU.mult),
                 reads=[("t", i), ("y", i)], writes=[("y", i)])
            S.dma("sp", GL[t * P:(t + 1) * P, :], y[:], key=("gl", i), reads=[("y", i)], is_output=True)

    def st_glu(S, A):
        blocks = [(c0, (AF.Copy if c0 < 2048 else AF.Sigmoid),
                   (lambda t, c0=c0: GAB[t * P:(t + 1) * P, c0:c0 + 256])) for c0 in range(0, 4096, 256)]
        gemm_body(S, A, "g", GL, NT, w_glu, blocks, ident, nkt=16)

    def st_s5fin(S, A):
        identf = A.sb("f_ident", [P, P], F32)
        identb = A.sb("f_identb", [P, P], BF16)
        S.dma("sp", identf[:], ident, key="ident", writes=["ident"])
        S.op("dve", lambda e: e.tensor_copy(out=identb[:], in_=identf[:]), reads=["ident"], writes=["identb"])
        ab = [A.sb("f_ab%d" % i, [P, 4096], F32) for i in range(2)]
        gg = [A.sb("f_g%d" % i, [P, 2048], F32) for i in range(2)]
        ob = [A.sb("f_ob%d" % i, [P, 16, P], BF16) for i in range(2)]
        oT = [A.sb("f_oT%d" % i, [P, 16, P], BF16) for i in range(2)]
        ptr = [A.ps("f_ptr%d" % i, [P, 8, P], BF16) for i in range(2)]
        cnt = 0
        for t in range(NT):
            i = t % 2
            S.dma("sp", ab[i][:], GAB[t * P:(t + 1) * P, :], key=("ab", i), writes=[("ab", i)])
            S.dma("sp", gg[i][:], zo[t * P:(t + 1) * P, 14336:16384], key=("gg", i), writes=[("gg", i)])
            S.op("dve", lambda e, i=i: e.tensor_tensor(out=gg[i][:], in0=gg[i][:], in1=ab[i][:, 2048:4096],
                                                       op=ALU.mult),
                 reads=[("gg", i), ("ab", i)], writes=[("gg", i)])
            S.op("dve", lambda e, i=i: e.tensor_tensor(out=ob[i][:].rearrange("p a b -> p (a b)"),
                                                       in0=ab[i][:, 0:2048], in1=gg[i][:], op=ALU.mult),
                 reads=[("gg", i), ("ab", i)], writes=[("ob", i)])
            for half in range(2):
                b = cnt % 2
                cnt += 1
                for j in range(8):
                    S.op("pe", lambda e, b=b, j=j, i=i, half=half: e.transpose(
                        ptr[b][:, j, :], ob[i][:, half * 8 + j, :], identb[:]),
                        reads=[("ob", i), "identb"], writes=[("ptr", b)])
                S.op("act", lambda e, b=b, i=i, half=half: e.activation(
                    out=oT[i][:, half * 8:(half + 1) * 8, :], in_=ptr[b][:], func=AF.Copy),
                    reads=[("ptr", b)], writes=[("oT", i, half)])
            S.dma("act", OT[t, :, 32:48, :], oT[i][:], key=("oTs", i), reads=[("oT", i, 0), ("oT", i, 1)],
                  is_output=True)
    if stages is None or 's5post' in stages:
        run_stage(nc, st_gelu)
        run_stage(nc, st_glu)
        run_stage(nc, st_s5fin)

    w_pa = din("w_proj_a", [4096, D_MODEL])
    w_pb = din("w_proj_b", [2048, D_MODEL])
    w_pc = din("w_proj_c", [2048, D_MODEL])
    w_o = din("w_out", [D_MODEL, D_MODEL])
    ln_g = din("ln_g", [1, D_MODEL])
    ln_b = din("ln_b", [1, D_MODEL])
    y_out = dout("y_out", [NTOK_OWN, D_MODEL])
    PR = [dscr("PR%d" % i, [NTOK_OWN, D_MODEL]) for i in range(3)]
    HP = dscr("HP", [NTOK_OWN, D_MODEL])
    NT = NTOK_OWN // P

    def proj_stage(i, w_ap, ft0, nkt, gate0):
        def body(S, A):
            gt = [A.sb("gt%d_%d" % (i, j), [P, NT, 256], F32) for j in range(2)]

            def pre_block(S, bi):
                c0 = bi * 256
                S.dma("sp", gt[bi % 2][:], zo[:, gate0 + c0:gate0 + c0 + 256].rearrange("(t p) c -> p t c", p=P),
                      key=("gt", bi % 2), writes=[("gt", bi % 2)])

            def epi(S, t, bi, ps_ap, pkey, ob_t, okey):
                j = bi % 2
                S.op("dve", lambda e, j=j, t=t: e.tensor_tensor(out=ob_t[:], in0=ps_ap, in1=gt[j][:, t, :],
                                                                op=ALU.mult),
                     reads=[pkey, ("gt", j)], writes=[okey])
            blocks = [(c0, None, (lambda t, c0=c0: PR[i][t * P:(t + 1) * P, c0:c0 + 256]))
                      for c0 in range(0, D_MODEL, 256)]
            gemm_body(S, A, "j%d" % i, None, NT, w_ap, blocks, ident, nkt=nkt, a_T=(OT, ft0), epi=epi,
                      pre_block=pre_block)
        return body
    if stages is None or 'tail' in stages:
        run_stage(nc, proj_stage(0, w_pa, 0, 32, 20480))
        run_stage(nc, proj_stage(1, w_pb, 32, 16, 24576))
        run_stage(nc, proj_stage(2, w_pc, 48, 16, 28672))

    def st_out(S, A):
        xr = [A.sb("xr%d" % j, [P, NT, 256], F32) for j in range(2)]
        alpha = float((2.0 * 1) ** 0.25)

        def pre_block(S, bi):
            c0 = bi * 256
            S.dma("sp", xr[bi % 2][:], xo[:, c0:c0 + 256].rearrange("(t p) c -> p t c", p=P),
                  key=("xr", bi % 2), writes=[("xr", bi % 2)])

        def epi(S, t, bi, ps_ap, pkey, ob_t, okey):
            j = bi % 2
            S.op("dve", lambda e, j=j, t=t: e.scalar_tensor_tensor(out=ob_t[:], in0=xr[j][:, t, :], scalar=alpha,
                                                                   in1=ps_ap, op0=ALU.mult, op1=ALU.add),
                 reads=[pkey, ("xr", j)], writes=[okey])
        blocks = [(c0, None, (lambda t, c0=c0: HP[t * P:(t + 1) * P, c0:c0 + 256]))
                  for c0 in range(0, D_MODEL, 256)]
        gemm_body(S, A, "w", PR[0], NT, w_o, blocks, ident, x_sum=[PR[1], PR[2]], epi=epi, pre_block=pre_block)
    if stages is None or 'tail' in stages or 'out' in stages:
        run_stage(nc, st_out)

    def st_ln(S, A):
        gb = A.sb("ln_gb", [P, D_MODEL], F32)
        bb = A.sb("ln_bb", [P, D_MODEL], F32)
        S.dma("sp", gb[:], ln_g.to_broadcast([P, D_MODEL]), key="gb", writes=["gb"])
        S.dma("sp", bb[:], ln_b.to_broadcast([P, D_MODEL]), key="bb", writes=["bb"])
        hb = [A.sb("ln_h%d" % j, [P, D_MODEL], F32) for j in range(2)]
        st = A.sb("ln_st", [P, 8, 6], F32)
        mv = A.sb("ln_mv", [P, 2], F32)
        nb = A.sb("ln_nb", [P, 1], F32)
        for t in range(NT):
            j = t % 2
            h = hb[j]
            S.dma("sp", h[:], HP[t * P:(t + 1) * P, :], key=("h", j), writes=[("h", j)])
            for c in range(8):
                S.op("dve", lambda e, c=c, h=h: e.bn_stats(out=st[:, c, :], in_=h[:, c * 512:(c + 1) * 512]),
                     reads=[("h", j)], writes=[("st", c)])
            S.op("dve", lambda e: e.bn_aggr(out=mv[:], in_=st[:].rearrange("p a b -> p (a b)")),
                 reads=[("st", c) for c in range(8)], writes=["mv"])
            S.op("dve", lambda e: e.tensor_scalar(out=mv[:, 1:2], in0=mv[:, 1:2], scalar1=1e-5, scalar2=None,
                                                  op0=ALU.add), reads=["mv"], writes=["mv"])
            S.op("act", lambda e: e.activation(out=mv[:, 1:2], in_=mv[:, 1:2], func=AF.Sqrt),
                 reads=["mv"], writes=["mv"])
            S.op("dve", lambda e: e.reciprocal(out=mv[:, 1:2], in_=mv[:, 1:2]), reads=["mv"], writes=["mv"])
            S.op("dve", lambda e: e.scalar_tensor_tensor(out=nb[:], in0=mv[:, 0:1], scalar=-1.0, in1=mv[:, 1:2],
                                                         op0=ALU.mult, op1=ALU.mult),
                 reads=["mv"], writes=["nb"])
            S.op("act", lambda e, h=h: e.activation(out=h[:], in_=h[:], func=AF.Identity, bias=nb[:],
                                                    scale=mv[:, 1:2]),
                 reads=[("h", j), "mv", "nb"], writes=[("h", j)])
            S.op("dve", lambda e, h=h: e.tensor_tensor(out=h[:], in0=h[:], in1=gb[:], op=ALU.mult),
                 reads=[("h", j), "gb"], writes=[("h", j)])
            S.op("dve", lambda e, h=h: e.tensor_tensor(out=h[:], in0=h[:], in1=bb[:], op=ALU.add),
                 reads=[("h", j), "bb"], writes=[("h", j)])
            S.dma("sp", y_out[t * P:(t + 1) * P, :], h[:], key=("hout", j), reads=[("h", j)], is_output=True)
    if stages is None or 'tail' in stages or 'ln' in stages:
        run_stage(nc, st_ln)

    return nc


def host_tables(hf):
    f = np.float32
    inv = (1.0 / (np.float32(10000.0) ** (np.arange(64, dtype=f) / np.float32(64)))).astype(f)
    g = np.array(RET_G, dtype=np.float64)
    i = np.arange(P)

    def tab(pos, il, kind):
        ang = pos.astype(f)[:, None] * inv[None, :]
        c, s_ = np.cos(ang).astype(f), np.sin(ang).astype(f)
        if kind == "q":
            sc = g[None, :] ** (il[:, None] + 1.0)
        else:
            sc = g[None, :] ** (-(il[:, None] + 1.0)) * (128.0 ** -0.5)
        out = np.empty((P, 2, 16, 64), f)
        out[:, 0] = (c[:, None, :] * sc[:, :, None]).astype(f)
        out[:, 1] = (s_[:, None, :] * sc[:, :, None]).astype(f)
        return out
    rq = np.stack([tab(hf * 1024 + t * P + i, i, "q") for t in range(8)] + [tab(16384 + (i % 8), i % 8, "q")])
    rk_own = np.stack([tab(hf * 1024 + t * P + i, i, "k") for t in range(8)] + [tab(16384 + (i % 8), i % 8, "k")])
    rk_pre = np.stack([tab(t * P + i, i, "k") for t in range(8)])
    mask_p = (i[None, :] >= i[:, None]).astype(f)
    same = (i[None, :] // 8) == (i[:, None] // 8)
    mask_s = (mask_p * same).astype(f)
    seqm = (i[:, None] // 8 == np.arange(16)[None, :]).astype(f)
    seqmT = np.ascontiguousarray(np.broadcast_to(seqm.T[None, :, :], (P, 16, P))).astype(f)
    maskM = ((i[None, :] // 16) >= (i[:, None] // 16)).astype(f)
    selm = (i[:, None] % 8 == np.arange(8)[None, :]).astype(f)
    return {"rq": rq, "rk_own": rk_own, "rk_pre": rk_pre, "mask_p": mask_p, "mask_s": mask_s,
            "seqm": seqm, "seqmT": seqmT, "maskM": maskM, "selm": selm}


_PROGRAM = None


def kernel(x_prompt, x_sample, mem_prompt, state_ret, state_s5_re, state_s5_im, cache_mem_k, cache_mem_v,
           w_in, w_mem_kv, s5_a_re, s5_a_im, s5_log_step, s5_b_re, s5_b_im, s5_c_re, s5_c_im, s5_d, w_glu,
           w_proj_a, w_proj_b, w_proj_c, w_out, ln_g, ln_b):
    global _PROGRAM
    if _PROGRAM is None:
        _PROGRAM = build_program()
    nc = _PROGRAM
    f = np.float32
    x_prompt = np.asarray(x_prompt, f)
    x_sample = np.asarray(x_sample, f)
    w_in0 = np.ascontiguousarray(np.asarray(w_in, f)[0])
    w_mem0 = np.ascontiguousarray(np.asarray(w_mem_kv, f)[0])
    ident = np.eye(P, dtype=f)
    wpa = np.ascontiguousarray(np.asarray(w_proj_a, f)[0])
    wpb = np.ascontiguousarray(np.asarray(w_proj_b, f)[0])
    wpc = np.ascontiguousarray(np.asarray(w_proj_c, f)[0])
    wout = np.ascontiguousarray(np.asarray(w_out, f)[0])
    lng = np.ascontiguousarray(np.asarray(ln_g, f).reshape(1, D_MODEL))
    lnb = np.ascontiguousarray(np.asarray(ln_b, f).reshape(1, D_MODEL))
    s5p = {"a_re": np.ascontiguousarray(np.asarray(s5_a_re, f)[0]), "a_im": np.ascontiguousarray(np.asarray(s5_a_im, f)[0]),
           "log_step": np.ascontiguousarray(np.asarray(s5_log_step, f).reshape(1, P)),
           "b_re": np.ascontiguousarray(np.asarray(s5_b_re, f)[0].reshape(P, 1024)),
           "b_im": np.ascontiguousarray(np.asarray(s5_b_im, f)[0].reshape(P, 1024)),
           "c_re": np.ascontiguousarray(np.asarray(s5_c_re, f)[0].reshape(P, 1024)),
           "c_im": np.ascontiguousarray(np.asarray(s5_c_im, f)[0].reshape(P, 1024)),
           "d": np.ascontiguousarray(np.asarray(s5_d, f).reshape(1, 2048))}
    wglu = np.ascontiguousarray(np.asarray(w_glu, f)[0])
    in_maps = []
    for c in range(NCORES):
        b, hf = c // 2, c % 2
        xo = np.concatenate([x_prompt[b, hf * 1024:(hf + 1) * 1024],
                             x_sample[16 * c:16 * c + 16].reshape(128, D_MODEL)], axis=0)
        xp = x_prompt[b, 0:1024] if hf == 1 else np.zeros((1024, D_MODEL), f)
        in_maps.append({
            "xo": np.ascontiguousarray(xo), "xp": np.ascontiguousarray(xp),
            "mem": np.ascontiguousarray(np.asarray(mem_prompt, f)[b]),
            "w_in": w_in0, "w_mem_kv": w_mem0, "ident": ident,
            "sret_in": np.ascontiguousarray(np.asarray(state_ret, f)[0, 16 * c:16 * c + 16]),
            "cmk": np.ascontiguousarray(np.asarray(cache_mem_k, f)[0, 16 * c:16 * c + 16]).reshape(16, 256, 2048),
            "cmv": np.ascontiguousarray(np.asarray(cache_mem_v, f)[0, 16 * c:16 * c + 16]).reshape(16, 256, 2048),
            "w_proj_a": wpa, "w_proj_b": wpb, "w_proj_c": wpc, "w_out": wout, "ln_g": lng, "ln_b": lnb,
            "s5_a_re": s5p["a_re"], "s5_a_im": s5p["a_im"], "s5_log_step": s5p["log_step"],
            "s5_b_re": s5p["b_re"], "s5_b_im": s5p["b_im"], "s5_c_re": s5p["c_re"], "s5_c_im": s5p["c_im"],
            "s5_d": s5p["d"], "w_glu": wglu,
            "s5in": np.ascontiguousarray(np.stack([np.asarray(state_s5_re, f)[0, 16 * c:16 * c + 16],
                                                   np.asarray(state_s5_im, f)[0, 16 * c:16 * c + 16]])),
        })
        in_maps[-1].update(host_tables(hf))
    res = run_bass_kernel_spmd(nc, in_maps, core_ids=list(range(NCORES)))
    R = res.results
    memk = np.stack([R[2 * b]["memkv"][:, 0:2048].reshape(256, 4, 512) for b in range(4)])[None]
    memv = np.stack([R[2 * b]["memkv"][:, 2048:4096].reshape(256, 4, 512) for b in range(4)])[None]
    y_p = np.stack([np.concatenate([R[2 * b]["y_out"][:1024], R[2 * b + 1]["y_out"][:1024]], axis=0)
                    for b in range(4)])
    y_s = np.concatenate([R[c]["y_out"][1024:] for c in range(NCORES)], axis=0).reshape(128, 8, D_MODEL)
    sretp = np.stack([R[2 * b + 1]["sretp_out"] for b in range(4)])[None]
    srets = np.concatenate([R[c]["sret_out"] for c in range(NCORES)], axis=0)[None]
    s5p_re = np.stack([R[2 * b + 1]["s5p_out"][0] for b in range(4)])[None]
    s5p_im = np.stack([R[2 * b + 1]["s5p_out"][1] for b in range(4)])[None]
    s5s_re = np.concatenate([R[c]["s5s_out"][0] for c in range(NCORES)], axis=0)[None]
    s5s_im = np.concatenate([R[c]["s5s_out"][1] for c in range(NCORES)], axis=0)[None]
    return (y_p, y_s, sretp, s5p_re, s5p_im, memk, memv, srets, s5s_re, s5s_im)
```

```python
import math
from contextlib import ExitStack

import numpy as np
import concourse.bass as bass
import concourse.mybir as mybir
from concourse.bass_utils import run_bass_kernel_spmd

F32 = mybir.dt.float32
BF16 = mybir.dt.bfloat16
AF = mybir.ActivationFunctionType
ALU = mybir.AluOpType
P = 128
NCORES = 8

D_MODEL = 4096
IN_WIDTH = 32768
NTOK_OWN = 1152
NTOK_PRE = 1024

ENGS = ("pe", "act", "dve", "pool", "sp")
SEM_LIMIT = 20000


class Ins:
    __slots__ = ("eng", "fn", "deps", "is_dma", "key", "need_inc", "semref")

    def __init__(self, eng, fn, is_dma=False, key=None):
        self.eng = eng
        self.fn = fn
        self.deps = []
        self.is_dma = is_dma
        self.key = key
        self.need_inc = False
        self.semref = None


class Sched:
    _stage = 0

    def __init__(self, nc):
        Sched._stage += 1
        self.sid = Sched._stage
        self.nc = nc
        self.ins = []
        self.last_w = {}
        self.readers = {}
        self.dma_count = {}
        self.out_keys = set()

    def _add(self, ins, reads, writes):
        deps = set()
        for k in reads:
            w = self.last_w.get(k)
            if w is not None:
                deps.add(w)
        for k in writes:
            w = self.last_w.get(k)
            if w is not None:
                deps.add(w)
            for r in self.readers.get(k, ()):
                deps.add(r)
        deps.discard(ins)
        for d in deps:
            if d.is_dma:
                ins.deps.append((d, 16 * self.dma_count[d.key]))
            elif d.eng == ins.eng and not ins.is_dma:
                if ins.eng != "pe":
                    ins.deps.append((d, 0))
                    d.need_inc = True
            else:
                ins.deps.append((d, 0))
                d.need_inc = True
        for k in reads:
            self.readers.setdefault(k, []).append(ins)
        for k in writes:
            self.last_w[k] = ins
            self.readers[k] = []
        self.ins.append(ins)
        return ins

    def op(self, eng, fn, reads=(), writes=()):
        return self._add(Ins(eng, fn), list(reads), list(writes))

    def dma(self, eng, out, in_, key, reads=(), writes=(), is_output=False):
        ins = Ins(eng, lambda e: e.dma_start(out=out, in_=in_), is_dma=True, key=key)
        if is_output:
            self.out_keys.add(key)
        self.dma_count.setdefault(key, 0)
        self._add(ins, list(reads), list(writes))
        self.dma_count[key] += 1
        return ins

    def emit(self):
        nc = self.nc
        sem_names = []
        cur = {}
        for ins in self.ins:
            if ins.is_dma or not ins.need_inc:
                continue
            c = cur.get(ins.eng)
            if c is None or c[1] >= SEM_LIMIT:
                c = [len(sem_names), 0]
                sem_names.append("c%d_%s_%d" % (self.sid, ins.eng, len(sem_names)))
                cur[ins.eng] = c
            c[1] += 1
            ins.semref = (c[0], c[1])
        dma_keys = sorted(self.dma_count.keys(), key=str)
        csem = [nc.alloc_semaphore(name=n) for n in sem_names]
        dsem = {k: nc.alloc_semaphore(name="d%d_%d" % (self.sid, i)) for i, k in enumerate(dma_keys)}
        streams = {e: [i for i in self.ins if i.eng == e] for e in ENGS}
        final_dma = dict((k, 16 * v) for k, v in self.dma_count.items())
        out_keys = self.out_keys

        def run(engname, e):
            waited = {}
            for ins in streams[engname]:
                need = {}
                for d, dv in ins.deps:
                    if d.is_dma:
                        sk = ("d", d.key)
                        v = dv
                    else:
                        sk = ("c", d.semref[0])
                        v = d.semref[1]
                    if v > need.get(sk, 0):
                        need[sk] = v
                for sk, v in need.items():
                    if waited.get(sk, 0) >= v:
                        continue
                    waited[sk] = v
                    sem = dsem[sk[1]] if sk[0] == "d" else csem[sk[1]]
                    e.wait_ge(sem, v)
                r = ins.fn(e)
                if ins.is_dma:
                    r.then_inc(dsem[ins.key], 16)
                elif ins.need_inc:
                    r.then_inc(csem[ins.semref[0]], 1)
            if engname == "sp":
                for k in sorted(out_keys, key=str):
                    e.wait_ge(dsem[k], final_dma[k])

        with nc.Block() as block:
            @block.tensor
            def _(e):
                run("pe", e)

            @block.scalar
            def _(e):
                run("act", e)

            @block.vector
            def _(e):
                run("dve", e)

            @block.gpsimd
            def _(e):
                run("pool", e)

            @block.sync
            def _(e):
                run("sp", e)

        if not getattr(Sched, "NOCLEAR", False):
            nc.clear_and_free_semaphores(csem + list(dsem.values()))
        if not getattr(Sched, "NOCLEAR", False):
            nc.all_engine_barrier()


class Alloc:
    def __init__(self, nc, st):
        self.nc = nc
        self.st = st

    def sb(self, name, shape, dt):
        return self.st.enter_context(self.nc.sbuf_tensor(name, list(shape), dt))

    def ps(self, name, shape, dt=F32):
        return self.st.enter_context(self.nc.psum_tensor(name, list(shape), dt))


def run_stage(nc, body):
    with ExitStack() as st:
        S = Sched(nc)
        A = Alloc(nc, st)
        body(S, A)
        S.emit()


def gemm_body(S, A, uid, x_ap, ntile, w_ap, blocks, ident_ap, nkt=32, a_T=None, x_sum=None, epi=None,
              store_eng="act", pre_block=None):
    xT = A.sb("xT" + uid, [P, ntile, nkt, P], BF16)
    ident = A.sb("ident" + uid, [P, P], F32)
    S.dma("sp", ident[:], ident_ap, key="ident", writes=["ident"])
    pT = [A.ps("pT%d%s" % (i, uid), [P, 4, P]) for i in range(2)]
    cnt = 0
    if a_T is not None:
        OTd, ft0 = a_T
        for t in range(ntile):
            S.dma("sp", xT[:, t, :, :], OTd[t, :, ft0:ft0 + nkt, :], key=("xTl", t % 4),
                  writes=[("xT", t, kq) for kq in range(nkt // 4)])
    else:
        xin = [A.sb("xin%d%s" % (i, uid), [P, nkt * P], F32) for i in range(2)]
        if x_sum:
            xad = A.sb("xad" + uid, [P, nkt * P], F32)
    for t in range(ntile if a_T is None else 0):
        xi = t % 2
        S.dma("sp", xin[xi][:], x_ap[t * P:(t + 1) * P, :], key=("xin", xi), writes=[("xin", xi)])
        for extra in (x_sum or ()):
            S.dma("sp", xad[:], extra[t * P:(t + 1) * P, :], key="xad", writes=["xad"])
            S.op("dve", lambda e, xi=xi: e.tensor_tensor(out=xin[xi][:], in0=xin[xi][:], in1=xad[:], op=ALU.add),
                 reads=[("xin", xi), "xad"], writes=[("xin", xi)])
        for kq in range(nkt // 4):
            b = cnt % 2
            cnt += 1
            for j in range(4):
                kt = kq * 4 + j
                S.op("pe", lambda e, b=b, j=j, xi=xi, kt=kt: e.transpose(
                    pT[b][:, j, :], xin[xi][:, kt * P:(kt + 1) * P], ident[:]),
                    reads=[("xin", xi), "ident"], writes=[("pT", b)])
            if kq % 2 == 0:
                S.op("act", lambda e, b=b, t=t, kq=kq: e.activation(
                    out=xT[:, t, kq * 4:(kq + 1) * 4, :], in_=pT[b][:], func=AF.Copy),
                    reads=[("pT", b)], writes=[("xT", t, kq)])
            else:
                S.op("dve", lambda e, b=b, t=t, kq=kq: e.tensor_copy(
                    out=xT[:, t, kq * 4:(kq + 1) * 4, :], in_=pT[b][:]),
                    reads=[("pT", b)], writes=[("xT", t, kq)])

    stg = [A.sb("stg%d%s" % (i, uid), [P, 8, 256], F32) for i in range(4)]
    wb = [A.sb("wb%d%s" % (i, uid), [P, nkt, 256], BF16) for i in range(2)]
    pz = [A.ps("pz%d%s" % (i, uid), [P, 512]) for i in range(4)]
    ob = [A.sb("ob%d%s" % (i, uid), [P, 256], F32) for i in range(4)]
    w_view = w_ap.rearrange("(kt p) c -> p kt c", p=P)
    ctr = {"stg": 0, "pz": 0}

    def load_block(bi):
        c0 = blocks[bi][0]
        if pre_block is not None:
            pre_block(S, bi)
        for c in range(nkt // 8):
            s = ctr["stg"] % 4
            ctr["stg"] += 1
            S.dma("sp", stg[s][:], w_view[:, c * 8:(c + 1) * 8, c0:c0 + 256],
                  key=("stg", s), writes=[("stg", s)])
            if c % 2 == 0:
                S.op("dve", lambda e, bi=bi, c=c, s=s: e.tensor_copy(
                    out=wb[bi % 2][:, c * 8:(c + 1) * 8, :], in_=stg[s][:]),
                    reads=[("stg", s)], writes=[("wb", bi % 2, c)])
            else:
                S.op("act", lambda e, bi=bi, c=c, s=s: e.activation(
                    out=wb[bi % 2][:, c * 8:(c + 1) * 8, :], in_=stg[s][:], func=AF.Copy),
                    reads=[("stg", s)], writes=[("wb", bi % 2, c)])

    nb = len(blocks)
    if nb:
        load_block(0)
    for bi in range(nb):
        if bi + 1 < nb:
            load_block(bi + 1)
        _, func, out_fn = blocks[bi]
        for t in range(ntile if not getattr(Sched, 'NOMM', False) else 0):
            pb = ctr["pz"] % 4
            ctr["pz"] += 1
            for kt in range(nkt):
                S.op("pe", lambda e, pb=pb, t=t, kt=kt, bi=bi: e.matmul(
                    pz[pb][:, 0:256], lhsT=xT[:, t, kt, :], rhs=wb[bi % 2][:, kt, :],
                    start=(kt == 0), stop=(kt == nkt - 1)),
                    reads=[("xT", t, kt // 4), ("wb", bi % 2, kt // 8)], writes=[("pz", pb)])
            if epi is not None:
                epi(S, t, bi, pz[pb][:, 0:256], ("pz", pb), ob[pb], ("ob", pb))
            else:
                S.op("act", lambda e, pb=pb, func=func: e.activation(
                    out=ob[pb][:], in_=pz[pb][:, 0:256], func=func),
                    reads=[("pz", pb)], writes=[("ob", pb)])
            S.dma(store_eng, out_fn(t), ob[pb][:], key=("ob", pb), reads=[("ob", pb)], is_output=True)


def col_func(c0):
    if 8192 <= c0 < 12288 or 14336 <= c0 < 16384 or 18432 <= c0 < 20480:
        return AF.Silu
    if c0 >= 20480:
        return AF.Sigmoid
    return AF.Copy


RET_G = [1.0 - 2.0 ** (-5.0 - h) for h in range(16)]


def retention_body(S, A, zo, zp, tabs, sret_in, sret_out, sretp_out, OT, ident_ap):
    ident = A.sb("r_ident", [P, P], F32)
    identb = A.sb("r_identb", [P, P], BF16)
    S.dma("sp", ident[:], ident_ap, key="ident", writes=["ident"])
    S.op("dve", lambda e: e.tensor_copy(out=identb[:], in_=ident[:]), reads=["ident"], writes=["identb"])
    maskp = A.sb("r_maskp", [P, P], F32)
    masks = A.sb("r_masks", [P, P], F32)
    seqm = A.sb("r_seqm", [P, 16], F32)
    seqmT = A.sb("r_seqmT", [P, 16, P], F32)
    S.dma("sp", maskp[:], tabs["mask_p"], key="maskp", writes=["maskp"])
    S.dma("sp", masks[:], tabs["mask_s"], key="masks", writes=["masks"])
    S.dma("sp", seqm[:], tabs["seqm"], key="seqm", writes=["seqm"])
    S.dma("sp", seqmT[:], tabs["seqmT"], key="seqmT", writes=["seqmT"])

    St = A.sb("r_S", [P, 16, 256], F32)
    Sb = A.sb("r_Sb", [P, 16, 256], BF16)
    S.op("pool", lambda e: e.memset(St[:], 0.0), writes=["S"])
    S.op("pool", lambda e: e.memset(Sb[:], 0.0), writes=["Sb"])

    qins = [A.sb("r_qin", [P, 2048], F32)] * 2
    kins = [A.sb("r_kin", [P, 2048], F32)] * 2
    vins = [A.sb("r_vin%d" % i, [P, 4096], F32) for i in range(2)]
    gins = [A.sb("r_gin%d" % i, [P, 4096], F32) for i in range(2)]
    cur = {"i": 0}
    rt = A.sb("r_rt", [P, 2, 16, 64], F32)
    t1 = A.sb("r_t1", [P, 16, 64], F32)
    t2 = A.sb("r_t2", [P, 16, 64], F32)
    qt = A.sb("r_qt", [P, 16, 128], BF16)
    kt_ = A.sb("r_kt", [P, 16, 128], BF16)
    vb = A.sb("r_vb", [P, 16, 256], BF16)
    qT = A.sb("r_qT", [P, 16, 128], BF16)
    kT = A.sb("r_kT", [P, 16, 128], BF16)
    scs = A.sb("r_scs", [P, 16, 128], BF16)
    osb = A.sb("r_osb", [P, 16, 256], F32)
    sq = vin.rearrange("p (h e) -> p h e", h=16) if False else None
    og = A.sb("r_og", [P, 16, 256], BF16)
    oT = A.sb("r_oT", [P, 32, 128], BF16)
    st1 = A.sb("r_st1", [P, 16], F32)
    st2 = A.sb("r_st2", [P, 16], F32)
    st3 = A.sb("r_st3", [P, 16], F32)
    dtmp = A.sb("r_dtmp", [P, 2, 256], F32)
    ptr = [A.ps("r_ptr%d" % i, [P, 8, 128], BF16) for i in range(2)]
    psc = [A.ps("r_psc%d" % i, [P, 4, 128]) for i in range(2)]
    po = [A.ps("r_po%d" % i, [P, 2, 256]) for i in range(2)]
    pd = [A.ps("r_pd%d" % i, [P, 2, 256]) for i in range(2)]
    cn = {"tr": 0, "sc": 0, "o": 0, "d": 0}

    def rotary(src, dst, rt_ap, rkey, skey, dkey):
        S.dma("sp", rt[:], rt_ap, key="rt", writes=["rt"])
        sv = src[:].rearrange("p (h j two) -> p h j two", h=16, two=2)
        dv = dst[:].rearrange("p h (j two) -> p h j two", two=2)
        S.op("dve", lambda e: e.tensor_tensor(out=t1[:], in0=sv[:, :, :, 0], in1=rt[:, 0], op=ALU.mult),
             reads=[skey, "rt"], writes=["t1"])
        S.op("pool", lambda e: e.tensor_tensor(out=t2[:], in0=sv[:, :, :, 1], in1=rt[:, 1], op=ALU.mult),
             reads=[skey, "rt"], writes=["t2"])
        S.op("dve", lambda e: e.tensor_tensor(out=dv[:, :, :, 0], in0=t1[:], in1=t2[:], op=ALU.subtract),
             reads=["t1", "t2"], writes=[dkey + "0"])
        S.op("dve", lambda e: e.tensor_tensor(out=t1[:], in0=sv[:, :, :, 0], in1=rt[:, 1], op=ALU.mult),
             reads=[skey, "rt", dkey + "0"], writes=["t1"])
        S.op("pool", lambda e: e.tensor_tensor(out=t2[:], in0=sv[:, :, :, 1], in1=rt[:, 0], op=ALU.mult),
             reads=[skey, "rt", dkey + "0"], writes=["t2"])
        S.op("dve", lambda e: e.tensor_tensor(out=dv[:, :, :, 1], in0=t1[:], in1=t2[:], op=ALU.add),
             reads=["t1", "t2"], writes=[dkey + "1"])

    def transpose16(src, dst, skeys, dkey):
        for half in range(2):
            b = cn["tr"] % 2
            cn["tr"] += 1
            for j in range(8):
                h = half * 8 + j
                S.op("pe", lambda e, b=b, j=j, h=h: e.transpose(ptr[b][:, j, :], src[:, h, :], identb[:]),
                     reads=list(skeys) + ["identb"], writes=[("ptr", b)])
            S.op("act", lambda e, b=b, half=half: e.activation(
                out=dst[:, half * 8:(half + 1) * 8, :], in_=ptr[b][:], func=AF.Copy),
                reads=[("ptr", b)], writes=[(dkey, half)])

    def state_update(g, sample_head=None):
        pass

    def chunk(kind, t):
        own = kind != "pre"
        cur["n"] = cur.get("n", -1) + 1
        ci = cur["n"] % 2
        cur["i"] = ci
        qin, kin, vin, gin = qins[ci], kins[ci], vins[ci], gins[ci]
        KI, VI, QI, GI = "kin", ("vin", ci), "qin", ("gin", ci)
        z = zo if own else zp
        r0 = t * P
        kcol = 2048 if own else 0
        vcol = 4096 if own else 2048
        S.dma("sp", kin[:], z[r0:r0 + P, kcol:kcol + 2048], key=KI, writes=[KI])
        S.dma("sp", vin[:], z[r0:r0 + P, vcol:vcol + 4096], key=VI, writes=[VI])
        S.op("act", lambda e: e.activation(out=vb[:].rearrange("p h e -> p (h e)"), in_=vin[:], func=AF.Copy),
             reads=[VI], writes=["vb"])
        rotary(kin, kt_, (tabs["rk_own"] if own else tabs["rk_pre"])[t], "rk", KI, "kt")
        if own:
            S.dma("sp", qin[:], z[r0:r0 + P, 0:2048], key=QI, writes=[QI])
            S.dma("sp", gin[:], z[r0:r0 + P, 8192:12288], key=GI, writes=[GI])
            rotary(qin, qt, tabs["rq"][t], "rq", QI, "qt")
            transpose16(qt, qT, ["qt0", "qt1"], "qT")
            transpose16(kt_, kT, ["kt0", "kt1"], "kT")
        return own

    def scores_and_out(mask, sample):
        for hq in range(4):
            b = cn["sc"] % 2
            cn["sc"] += 1
            for j in range(4):
                h = hq * 4 + j
                S.op("pe", lambda e, b=b, j=j, h=h: e.matmul(psc[b][:, j, :], lhsT=kT[:, h, :], rhs=qT[:, h, :],
                                                             start=True, stop=True),
                     reads=[("kT", h // 8), ("qT", h // 8)], writes=[("psc", b)])
            S.op("dve", lambda e, b=b, hq=hq: e.tensor_tensor(
                out=scs[:, hq * 4:(hq + 1) * 4, :], in0=psc[b][:],
                in1=mask[:].unsqueeze(1).to_broadcast([P, 4, P]), op=ALU.mult),
                reads=[("psc", b), "maskp", "masks"], writes=[("scs", hq)])

    def finish_out(t):
        ci = cur["i"]
        vin, gin = vins[ci], gins[ci]
        VI, GI = ("vin", ci), ("gin", ci)
        S.op("dve", lambda e: e.tensor_reduce(out=st1[:], in_=osb[:], op=ALU.add, axis=mybir.AxisListType.X),
             reads=["osb"], writes=["st1"])
        sqv = vin[:].rearrange("p (h e) -> p h e", h=16)
        S.op("act", lambda e: e.activation(out=sqv, in_=osb[:], func=AF.Square),
             reads=["osb"], writes=[VI])
        S.op("dve", lambda e: e.tensor_reduce(out=st2[:], in_=sqv, op=ALU.add, axis=mybir.AxisListType.X),
             reads=[VI], writes=["st2"])
        S.op("dve", lambda e: e.tensor_scalar(out=st1[:], in0=st1[:], scalar1=1.0 / 256, scalar2=None, op0=ALU.mult),
             reads=["st1"], writes=["st1"])
        S.op("dve", lambda e: e.tensor_tensor(out=st3[:], in0=st1[:], in1=st1[:], op=ALU.mult),
             reads=["st1"], writes=["st3"])
        S.op("dve", lambda e: e.scalar_tensor_tensor(out=st2[:], in0=st2[:], scalar=1.0 / 256, in1=st3[:],
                                                     op0=ALU.mult, op1=ALU.subtract),
             reads=["st2", "st3"], writes=["st2"])
        S.op("dve", lambda e: e.tensor_scalar(out=st2[:], in0=st2[:], scalar1=1e-5, scalar2=None, op0=ALU.add),
             reads=["st2"], writes=["st2"])
        S.op("act", lambda e: e.activation(out=st2[:], in_=st2[:], func=AF.Sqrt), reads=["st2"], writes=["st2"])
        S.op("dve", lambda e: e.reciprocal(out=st2[:], in_=st2[:]), reads=["st2"], writes=["st2"])
        S.op("dve", lambda e: e.tensor_tensor(out=osb[:], in0=osb[:],
                                              in1=st1[:].unsqueeze(2).to_broadcast([P, 16, 256]), op=ALU.subtract),
             reads=["osb", "st1"], writes=["osb"])
        S.op("dve", lambda e: e.tensor_tensor(out=osb[:], in0=osb[:],
                                              in1=st2[:].unsqueeze(2).to_broadcast([P, 16, 256]), op=ALU.mult),
             reads=["osb", "st2"], writes=["osb"])
        S.op("dve", lambda e: e.tensor_tensor(out=og[:].rearrange("p h e -> p (h e)"),
                                              in0=osb[:].rearrange("p h e -> p (h e)"), in1=gin[:], op=ALU.mult),
             reads=["osb", GI], writes=["og"])
        ogv = og[:].rearrange("p h (two e) -> p (h two) e", two=2)
        for q4 in range(4):
            b = cn["tr"] % 2
            cn["tr"] += 1
            for j in range(8):
                ft = q4 * 8 + j
                S.op("pe", lambda e, b=b, j=j, ft=ft: e.transpose(ptr[b][:, j, :], ogv[:, ft, :], identb[:]),
                     reads=["og", "identb"], writes=[("ptr", b)])
            S.op("act", lambda e, b=b, q4=q4: e.activation(out=oT[:, q4 * 8:(q4 + 1) * 8, :], in_=ptr[b][:],
                                                           func=AF.Copy),
                 reads=[("ptr", b)], writes=[("oT", q4)])
        S.dma("act", OT[t, :, 0:32, :], oT[:], key="oT", reads=[("oT", q) for q in range(4)], is_output=True)

    def prompt_state_update():
        for hp in range(8):
            b = cn["d"] % 2
            cn["d"] += 1
            for j in range(2):
                h = hp * 2 + j
                S.op("pe", lambda e, b=b, j=j, h=h: e.matmul(pd[b][:, j, :], lhsT=kt_[:, h, :], rhs=vb[:, h, :],
                                                             start=True, stop=True),
                     reads=["kt0", "kt1", "vb"], writes=[("pd", b)])
            for j in range(2):
                h = hp * 2 + j
                g = float(RET_G[h] ** 128)
                S.op("act", lambda e, b=b, j=j, g=g: e.activation(out=dtmp[:, j, :], in_=pd[b][:, j, :],
                                                                  func=AF.Copy, scale=g),
                     reads=[("pd", b)], writes=[("dtmp", j)])
                S.op("dve", lambda e, h=h, j=j, g=g: e.scalar_tensor_tensor(
                    out=St[:, h, :], in0=St[:, h, :], scalar=g, in1=dtmp[:, j, :], op0=ALU.mult, op1=ALU.add),
                    reads=[("dtmp", j), "S"], writes=["S"])
        S.op("act", lambda e: e.activation(out=Sb[:], in_=St[:], func=AF.Copy), reads=["S"], writes=["Sb"])

    for t in range(8):
        chunk("pre", t)
        prompt_state_update()

    for t in range(8):
        chunk("own", t)
        scores_and_out(maskp, False)
        for hp in range(8):
            b = cn["o"] % 2
            cn["o"] += 1
            for j in range(2):
                h = hp * 2 + j
                S.op("pe", lambda e, b=b, j=j, h=h: e.matmul(po[b][:, j, :], lhsT=scs[:, h, :], rhs=vb[:, h, :],
                                                             start=True, stop=False),
                     reads=[("scs", h // 4), "vb"], writes=[("po", b)])
                S.op("pe", lambda e, b=b, j=j, h=h: e.matmul(po[b][:, j, :], lhsT=qT[:, h, :], rhs=Sb[:, h, :],
                                                             start=False, stop=True),
                     reads=[("qT", h // 8), "Sb"], writes=[("po", b)])
            S.op("act", lambda e, b=b, hp=hp: e.activation(out=osb[:, hp * 2:hp * 2 + 2, :], in_=po[b][:],
                                                           func=AF.Copy),
                 reads=[("po", b)], writes=["osb"])
        prompt_state_update()
        finish_out(t)
    S.dma("sp", sretp_out.rearrange("h d e -> d h e"), St[:], key="St_out", reads=["S"], is_output=True)

    t = 8
    chunk("own", t)
    scores_and_out(masks, True)
    qTm = oT[:, 0:16, :]
    ktm = oT[:, 16:32, :]
    Ss_b = [vins[i][:].rearrange("p (h e) -> p h e", h=16) for i in range(2)]
    Ss_k = [("vin", i) for i in range(2)]
    Ssb_b = [og, A.sb("r_Ssb1", [P, 16, 256], BF16)]
    Ssb_k = ["og", "Ssb1"]

    def load_state(h):
        i = h % 2
        S.dma("sp", Ss_b[i], sret_in[:, h].rearrange("s d e -> d s e"), key=("Ss", i), writes=[Ss_k[i]])
    load_state(0)
    for h in range(16):
        bi_ = h % 2
        Ss, VI, Ssb, SBK = Ss_b[bi_], Ss_k[bi_], Ssb_b[bi_], Ssb_k[bi_]
        g8 = float(RET_G[h] ** 8)
        if h + 1 < 16:
            load_state(h + 1)
        S.op("act", lambda e, Ssb=Ssb, Ss=Ss: e.activation(out=Ssb[:], in_=Ss, func=AF.Copy), reads=[VI], writes=[SBK])
        S.op("dve", lambda e, h=h: e.tensor_tensor(
            out=qTm, in0=qT[:, h, :].unsqueeze(1).to_broadcast([P, 16, P]), in1=seqmT[:], op=ALU.mult),
            reads=[("qT", h // 8), "seqmT"], writes=[("oT", 0), ("oT", 1)])
        S.op("dve", lambda e, h=h: e.tensor_tensor(
            out=ktm, in0=kt_[:, h, :].unsqueeze(1).to_broadcast([P, 16, P]),
            in1=seqm[:].unsqueeze(2).to_broadcast([P, 16, P]), op=ALU.mult),
            reads=["kt0", "kt1", "seqm"], writes=[("oT", 2), ("oT", 3)])
        b = cn["o"] % 2
        cn["o"] += 1
        S.op("pe", lambda e, b=b, h=h: e.matmul(po[b][:, 0, :], lhsT=scs[:, h, :], rhs=vb[:, h, :],
                                                start=True, stop=False),
             reads=[("scs", h // 4), "vb"], writes=[("po", b)])
        for s_ in range(16):
            S.op("pe", lambda e, b=b, s_=s_, Ssb=Ssb: e.matmul(po[b][:, 0, :], lhsT=qTm[:, s_, :],
                                                               rhs=Ssb[:, s_, :], start=False, stop=(s_ == 15)),
                 reads=[("oT", 0), ("oT", 1), SBK], writes=[("po", b)])
        S.op("act", lambda e, b=b, h=h: e.activation(out=osb[:, h, :], in_=po[b][:, 0, :], func=AF.Copy),
             reads=[("po", b)], writes=["osb"])
        for sp_ in range(8):
            b2 = cn["d"] % 2
            cn["d"] += 1
            for j in range(2):
                s_ = sp_ * 2 + j
                S.op("pe", lambda e, b2=b2, j=j, s_=s_, h=h: e.matmul(
                    pd[b2][:, j, :], lhsT=ktm[:, s_, :], rhs=vb[:, h, :], start=True, stop=True),
                    reads=[("oT", 2), ("oT", 3), "vb"], writes=[("pd", b2)])
            for j in range(2):
                s_ = sp_ * 2 + j
                S.op("act", lambda e, b2=b2, j=j, g8=g8: e.activation(out=dtmp[:, j, :], in_=pd[b2][:, j, :],
                                                                      func=AF.Copy, scale=g8),
                     reads=[("pd", b2)], writes=[("dtmp", j)])
                S.op("dve", lambda e, s_=s_, j=j, g8=g8, Ss=Ss: e.scalar_tensor_tensor(
                    out=Ss[:, s_, :], in0=Ss[:, s_, :], scalar=g8, in1=dtmp[:, j, :], op0=ALU.mult, op1=ALU.add),
                    reads=[("dtmp", j), VI, SBK], writes=[VI])
        S.dma("sp", sret_out[:, h].rearrange("s d e -> d s e"), Ss, key=("Ss_out", bi_), reads=[VI],
              is_output=True)
    finish_out(8)


def xattn_body(S, A, zo, memkv, cmk, cmv, seqmT_ap, OT, ident_ap):
    X = mybir.AxisListType.X
    scale = 512.0 ** -0.5
    ident = A.sb("x_ident", [P, P], F32)
    identb = A.sb("x_identb", [P, P], BF16)
    S.dma("sp", ident[:], ident_ap, key="ident", writes=["ident"])
    S.op("dve", lambda e: e.tensor_copy(out=identb[:], in_=ident[:]), reads=["ident"], writes=["identb"])
    seqmT = A.sb("x_seqmT", [P, 16, P], F32)
    S.dma("sp", seqmT[:], seqmT_ap, key="seqmT", writes=["seqmT"])

    qin = A.sb("x_qin", [P, 2048], F32)
    gin = A.sb("x_gin", [P, 2048], F32)
    qb = A.sb("x_qb", [P, 16, 128], BF16)
    qT = A.sb("x_qT", [P, 16, 128], BF16)
    kvin = [A.sb("x_kvin%d" % i, [P, 2, 2048], F32) for i in range(2)]
    kb = A.sb("x_kb", [P, 2, 16, 128], BF16)
    KT = A.sb("x_KT", [P, 16, 256], BF16)
    Vb = A.sb("x_Vb", [P, 2, 2048], BF16)
    pb = A.sb("x_pb", [P, 4, 256], BF16)
    pT = A.sb("x_pT", [P, 4, 2, 128], BF16)
    pTm = A.sb("x_pTm", [P, 16, 128], BF16)
    qTm = A.sb("x_qTm", [P, 16, 128], BF16)
    ob = A.sb("x_ob", [P, 16, 128], BF16)
    oT = A.sb("x_oT", [P, 16, 128], BF16)
    mx = A.sb("x_mx", [P, 4], F32)
    sm = A.sb("x_sm", [P, 4], F32)
    ptr = [A.ps("x_ptr%d" % i, [P, 8, 128], BF16) for i in range(2)]
    pso = [A.ps("x_pso%d" % i, [P, 512]) for i in range(4)]
    cn = {"tr": 0, "kv": 0}

    def tr_group(srcs, dst_ap, skeys, dkey):
        b = cn["tr"] % 2
        cn["tr"] += 1
        for j, src in enumerate(srcs):
            S.op("pe", lambda e, b=b, j=j, src=src: e.transpose(ptr[b][:, j, :], src, identb[:]),
                 reads=list(skeys) + ["identb"], writes=[("ptr", b)])
        n = len(srcs)
        S.op("act", lambda e, b=b, n=n: e.activation(out=dst_ap, in_=ptr[b][:, 0:n, :], func=AF.Copy),
             reads=[("ptr", b)], writes=[dkey])

    def load_q(t):
        r0 = t * P
        S.dma("sp", qin[:], zo[r0:r0 + P, 16384:18432], key="qin", writes=["qin"])
        S.dma("sp", gin[:], zo[r0:r0 + P, 18432:20480], key="gin", writes=["gin"])
        S.op("dve", lambda e: e.tensor_copy(out=qb[:].rearrange("p a b -> p (a b)"), in_=qin[:]),
             reads=["qin"], writes=["qb"])
        for half in range(2):
            tr_group([qb[:, half * 8 + j, :] for j in range(8)], qT[:, half * 8:(half + 1) * 8, :],
                     ["qb"], ("qT", half))

    def load_kv(src_ap, which):
        i = cn["kv"] % 2
        cn["kv"] += 1
        S.dma("sp", kvin[i][:], src_ap.rearrange("(mt p) c -> p mt c", p=P), key=("kvin", i),
              writes=[("kvin", i)])
        return i

    def make_KT(i):
        S.op("dve", lambda e: e.tensor_copy(out=kb[:].rearrange("p m a b -> p m (a b)"), in_=kvin[i][:]),
             reads=[("kvin", i)], writes=["kb"])
        for mt in range(2):
            for half in range(2):
                tr_group([kb[:, mt, half * 8 + j, :] for j in range(8)],
                         KT[:, half * 8:(half + 1) * 8, mt * P:(mt + 1) * P], ["kb"], ("KT", mt, half))

    def make_V(i):
        S.op("act", lambda e: e.activation(out=Vb[:], in_=kvin[i][:], func=AF.Copy), reads=[("kvin", i)],
             writes=["Vb"])

    KT_keys = [("KT", mt, half) for mt in range(2) for half in range(2)]

    def sc_loc(h, sample):
        return pso[h][:, 0:256], ("pso", h)

    def score_mm(lhs, lkeys, first, last, sample=False):
        for h in range(4):
            for dt in range(4):
                k = h * 4 + dt
                loc, lk = sc_loc(h, sample)
                S.op("pe", lambda e, loc=loc, k=k, dt=dt: e.matmul(
                    loc, lhsT=lhs[:, k, :], rhs=KT[:, k, :],
                    start=(first and dt == 0), stop=(last and dt == 3)),
                    reads=list(lkeys) + KT_keys, writes=[lk])

    def softmax(sample=False):
        for h in range(4):
            sv, lk = sc_loc(h, sample)
            S.op("dve", lambda e, h=h, sv=sv: e.tensor_reduce(out=mx[:, h:h + 1], in_=sv, op=ALU.max, axis=X),
                 reads=[lk], writes=[("mx", h)])
            S.op("dve", lambda e, h=h: e.tensor_scalar(out=mx[:, h:h + 1], in0=mx[:, h:h + 1], scalar1=-scale,
                                                       scalar2=None, op0=ALU.mult),
                 reads=[("mx", h)], writes=[("mx", h)])
            S.op("act", lambda e, h=h, sv=sv: e.activation(out=pb[:, h, :], in_=sv, func=AF.Exp,
                                                           bias=mx[:, h:h + 1], scale=scale,
                                                           accum_out=sm[:, h:h + 1]),
                 reads=[lk, ("mx", h)], writes=[("pb", h), ("sm", h)])
            S.op("dve", lambda e, h=h: e.reciprocal(out=sm[:, h:h + 1], in_=sm[:, h:h + 1]),
                 reads=[("sm", h)], writes=[("sm", h)])
        tr_group([pb[:, h, mt * P:(mt + 1) * P] for h in range(4) for mt in range(2)],
                 pT[:].rearrange("p h m l -> p (h m) l"), [("pb", h) for h in range(4)], "pT")

    def finish(t):
        for h in range(4):
            S.op("dve", lambda e, h=h: e.scalar_tensor_tensor(
                out=ob[:, h * 4:(h + 1) * 4, :].rearrange("p a b -> p (a b)"), in0=pso[h][:],
                scalar=sm[:, h:h + 1], in1=gin[:, h * 512:(h + 1) * 512], op0=ALU.mult, op1=ALU.mult),
                reads=[("pso", h), ("sm", h), "gin"], writes=[("ob", h)])
        for half in range(2):
            tr_group([ob[:, half * 8 + j, :] for j in range(8)], oT[:, half * 8:(half + 1) * 8, :],
                     [("ob", h) for h in range(4)], ("oT", half))
        S.dma("act", OT[t, :, 48:64, :], oT[:], key="oT", reads=[("oT", 0), ("oT", 1)], is_output=True)

    i = load_kv(memkv[:, 0:2048], "k")
    make_KT(i)
    i = load_kv(memkv[:, 2048:4096], "v")
    make_V(i)
    for t in range(8):
        load_q(t)
        score_mm(qT, [("qT", 0), ("qT", 1)], True, True)
        softmax()
        for h in range(4):
            for mt in range(2):
                S.op("pe", lambda e, h=h, mt=mt: e.matmul(pso[h][:], lhsT=pT[:, h, mt, :],
                                                          rhs=Vb[:, mt, h * 512:(h + 1) * 512],
                                                          start=(mt == 0), stop=(mt == 1)),
                     reads=["pT", "Vb"], writes=[("pso", h)])
        finish(t)

    load_q(8)
    for s_ in range(16):
        i = load_kv(cmk[s_], "k")
        make_KT(i)
        S.op("dve", lambda e, s_=s_: e.tensor_tensor(
            out=qTm[:], in0=qT[:], in1=seqmT[:, s_, :].unsqueeze(1).to_broadcast([P, 16, P]), op=ALU.mult),
            reads=[("qT", 0), ("qT", 1), "seqmT"], writes=["qTm"])
        score_mm(qTm, ["qTm"], s_ == 0, s_ == 15, sample=True)
    softmax(sample=True)
    for s_ in range(16):
        i = load_kv(cmv[s_], "v")
        make_V(i)
        for h in range(4):
            S.op("dve", lambda e, h=h, s_=s_: e.tensor_tensor(
                out=pTm[:, h * 2:(h + 1) * 2, :], in0=pT[:, h, :, :],
                in1=seqmT[:, s_, :].unsqueeze(1).to_broadcast([P, 2, P]), op=ALU.mult),
                reads=["pT", "seqmT"], writes=[("pTm", h)])
            for mt in range(2):
                S.op("pe", lambda e, h=h, mt=mt, s_=s_: e.matmul(
                    pso[h][:], lhsT=pTm[:, h * 2 + mt, :], rhs=Vb[:, mt, h * 512:(h + 1) * 512],
                    start=(s_ == 0 and mt == 0), stop=(s_ == 15 and mt == 1)),
                    reads=[("pTm", h), "Vb"], writes=[("pso", h)])
    finish(8)


def s5prep_body(S, A, prm, ident_ap, maskM_ap, SM, SG, SE, A8S):
    PI = math.pi
    ident = A.sb("q_ident", [P, P], F32)
    identb = A.sb("q_identb", [P, P], BF16)
    S.dma("sp", ident[:], ident_ap, key="ident", writes=["ident"])
    S.op("dve", lambda e: e.tensor_copy(out=identb[:], in_=ident[:]), reads=["ident"], writes=["identb"])
    maskM = A.sb("q_maskM", [P, P], F32)
    S.dma("sp", maskM[:], maskM_ap, key="maskM", writes=["maskM"])
    H = 64
    uid = [0]

    def tl(shape, dt=F32):
        uid[0] += 1
        return A.sb("q_t%d" % uid[0], shape, dt)

    def dve(fn, reads, writes):
        S.op("dve", fn, reads=reads, writes=writes)

    def tt(out, a, b, op, okey, akey, bkey):
        dve(lambda e: e.tensor_tensor(out=out, in0=a, in1=b, op=op), [akey, bkey], [okey])

    araw = tl([P, 2, 64])
    S.dma("sp", araw[:, 0, :], prm["a_re"], key="araw0", writes=["araw"])
    S.dma("sp", araw[:, 1, :], prm["a_im"], key="araw1", writes=["araw"])
    pA = A.ps("q_pA", [P, 4, P])
    pA2 = A.ps("q_pA2", [P, 4, P])
    ar = tl([H, P]); ai = tl([H, P])
    for j in range(2):
        S.op("pe", lambda e, j=j: e.transpose(pA[0:H, j, :], araw[:, j, :], ident[:]), reads=["araw", "ident"],
             writes=["pA"])
    dve(lambda e: e.tensor_copy(out=ar[:], in_=pA[0:H, 0, :]), ["pA"], ["ar"])
    dve(lambda e: e.tensor_copy(out=ai[:], in_=pA[0:H, 1, :]), ["pA"], ["ai"])
    dtb = tl([H, P])
    S.dma("sp", dtb[:], prm["log_step"].to_broadcast([H, P]), key="dtb", writes=["dtb"])
    S.op("act", lambda e: e.activation(out=dtb[:], in_=dtb[:], func=AF.Exp), reads=["dtb"], writes=["dtb"])
    dtar = tl([H, P]); dtai = tl([H, P]); mag = tl([H, P])
    tt(dtar[:], dtb[:], ar[:], ALU.mult, "dtar", "dtb", "ar")
    tt(dtai[:], dtb[:], ai[:], ALU.mult, "dtai", "dtb", "ai")
    kq = tl([H, P]); ki = tl([H, P], mybir.dt.int32); rr = tl([H, P])
    dve(lambda e: e.tensor_scalar(out=kq[:], in0=dtai[:], scalar1=1.0 / (2 * PI), scalar2=None, op0=ALU.mult),
        ["dtai"], ["kq"])
    dve(lambda e: e.tensor_copy(out=ki[:], in_=kq[:]), ["kq"], ["ki"])
    dve(lambda e: e.tensor_copy(out=kq[:], in_=ki[:]), ["ki"], ["kq"])
    dve(lambda e: e.scalar_tensor_tensor(out=rr[:], in0=kq[:], scalar=-2 * PI, in1=dtai[:], op0=ALU.mult,
                                         op1=ALU.add), ["kq", "dtai"], ["rr"])
    rs = tl([H, P]); rc = tl([H, P]); sn = tl([H, P]); cs = tl([H, P])
    msk = tl([H, P])
    for t_, k_, sh in ((rs, "rs", 0.0), (rc, "rc", PI / 2)):
        dve(lambda e, t_=t_, sh=sh: e.tensor_scalar(out=t_[:], in0=rr[:], scalar1=sh, scalar2=None, op0=ALU.add),
            ["rr"], [k_])
        dve(lambda e, t_=t_: e.tensor_scalar(out=msk[:], in0=t_[:], scalar1=PI, scalar2=None, op0=ALU.is_gt),
            [k_], ["msk"])
        dve(lambda e, t_=t_: e.scalar_tensor_tensor(out=t_[:], in0=msk[:], scalar=-2 * PI, in1=t_[:],
                                                    op0=ALU.mult, op1=ALU.add), ["msk", k_], [k_])
        dve(lambda e, t_=t_: e.tensor_scalar(out=msk[:], in0=t_[:], scalar1=-PI, scalar2=None, op0=ALU.is_lt),
            [k_], ["msk"])
        dve(lambda e, t_=t_: e.scalar_tensor_tensor(out=t_[:], in0=msk[:], scalar=2 * PI, in1=t_[:],
                                                    op0=ALU.mult, op1=ALU.add), ["msk", k_], [k_])
    for t_, k_ in ((rs, "rs"), (rc, "rc")):
        dve(lambda e, t_=t_: e.tensor_scalar(out=t_[:], in0=t_[:], scalar1=3.1415925, scalar2=-3.1415925,
                                             op0=ALU.min, op1=ALU.max), [k_], [k_])
    hh = tl([H, P]); x2 = tl([H, P]); sh_ = tl([H, P]); ch_ = tl([H, P])
    dve(lambda e: e.tensor_scalar(out=hh[:], in0=rs[:], scalar1=0.5, scalar2=None, op0=ALU.mult), ["rs"], ["hh"])
    tt(x2[:], hh[:], hh[:], ALU.mult, "x2", "hh", "hh")
    sc_ = [(-1.0) ** k / math.factorial(2 * k + 1) for k in range(9)]
    cc_ = [(-1.0) ** k / math.factorial(2 * k) for k in range(9)]

    def horner(dst, dkey, co):
        dve(lambda e: e.tensor_scalar(out=dst[:], in0=x2[:], scalar1=co[-1], scalar2=None, op0=ALU.mult),
            ["x2"], [dkey])
        for c_ in co[-2:0:-1]:
            dve(lambda e, c_=c_: e.scalar_tensor_tensor(out=dst[:], in0=dst[:], scalar=c_, in1=x2[:],
                                                        op0=ALU.add, op1=ALU.mult), [dkey, "x2"], [dkey])
        dve(lambda e: e.tensor_scalar(out=dst[:], in0=dst[:], scalar1=co[0], scalar2=None, op0=ALU.add),
            [dkey], [dkey])
    horner(sh_, "sh", sc_)
    tt(sh_[:], sh_[:], hh[:], ALU.mult, "sh", "sh", "hh")
    horner(ch_, "ch", cc_)
    dve(lambda e: e.scalar_tensor_tensor(out=sn[:], in0=sh_[:], scalar=2.0, in1=ch_[:], op0=ALU.mult,
                                         op1=ALU.mult), ["sh", "ch"], ["sn"])
    tt(cs[:], sh_[:], sh_[:], ALU.mult, "cs", "sh", "sh")
    dve(lambda e: e.tensor_scalar(out=cs[:], in0=cs[:], scalar1=-2.0, scalar2=1.0, op0=ALU.mult, op1=ALU.add),
        ["cs"], ["cs"])
    ec_ = [1.0 / math.factorial(k) for k in range(9)]
    dve(lambda e: e.tensor_scalar(out=mag[:], in0=dtar[:], scalar1=ec_[-1], scalar2=None, op0=ALU.mult),
        ["dtar"], ["mag"])
    for c_ in ec_[-2:0:-1]:
        dve(lambda e, c_=c_: e.scalar_tensor_tensor(out=mag[:], in0=mag[:], scalar=c_, in1=dtar[:],
                                                    op0=ALU.add, op1=ALU.mult), ["mag", "dtar"], ["mag"])
    dve(lambda e: e.tensor_scalar(out=mag[:], in0=mag[:], scalar1=1.0, scalar2=None, op0=ALU.add),
        ["mag"], ["mag"])
    PW = tl([H, 16, 2, P])
    tmp = [tl([H, P]) for _ in range(4)]
    cm = [0]

    def cmul(ore, oim, xr, xi, yr, yi, okeys, ikeys):
        cm[0] += 1
        k = ["cm%d_%d" % (cm[0], i) for i in range(4)]
        dve(lambda e: e.tensor_tensor(out=tmp[0][:], in0=xr, in1=yr, op=ALU.mult), ikeys, ["tmp0"])
        dve(lambda e: e.tensor_tensor(out=tmp[1][:], in0=xi, in1=yi, op=ALU.mult), ikeys, ["tmp1"])
        dve(lambda e: e.tensor_tensor(out=tmp[2][:], in0=xr, in1=yi, op=ALU.mult), ikeys, ["tmp2"])
        dve(lambda e: e.tensor_tensor(out=tmp[3][:], in0=xi, in1=yr, op=ALU.mult), ikeys, ["tmp3"])
        dve(lambda e: e.tensor_tensor(out=ore, in0=tmp[0][:], in1=tmp[1][:], op=ALU.subtract),
            ["tmp0", "tmp1"], [okeys[0]])
        dve(lambda e: e.tensor_tensor(out=oim, in0=tmp[2][:], in1=tmp[3][:], op=ALU.add),
            ["tmp2", "tmp3"], [okeys[1]])

    def pw(e_, c):
        return PW[:, e_ + 7, c, :]

    def pk(e_):
        return ["pw%d_0" % e_, "pw%d_1" % e_]
    S.op("pool", lambda e: e.memset(pw(0, 0), 1.0), writes=["pw0_0"])
    S.op("pool", lambda e: e.memset(pw(0, 1), 0.0), writes=["pw0_1"])
    tt(pw(1, 0), mag[:], cs[:], ALU.mult, "pw1_0", "mag", "cs")
    tt(pw(1, 1), mag[:], sn[:], ALU.mult, "pw1_1", "mag", "sn")
    for e_ in range(2, 9):
        cmul(pw(e_, 0), pw(e_, 1), pw(e_ - 1, 0), pw(e_ - 1, 1), pw(1, 0), pw(1, 1), pk(e_), pk(e_ - 1) + pk(1))
    im2 = tl([H, P])
    tt(im2[:], mag[:], mag[:], ALU.mult, "im2", "mag", "mag")
    dve(lambda e: e.reciprocal(out=im2[:], in_=im2[:]), ["im2"], ["im2"])
    tt(pw(-1, 0), pw(1, 0), im2[:], ALU.mult, "pw-1_0", "pw1_0", "im2")
    dve(lambda e: e.scalar_tensor_tensor(out=pw(-1, 1), in0=pw(1, 1), scalar=-1.0, in1=im2[:], op0=ALU.mult,
                                         op1=ALU.mult), ["pw1_1", "im2"], ["pw-1_1"])
    for e_ in range(2, 8):
        cmul(pw(-e_, 0), pw(-e_, 1), pw(-e_ + 1, 0), pw(-e_ + 1, 1), pw(-1, 0), pw(-1, 1),
             pk(-e_), pk(-e_ + 1) + pk(-1))
    allpw = [k for e_ in range(-7, 9) for k in pk(e_)]
    S.dma("sp", A8S, PW[:, 15, :, :], key="a8s", reads=pk(8), is_output=True)
    den = tl([H, P]); xr_ = tl([H, P]); fre = tl([H, P]); fim = tl([H, P])
    tt(den[:], ar[:], ar[:], ALU.mult, "den", "ar", "ar")
    tt(tmp[0][:], ai[:], ai[:], ALU.mult, "tmp0", "ai", "ai")
    tt(den[:], den[:], tmp[0][:], ALU.add, "den", "den", "tmp0")
    dve(lambda e: e.reciprocal(out=den[:], in_=den[:]), ["den"], ["den"])
    dve(lambda e: e.tensor_scalar(out=xr_[:], in0=pw(1, 0), scalar1=-1.0, scalar2=None, op0=ALU.add),
        ["pw1_0"], ["xr"])
    tt(tmp[0][:], xr_[:], ar[:], ALU.mult, "tmp0", "xr", "ar")
    tt(tmp[1][:], pw(1, 1), ai[:], ALU.mult, "tmp1", "pw1_1", "ai")
    tt(fre[:], tmp[0][:], tmp[1][:], ALU.add, "fre", "tmp0", "tmp1")
    tt(fre[:], fre[:], den[:], ALU.mult, "fre", "fre", "den")
    tt(tmp[2][:], pw(1, 1), ar[:], ALU.mult, "tmp2", "pw1_1", "ar")
    tt(tmp[3][:], xr_[:], ai[:], ALU.mult, "tmp3", "xr", "ai")
    tt(fim[:], tmp[2][:], tmp[3][:], ALU.subtract, "fim", "tmp2", "tmp3")
    tt(fim[:], fim[:], den[:], ALU.mult, "fim", "fim", "den")
    Pst = tl([P, P, 8]); Qst = tl([P, P, 8])
    for s_ in range(8):
        cmul(Pst[0:H, :, s_], Qst[H:P, :, s_], pw(7 - s_, 0), pw(7 - s_, 1), fre[:], fim[:],
             ["Pst_lo", "Qst_hi"], pk(7 - s_) + ["fre", "fim"])
    S.op("act", lambda e: e.activation(out=Pst[H:P, :, :], in_=Pst[0:H, :, :], func=AF.Copy),
         reads=["Pst_lo"], writes=["Pst_hi"])
    S.op("act", lambda e: e.activation(out=Qst[0:H, :, :], in_=Qst[H:P, :, :], func=AF.Copy, scale=-1.0),
         reads=["Qst_hi"], writes=["Qst_lo"])
    Pv = tl([P, 16, P]); Qv = tl([P, 16, P])
    S.op("act", lambda e: e.activation(out=Pv[0:H], in_=PW[:, :, 0, :], func=AF.Copy), reads=allpw, writes=["Pv_lo"])
    S.op("act", lambda e: e.activation(out=Pv[H:P], in_=PW[:, :, 0, :], func=AF.Copy), reads=allpw, writes=["Pv_hi"])
    S.op("act", lambda e: e.activation(out=Qv[0:H], in_=PW[:, :, 1, :], func=AF.Copy, scale=-1.0), reads=allpw,
         writes=["Qv_lo"])
    S.op("act", lambda e: e.activation(out=Qv[H:P], in_=PW[:, :, 1, :], func=AF.Copy, scale=-1.0), reads=allpw,
         writes=["Qv_hi"])
    t1 = tl([P, 32, 128]); t2 = tl([P, 32, 128])
    raw = t1[:].rearrange("p a b -> p (a b)")
    R = tl([P, P, 16]); Sx = tl([P, P, 16]); Rp = tl([P, P, 16]); Sp = tl([P, P, 16])
    srcs = (("b_re", 0), ("b_im", 1), ("c_re", 2), ("c_im", 3))
    for nm, idx in srcs:
        S.dma("sp", raw[:, idx * 1024:(idx + 1) * 1024], prm[nm], key="raw%d" % idx, writes=["raw%d" % idx])
    cnt = [0]
    for nm, idx in srcs:
        rv = raw[:, idx * 1024:(idx + 1) * 1024]
        for q4 in range(4):
            pb_ = pA if cnt[0] % 2 == 0 else pA2
            pkey = "pA" if cnt[0] % 2 == 0 else "pA2"
            cnt[0] += 1
            for j in range(4):
                qq = q4 * 4 + j
                if idx < 2:
                    src = rv.rearrange("p (n q) -> p q n", q=16)[:, qq, :]
                else:
                    src = rv[:, qq * 64:(qq + 1) * 64]
                S.op("pe", lambda e, pb_=pb_, j=j, src=src: e.transpose(pb_[0:H, j, :], src, ident[:]),
                     reads=["raw%d" % idx, "ident"], writes=[pkey])
            qs = slice(q4 * 4, q4 * 4 + 4)
            pin = pb_[0:H, :, :]

            def outv(tile_, lo):
                v = tile_[0:H] if lo else tile_[H:P]
                return v.rearrange("p g q -> p q g")[:, qs, :]
            if idx == 0:
                dsts = ((R, True, 1.0), (Sx, False, 1.0))
            elif idx == 1:
                dsts = ((R, False, 1.0), (Sx, True, 1.0))
            elif idx == 2:
                dsts = ((Rp, True, 1.0), (Sp, False, 1.0))
            else:
                dsts = ((Rp, False, -1.0), (Sp, True, 1.0))
            for (tile_, lo, sc) in dsts:
                ov = outv(tile_, lo)
                S.op("act", lambda e, ov=ov, pin=pin, sc=sc: e.activation(out=ov, in_=pin, func=AF.Copy, scale=sc),
                     reads=[pkey], writes=["tab%d_%d_%d" % (id(tile_) % 997, lo, q4)])
    tabkeys = None
    X7c = tl([P, 32, 128], BF16); Ypc = tl([P, 32, 128], BF16); Ec = tl([P, 32, 128], BF16)
    Mc = tl([P, 32, 128], BF16); Gc = tl([P, 32, 128], BF16)
    pM = [A.ps("q_pM%d" % i, [P, 4, P]) for i in range(2)]
    pG = [A.ps("q_pG%d" % i, [P, 8, P], BF16) for i in range(2)]
    anytab = [k for k in S.last_w.keys() if isinstance(k, str) and k.startswith("tab")]
    for ch in range(4):
        gs = slice(ch * 32, ch * 32 + 32)
        t1v = t1[:].rearrange("p g (s q) -> p g s q", q=16)
        t2v = t2[:].rearrange("p g (s q) -> p g s q", q=16)

        def build(dst, dkey, Pt, Qt, Rt, St, pkeys):
            dve(lambda e: e.tensor_tensor(out=t1v, in0=Pt.unsqueeze(3).to_broadcast([P, 32, 8, 16]),
                                          in1=Rt.unsqueeze(2).to_broadcast([P, 32, 8, 16]), op=ALU.mult),
                pkeys + anytab + ["raw0", "raw1", "raw2", "raw3"], ["t1"])
            S.op("dve", lambda e: e.tensor_tensor(out=t2v, in0=Qt.unsqueeze(3).to_broadcast([P, 32, 8, 16]),
                                                  in1=St.unsqueeze(2).to_broadcast([P, 32, 8, 16]), op=ALU.mult),
                 reads=pkeys + anytab, writes=["t2"])
            dve(lambda e: e.tensor_tensor(out=dst[:], in0=t1[:], in1=t2[:], op=ALU.add), ["t1", "t2"], [dkey])
        build(X7c, "X7c", Pst[:, gs, :], Qst[:, gs, :], R[:, gs, :], Sx[:, gs, :],
              ["Pst_lo", "Pst_hi", "Qst_lo", "Qst_hi"])
        pvk = ["Pv_lo", "Pv_hi", "Qv_lo", "Qv_hi"]
        build(Ypc, "Ypc", Pv[:, 0:8, gs].rearrange("p e g -> p g e"), Qv[:, 0:8, gs].rearrange("p e g -> p g e"),
              Rp[:, gs, :], Sp[:, gs, :], pvk)
        build(Ec, "Ec", Pv[:, 8:16, gs].rearrange("p e g -> p g e"), Qv[:, 8:16, gs].rearrange("p e g -> p g e"),
              Rp[:, gs, :], Sp[:, gs, :], pvk)
        for g4 in range(8):
            b = g4 % 2
            for j in range(4):
                g = g4 * 4 + j
                S.op("pe", lambda e, b=b, j=j, g=g: e.matmul(pM[b][:, j, :], lhsT=X7c[:, g, :], rhs=Ypc[:, g, :],
                                                             start=True, stop=True),
                     reads=["X7c", "Ypc"], writes=[("pM", b)])
            dve(lambda e, b=b, g4=g4: e.tensor_tensor(
                out=Mc[:, g4 * 4:(g4 + 1) * 4, :], in0=pM[b][:],
                in1=maskM[:].unsqueeze(1).to_broadcast([P, 4, P]), op=ALU.mult),
                [("pM", b), "maskM"], [("Mc", g4)])
        for g8 in range(4):
            b = g8 % 2
            for j in range(8):
                g = g8 * 8 + j
                S.op("pe", lambda e, b=b, j=j, g=g: e.transpose(pG[b][:, j, :], X7c[:, g, :], identb[:]),
                     reads=["X7c", "identb"], writes=[("pG", b)])
            S.op("act", lambda e, b=b, g8=g8: e.activation(out=Gc[:, g8 * 8:(g8 + 1) * 8, :], in_=pG[b][:],
                                                           func=AF.Copy),
                 reads=[("pG", b)], writes=[("Gc", g8)])
        S.dma("sp", SM[:, gs, :], Mc[:], key="SMst", reads=[("Mc", i) for i in range(8)], is_output=True)
        S.dma("sp", SG[:, gs, :], Gc[:], key="SGst", reads=[("Gc", i) for i in range(4)], is_output=True)
        S.dma("sp", SE[:, gs, :], Ec[:], key="SEst", reads=["Ec"], is_output=True)


def s5main_body(S, A, zo, zp, SM, SG, SE, A8S, selm_ap, seqm_ap, ident_ap, s5in, YS, s5p_out, s5s_out):
    H = 64
    ident = A.sb("m_ident", [P, P], F32)
    S.dma("sp", ident[:], ident_ap, key="ident", writes=["ident"])
    selm = A.sb("m_selm", [P, 8], F32)
    S.dma("sp", selm[:], selm_ap, key="selm", writes=["selm"])
    bsf = A.sb("m_bsf", [P, 16], F32)
    bsel = A.sb("m_bsel", [P, 16], BF16)
    S.dma("sp", bsf[:], seqm_ap, key="bsf", writes=["bsf"])
    S.op("dve", lambda e: e.tensor_copy(out=bsel[:], in_=bsf[:]), reads=["bsf"], writes=["bsel"])
    a8 = A.sb("m_a8", [H, 2, P], F32)
    S.dma("sp", a8[:], A8S, key="a8", writes=["a8"])
    AA = A.sb("m_AA", [H, 2, P], F32)
    AB = A.sb("m_AB", [H, 2, P], F32)
    S.op("dve", lambda e: e.tensor_copy(out=AA[:, 0, :], in_=a8[:, 0, :]), reads=["a8"], writes=["AA0"])
    S.op("dve", lambda e: e.tensor_copy(out=AA[:, 1, :], in_=a8[:, 0, :]), reads=["a8"], writes=["AA1"])
    S.op("dve", lambda e: e.tensor_scalar(out=AB[:, 0, :], in0=a8[:, 1, :], scalar1=-1.0, scalar2=None,
                                          op0=ALU.mult), reads=["a8"], writes=["AB0"])
    S.op("dve", lambda e: e.tensor_copy(out=AB[:, 1, :], in_=a8[:, 1, :]), reads=["a8"], writes=["AB1"])
    AK = ["AA0", "AA1", "AB0", "AB1"]
    Gm = A.sb("m_G", [P, P, P], BF16)
    for ch in range(4):
        S.dma("sp", Gm[:, ch * 32:(ch + 1) * 32, :], SG[:, ch * 32:(ch + 1) * 32, :], key=("Gl", ch),
              writes=[("G", ch)])
    MEc = [A.sb("m_ME%d" % i, [P, 32, 2, P], BF16) for i in range(2)]
    uin = [A.sb("m_uin%d" % i, [P, 2048], F32) for i in range(2)]
    urep = A.sb("m_urep", [P, 32, 128], BF16)
    Uts = [A.sb("m_Ut%d" % i, [P, P, 16], BF16) for i in range(2)]
    VH = A.sb("m_VH", [H, 2, P, 17], F32)
    Hbfs = [A.sb("m_Hbf%d" % i, [P, P, 16], BF16) for i in range(2)]
    ysbs = [A.sb("m_ysb%d" % i, [16, 32, 128], F32) for i in range(2)]
    P1 = A.sb("m_P1", [H, 2, P], F32)
    P2 = A.sb("m_P2", [H, 2, P], F32)
    Vs = A.sb("m_Vs", [H, 2, P, 16], F32)
    H0s = A.sb("m_H0s", [H, 2, P, 16], F32)
    psU = [A.ps("m_psU%d" % i, [P, 32, 16]) for i in range(2)]
    psV = [A.ps("m_psV%d" % i, [P, 32, 16]) for i in range(2)]
    psY = [A.ps("m_psY%d" % i, [P, 4, P]) for i in range(2)]
    ptr = A.ps("m_ptr", [P, 4, P])
    S.op("pool", lambda e: e.memset(VH[:], 0.0), writes=["VH"])
    cn = {"u": 0, "U": 0, "V": 0, "Y": 0, "me": 0, "ys": 0}

    def make_U(src_ap, ub):
        Ut = Uts[ub]
        i = cn["u"] % 2
        cn["u"] += 1
        S.dma("sp", uin[i][:], src_ap, key=("uin", i), writes=[("uin", i)])
        for ch in range(4):
            uv = uin[i][:, ch * 512:(ch + 1) * 512].rearrange("p (g q) -> p g q", q=16)
            S.op("dve", lambda e, uv=uv: e.tensor_tensor(
                out=urep[:].rearrange("p g (s q) -> p g s q", q=16),
                in0=uv.unsqueeze(2).to_broadcast([P, 32, 8, 16]),
                in1=selm[:].unsqueeze(1).unsqueeze(3).to_broadcast([P, 32, 8, 16]), op=ALU.mult),
                reads=[("uin", i), "selm"], writes=["urep"])
            b = cn["U"] % 2
            cn["U"] += 1
            for j in range(32):
                S.op("pe", lambda e, b=b, j=j: e.matmul(psU[b][:, j, :], lhsT=urep[:, j, :], rhs=bsel[:],
                                                        start=True, stop=True),
                     reads=["urep", "bsel"], writes=[("psU", b)])
            S.op("act", lambda e, b=b, ch=ch: e.activation(out=Ut[:, ch * 32:(ch + 1) * 32, :], in_=psU[b][:],
                                                           func=AF.Copy),
                 reads=[("psU", b)], writes=[("Ut", ub, ch)])

    def make_V(dst_fn, dkey, ub):
        Ut = Uts[ub]
        for ch in range(4):
            b = cn["V"] % 2
            cn["V"] += 1
            for j in range(32):
                g = ch * 32 + j
                S.op("pe", lambda e, b=b, j=j, g=g: e.matmul(psV[b][:, j, :], lhsT=Gm[:, g, :], rhs=Ut[:, g, :],
                                                             start=True, stop=True),
                     reads=[("G", ch), ("Ut", ub, ch)], writes=[("psV", b)])
            for c in range(2):
                S.op("act", lambda e, b=b, c=c, ch=ch: e.activation(
                    out=dst_fn(c, ch), in_=psV[b][c * H:(c + 1) * H, :, :], func=AF.Copy),
                    reads=[("psV", b)], writes=[dkey])

    def cstep(Hj0, Hj1, Hj, Hn, key, w, extra=(), tk=("P1", "P2a", "P2b")):
        p1, p2 = w
        S.op("dve", lambda e: e.tensor_tensor(out=p1, in0=AA_v(Hj), in1=Hj, op=ALU.mult),
             reads=[key] + AK + list(extra), writes=[tk[0]])
        S.op("dve", lambda e: e.tensor_tensor(out=sub(p2, 0), in0=AB_v(Hj, 0), in1=Hj1, op=ALU.mult),
             reads=[key] + AK + list(extra), writes=[tk[1]])
        S.op("dve", lambda e: e.tensor_tensor(out=sub(p2, 1), in0=AB_v(Hj, 1), in1=Hj0, op=ALU.mult),
             reads=[key] + AK + list(extra), writes=[tk[2]])
        S.op("dve", lambda e: e.tensor_tensor(out=Hn, in0=Hn, in1=p1, op=ALU.add), reads=[key, tk[0]], writes=[key])
        S.op("dve", lambda e: e.tensor_tensor(out=Hn, in0=Hn, in1=p2, op=ALU.add),
             reads=[key, tk[1], tk[2]], writes=[key])

    def sub(ap, c):
        return ap[:, c]

    def AA_v(like):
        if len(like.shape) == 3:
            return AA[:]
        return AA[:].unsqueeze(3).to_broadcast([H, 2, P, like.shape[-1]])

    def AB_v(like, c):
        if len(like.shape) == 3:
            return AB[:, c, :]
        return AB[:, c, :].unsqueeze(2).to_broadcast([H, P, like.shape[-1]])

    def make_Hbf(src, skey, hb):
        Hbf = Hbfs[hb]
        S.op("act", lambda e: e.activation(out=Hbf[0:H], in_=src[:, 0, :, 0:16], func=AF.Copy),
             reads=[skey], writes=[("Hbf0", hb)])
        S.op("act", lambda e: e.activation(out=Hbf[H:P], in_=src[:, 1, :, 0:16], func=AF.Copy),
             reads=[skey], writes=[("Hbf1", hb)])

    def make_Y(t, ub, hb):
        Ut = Uts[ub]
        Hbf = Hbfs[hb]
        for ch in range(4):
            ysi = cn["ys"] % 2
            cn["ys"] += 1
            ysb = ysbs[ysi]
            gs = slice(ch * 32, ch * 32 + 32)
            mi = cn["me"] % 2
            cn["me"] += 1
            S.dma("sp", MEc[mi][:, :, 0, :], SM[:, gs, :], key=("ME", mi), writes=[("ME", mi)])
            S.dma("sp", MEc[mi][:, :, 1, :], SE[:, gs, :], key=("ME", mi), writes=[("ME", mi)])
            for g4 in range(8):
                b = cn["Y"] % 2
                cn["Y"] += 1
                for j in range(4):
                    gl = g4 * 4 + j
                    g = ch * 32 + gl
                    S.op("pe", lambda e, b=b, j=j, g=g, gl=gl, mi=mi: e.matmul(
                        psY[b][0:16, j, :], lhsT=Ut[:, g, :], rhs=MEc[mi][:, gl, 0, :], start=True, stop=False),
                        reads=[("Ut", ub, ch), ("ME", mi)], writes=[("psY", b)])
                    S.op("pe", lambda e, b=b, j=j, g=g, gl=gl, mi=mi: e.matmul(
                        psY[b][0:16, j, :], lhsT=Hbf[:, g, :], rhs=MEc[mi][:, gl, 1, :], start=False, stop=True),
                        reads=[("Hbf0", hb), ("Hbf1", hb), ("ME", mi)], writes=[("psY", b)])
                S.op("act", lambda e, b=b, g4=g4, ysb=ysb: e.activation(out=ysb[:, g4 * 4:(g4 + 1) * 4, :],
                                                                        in_=psY[b][0:16, :, :], func=AF.Copy),
                     reads=[("psY", b)], writes=[("ysb", ysi)])
            dst = YS[t * P:(t + 1) * P, ch * 512:(ch + 1) * 512].rearrange("(b i) (g p) -> b i g p", i=8, p=16)
            for i_ in range(8):
                S.dma("act", dst[:, i_], ysb[:, :, i_ * 16:(i_ + 1) * 16], key=("ysb_st", ysi),
                      reads=[("ysb", ysi)], is_output=True)

    def vh_dst(c, ch):
        return VH[:, c, ch * 32:(ch + 1) * 32, 1:17]

    tiles = [(zp[t * P:(t + 1) * P, 6144:8192], t, False) for t in range(8)] + \
            [(zo[t * P:(t + 1) * P, 12288:14336], t, True) for t in range(8)]
    make_U(tiles[0][0], 0)
    make_V(vh_dst, "VH", 0)
    for idx, (src_ap, t, own) in enumerate(tiles):
        cur = idx % 2
        nxt = idx + 1 < len(tiles)
        if nxt:
            make_U(tiles[idx + 1][0], 1 - cur)
        for j in range(16):
            cstep(VH[:, 0, :, j], VH[:, 1, :, j], VH[:, :, :, j], VH[:, :, :, j + 1], "VH", (P1[:], P2[:]))
        if own:
            make_Hbf(VH, "VH", cur)
        S.op("dve", lambda e: e.tensor_copy(out=VH[:, :, :, 0], in_=VH[:, :, :, 16]), reads=["VH"], writes=["VH"])
        if nxt:
            make_V(vh_dst, "VH", 1 - cur)
        if own:
            make_Y(t, cur, cur)
    hout = A.sb("m_hout", [P, 16, H], F32)
    for c in range(2):
        S.op("pe", lambda e, c=c: e.transpose(ptr[:, c, 0:H], VH[:, c, :, 0], ident[0:H, 0:H]),
             reads=["VH", "ident"], writes=["ptr"])
    S.op("act", lambda e: e.activation(out=hout[:, 0:2, :], in_=ptr[:, 0:2, 0:H], func=AF.Copy),
         reads=["ptr"], writes=["hout"])
    S.dma("sp", s5p_out.rearrange("c g n -> g c n"), hout[:, 0:2, :], key="s5p_st", reads=["hout"],
          writes=["s5p_dram"], is_output=True)

    hraw = A.sb("m_hraw", [P, 16, H], F32)
    for c in range(2):
        S.dma("sp", hraw[:], s5in[c].rearrange("s g n -> g s n"), key="hraw", writes=["hraw"])
        for s4 in range(4):
            for j in range(4):
                s_ = s4 * 4 + j
                S.op("pe", lambda e, j=j, s_=s_: e.transpose(ptr[0:H, j, :], hraw[:, s_, :], ident[:]),
                     reads=["hraw", "ident"], writes=["ptr"])
            S.op("act", lambda e, c=c, s4=s4: e.activation(
                out=H0s[:, c, :, s4 * 4:(s4 + 1) * 4].rearrange("p g s -> p s g"), in_=ptr[0:H, :, :],
                func=AF.Copy), reads=["ptr"], writes=["H0s"])
    make_U(zo[1024:1152, 12288:14336], 0)
    make_V(lambda c, ch: Vs[:, c, ch * 32:(ch + 1) * 32, :], "Vs", 0)
    make_Hbf(H0s, "H0s", 0)
    make_Y(8, 0, 0)
    P1s = uin[0][0:H, :].rearrange("p (c g s) -> p c g s", c=2, g=P)
    P2s = uin[1][0:H, :].rearrange("p (c g s) -> p c g s", c=2, g=P)
    for hh_ in range(2):
        hs = slice(hh_ * 8, hh_ * 8 + 8)
        cstep(H0s[:, 0, :, hs], H0s[:, 1, :, hs], H0s[:, :, :, hs], Vs[:, :, :, hs], "Vs", (P1s, P2s),
              extra=["H0s"], tk=(("uin", 0), ("uin", 1), ("uin", 1)))
    for c in range(2):
        for s8 in range(2):
            for j in range(8):
                s_ = s8 * 8 + j
                S.op("pe", lambda e, c=c, j=j, s_=s_: e.transpose(
                    ptr[:, j // 2, (j % 2) * H:(j % 2 + 1) * H], Vs[:, c, :, s_], ident[0:H, 0:H]),
                    reads=["Vs", "ident"], writes=["ptr"])
            S.op("act", lambda e, s8=s8: e.activation(
                out=hout[:, s8 * 8:(s8 + 1) * 8, :], in_=ptr[:].rearrange("p a (b n) -> p (a b) n", n=H),
                func=AF.Copy), reads=["ptr"], writes=["hout"])
        S.dma("sp", s5s_out[c].rearrange("s g n -> g s n"), hout[:], key="s5s_st", reads=["hout"],
              writes=["s5s_dram"], is_output=True)


def build_program(debug=False, stages=None, scr_in=()):
    nc = bass.Bass("TRN2", target_bir_lowering=False)
    NT = NTOK_OWN // P

    def din(name, shape, dt=F32):
        return nc.dram_tensor(name, list(shape), dt, kind="ExternalInput").ap()

    def dout(name, shape, dt=F32):
        return nc.dram_tensor(name, list(shape), dt, kind="ExternalOutput").ap()

    def dscr(name, shape, dt=F32):
        kind = "ExternalOutput" if debug else "Internal"
        if name in scr_in:
            kind = "ExternalInput"
        return nc.dram_tensor(name, list(shape), dt, kind=kind).ap()

    xo = din("xo", [NTOK_OWN, D_MODEL])
    xp = din("xp", [NTOK_PRE, D_MODEL])
    mem = din("mem", [256, D_MODEL])
    w_in = din("w_in", [D_MODEL, IN_WIDTH])
    w_mem_kv = din("w_mem_kv", [D_MODEL, 4096])
    ident = din("ident", [P, P])

    memkv = dout("memkv", [256, 4096])
    zo = dscr("zo", [NTOK_OWN, IN_WIDTH])
    zp = dscr("zp", [NTOK_PRE, 8192])

    def st_mem(S, A):
        blocks = [(c0, AF.Copy, (lambda t, c0=c0: memkv[t * P:(t + 1) * P, c0:c0 + 256]))
                  for c0 in range(0, 4096, 256)]
        gemm_body(S, A, "m", mem, 2, w_mem_kv, blocks, ident)
    if stages is None or 'mem' in stages:
        run_stage(nc, st_mem)

    def st_pre(S, A):
        blocks = []
        for (src0, n, dst0) in ((2048, 2048, 0), (4096, 4096, 2048), (12288, 2048, 6144)):
            for c in range(0, n, 256):
                blocks.append((src0 + c, AF.Copy,
                               (lambda t, d=dst0 + c: zp[t * P:(t + 1) * P, d:d + 256])))
        gemm_body(S, A, "p", xp, NTOK_PRE // P, w_in, blocks, ident)
    if stages is None or 'pre' in stages:
        run_stage(nc, st_pre)

    def st_own(S, A):
        blocks = [(c0, col_func(c0), (lambda t, c0=c0: zo[t * P:(t + 1) * P, c0:c0 + 256]))
                  for c0 in range(0, IN_WIDTH, 256)]
        gemm_body(S, A, "o", xo, NTOK_OWN // P, w_in, blocks, ident)
    if stages is None or 'own' in stages:
        run_stage(nc, st_own)

    rq = din("rq", [9, P, 2, 16, 64])
    rk_own = din("rk_own", [9, P, 2, 16, 64])
    rk_pre = din("rk_pre", [8, P, 2, 16, 64])
    tabs = {"rq": rq, "rk_own": rk_own, "rk_pre": rk_pre,
            "mask_p": din("mask_p", [P, P]), "mask_s": din("mask_s", [P, P]),
            "seqm": din("seqm", [P, 16]), "seqmT": din("seqmT", [P, 16, P])}
    sret_in = din("sret_in", [16, 16, P, 256])
    sret_out = dout("sret_out", [16, 16, P, 256])
    sretp_out = dout("sretp_out", [16, P, 256])
    OT = dscr("OT", [9, P, 64, P], BF16)

    def st_ret(S, A):
        retention_body(S, A, zo, zp, tabs, sret_in, sret_out, sretp_out, OT, ident)
    if stages is None or 'ret' in stages:
        run_stage(nc, st_ret)

    cmk = din("cmk", [16, 256, 2048])
    cmv = din("cmv", [16, 256, 2048])

    def st_x(S, A):
        xattn_body(S, A, zo, memkv, cmk, cmv, tabs["seqmT"], OT, ident)
    if stages is None or 'x' in stages:
        run_stage(nc, st_x)

    prm = {"a_re": din("s5_a_re", [P, 64]), "a_im": din("s5_a_im", [P, 64]), "log_step": din("s5_log_step", [1, P]),
           "b_re": din("s5_b_re", [P, 1024]), "b_im": din("s5_b_im", [P, 1024]),
           "c_re": din("s5_c_re", [P, 1024]), "c_im": din("s5_c_im", [P, 1024])}
    maskM = din("maskM", [P, P])
    SM = dscr("SM", [P, P, P], BF16)
    SG = dscr("SG", [P, P, P], BF16)
    SE = dscr("SE", [P, P, P], BF16)
    A8S = dscr("A8S", [64, 2, P])

    def st_s5prep(S, A):
        s5prep_body(S, A, prm, ident, maskM, SM, SG, SE, A8S)
    if stages is None or 's5prep' in stages:
        run_stage(nc, st_s5prep)

    selm = din("selm", [P, 8])
    s5in = din("s5in", [2, 16, P, 64])
    YS = dscr("YS", [NTOK_OWN, 2048])
    s5p_out = dout("s5p_out", [2, P, 64])
    s5s_out = dout("s5s_out", [2, 16, P, 64])

    def st_s5main(S, A):
        s5main_body(S, A, zo, zp, SM, SG, SE, A8S, selm, tabs["seqm"], ident, s5in, YS, s5p_out, s5s_out)
    if stages is None or 's5main' in stages:
        run_stage(nc, st_s5main)

    s5d = din("s5_d", [1, 2048])
    w_glu = din("w_glu", [2048, 4096])
    GL = dscr("GL", [NTOK_OWN, 2048])
    GAB = dscr("GAB", [NTOK_OWN, 4096])

    def st_gelu(S, A):
        db = A.sb("g_db", [P, 2048], F32)
        S.dma("sp", db[:], s5d.to_broadcast([P, 2048]), key="db", writes=["db"])
        yb = [A.sb("g_y%d" % i, [P, 2048], F32) for i in range(2)]
        ub = [A.sb("g_u%d" % i, [P, 2048], F32) for i in range(2)]
        tb = [A.sb("g_t%d" % i, [P, 2048], F32) for i in range(2)]
        for t in range(NT):
            i = t % 2
            y, u, tt_ = yb[i], ub[i], tb[i]
            S.dma("sp", y[:], YS[t * P:(t + 1) * P, :], key=("y", i), writes=[("y", i)])
            S.dma("sp", u[:], zo[t * P:(t + 1) * P, 12288:14336], key=("u", i), writes=[("u", i)])
            S.op("dve", lambda e, u=u: e.tensor_tensor(out=u[:], in0=u[:], in1=db[:], op=ALU.mult),
                 reads=[("u", i), "db"], writes=[("u", i)])
            S.op("dve", lambda e, y=y, u=u: e.tensor_tensor(out=y[:], in0=y[:], in1=u[:], op=ALU.add),
                 reads=[("y", i), ("u", i)], writes=[("y", i)])
            S.op("act", lambda e, y=y, tt_=tt_: e.activation(out=tt_[:], in_=y[:], func=AF.Square),
                 reads=[("y", i)], writes=[("t", i)])
            S.op("dve", lambda e, tt_=tt_: e.tensor_scalar(out=tt_[:], in0=tt_[:], scalar1=0.044715, scalar2=1.0,
                                                           op0=ALU.mult, op1=ALU.add),
                 reads=[("t", i)], writes=[("t", i)])
            S.op("dve", lambda e, y=y, tt_=tt_: e.tensor_tensor(out=tt_[:], in0=tt_[:], in1=y[:], op=ALU.mult),
                 reads=[("t", i), ("y", i)], writes=[("t", i)])
            S.op("act", lambda e, tt_=tt_: e.activation(out=tt_[:], in_=tt_[:], func=AF.Sigmoid,
                                                        scale=1.5957691216057308),
                 reads=[("t", i)], writes=[("t", i)])
            S.op("dve", lambda e, y=y, tt_=tt_: e.tensor_tensor(out=y[:], in0=y[:], in1=tt_[:], op=ALU.mult),
                 reads=[("t", i), ("y", i)], writes=[("y", i)])
            S.dma("sp", GL[t * P:(t + 1) * P, :], y[:], key=("gl", i), reads=[("y", i)], is_output=True)

    def st_glu(S, A):
        blocks = [(c0, (AF.Copy if c0 < 2048 else AF.Sigmoid),
                   (lambda t, c0=c0: GAB[t * P:(t + 1) * P, c0:c0 + 256])) for c0 in range(0, 4096, 256)]
        gemm_body(S, A, "g", GL, NT, w_glu, blocks, ident, nkt=16)

    def st_s5fin(S, A):
        identf = A.sb("f_ident", [P, P], F32)
        identb = A.sb("f_identb", [P, P], BF16)
        S.dma("sp", identf[:], ident, key="ident", writes=["ident"])
        S.op("dve", lambda e: e.tensor_copy(out=identb[:], in_=identf[:]), reads=["ident"], writes=["identb"])
        ab = [A.sb("f_ab%d" % i, [P, 4096], F32) for i in range(2)]
        gg = [A.sb("f_g%d" % i, [P, 2048], F32) for i in range(2)]
        ob = [A.sb("f_ob%d" % i, [P, 16, P], BF16) for i in range(2)]
        oT = [A.sb("f_oT%d" % i, [P, 16, P], BF16) for i in range(2)]
        ptr = [A.ps("f_ptr%d" % i, [P, 8, P], BF16) for i in range(2)]
        cnt = 0
        for t in range(NT):
            i = t % 2
            S.dma("sp", ab[i][:], GAB[t * P:(t + 1) * P, :], key=("ab", i), writes=[("ab", i)])
            S.dma("sp", gg[i][:], zo[t * P:(t + 1) * P, 14336:16384], key=("gg", i), writes=[("gg", i)])
            S.op("dve", lambda e, i=i: e.tensor_tensor(out=gg[i][:], in0=gg[i][:], in1=ab[i][:, 2048:4096],
                                                       op=ALU.mult),
                 reads=[("gg", i), ("ab", i)], writes=[("gg", i)])
            S.op("dve", lambda e, i=i: e.tensor_tensor(out=ob[i][:].rearrange("p a b -> p (a b)"),
                                                       in0=ab[i][:, 0:2048], in1=gg[i][:], op=ALU.mult),
                 reads=[("gg", i), ("ab", i)], writes=[("ob", i)])
            for half in range(2):
                b = cnt % 2
                cnt += 1
                for j in range(8):
                    S.op("pe", lambda e, b=b, j=j, i=i, half=half: e.transpose(
                        ptr[b][:, j, :], ob[i][:, half * 8 + j, :], identb[:]),
                        reads=[("ob", i), "identb"], writes=[("ptr", b)])
                S.op("act", lambda e, b=b, i=i, half=half: e.activation(
                    out=oT[i][:, half * 8:(half + 1) * 8, :], in_=ptr[b][:], func=AF.Copy),
                    reads=[("ptr", b)], writes=[("oT", i, half)])
            S.dma("act", OT[t, :, 32:48, :], oT[i][:], key=("oTs", i), reads=[("oT", i, 0), ("oT", i, 1)],
                  is_output=True)
    if stages is None or 's5post' in stages:
        run_stage(nc, st_gelu)
        run_stage(nc, st_glu)
        run_stage(nc, st_s5fin)

    w_pa = din("w_proj_a", [4096, D_MODEL])
    w_pb = din("w_proj_b", [2048, D_MODEL])
    w_pc = din("w_proj_c", [2048, D_MODEL])
    w_o = din("w_out", [D_MODEL, D_MODEL])
    ln_g = din("ln_g", [1, D_MODEL])
    ln_b = din("ln_b", [1, D_MODEL])
    y_out = dout("y_out", [NTOK_OWN, D_MODEL])
    PR = [dscr("PR%d" % i, [NTOK_OWN, D_MODEL]) for i in range(3)]
    HP = dscr("HP", [NTOK_OWN, D_MODEL])
    NT = NTOK_OWN // P

    def proj_stage(i, w_ap, ft0, nkt, gate0):
        def body(S, A):
            gt = [A.sb("gt%d_%d" % (i, j), [P, NT, 256], F32) for j in range(2)]

            def pre_block(S, bi):
                c0 = bi * 256
                S.dma("sp", gt[bi % 2][:], zo[:, gate0 + c0:gate0 + c0 + 256].rearrange("(t p) c -> p t c", p=P),
                      key=("gt", bi % 2), writes=[("gt", bi % 2)])

            def epi(S, t, bi, ps_ap, pkey, ob_t, okey):
                j = bi % 2
                S.op("dve", lambda e, j=j, t=t: e.tensor_tensor(out=ob_t[:], in0=ps_ap, in1=gt[j][:, t, :],
                                                                op=ALU.mult),
                     reads=[pkey, ("gt", j)], writes=[okey])
            blocks = [(c0, None, (lambda t, c0=c0: PR[i][t * P:(t + 1) * P, c0:c0 + 256]))
                      for c0 in range(0, D_MODEL, 256)]
            gemm_body(S, A, "j%d" % i, None, NT, w_ap, blocks, ident, nkt=nkt, a_T=(OT, ft0), epi=epi,
                      pre_block=pre_block)
        return body
    if stages is None or 'tail' in stages:
        run_stage(nc, proj_stage(0, w_pa, 0, 32, 20480))
        run_stage(nc, proj_stage(1, w_pb, 32, 16, 24576))
        run_stage(nc, proj_stage(2, w_pc, 48, 16, 28672))

    def st_out(S, A):
        xr = [A.sb("xr%d" % j, [P, NT, 256], F32) for j in range(2)]
        alpha = float((2.0 * 1) ** 0.25)

        def pre_block(S, bi):
            c0 = bi * 256
            S.dma("sp", xr[bi % 2][:], xo[:, c0:c0 + 256].rearrange("(t p) c -> p t c", p=P),
                  key=("xr", bi % 2), writes=[("xr", bi % 2)])

        def epi(S, t, bi, ps_ap, pkey, ob_t, okey):
            j = bi % 2
            S.op("dve", lambda e, j=j, t=t: e.scalar_tensor_tensor(out=ob_t[:], in0=xr[j][:, t, :], scalar=alpha,
                                                                   in1=ps_ap, op0=ALU.mult, op1=ALU.add),
                 reads=[pkey, ("xr", j)], writes=[okey])
        blocks = [(c0, None, (lambda t, c0=c0: HP[t * P:(t + 1) * P, c0:c0 + 256]))
                  for c0 in range(0, D_MODEL, 256)]
        gemm_body(S, A, "w", PR[0], NT, w_o, blocks, ident, x_sum=[PR[1], PR[2]], epi=epi, pre_block=pre_block)
    if stages is None or 'tail' in stages or 'out' in stages:
        run_stage(nc, st_out)

    def st_ln(S, A):
        gb = A.sb("ln_gb", [P, D_MODEL], F32)
        bb = A.sb("ln_bb", [P, D_MODEL], F32)
        S.dma("sp", gb[:], ln_g.to_broadcast([P, D_MODEL]), key="gb", writes=["gb"])
        S.dma("sp", bb[:], ln_b.to_broadcast([P, D_MODEL]), key="bb", writes=["bb"])
        hb = [A.sb("ln_h%d" % j, [P, D_MODEL], F32) for j in range(2)]
        st = A.sb("ln_st", [P, 8, 6], F32)
        mv = A.sb("ln_mv", [P, 2], F32)
        nb = A.sb("ln_nb", [P, 1], F32)
        for t in range(NT):
            j = t % 2
            h = hb[j]
            S.dma("sp", h[:], HP[t * P:(t + 1) * P, :], key=("h", j), writes=[("h", j)])
            for c in range(8):
                S.op("dve", lambda e, c=c, h=h: e.bn_stats(out=st[:, c, :], in_=h[:, c * 512:(c + 1) * 512]),
                     reads=[("h", j)], writes=[("st", c)])
            S.op("dve", lambda e: e.bn_aggr(out=mv[:], in_=st[:].rearrange("p a b -> p (a b)")),
                 reads=[("st", c) for c in range(8)], writes=["mv"])
            S.op("dve", lambda e: e.tensor_scalar(out=mv[:, 1:2], in0=mv[:, 1:2], scalar1=1e-5, scalar2=None,
                                                  op0=ALU.add), reads=["mv"], writes=["mv"])
            S.op("act", lambda e: e.activation(out=mv[:, 1:2], in_=mv[:, 1:2], func=AF.Sqrt),
                 reads=["mv"], writes=["mv"])
            S.op("dve", lambda e: e.reciprocal(out=mv[:, 1:2], in_=mv[:, 1:2]), reads=["mv"], writes=["mv"])
            S.op("dve", lambda e: e.scalar_tensor_tensor(out=nb[:], in0=mv[:, 0:1], scalar=-1.0, in1=mv[:, 1:2],
                                                         op0=ALU.mult, op1=ALU.mult),
                 reads=["mv"], writes=["nb"])
            S.op("act", lambda e, h=h: e.activation(out=h[:], in_=h[:], func=AF.Identity, bias=nb[:],
                                                    scale=mv[:, 1:2]),
                 reads=[("h", j), "mv", "nb"], writes=[("h", j)])
            S.op("dve", lambda e, h=h: e.tensor_tensor(out=h[:], in0=h[:], in1=gb[:], op=ALU.mult),
                 reads=[("h", j), "gb"], writes=[("h", j)])
            S.op("dve", lambda e, h=h: e.tensor_tensor(out=h[:], in0=h[:], in1=bb[:], op=ALU.add),
                 reads=[("h", j), "bb"], writes=[("h", j)])
            S.dma("sp", y_out[t * P:(t + 1) * P, :], h[:], key=("hout", j), reads=[("h", j)], is_output=True)
    if stages is None or 'tail' in stages or 'ln' in stages:
        run_stage(nc, st_ln)

    return nc


def host_tables(hf):
    f = np.float32
    inv = (1.0 / (np.float32(10000.0) ** (np.arange(64, dtype=f) / np.float32(64)))).astype(f)
    g = np.array(RET_G, dtype=np.float64)
    i = np.arange(P)

    def tab(pos, il, kind):
        ang = pos.astype(f)[:, None] * inv[None, :]
        c, s_ = np.cos(ang).astype(f), np.sin(ang).astype(f)
        if kind == "q":
            sc = g[None, :] ** (il[:, None] + 1.0)
        else:
            sc = g[None, :] ** (-(il[:, None] + 1.0)) * (128.0 ** -0.5)
        out = np.empty((P, 2, 16, 64), f)
        out[:, 0] = (c[:, None, :] * sc[:, :, None]).astype(f)
        out[:, 1] = (s_[:, None, :] * sc[:, :, None]).astype(f)
        return out
    rq = np.stack([tab(hf * 1024 + t * P + i, i, "q") for t in range(8)] + [tab(16384 + (i % 8), i % 8, "q")])
    rk_own = np.stack([tab(hf * 1024 + t * P + i, i, "k") for t in range(8)] + [tab(16384 + (i % 8), i % 8, "k")])
    rk_pre = np.stack([tab(t * P + i, i, "k") for t in range(8)])
    mask_p = (i[None, :] >= i[:, None]).astype(f)
    same = (i[None, :] // 8) == (i[:, None] // 8)
    mask_s = (mask_p * same).astype(f)
    seqm = (i[:, None] // 8 == np.arange(16)[None, :]).astype(f)
    seqmT = np.ascontiguousarray(np.broadcast_to(seqm.T[None, :, :], (P, 16, P))).astype(f)
    maskM = ((i[None, :] // 16) >= (i[:, None] // 16)).astype(f)
    selm = (i[:, None] % 8 == np.arange(8)[None, :]).astype(f)
    return {"rq": rq, "rk_own": rk_own, "rk_pre": rk_pre, "mask_p": mask_p, "mask_s": mask_s,
            "seqm": seqm, "seqmT": seqmT, "maskM": maskM, "selm": selm}


_PROGRAM = None


def kernel(x_prompt, x_sample, mem_prompt, state_ret, state_s5_re, state_s5_im, cache_mem_k, cache_mem_v,
           w_in, w_mem_kv, s5_a_re, s5_a_im, s5_log_step, s5_b_re, s5_b_im, s5_c_re, s5_c_im, s5_d, w_glu,
           w_proj_a, w_proj_b, w_proj_c, w_out, ln_g, ln_b):
    global _PROGRAM
    if _PROGRAM is None:
        _PROGRAM = build_program()
    nc = _PROGRAM
    f = np.float32
    x_prompt = np.asarray(x_prompt, f)
    x_sample = np.asarray(x_sample, f)
    w_in0 = np.ascontiguousarray(np.asarray(w_in, f)[0])
    w_mem0 = np.ascontiguousarray(np.asarray(w_mem_kv, f)[0])
    ident = np.eye(P, dtype=f)
    wpa = np.ascontiguousarray(np.asarray(w_proj_a, f)[0])
    wpb = np.ascontiguousarray(np.asarray(w_proj_b, f)[0])
    wpc = np.ascontiguousarray(np.asarray(w_proj_c, f)[0])
    wout = np.ascontiguousarray(np.asarray(w_out, f)[0])
    lng = np.ascontiguousarray(np.asarray(ln_g, f).reshape(1, D_MODEL))
    lnb = np.ascontiguousarray(np.asarray(ln_b, f).reshape(1, D_MODEL))
    s5p = {"a_re": np.ascontiguousarray(np.asarray(s5_a_re, f)[0]), "a_im": np.ascontiguousarray(np.asarray(s5_a_im, f)[0]),
           "log_step": np.ascontiguousarray(np.asarray(s5_log_step, f).reshape(1, P)),
           "b_re": np.ascontiguousarray(np.asarray(s5_b_re, f)[0].reshape(P, 1024)),
           "b_im": np.ascontiguousarray(np.asarray(s5_b_im, f)[0].reshape(P, 1024)),
           "c_re": np.ascontiguousarray(np.asarray(s5_c_re, f)[0].reshape(P, 1024)),
           "c_im": np.ascontiguousarray(np.asarray(s5_c_im, f)[0].reshape(P, 1024)),
           "d": np.ascontiguousarray(np.asarray(s5_d, f).reshape(1, 2048))}
    wglu = np.ascontiguousarray(np.asarray(w_glu, f)[0])
    in_maps = []
    for c in range(NCORES):
        b, hf = c // 2, c % 2
        xo = np.concatenate([x_prompt[b, hf * 1024:(hf + 1) * 1024],
                             x_sample[16 * c:16 * c + 16].reshape(128, D_MODEL)], axis=0)
        xp = x_prompt[b, 0:1024] if hf == 1 else np.zeros((1024, D_MODEL), f)
        in_maps.append({
            "xo": np.ascontiguousarray(xo), "xp": np.ascontiguousarray(xp),
            "mem": np.ascontiguousarray(np.asarray(mem_prompt, f)[b]),
            "w_in": w_in0, "w_mem_kv": w_mem0, "ident": ident,
            "sret_in": np.ascontiguousarray(np.asarray(state_ret, f)[0, 16 * c:16 * c + 16]),
            "cmk": np.ascontiguousarray(np.asarray(cache_mem_k, f)[0, 16 * c:16 * c + 16]).reshape(16, 256, 2048),
            "cmv": np.ascontiguousarray(np.asarray(cache_mem_v, f)[0, 16 * c:16 * c + 16]).reshape(16, 256, 2048),
            "w_proj_a": wpa, "w_proj_b": wpb, "w_proj_c": wpc, "w_out": wout, "ln_g": lng, "ln_b": lnb,
            "s5_a_re": s5p["a_re"], "s5_a_im": s5p["a_im"], "s5_log_step": s5p["log_step"],
            "s5_b_re": s5p["b_re"], "s5_b_im": s5p["b_im"], "s5_c_re": s5p["c_re"], "s5_c_im": s5p["c_im"],
            "s5_d": s5p["d"], "w_glu": wglu,
            "s5in": np.ascontiguousarray(np.stack([np.asarray(state_s5_re, f)[0, 16 * c:16 * c + 16],
                                                   np.asarray(state_s5_im, f)[0, 16 * c:16 * c + 16]])),
        })
        in_maps[-1].update(host_tables(hf))
    res = run_bass_kernel_spmd(nc, in_maps, core_ids=list(range(NCORES)))
    R = res.results
    memk = np.stack([R[2 * b]["memkv"][:, 0:2048].reshape(256, 4, 512) for b in range(4)])[None]
    memv = np.stack([R[2 * b]["memkv"][:, 2048:4096].reshape(256, 4, 512) for b in range(4)])[None]
    y_p = np.stack([np.concatenate([R[2 * b]["y_out"][:1024], R[2 * b + 1]["y_out"][:1024]], axis=0)
                    for b in range(4)])
    y_s = np.concatenate([R[c]["y_out"][1024:] for c in range(NCORES)], axis=0).reshape(128, 8, D_MODEL)
    sretp = np.stack([R[2 * b + 1]["sretp_out"] for b in range(4)])[None]
    srets = np.concatenate([R[c]["sret_out"] for c in range(NCORES)], axis=0)[None]
    s5p_re = np.stack([R[2 * b + 1]["s5p_out"][0] for b in range(4)])[None]
    s5p_im = np.stack([R[2 * b + 1]["s5p_out"][1] for b in range(4)])[None]
    s5s_re = np.concatenate([R[c]["s5s_out"][0] for c in range(NCORES)], axis=0)[None]
    s5s_im = np.concatenate([R[c]["s5s_out"][1] for c in range(NCORES)], axis=0)[None]
    return (y_p, y_s, sretp, s5p_re, s5p_im, memk, memv, srets, s5s_re, s5s_im)
```
